# Optimizing a Trainium2 kernel written in Bass

```python
import jax, jax.numpy as jnp
from jax import lax
import numpy as np

D_MODEL = 1024
BATCH = 2
SEQ = 16384
DEPTH = 2

N_MIXERS = 2
ALPHA = (2 * DEPTH) ** 0.25
BETA = (8 * DEPTH) ** -0.25
LN_EPS = 1e-5
D_FF = 2816
GM_WIDTH = 3 * D_MODEL
GM_CHUNK = 128
GM_GROUPS = 16
GM_GCH = GM_WIDTH // GM_GROUPS
HEAD_DIM = 64
N_HEADS = D_MODEL // HEAD_DIM
N_KV = 4
HPG = N_HEADS // N_KV
CMP_BLOCK = 32
CMP_STRIDE = 16
CMP_HIDDEN = 256
SLC_BLOCK = 64
SLC_TOP = 16
WINDOW = 512
Q_BLOCK = 128
ROPE_THETA = 10000.0
NEG = -1e30
NSA_COLS = N_HEADS * HEAD_DIM + 6 * N_KV * HEAD_DIM + 3 * N_HEADS

kernel_name = 'hybrid_gmlp_nsa_macaron_deepnorm'


def layer_norm(x, g, b):
    xf = x.astype(jnp.float32)
    mu = xf.mean(-1, keepdims=True)
    var = jnp.square(xf - mu).mean(-1, keepdims=True)
    return ((xf - mu) * lax.rsqrt(var + LN_EPS) * g.astype(jnp.float32) + b.astype(jnp.float32)).astype(x.dtype)


def swiglu(x, w_in, w_out):
    gate, up = jnp.split(x @ w_in, 2, axis=-1)
    return (jax.nn.silu(gate) * up) @ w_out


def rope(x, pos):
    half = x.shape[-1] // 2
    freq = ROPE_THETA ** (-jnp.arange(half, dtype=jnp.float32) / half)
    ang = pos.astype(jnp.float32)[:, None] * freq
    shape = (pos.shape[0],) + (1,) * (x.ndim - 3) + (half,)
    cos = jnp.cos(ang).reshape(shape)
    sin = jnp.sin(ang).reshape(shape)
    xf = x.astype(jnp.float32)
    x1, x2 = xf[..., :half], xf[..., half:]
    return jnp.concatenate([x1 * cos - x2 * sin, x2 * cos + x1 * sin], axis=-1).astype(x.dtype)


def masked_softmax(s, mask):
    p = jax.nn.softmax(jnp.where(mask, s, NEG), axis=-1)
    return p * jnp.any(mask, axis=-1, keepdims=True)


def chunked_gmlp(x, w_in, ln_g, ln_b, w_s, b_s, w_out):
    B_, S_, _ = x.shape
    u, v = jnp.split(jax.nn.gelu(x @ w_in), 2, axis=-1)
    v = layer_norm(v, ln_g, ln_b)
    v = v.reshape(B_, S_ // GM_CHUNK, GM_CHUNK, GM_GROUPS, GM_GCH)
    causal = jnp.tril(jnp.ones((GM_CHUNK, GM_CHUNK), dtype=bool))
    w = jnp.where(causal, w_s, 0)
    v = jnp.einsum('gts,bnsgc->bntgc', w, v) + b_s.T[:, :, None]
    return (u * v.reshape(B_, S_, GM_WIDTH)) @ w_out


def nsa(x, w_in, cmp_pe_k, cmp_w1_k, cmp_w2_k, cmp_pe_v, cmp_w1_v, cmp_w2_v, w_out):
    B_, S_, _ = x.shape
    dt = x.dtype
    kvw = N_KV * HEAD_DIM
    splits = np.cumsum([N_HEADS * HEAD_DIM] + [kvw] * 6).tolist()
    q, k_c, v_c, k_s, v_s, k_w, v_w, g = jnp.split(x @ w_in, splits, axis=-1)
    kv = lambda t: t.reshape(B_, S_, N_KV, HEAD_DIM)
    pos = jnp.arange(S_)
    q = rope(q.reshape(B_, S_, N_KV, HPG, HEAD_DIM), pos)
    k_s = rope(kv(k_s), pos)
    k_w = rope(kv(k_w), pos)
    v_s, v_w = kv(v_s), kv(v_w)
    gates = jax.nn.sigmoid(g.astype(jnp.float32)).reshape(B_, S_, N_KV, HPG, 3).astype(dt)

    n_cmp = S_ // CMP_STRIDE - 1

    def compress(t, pe, w1, w2):
        ch = kv(t).reshape(B_, S_ // CMP_STRIDE, CMP_STRIDE, N_KV, HEAD_DIM)
        blk = jnp.concatenate([ch[:, :-1], ch[:, 1:]], axis=2) + pe[:, None, :]
        h = jax.nn.gelu(jnp.einsum('bnlgd,ldh->bngh', blk, w1))
        return h @ w2

    cmp_end = CMP_STRIDE * jnp.arange(n_cmp) + CMP_BLOCK - 1
    k_cmp = rope(compress(k_c, cmp_pe_k, cmp_w1_k, cmp_w2_k), cmp_end)
    v_cmp = compress(v_c, cmp_pe_v, cmp_w1_v, cmp_w2_v)

    n_slc = S_ // SLC_BLOCK
    n_top = min(SLC_TOP, n_slc)
    r = SLC_BLOCK // CMP_STRIDE
    k_sb = k_s.reshape(B_, n_slc, SLC_BLOCK, N_KV, HEAD_DIM)
    v_sb = v_s.reshape(B_, n_slc, SLC_BLOCK, N_KV, HEAD_DIM)
    k_wp = jnp.pad(k_w, ((0, 0), (WINDOW, 0), (0, 0), (0, 0)))
    v_wp = jnp.pad(v_w, ((0, 0), (WINDOW, 0), (0, 0), (0, 0)))
    scale = HEAD_DIM ** -0.5
    bi = jnp.arange(B_)[:, None, None, None]
    gi = jnp.arange(N_KV)[None, None, :, None]
    slc_ids = jnp.arange(n_slc)[None, :]
    n_qb = S_ // Q_BLOCK

    def block(args):
        qb_idx, qb, gb = args
        t = qb_idx * Q_BLOCK + jnp.arange(Q_BLOCK)
        s = jnp.einsum('bqghd,bngd->bqghn', qb, k_cmp).astype(jnp.float32) * scale
        p_cmp = masked_softmax(s, (cmp_end[None, :] <= t[:, None])[None, :, None, None, :])
        o_cmp = jnp.einsum('bqghn,bngd->bqghd', p_cmp.astype(dt), v_cmp)
        imp = jnp.pad(p_cmp.sum(axis=3), ((0, 0), (0, 0), (0, 0), (1, 1)))
        imp = imp[..., :r * n_slc].reshape(B_, Q_BLOCK, N_KV, n_slc, r).sum(-1) + imp[..., r::r]
        cur = (t // SLC_BLOCK)[:, None]
        forced = (slc_ids == 0) | (slc_ids == cur) | (slc_ids == cur - 1)
        imp = jnp.where(forced[:, None, :], 1e9, jnp.where((slc_ids <= cur)[:, None, :], imp, -1e9))
        _, idx = lax.top_k(imp, n_top)
        k_sel = k_sb[bi, idx, :, gi, :]
        v_sel = v_sb[bi, idx, :, gi, :].reshape(B_, Q_BLOCK, N_KV, n_top * SLC_BLOCK, HEAD_DIM)
        key_pos = idx[..., None] * SLC_BLOCK + jnp.arange(SLC_BLOCK)
        m_sel = (key_pos <= t[None, :, None, None, None]).reshape(B_, Q_BLOCK, N_KV, 1, n_top * SLC_BLOCK)
        s = jnp.einsum('bqghd,bqgnld->bqghnl', qb, k_sel).astype(jnp.float32)
        s = s.reshape(B_, Q_BLOCK, N_KV, HPG, n_top * SLC_BLOCK) * scale
        o_slc = jnp.einsum('bqghm,bqgmd->bqghd', masked_softmax(s, m_sel).astype(dt), v_sel)
        k_win = lax.dynamic_slice_in_dim(k_wp, qb_idx * Q_BLOCK, WINDOW + Q_BLOCK, axis=1)
        v_win = lax.dynamic_slice_in_dim(v_wp, qb_idx * Q_BLOCK, WINDOW + Q_BLOCK, axis=1)
        s_pos = (qb_idx * Q_BLOCK - WINDOW + jnp.arange(WINDOW + Q_BLOCK))[None, :]
        m_win = (s_pos <= t[:, None]) & (s_pos > t[:, None] - WINDOW) & (s_pos >= 0)
        s = jnp.einsum('bqghd,bkgd->bqghk', qb, k_win).astype(jnp.float32) * scale
        o_win = jnp.einsum('bqghk,bkgd->bqghd', masked_softmax(s, m_win[None, :, None, None, :]).astype(dt), v_win)
        return gb[..., 0:1] * o_cmp + gb[..., 1:2] * o_slc + gb[..., 2:3] * o_win

    q_blocks = q.reshape(B_, n_qb, Q_BLOCK, N_KV, HPG, HEAD_DIM).swapaxes(0, 1)
    g_blocks = gates.reshape(B_, n_qb, Q_BLOCK, N_KV, HPG, 3).swapaxes(0, 1)
    o = lax.map(block, (jnp.arange(n_qb), q_blocks, g_blocks))
    return o.swapaxes(0, 1).reshape(B_, S_, N_HEADS * HEAD_DIM) @ w_out


def _normal(key, shape, scale):
    return jax.random.normal(key, shape, jnp.float32) * scale


def _ln(key, n):
    kg, kb = jax.random.split(key)
    return 1.0 + _normal(kg, (n,), 0.01), _normal(kb, (n,), 0.01)


def _layer(key, pre, mixer):
    ks = jax.random.split(key, 20)
    p = {}
    p[pre + 'ffn1_w_in'] = _normal(ks[0], (D_MODEL, 2 * D_FF), D_MODEL ** -0.5)
    p[pre + 'ffn1_w_out'] = _normal(ks[1], (D_FF, D_MODEL), BETA * D_FF ** -0.5)
    p[pre + 'ln1_g'], p[pre + 'ln1_b'] = _ln(ks[2], D_MODEL)
    if mixer == 0:
        p[pre + 'gm_w_in'] = _normal(ks[3], (D_MODEL, 2 * GM_WIDTH), D_MODEL ** -0.5)
        p[pre + 'gm_ln_g'], p[pre + 'gm_ln_b'] = _ln(ks[4], GM_WIDTH)
        p[pre + 'gm_w_s'] = _normal(ks[5], (GM_GROUPS, GM_CHUNK, GM_CHUNK), GM_CHUNK ** -0.5)
        p[pre + 'gm_b_s'] = 1.0 + _normal(ks[6], (GM_GROUPS, GM_CHUNK), 0.1)
        p[pre + 'gm_w_out'] = _normal(ks[7], (GM_WIDTH, D_MODEL), BETA * GM_WIDTH ** -0.5)
    else:
        p[pre + 'nsa_w_in'] = _normal(ks[3], (D_MODEL, NSA_COLS), D_MODEL ** -0.5)
        p[pre + 'nsa_cmp_pe_k'] = _normal(ks[4], (CMP_BLOCK, HEAD_DIM), 0.1)
        p[pre + 'nsa_cmp_w1_k'] = _normal(ks[5], (CMP_BLOCK, HEAD_DIM, CMP_HIDDEN), (CMP_BLOCK * HEAD_DIM) ** -0.5)
        p[pre + 'nsa_cmp_w2_k'] = _normal(ks[6], (CMP_HIDDEN, HEAD_DIM), CMP_HIDDEN ** -0.5)
        p[pre + 'nsa_cmp_pe_v'] = _normal(ks[7], (CMP_BLOCK, HEAD_DIM), 0.1)
        p[pre + 'nsa_cmp_w1_v'] = _normal(ks[8], (CMP_BLOCK, HEAD_DIM, CMP_HIDDEN), (CMP_BLOCK * HEAD_DIM) ** -0.5)
        p[pre + 'nsa_cmp_w2_v'] = _normal(ks[9], (CMP_HIDDEN, HEAD_DIM), CMP_HIDDEN ** -0.5)
        p[pre + 'nsa_w_out'] = _normal(ks[10], (N_HEADS * HEAD_DIM, D_MODEL), BETA * (N_HEADS * HEAD_DIM) ** -0.5)
    p[pre + 'ln2_g'], p[pre + 'ln2_b'] = _ln(ks[11], D_MODEL)
    p[pre + 'ffn2_w_in'] = _normal(ks[12], (D_MODEL, 2 * D_FF), D_MODEL ** -0.5)
    p[pre + 'ffn2_w_out'] = _normal(ks[13], (D_FF, D_MODEL), BETA * D_FF ** -0.5)
    p[pre + 'ln3_g'], p[pre + 'ln3_b'] = _ln(ks[14], D_MODEL)
    return p


def setup_inputs(seed: int = 0) -> dict:
    key = jax.random.key(seed)
    kx, k0, k1 = jax.random.split(key, 3)
    out = {'x': jax.random.normal(kx, (BATCH, SEQ, D_MODEL), jnp.float32)}
    out.update(_layer(k0, 'l0_', 0))
    out.update(_layer(k1, 'l1_', 1))
    return out


def reference(x, l0_ffn1_w_in, l0_ffn1_w_out, l0_ln1_g, l0_ln1_b, l0_gm_w_in, l0_gm_ln_g, l0_gm_ln_b, l0_gm_w_s, l0_gm_b_s, l0_gm_w_out, l0_ln2_g, l0_ln2_b, l0_ffn2_w_in, l0_ffn2_w_out, l0_ln3_g, l0_ln3_b, l1_ffn1_w_in, l1_ffn1_w_out, l1_ln1_g, l1_ln1_b, l1_nsa_w_in, l1_nsa_cmp_pe_k, l1_nsa_cmp_w1_k, l1_nsa_cmp_w2_k, l1_nsa_cmp_pe_v, l1_nsa_cmp_w1_v, l1_nsa_cmp_w2_v, l1_nsa_w_out, l1_ln2_g, l1_ln2_b, l1_ffn2_w_in, l1_ffn2_w_out, l1_ln3_g, l1_ln3_b):
    layers = (
        dict(ffn1=(l0_ffn1_w_in, l0_ffn1_w_out), ln1=(l0_ln1_g, l0_ln1_b),
             mixer=(l0_gm_w_in, l0_gm_ln_g, l0_gm_ln_b, l0_gm_w_s, l0_gm_b_s, l0_gm_w_out),
             ln2=(l0_ln2_g, l0_ln2_b), ffn2=(l0_ffn2_w_in, l0_ffn2_w_out), ln3=(l0_ln3_g, l0_ln3_b)),
        dict(ffn1=(l1_ffn1_w_in, l1_ffn1_w_out), ln1=(l1_ln1_g, l1_ln1_b),
             mixer=(l1_nsa_w_in, l1_nsa_cmp_pe_k, l1_nsa_cmp_w1_k, l1_nsa_cmp_w2_k,
                    l1_nsa_cmp_pe_v, l1_nsa_cmp_w1_v, l1_nsa_cmp_w2_v, l1_nsa_w_out),
             ln2=(l1_ln2_g, l1_ln2_b), ffn2=(l1_ffn2_w_in, l1_ffn2_w_out), ln3=(l1_ln3_g, l1_ln3_b)),
    )
    for i in range(DEPTH):
        p = layers[i]
        x = layer_norm(ALPHA * x + 0.5 * swiglu(x, *p['ffn1']), *p['ln1'])
        if i % N_MIXERS == 0:
            mix = chunked_gmlp(x, *p['mixer'])
        else:
            mix = nsa(x, *p['mixer'])
        x = layer_norm(ALPHA * x + mix, *p['ln2'])
        x = layer_norm(ALPHA * x + 0.5 * swiglu(x, *p['ffn2']), *p['ln3'])
    return x
```

```python
import math
from contextlib import ExitStack

import numpy as np
import concourse.bass as bass
import concourse.mybir as mybir
from concourse.bass_utils import run_bass_kernel_spmd

F32 = mybir.dt.float32
BF16 = mybir.dt.bfloat16
AF = mybir.ActivationFunctionType
ALU = mybir.AluOpType
AX = mybir.AxisListType

D = 1024
DFF = 2816
DEPTH = 2
ALPHA = (2 * DEPTH) ** 0.25
LN_EPS = 1e-5
NCORES = 8


class Buf:
    __slots__ = ("name", "writers", "dma_writers", "readers", "dma_readers")

    def __init__(self, name):
        self.name = name
        self.writers = {}
        self.dma_writers = []
        self.readers = {}
        self.dma_readers = []


class Op:
    __slots__ = ("eng", "fn", "deps", "is_dma", "dsem", "dcount", "signal", "sigval", "emitted")

    def __init__(self, eng, fn, is_dma):
        self.eng = eng
        self.fn = fn
        self.deps = []
        self.is_dma = is_dma
        self.dsem = None
        self.dcount = 0
        self.signal = False
        self.sigval = 0
        self.emitted = False


class Sched:
    def __init__(self, nc, stack):
        self.nc = nc
        self.engs = {"pe": nc.tensor, "act": nc.scalar, "dve": nc.vector, "pool": nc.gpsimd, "sp": nc.sync}
        self.esem = {e: stack.enter_context(nc.semaphore("es_" + e)) for e in self.engs}
        self.stack = stack
        self.pending = []
        self.sigcount = {e: 0 for e in self.engs}
        self.waited = {e: {} for e in self.engs}
        self.dsems = {}
        self.n_inst = 0

    def buf(self, name):
        return Buf(name)

    def bufs(self, name, n):
        return [Buf("%s%d" % (name, i)) for i in range(n)]

    def _add_dep(self, op, p):
        if p is op:
            return
        if (not p.is_dma) and (not op.is_dma) and p.eng == op.eng and op.eng == "pe":
            return
        op.deps.append(p)

    def _track(self, op, reads, writes):
        for b in reads:
            for p in b.writers.values():
                self._add_dep(op, p)
            for p in b.dma_writers:
                self._add_dep(op, p)
        for b in writes:
            if b.readers or b.dma_readers:
                for p in b.readers.values():
                    self._add_dep(op, p)
                for p in b.dma_readers:
                    self._add_dep(op, p)
                for p in b.writers.values():
                    self._add_dep(op, p)
                for p in b.dma_writers:
                    self._add_dep(op, p)
                b.readers = {}
                b.dma_readers = []
                b.writers = {}
                b.dma_writers = []
            else:
                for e, p in b.writers.items():
                    if e != op.eng or op.is_dma:
                        self._add_dep(op, p)
                for p in b.dma_writers:
                    self._add_dep(op, p)
        for b in reads:
            if op.is_dma:
                b.dma_readers.append(op)
            else:
                b.readers[op.eng] = op
        for b in writes:
            if op.is_dma:
                b.dma_writers.append(op)
            else:
                b.writers[op.eng] = op

    def op(self, eng, fn, reads=(), writes=()):
        o = Op(eng, fn, False)
        self._track(o, reads, writes)
        self.pending.append(o)
        return o

    def dma(self, eng, out, in_, reads=(), writes=(), key=None, slow=False):
        assert key is not None
        if key not in self.dsems:
            self.dsems[key] = [self.stack.enter_context(self.nc.semaphore("ds_" + key)), 0, 16]
        ent = self.dsems[key]
        ent[1] += 1
        if slow:
            o = Op(eng, (lambda: self.engs[eng].dma_start(out=out, in_=in_, allow_slow_non_contiguous=True)), True)
        else:
            o = Op(eng, (lambda: self.engs[eng].dma_start(out=out, in_=in_)), True)
        o.dsem = key
        o.dcount = ent[1]
        self._track(o, reads, writes)
        self.pending.append(o)
        return o

    def cc(self, kind, groups, in_ap, out_ap, reads=(), writes=(), key=None):
        assert key not in self.dsems
        self.dsems[key] = [self.stack.enter_context(self.nc.semaphore("ds_" + key)), 1, 1]
        o = Op("pool", (lambda: self.nc.gpsimd.collective_compute(kind, ALU.bypass, replica_groups=groups,
                                                                  ins=[in_ap.opt()], outs=[out_ap.opt()])), True)
        o.dsem = key
        o.dcount = 1
        self._track(o, reads, writes)
        self.pending.append(o)
        return o

    def _wait(self, eng, semkey, sem, val):
        w = self.waited[eng]
        if w.get(semkey, 0) >= val:
            return
        w[semkey] = val
        self.engs[eng].wait_ge(sem, val)
        self.n_inst += 1

    def flush(self):
        for o in self.pending:
            o.deps = [p for p in o.deps if not p.emitted]
            for p in o.deps:
                if not p.is_dma:
                    p.signal = True
        last = {}
        for o in self.pending:
            if not o.is_dma:
                last[o.eng] = o
        for o in last.values():
            o.signal = True
        for o in self.pending:
            for p in o.deps:
                assert p.emitted, "dependency on later op"
                if p.is_dma:
                    self._wait(o.eng, "d_" + p.dsem, self.dsems[p.dsem][0], self.dsems[p.dsem][2] * p.dcount)
                else:
                    self._wait(o.eng, "e_" + p.eng, self.esem[p.eng], p.sigval)
            ins = o.fn()
            self.n_inst += 1
            if o.is_dma:
                if self.dsems[o.dsem][2] == 16:
                    ins.then_inc(self.dsems[o.dsem][0], 16)
                else:
                    ins.then_inc(self.dsems[o.dsem][0])
            elif o.signal:
                self.sigcount[o.eng] += 1
                o.sigval = self.sigcount[o.eng]
                ins.then_inc(self.esem[o.eng], 1)
            o.emitted = True
            o.fn = None
        self.pending = []
        self.barrier()

    def barrier(self):
        for e in self.engs:
            for e2 in self.engs:
                if e2 != e and self.sigcount[e2] > 0:
                    self._wait(e, "e_" + e2, self.esem[e2], self.sigcount[e2])
            for key, (sem, cnt, unit) in self.dsems.items():
                if cnt > 0:
                    self._wait(e, "d_" + key, sem, unit * cnt)


class Ctx:
    def __init__(self, nc, stack):
        self.nc = nc
        self.stack = stack
        self.S = Sched(nc, stack)
        self.uid = 0

    def name(self, s):
        self.uid += 1
        return "%s_%d" % (s, self.uid)


def sb(cx, st, name, shape, dt):
    return st.enter_context(cx.nc.sbuf_tensor(cx.name(name), shape, dt))


def ps(cx, st, name, shape, dt):
    return st.enter_context(cx.nc.psum_tensor(cx.name(name), shape, dt))


def make_ident(cx, st):
    nc, S = cx.nc, cx.S
    ident = sb(cx, st, "ident", [128, 128], BF16)
    IB = S.buf("ident")
    S.dma("pool", ident[:], cx.ident_d[:, :], writes=[IB], key="ident")
    return ident, IB


def ln_epilogue(cx, z, ZB, gb, GB, st_t, mv_t, sd_t, rs_t, SB_, eps):
    nc, S = cx.nc, cx.S
    S.op("dve", lambda: nc.vector.bn_stats(out=st_t[:, 0, :], in_=z[:, 0:512]), reads=[ZB], writes=[SB_])
    S.op("dve", lambda: nc.vector.bn_stats(out=st_t[:, 1, :], in_=z[:, 512:1024]), reads=[ZB], writes=[SB_])
    S.op("dve", lambda: nc.vector.bn_aggr(out=mv_t[:], in_=st_t[:]), reads=[SB_], writes=[SB_])
    S.op("act", lambda: nc.scalar.activation(out=sd_t[:], in_=mv_t[:, 1:2], func=AF.Sqrt, bias=eps, scale=1.0),
         reads=[SB_], writes=[SB_])
    S.op("dve", lambda: nc.vector.reciprocal(out=rs_t[:], in_=sd_t[:]), reads=[SB_], writes=[SB_])
    S.op("dve", lambda: nc.vector.tensor_scalar(out=z[:], in0=z[:], scalar1=mv_t[:, 0:1], scalar2=rs_t[:, 0:1],
                                                op0=ALU.subtract, op1=ALU.mult), reads=[ZB, SB_], writes=[ZB])
    S.op("pool", lambda: nc.gpsimd.tensor_tensor(out=z[:], in0=z[:], in1=gb[:, 0, :], op=ALU.mult),
         reads=[ZB, GB], writes=[ZB])
    S.op("pool", lambda: nc.gpsimd.tensor_tensor(out=z[:], in0=z[:], in1=gb[:, 1, :], op=ALU.add),
         reads=[ZB, GB], writes=[ZB])


def load_gb(cx, st, g_d, b_d, name):
    nc, S = cx.nc, cx.S
    gb = sb(cx, st, name, [128, 2, D], F32)
    GB = S.buf(name)
    S.dma("sp", gb[:, 0, :], g_d.partition_broadcast(128), writes=[GB], key=name)
    S.dma("sp", gb[:, 1, :], b_d.partition_broadcast(128), writes=[GB], key=name)
    return gb, GB


def emit_xT(cx, xs_t, XSb, xbf, XBF, tp, TP, xT_t, XTb, ident, IB, NS):
    nc, S = cx.nc, cx.S
    KC = D // 128
    S.op("pool", lambda: nc.gpsimd.tensor_copy(out=xbf[:], in_=xs_t[:]), reads=[XSb], writes=[XBF])
    for s in range(NS):
        b = s % len(tp)
        for k in range(KC):
            S.op("pe", lambda s=s, k=k, b=b: nc.tensor.transpose(out=tp[b][:, k * 128:(k + 1) * 128],
                                                                  in_=xbf[:, s, k * 128:(k + 1) * 128], identity=ident[:]),
                 reads=[XBF, IB], writes=[TP[b]])
        S.op("dve", lambda s=s, b=b: nc.vector.tensor_copy(
            out=xT_t[:, :, s * 128:(s + 1) * 128], in_=tp[b][:].rearrange("p (k t) -> p k t", k=KC)),
            reads=[TP[b]], writes=[XTb])


def phase_ffn(cx, xin, xout, w_in_d, w_out_d, g_d, b_d, NT, tag):
    nc, S = cx.nc, cx.S
    TT = 256
    NS = TT // 128
    KC = D // 128
    MC = DFF // 128
    c_res = 0.5 / ALPHA
    eps = LN_EPS / (ALPHA * ALPHA)
    with ExitStack() as st:
        w1 = sb(cx, st, "w1", [128, KC, 2 * DFF], BF16)
        w2 = sb(cx, st, "w2", [128, MC, D], BF16)
        W1, W2 = S.buf("W1"), S.buf("W2")
        for k in range(KC):
            S.dma("pool", w1[:, k, :], w_in_d[k * 128:(k + 1) * 128, :], writes=[W1], key="w1")
        for m in range(MC):
            S.dma("pool", w2[:, m, :], w_out_d[m * 128:(m + 1) * 128, :], writes=[W2], key="w2")
        gb, GB = load_gb(cx, st, g_d, b_d, "gb")
        ident, IB = make_ident(cx, st)
        xs = [sb(cx, st, "xs", [128, NS, D], F32) for _ in range(2)]
        XS = S.bufs("xs", 2)
        xbf = sb(cx, st, "xbf", [128, NS, D], BF16)
        XBF = S.buf("xbf")
        xT = [sb(cx, st, "xT", [128, KC, TT], BF16) for _ in range(2)]
        XT = S.bufs("xT", 2)
        hT = [sb(cx, st, "hT", [128, MC, TT], BF16) for _ in range(2)]
        HT = S.bufs("hT", 2)
        sg = [sb(cx, st, "sg", [128, TT], BF16) for _ in range(2)]
        SG = S.bufs("sg", 2)
        z = [sb(cx, st, "z", [128, D], F32) for _ in range(2)]
        Z = S.bufs("z", 2)
        st_t = [sb(cx, st, "st", [128, 2, 6], F32) for _ in range(2)]
        mv_t = [sb(cx, st, "mv", [128, 2], F32) for _ in range(2)]
        sd_t = [sb(cx, st, "sd", [128, 1], F32) for _ in range(2)]
        rs_t = [sb(cx, st, "rs", [128, 1], F32) for _ in range(2)]
        STB = S.bufs("stb", 2)
        tp = [ps(cx, st, "tp", [128, D], BF16) for _ in range(2)]
        TP = S.bufs("tp", 2)
        gu = [ps(cx, st, "gu", [128, 512], F32) for _ in range(4)]
        GU = S.bufs("gu", 4)
        op_ = [ps(cx, st, "o", [128, 512], F32) for _ in range(2)]
        OP = S.bufs("o", 2)

        ntile = NT // TT
        xin_v = xin.rearrange("(t s p) d -> t p s d", s=NS, p=128)
        xout_v = xout.rearrange("(t s p) d -> t s p d", s=NS, p=128)

        def load(t):
            S.dma("sp", xs[t % 2][:], xin_v[t], writes=[XS[t % 2]], key="xs%d" % (t % 2))

        load(0)
        for t in range(ntile):
            b2 = t % 2
            if t + 1 < ntile:
                load(t + 1)
            emit_xT(cx, xs[b2], XS[b2], xbf, XBF, tp, TP, xT[b2], XT[b2], ident, IB, NS)
            for m in range(MC):
                g_ps, u_ps = gu[2 * (m % 2)], gu[2 * (m % 2) + 1]
                GP, UP = GU[2 * (m % 2)], GU[2 * (m % 2) + 1]
                for k in range(KC):
                    S.op("pe", lambda m=m, k=k, g_ps=g_ps, b2=b2: nc.tensor.matmul(
                        g_ps[:, 0:TT], w1[:, k, m * 128:(m + 1) * 128], xT[b2][:, k, :], start=(k == 0), stop=(k == KC - 1)),
                        reads=[W1, XT[b2]], writes=[GP])
                for k in range(KC):
                    S.op("pe", lambda m=m, k=k, u_ps=u_ps, b2=b2: nc.tensor.matmul(
                        u_ps[:, 0:TT], w1[:, k, DFF + m * 128:DFF + (m + 1) * 128], xT[b2][:, k, :], start=(k == 0), stop=(k == KC - 1)),
                        reads=[W1, XT[b2]], writes=[UP])
                S.op("act", lambda m=m, g_ps=g_ps: nc.scalar.activation(out=sg[m % 2][:], in_=g_ps[:, 0:TT], func=AF.Silu),
                     reads=[GP], writes=[SG[m % 2]])
                S.op("dve", lambda m=m, u_ps=u_ps, b2=b2: nc.vector.tensor_tensor(
                    out=hT[b2][:, m, :], in0=sg[m % 2][:], in1=u_ps[:, 0:TT], op=ALU.mult),
                    reads=[SG[m % 2], UP], writes=[HT[b2]])
            for s in range(NS):
                for n in range(2):
                    for m in range(MC):
                        S.op("pe", lambda s=s, n=n, m=m, b2=b2: nc.tensor.matmul(
                            op_[n][:], hT[b2][:, m, s * 128:(s + 1) * 128], w2[:, m, n * 512:(n + 1) * 512],
                            start=(m == 0), stop=(m == MC - 1)), reads=[HT[b2], W2], writes=[OP[n]])
                    S.op("dve", lambda s=s, n=n, b2=b2: nc.vector.scalar_tensor_tensor(
                        out=z[s][:, n * 512:(n + 1) * 512], in0=op_[n][:], scalar=c_res, in1=xs[b2][:, s, n * 512:(n + 1) * 512],
                        op0=ALU.mult, op1=ALU.add), reads=[OP[n], XS[b2]], writes=[Z[s]])
                ln_epilogue(cx, z[s], Z[s], gb, GB, st_t[s], mv_t[s], sd_t[s], rs_t[s], STB[s], eps)
                S.dma("sp", xout_v[t, s], z[s][:], reads=[Z[s]], key="zst%d" % s)
        S.flush()


GW = 3072


def ln_stats(cx, src, SRC, nchunk, st_t, mv_t, sd_t, rs_t, SB_, eps):
    nc, S = cx.nc, cx.S
    for c in range(nchunk):
        S.op("dve", lambda c=c: nc.vector.bn_stats(out=st_t[:, c, :], in_=src[:, c * 512:(c + 1) * 512]),
             reads=[SRC], writes=[SB_])
    S.op("dve", lambda: nc.vector.bn_aggr(out=mv_t[:], in_=st_t[:]), reads=[SB_], writes=[SB_])
    S.op("act", lambda: nc.scalar.activation(out=sd_t[:], in_=mv_t[:, 1:2], func=AF.Sqrt, bias=eps, scale=1.0),
         reads=[SB_], writes=[SB_])
    S.op("dve", lambda: nc.vector.reciprocal(out=rs_t[:], in_=sd_t[:]), reads=[SB_], writes=[SB_])


def phase_g1(cx, xin, uT_d, vln_d, w_in_d, lng_d, lnb_d, NT):
    nc, S = cx.nc, cx.S
    TT, NS, KC, JC = 256, 2, 8, 24
    with ExitStack() as st:
        wg = sb(cx, st, "wg", [128, KC, 2 * GW], BF16)
        WG = S.buf("WG")
        for k in range(KC):
            S.dma("pool", wg[:, k, :], w_in_d[k * 128:(k + 1) * 128, :], writes=[WG], key="wg")
        gbv = sb(cx, st, "gbv", [128, 2, GW], F32)
        GBV = S.buf("gbv")
        S.dma("sp", gbv[:, 0, :], lng_d.partition_broadcast(128), writes=[GBV], key="gbv")
        S.dma("sp", gbv[:, 1, :], lnb_d.partition_broadcast(128), writes=[GBV], key="gbv")
        ident, IB = make_ident(cx, st)
        xs = [sb(cx, st, "xs", [128, NS, D], F32) for _ in range(2)]
        XS = S.bufs("xs", 2)
        xbf = sb(cx, st, "xbf", [128, NS, D], BF16)
        XBF = S.buf("xbf")
        xT = [sb(cx, st, "xT", [128, KC, TT], BF16) for _ in range(2)]
        XT = S.bufs("xT", 2)
        ust = [sb(cx, st, "ust", [128, 4, TT], BF16) for _ in range(2)]
        UST = S.bufs("ust", 2)
        v = [sb(cx, st, "v", [128, GW], F32) for _ in range(2)]
        V = S.bufs("v", 2)
        vln = [sb(cx, st, "vln", [128, GW], BF16) for _ in range(2)]
        VLN = S.bufs("vln", 2)
        st_t = [sb(cx, st, "st", [128, 6, 6], F32) for _ in range(2)]
        mv_t = [sb(cx, st, "mv", [128, 2], F32) for _ in range(2)]
        sd_t = [sb(cx, st, "sd", [128, 1], F32) for _ in range(2)]
        rs_t = [sb(cx, st, "rs", [128, 1], F32) for _ in range(2)]
        STB = S.bufs("stb", 2)
        tp = [ps(cx, st, "tp", [128, D], BF16) for _ in range(2)]
        TP = S.bufs("tp", 2)
        pu = [ps(cx, st, "pu", [128, 512], F32) for _ in range(2)]
        PU = S.bufs("pu", 2)
        pv = [ps(cx, st, "pv", [128, 512], F32) for _ in range(3)]
        PV = S.bufs("pv", 3)
        ntile = NT // TT
        xin_v = xin.rearrange("(t s p) d -> t p s d", s=NS, p=128)
        uT_v = uT_d.rearrange("(j p) n -> p j n", p=128)

        def load(t):
            S.dma("sp", xs[t % 2][:], xin_v[t], writes=[XS[t % 2]], key="xs%d" % (t % 2))

        load(0)
        for t in range(ntile):
            b2 = t % 2
            if t + 1 < ntile:
                load(t + 1)
            emit_xT(cx, xs[b2], XS[b2], xbf, XBF, tp, TP, xT[b2], XT[b2], ident, IB, NS)
            for j in range(JC):
                q = (j // 4) % 2
                for k in range(KC):
                    S.op("pe", lambda j=j, k=k, b2=b2: nc.tensor.matmul(
                        pu[j % 2][:, 0:TT], wg[:, k, j * 128:(j + 1) * 128], xT[b2][:, k, :],
                        start=(k == 0), stop=(k == KC - 1)), reads=[WG, XT[b2]], writes=[PU[j % 2]])
                S.op("act", lambda j=j, q=q: nc.scalar.activation(out=ust[q][:, j % 4, :], in_=pu[j % 2][:, 0:TT],
                                                                    func=AF.Gelu_apprx_tanh),
                     reads=[PU[j % 2]], writes=[UST[q]])
                if j % 4 == 3:
                    S.dma("sp", uT_v[:, j - 3:j + 1, t * TT:(t + 1) * TT], ust[q][:], reads=[UST[q]], key="ust%d" % q)
            for s in range(NS):
                for n in range(6):
                    for k in range(KC):
                        S.op("pe", lambda s=s, n=n, k=k, b2=b2: nc.tensor.matmul(
                            pv[n % 3][:], xT[b2][:, k, s * 128:(s + 1) * 128], wg[:, k, GW + n * 512:GW + (n + 1) * 512],
                            start=(k == 0), stop=(k == KC - 1)), reads=[WG, XT[b2]], writes=[PV[n % 3]])
                    S.op("act", lambda s=s, n=n: nc.scalar.activation(out=v[s][:, n * 512:(n + 1) * 512], in_=pv[n % 3][:],
                                                                        func=AF.Gelu_apprx_tanh),
                         reads=[PV[n % 3]], writes=[V[s]])
                ln_stats(cx, v[s], V[s], 6, st_t[s], mv_t[s], sd_t[s], rs_t[s], STB[s], LN_EPS)
                S.op("dve", lambda s=s: nc.vector.tensor_scalar(out=v[s][:], in0=v[s][:], scalar1=mv_t[s][:, 0:1],
                                                                scalar2=rs_t[s][:, 0:1], op0=ALU.subtract, op1=ALU.mult),
                     reads=[V[s], STB[s]], writes=[V[s]])
                S.op("pool", lambda s=s: nc.gpsimd.tensor_tensor(out=v[s][:], in0=v[s][:], in1=gbv[:, 0, :], op=ALU.mult),
                     reads=[V[s], GBV], writes=[V[s]])
                S.op("pool", lambda s=s: nc.gpsimd.tensor_tensor(out=vln[s][:], in0=v[s][:], in1=gbv[:, 1, :], op=ALU.add),
                     reads=[V[s], GBV], writes=[VLN[s]])
                S.dma("sp", vln_d[t * TT + s * 128:t * TT + (s + 1) * 128, :], vln[s][:], reads=[VLN[s]], key="vln%d" % s)
        S.flush()


def phase_g2(cx, xin, xout, uT_d, vln_d, ws_d, bs_d, w_out_d, g_d, b_d, NT):
    nc, S = cx.nc, cx.S
    JC = 24
    c_res = 1.0 / ALPHA
    eps = LN_EPS / (ALPHA * ALPHA)
    with ExitStack() as st:
        w2 = sb(cx, st, "w2g", [128, JC, D], BF16)
        W2 = S.buf("W2g")
        for j in range(JC):
            S.dma("pool", w2[:, j, :], w_out_d[j * 128:(j + 1) * 128, :], writes=[W2], key="w2g")
        gb, GB = load_gb(cx, st, g_d, b_d, "gb2")
        ident, IB = make_ident(cx, st)
        WT = sb(cx, st, "WT", [128, 16, 128], BF16)
        WTB = S.buf("WT")
        bhi = sb(cx, st, "bhi", [1, 2048], BF16)
        blo = sb(cx, st, "blo", [1, 2048], BF16)
        ones = sb(cx, st, "ones", [1, 128], BF16)
        BB = S.buf("bias")
        with ExitStack() as st2:
            wsf = sb(cx, st2, "wsf", [128, 16, 128], F32)
            WSF = S.buf("wsf")
            S.dma("sp", wsf[:], ws_d.rearrange("g t s -> t g s"), writes=[WSF], key="wsf")
            tril = sb(cx, st2, "tril", [128, 128], F32)
            TR = S.buf("tril")
            S.dma("sp", tril[:], cx.tril_d[:, :], writes=[TR], key="tril")
            wsm = sb(cx, st2, "wsm", [128, 16, 128], BF16)
            WSM = S.buf("wsm")
            S.op("dve", lambda: nc.vector.tensor_tensor(out=wsm[:], in0=wsf[:], in1=tril[:].unsqueeze(1).broadcast_to([128, 16, 128]),
                                                        op=ALU.mult), reads=[WSF, TR], writes=[WSM])
            tpw = [ps(cx, st2, "tpw", [128, D], BF16) for _ in range(2)]
            TPW = S.bufs("tpw", 2)
            for g in range(16):
                S.op("pe", lambda g=g: nc.tensor.transpose(out=tpw[g // 8][:, (g % 8) * 128:(g % 8 + 1) * 128],
                                                           in_=wsm[:, g, :], identity=ident[:]),
                     reads=[WSM, IB], writes=[TPW[g // 8]])
            for h in range(2):
                S.op("dve", lambda h=h: nc.vector.tensor_copy(out=WT[:, h * 8:(h + 1) * 8, :],
                                                              in_=tpw[h][:].rearrange("p (g t) -> p g t", g=8)),
                     reads=[TPW[h]], writes=[WTB])
            bsf = sb(cx, st2, "bsf", [1, 2048], F32)
            BSF = S.buf("bsf")
            S.dma("sp", bsf[:], bs_d.rearrange("g t -> (g t)").unsqueeze(0), writes=[BSF], key="bsf")
            S.op("dve", lambda: nc.vector.tensor_copy(out=bhi[:], in_=bsf[:]), reads=[BSF], writes=[BB])
            S.op("dve", lambda: nc.vector.tensor_tensor(out=blo[:], in0=bsf[:], in1=bhi[:], op=ALU.subtract),
                 reads=[BSF, BB], writes=[BB])
            S.op("dve", lambda: nc.vector.memset(ones[:], 1.0), writes=[BB])
            S.flush()
        ut = [sb(cx, st, "ut", [128, JC, 256], BF16) for _ in range(2)]
        UT = S.bufs("ut", 2)
        vl = [sb(cx, st, "vl", [128, GW], BF16) for _ in range(2)]
        VL = S.bufs("vl", 2)
        xs = [sb(cx, st, "xs", [128, D], F32) for _ in range(2)]
        XS = S.bufs("xs", 2)
        uv = [sb(cx, st, "uv", [128, JC, 128], BF16) for _ in range(2)]
        UV = S.bufs("uv", 2)
        z = [sb(cx, st, "z", [128, D], F32) for _ in range(2)]
        Z = S.bufs("z", 2)
        st_t = [sb(cx, st, "st", [128, 2, 6], F32) for _ in range(2)]
        mv_t = [sb(cx, st, "mv", [128, 2], F32) for _ in range(2)]
        sd_t = [sb(cx, st, "sd", [128, 1], F32) for _ in range(2)]
        rs_t = [sb(cx, st, "rs", [128, 1], F32) for _ in range(2)]
        STB = S.bufs("stb", 2)
        P = [ps(cx, st, "P", [128, 512], F32) for _ in range(6)]
        PB = S.bufs("P", 6)
        op_ = [ps(cx, st, "o", [128, 512], F32) for _ in range(2)]
        OP = S.bufs("o", 2)
        nchunk = NT // 128
        uT_v = uT_d.rearrange("(j p) n -> p j n", p=128)
        pieces = []
        for g in range(16):
            a = g // 2
            if g % 2 == 0:
                pieces.append((g, 192 * g, 192 * g + 128, 3 * a, 0, 128))
                pieces.append((g, 192 * g + 128, 192 * g + 192, 3 * a + 1, 0, 64))
            else:
                pieces.append((g, 192 * g, 192 * g + 64, 3 * a + 1, 64, 128))
                pieces.append((g, 192 * g + 64, 192 * g + 192, 3 * a + 2, 0, 128))

        def load_u(tt):
            S.dma("sp", ut[tt % 2][:], uT_v[:, :, tt * 256:(tt + 1) * 256], writes=[UT[tt % 2]], key="ut%d" % (tt % 2))

        def load_c(c):
            S.dma("sp", vl[c % 2][:], vln_d[c * 128:(c + 1) * 128, :], writes=[VL[c % 2]], key="vl%d" % (c % 2))
            S.dma("sp", xs[c % 2][:], xin[c * 128:(c + 1) * 128, :], writes=[XS[c % 2]], key="xsg%d" % (c % 2))

        load_u(0)
        load_c(0)
        for c in range(nchunk):
            c2 = c % 2
            tt, cs = c // 2, c % 2
            if c + 1 < nchunk:
                load_c(c + 1)
                if (c + 1) % 2 == 0:
                    load_u((c + 1) // 2)
            for (g, f0, f1, j, r0, r1) in pieces:
                M = f1 - f0
                out_ap = lambda j=j, r0=r0, r1=r1: P[j // 4][r0:r1, (j % 4) * 128:(j % 4 + 1) * 128]
                S.op("pe", lambda g=g, f0=f0, f1=f1, out_ap=out_ap, c2=c2: nc.tensor.matmul(
                    out_ap(), vl[c2][:, f0:f1], WT[:, g, :], start=True, stop=False),
                    reads=[VL[c2], WTB], writes=[PB[j // 4]])
                S.op("pe", lambda g=g, M=M, out_ap=out_ap: nc.tensor.matmul(
                    out_ap(), ones[0:1, 0:M], bhi[0:1, g * 128:(g + 1) * 128], start=False, stop=False),
                    reads=[BB], writes=[PB[j // 4]])
                S.op("pe", lambda g=g, M=M, out_ap=out_ap: nc.tensor.matmul(
                    out_ap(), ones[0:1, 0:M], blo[0:1, g * 128:(g + 1) * 128], start=False, stop=True),
                    reads=[BB], writes=[PB[j // 4]])
            for jj in range(6):
                S.op("dve", lambda jj=jj, c2=c2, tt=tt, cs=cs: nc.vector.tensor_tensor(
                    out=uv[c2][:, 4 * jj:4 * jj + 4, :], in0=P[jj][:].rearrange("p (j t) -> p j t", j=4),
                    in1=ut[tt % 2][:, 4 * jj:4 * jj + 4, cs * 128:(cs + 1) * 128], op=ALU.mult),
                    reads=[PB[jj], UT[tt % 2]], writes=[UV[c2]])
            for n in range(2):
                for j in range(JC):
                    S.op("pe", lambda n=n, j=j, c2=c2: nc.tensor.matmul(
                        op_[n][:], uv[c2][:, j, :], w2[:, j, n * 512:(n + 1) * 512], start=(j == 0), stop=(j == JC - 1)),
                        reads=[UV[c2], W2], writes=[OP[n]])
                S.op("dve", lambda n=n, c2=c2: nc.vector.scalar_tensor_tensor(
                    out=z[c2][:, n * 512:(n + 1) * 512], in0=op_[n][:], scalar=c_res, in1=xs[c2][:, n * 512:(n + 1) * 512],
                    op0=ALU.mult, op1=ALU.add), reads=[OP[n], XS[c2]], writes=[Z[c2]])
            ln_epilogue(cx, z[c2], Z[c2], gb, GB, st_t[c2], mv_t[c2], sd_t[c2], rs_t[c2], STB[c2], eps)
            S.dma("sp", xout[c * 128:(c + 1) * 128, :], z[c2][:], reads=[Z[c2]], key="zg%d" % c2)
        S.flush()


class NsaLocal:
    def __init__(self, feat, tok, QT, gates):
        self.feat, self.tok, self.QT, self.gates = feat, tok, QT, gates


class NsaGathered:
    def __init__(self, feat, tok):
        self.feat, self.tok = feat, tok


NSA_COLS = 2608
HD = 64


def phase_n1(cx, xin, w_in_d, cos_d, sin_d, nl, NT):
    nc, S = cx.nc, cx.S
    TT, NS, KC = 512, 4, 8
    with ExitStack() as st:
        w = sb(cx, st, "wn", [128, KC, NSA_COLS], BF16)
        W = S.buf("wn")
        for k in range(KC):
            S.dma("pool", w[:, k, :], w_in_d[k * 128:(k + 1) * 128, :], writes=[W], key="wn")
        wqr = sb(cx, st, "wqr", [128, KC, 1024], BF16)
        wkr = sb(cx, st, "wkr", [128, KC, 512], BF16)
        WR = S.buf("wrot")

        def rot(dst, src):
            sv = src.rearrange("p k (b t f) -> p k b t f", t=2, f=32)
            dv = dst.rearrange("p k (b t f) -> p k b t f", t=2, f=32)
            for k in range(KC):
                S.op("dve", lambda k=k: nc.vector.tensor_scalar(out=dv[:, k, :, 0, :], in0=sv[:, k, :, 1, :], scalar1=-1.0,
                                                                scalar2=None, op0=ALU.mult), reads=[W, WR], writes=[WR])
                S.op("dve", lambda k=k: nc.vector.tensor_copy(out=dv[:, k, :, 1, :], in_=sv[:, k, :, 0, :]),
                     reads=[W, WR], writes=[WR])

        rot(wqr[:], w[:, :, 0:1024])
        rot(wkr[:, :, 0:256], w[:, :, 1536:1792])
        rot(wkr[:, :, 256:512], w[:, :, 2048:2304])
        ident, IB = make_ident(cx, st)
        xs = [sb(cx, st, "xs", [128, NS, D], F32) for _ in range(2)]
        XS = S.bufs("xs", 2)
        xbf = sb(cx, st, "xbf", [128, NS, D], BF16)
        XBF = S.buf("xbf")
        xT = [sb(cx, st, "xT", [128, KC, TT], BF16) for _ in range(2)]
        XT = S.bufs("xT", 2)
        cs = [sb(cx, st, "cs", [128, 2, TT], F32) for _ in range(2)]
        CS = S.bufs("cs", 2)
        t1 = [sb(cx, st, "t1", [128, TT], F32) for _ in range(2)]
        t2 = [sb(cx, st, "t2", [128, TT], F32) for _ in range(2)]
        T1 = S.bufs("t1", 2)
        T2 = S.bufs("t2", 2)
        og = [sb(cx, st, "og", [128, TT], BF16) for _ in range(2)]
        OG = S.bufs("og", 2)
        vt = [sb(cx, st, "vt", [128, 512], BF16) for _ in range(2)]
        VT = S.bufs("vt", 2)
        gt = [sb(cx, st, "gt", [128, 48], F32) for _ in range(2)]
        GT = S.bufs("gt", 2)
        tp = [ps(cx, st, "tp", [128, D], BF16) for _ in range(2)]
        TP = S.bufs("tp", 2)
        pa = [ps(cx, st, "pa", [128, 512], F32) for _ in range(2)]
        PA = S.bufs("pa", 2)
        pb = [ps(cx, st, "pb", [128, 512], F32) for _ in range(2)]
        PB = S.bufs("pb", 2)
        pt = [ps(cx, st, "pt", [128, 512], F32) for _ in range(2)]
        PT = S.bufs("pt", 2)
        ntile = NT // TT
        xin_v = xin.rearrange("(t s p) d -> t p s d", s=NS, p=128)

        def load(t):
            S.dma("sp", xs[t % 2][:], xin_v[t], writes=[XS[t % 2]], key="xs%d" % (t % 2))
            S.dma("sp", cs[t % 2][:, 0, :], cos_d[:, t * TT:(t + 1) * TT], writes=[CS[t % 2]], key="cs%d" % (t % 2))
            S.dma("sp", cs[t % 2][:, 1, :], sin_d[:, t * TT:(t + 1) * TT], writes=[CS[t % 2]], key="cs%d" % (t % 2))

        units = []
        qf = lambda h: nl.QT[:, h, :]
        for a in range(8):
            units.append((qf, a, w, wqr, a * 128, a * 128))
        for a in range(2):
            units.append(((lambda g: nl.feat("KsT", g)), a, w, wkr, 1536 + a * 128, a * 128))
        for a in range(2):
            units.append(((lambda g: nl.feat("KwT", g)), a, w, wkr, 2048 + a * 128, 256 + a * 128))
        kcf = lambda g: nl.feat("KcT", g)
        vcf = lambda g: nl.feat("VcT", g)
        plain = [(kcf, 0, 1024), (kcf, 1, 1152), (vcf, 0, 1280), (vcf, 1, 1408)]
        load(0)
        cnt = 0
        for t in range(ntile):
            b2 = t % 2
            if t + 1 < ntile:
                load(t + 1)
            emit_xT(cx, xs[b2], XS[b2], xbf, XBF, tp, TP, xT[b2], XT[b2], ident, IB, NS)
            tok = slice(t * TT, (t + 1) * TT)
            for (dst, di, wa, wb, ca, cb) in units:
                q = cnt % 2
                cnt += 1
                for k in range(KC):
                    S.op("pe", lambda k=k, q=q, wa=wa, ca=ca, b2=b2: nc.tensor.matmul(
                        pa[q][:], wa[:, k, ca:ca + 128], xT[b2][:, k, :], start=(k == 0), stop=(k == KC - 1)),
                        reads=[W, WR, XT[b2]], writes=[PA[q]])
                for k in range(KC):
                    S.op("pe", lambda k=k, q=q, wb=wb, cb=cb, b2=b2: nc.tensor.matmul(
                        pb[q][:], wb[:, k, cb:cb + 128], xT[b2][:, k, :], start=(k == 0), stop=(k == KC - 1)),
                        reads=[WR, XT[b2]], writes=[PB[q]])
                S.op("dve", lambda q=q, b2=b2: nc.vector.tensor_tensor(out=t1[q][:], in0=pa[q][:], in1=cs[b2][:, 0, :], op=ALU.mult),
                     reads=[PA[q], CS[b2]], writes=[T1[q]])
                S.op("dve", lambda q=q, b2=b2: nc.vector.tensor_tensor(out=t2[q][:], in0=pb[q][:], in1=cs[b2][:, 1, :], op=ALU.mult),
                     reads=[PB[q], CS[b2]], writes=[T2[q]])
                S.op("pool", lambda q=q: nc.gpsimd.tensor_tensor(out=og[q][:], in0=t1[q][:], in1=t2[q][:], op=ALU.add),
                     reads=[T1[q], T2[q]], writes=[OG[q]])
                for h in range(2):
                    S.dma("sp", dst(2 * di + h)[:, tok], og[q][64 * h:64 * h + 64, :], reads=[OG[q]], key="og%d" % q)
            for (dst, di, c0) in plain:
                q = cnt % 2
                cnt += 1
                for k in range(KC):
                    S.op("pe", lambda k=k, q=q, c0=c0, b2=b2: nc.tensor.matmul(
                        pa[q][:], w[:, k, c0:c0 + 128], xT[b2][:, k, :], start=(k == 0), stop=(k == KC - 1)),
                        reads=[W, XT[b2]], writes=[PA[q]])
                S.op("act", lambda q=q: nc.scalar.copy(out=og[q][:], in_=pa[q][:]), reads=[PA[q]], writes=[OG[q]])
                for h in range(2):
                    S.dma("sp", dst(2 * di + h)[:, tok], og[q][64 * h:64 * h + 64, :], reads=[OG[q]], key="og%d" % q)
            for s in range(NS):
                q = s % 2
                t0 = t * TT + s * 128
                rows = slice(t0, t0 + 128)
                for k in range(KC):
                    S.op("pe", lambda k=k, q=q, s=s, b2=b2: nc.tensor.matmul(
                        pt[q][:, 0:256], xT[b2][:, k, s * 128:(s + 1) * 128], w[:, k, 1792:2048],
                        start=(k == 0), stop=(k == KC - 1)), reads=[W, XT[b2]], writes=[PT[q]])
                S.op("act", lambda q=q: nc.scalar.copy(out=vt[q][:, 0:256], in_=pt[q][:, 0:256]), reads=[PT[q]], writes=[VT[q]])
                S.dma("sp", nl.tok("Vs", t0), vt[q][:, 0:256], reads=[VT[q]], key="vt%d" % q)
                for k in range(KC):
                    S.op("pe", lambda k=k, q=q, s=s, b2=b2: nc.tensor.matmul(
                        pt[q][:, 0:304], xT[b2][:, k, s * 128:(s + 1) * 128], w[:, k, 2304:2608],
                        start=(k == 0), stop=(k == KC - 1)), reads=[W, XT[b2]], writes=[PT[q]])
                S.op("act", lambda q=q: nc.scalar.copy(out=vt[q][:, 256:512], in_=pt[q][:, 0:256]), reads=[PT[q]], writes=[VT[q]])
                S.op("act", lambda q=q: nc.scalar.activation(out=gt[q][:], in_=pt[q][:, 256:304], func=AF.Sigmoid),
                     reads=[PT[q]], writes=[GT[q]])
                S.dma("sp", nl.tok("Vw", t0), vt[q][:, 256:512], reads=[VT[q]], key="vt%d" % q)
                S.dma("sp", nl.gates[rows, :], gt[q][:], reads=[GT[q]], key="gt%d" % q)
        S.flush()


NEGM = -30000.0
DBG = 0


class StopBuild(Exception):
    pass


def dbg_stop(n):
    if DBG == n:
        raise StopBuild()


def phase_attn(cx, al, QT_d, gates_d, O_d, w1k_d, w2k_d, pek_d, w1v_d, w2v_d, pev_d, NB):
    nc, S = cx.nc, cx.S
    NT = NB * 128
    SEQ = 4 * NT
    KT = 4 * NB
    NCMP = SEQ // 16 - 1
    NCP = SEQ // 16
    with ExitStack() as st:
        ident, IB = make_ident(cx, st)
        identf = sb(cx, st, "identf", [128, 128], F32)
        S.dma("sp", identf[:], cx.ident_d[:, :], writes=[IB], key="identf")
        kcmpT = sb(cx, st, "kcmpT", [64, 4, NCP], BF16)
        vcmp = sb(cx, st, "vcmp", [128, NCP // 128, 4, 65], BF16)
        KCMP = S.buf("kcmp")
        VCMP = S.buf("vcmp")
        S.op("pool", lambda: nc.gpsimd.memset(vcmp[:], 0.0), writes=[VCMP])
        S.op("pool", lambda: nc.gpsimd.memset(kcmpT[:], 0.0), writes=[KCMP])
        with ExitStack() as st2:
            w1s = sb(cx, st2, "w1s", [64, 32, 256], BF16)
            w2d = sb(cx, st2, "w2d", [128, 2, 64], BF16)
            w2r = sb(cx, st2, "w2r", [128, 2, 64], BF16)
            peT = sb(cx, st2, "peT", [64, 32], BF16)
            kc = sb(cx, st2, "kc", [64, SEQ], BF16)
            hT = sb(cx, st2, "hTc", [128, 2, NCP], BF16)
            cbias = sb(cx, st2, "cbias", [128, 2], F32)
            ccs = sb(cx, st2, "ccs", [64, 2, NCP], F32)
            t1 = sb(cx, st2, "ct1", [128, 512], F32)
            t2 = sb(cx, st2, "ct2", [128, 512], F32)
            W1S, W2D, PET, KC, HT, CB, CCS, T1, T2 = [S.buf(n) for n in ("w1s", "w2d", "peT", "kc", "hTc", "cb", "ccs", "ct1", "ct2")]
            ph = [ps(cx, st2, "ph", [128, 512], F32) for _ in range(2)]
            PH = S.bufs("ph", 2)
            pc = ps(cx, st2, "pc", [128, 512], F32)
            PC = S.buf("pc")
            pA = ps(cx, st2, "pA", [128, 512], F32)
            pB = ps(cx, st2, "pB", [128, 512], F32)
            PA_, PB_ = S.buf("pA"), S.buf("pB")
            S.dma("sp", ccs[:, 0, :], cx.cosc_d[0:64, :], writes=[CCS], key="ccs")
            S.dma("sp", ccs[:, 1, :], cx.sinc_d[0:64, :], writes=[CCS], key="ccs")
            ntiles = [(n0, min(n0 + 512, NCMP)) for n0 in range(0, NCMP, 512)]
            for which in range(2):
                w1_d, w2_d, pe_d = (w1k_d, w2k_d, pek_d) if which == 0 else (w1v_d, w2v_d, pev_d)
                src_k = "KcT" if which == 0 else "VcT"
                S.dma("pool", w1s[:, :, :], w1_d.rearrange("l d h -> d l h"), writes=[W1S], key="w1s")
                S.dma("pool", peT[:, :], pe_d.rearrange("l d -> d l"), writes=[PET], key="peT", slow=True)
                for hc in range(2):
                    S.dma("pool", w2d[:, hc, :], w2_d[hc * 128:(hc + 1) * 128, :], writes=[W2D], key="w2d")
                if which == 0:
                    S.op("dve", lambda: nc.vector.tensor_scalar(out=w2r[:, :, 0:32], in0=w2d[:, :, 32:64], scalar1=-1.0,
                                                                scalar2=None, op0=ALU.mult), reads=[W2D], writes=[W2D])
                    S.op("dve", lambda: nc.vector.tensor_copy(out=w2r[:, :, 32:64], in_=w2d[:, :, 0:32]),
                         reads=[W2D], writes=[W2D])
                for hc in range(2):
                    for l in range(32):
                        S.op("pe", lambda hc=hc, l=l: nc.tensor.matmul(
                            pc[:, hc:hc + 1], w1s[:, l, hc * 128:(hc + 1) * 128], peT[:, l:l + 1],
                            start=(l == 0), stop=(l == 31)), reads=[W1S, PET], writes=[PC])
                S.op("dve", lambda: nc.vector.tensor_copy(out=cbias[:], in_=pc[:, 0:2]), reads=[PC], writes=[CB])
                for g in range(4):
                    for r_ in range(4):
                        S.dma("sp", kc[:, :].rearrange("p (i r q) -> p i r q", r=4, q=128)[:, :, r_, :],
                              al.feat(src_k, r_, g).rearrange("p (i q) -> p i q", q=128), writes=[KC], key="kc")
                    for hc in range(2):
                        for (n0, n1) in ntiles:
                            cnt = n1 - n0
                            q = (hc + (n0 // 512)) % 2
                            for l in range(32):
                                S.op("pe", lambda hc=hc, l=l, n0=n0, cnt=cnt, q=q: nc.tensor.matmul(
                                    ph[q][:, 0:cnt], w1s[:, l, hc * 128:(hc + 1) * 128],
                                    kc[:, 16 * n0 + l:16 * n0 + l + 16 * (cnt - 1) + 1:16],
                                    start=(l == 0), stop=(l == 31)), reads=[W1S, KC], writes=[PH[q]])
                            S.op("act", lambda hc=hc, n0=n0, cnt=cnt, q=q: nc.scalar.activation(
                                out=hT[:, hc, n0:n0 + cnt], in_=ph[q][:, 0:cnt], func=AF.Gelu_apprx_tanh,
                                bias=cbias[:, hc:hc + 1], scale=1.0), reads=[PH[q], CB], writes=[HT])
                    if which == 0:
                        for (n0, n1) in ntiles:
                            cnt = n1 - n0
                            for hc in range(2):
                                S.op("pe", lambda hc=hc, n0=n0, cnt=cnt: nc.tensor.matmul(
                                    pA[0:64, 0:cnt], w2d[:, hc, :], hT[:, hc, n0:n0 + cnt], start=(hc == 0), stop=(hc == 1)),
                                    reads=[W2D, HT], writes=[PA_])
                            for hc in range(2):
                                S.op("pe", lambda hc=hc, n0=n0, cnt=cnt: nc.tensor.matmul(
                                    pB[0:64, 0:cnt], w2r[:, hc, :], hT[:, hc, n0:n0 + cnt], start=(hc == 0), stop=(hc == 1)),
                                    reads=[W2D, HT], writes=[PB_])
                            S.op("dve", lambda n0=n0, cnt=cnt: nc.vector.tensor_tensor(
                                out=t1[0:64, 0:cnt], in0=pA[0:64, 0:cnt], in1=ccs[:, 0, n0:n0 + cnt], op=ALU.mult),
                                reads=[PA_, CCS], writes=[T1])
                            S.op("dve", lambda n0=n0, cnt=cnt: nc.vector.tensor_tensor(
                                out=t2[0:64, 0:cnt], in0=pB[0:64, 0:cnt], in1=ccs[:, 1, n0:n0 + cnt], op=ALU.mult),
                                reads=[PB_, CCS], writes=[T2])
                            S.op("pool", lambda n0=n0, cnt=cnt, g=g: nc.gpsimd.tensor_tensor(
                                out=kcmpT[:, g, n0:n0 + cnt], in0=t1[0:64, 0:cnt], in1=t2[0:64, 0:cnt], op=ALU.add),
                                reads=[T1, T2], writes=[KCMP])
                    else:
                        for nt in range((NCMP + 127) // 128):
                            c0 = nt * 128
                            cnt = min(128, NCMP - c0)
                            for hc in range(2):
                                S.op("pe", lambda hc=hc, c0=c0, cnt=cnt: nc.tensor.matmul(
                                    pA[0:cnt, 0:64], hT[:, hc, c0:c0 + cnt], w2d[:, hc, 0:64], start=(hc == 0), stop=(hc == 1)),
                                    reads=[W2D, HT], writes=[PA_])
                            S.op("act", lambda nt=nt, cnt=cnt, g=g: nc.scalar.copy(out=vcmp[0:cnt, nt, g, 0:64], in_=pA[0:cnt, 0:64]),
                                 reads=[PA_], writes=[VCMP])
            S.op("pool", lambda: nc.gpsimd.memset(vcmp[:, :, :, 64:65], 1.0), reads=[VCMP], writes=[VCMP])
            S.flush()
        if DBG == 1:
            return
        ks = sb(cx, st, "ks", [64, SEQ], BF16)
        kw = sb(cx, st, "kw", [64, SEQ], BF16)
        vs = sb(cx, st, "vs", [128, KT, 65], BF16)
        vw = sb(cx, st, "vw", [128, KT, 65], BF16)
        KS, KW, VS, VW = S.buf("ks"), S.buf("kw"), S.buf("vs"), S.buf("vw")
        S.op("pool", lambda: nc.gpsimd.memset(vs[:, :, 64:65], 1.0), writes=[VS])
        S.op("pool", lambda: nc.gpsimd.memset(vw[:, :, 64:65], 1.0), writes=[VW])
        cmpmask = sb(cx, st, "cmpmask", [128, 33], BF16)
        keep = sb(cx, st, "keep", [128, 9], F32)
        addt = sb(cx, st, "addt", [128, 9], F32)
        caus = sb(cx, st, "caus", [128, 4, 128], BF16)
        wmask = sb(cx, st, "wmask", [128, 8, 128], BF16)
        TB = S.buf("tables")
        S.dma("pool", cmpmask[:], cx.cmpmask_d[:, :], writes=[TB], key="tb")
        S.dma("sp", keep[:], cx.keep_d[:, :], writes=[TB], key="tb")
        S.dma("sp", addt[:], cx.addt_d[:, :], writes=[TB], key="tb")
        S.dma("pool", caus[:], cx.caus_d[:, :, :], writes=[TB], key="tb")
        S.dma("pool", wmask[:], cx.wmask_d[:, :, :], writes=[TB], key="tb")
        qt = [sb(cx, st, "qt", [64, 4, 128], BF16) for _ in range(2)]
        QTB = S.bufs("qt", 2)
        gts = [sb(cx, st, "gts", [128, 12], F32) for _ in range(2)]
        GTS = S.bufs("gts", 2)
        Pf = [sb(cx, st, "Pf", [128, NCP], F32) for _ in range(2)]
        PF = S.bufs("Pf", 2)
        Pb = [sb(cx, st, "Pb", [128, NCP], BF16) for _ in range(4)]
        PBB = S.bufs("Pb", 4)
        for cb in range(4):
            S.op("pool", lambda cb=cb: nc.gpsimd.memset(Pb[cb][:], 0.0), writes=[PBB[cb]])
        PTs = [sb(cx, st, "PTs", [128, 512], BF16) for _ in range(2)]
        PTS = S.bufs("PTs", 2)
        pns = sb(cx, st, "pns", [128, NCP + 8], F32)
        PNS = S.buf("pns")
        S.op("pool", lambda: nc.gpsimd.memset(pns[:], 0.0), writes=[PNS])
        NMX = max(8 * NB, 16)
        imp = sb(cx, st, "imp", [128, NMX], F32)
        imp4 = sb(cx, st, "imp4", [128, NMX], F32)
        work = sb(cx, st, "work", [128, NMX], F32)
        msk = sb(cx, st, "msk", [128, NMX], BF16)
        m8a = sb(cx, st, "m8a", [128, 8], F32)
        m8b = sb(cx, st, "m8b", [128, 8], F32)
        thr = sb(cx, st, "thr", [128, 1], F32)
        IMP, MSK = S.buf("imp"), S.buf("msk")
        mskx = sb(cx, st, "mskx", [128, NMX * 64], BF16)
        MSKX = S.buf("mskx")
        smx = sb(cx, st, "smx", [128, 4], F32)
        snb = sb(cx, st, "snb", [128, 1], F32)
        rsum = sb(cx, st, "rsum", [128, 4], F32)
        rinv = sb(cx, st, "rinv", [128, 1], F32)
        SMX = S.buf("smx")
        E = [sb(cx, st, "E", [128, 512], BF16) for _ in range(3)]
        EB = S.bufs("E", 3)
        PM = [sb(cx, st, "PM", [128, 512], BF16) for _ in range(3)]
        PMB = S.bufs("PM", 3)
        cm = [sb(cx, st, "cm", [128, 128], BF16) for _ in range(2)]
        CM = S.bufs("cm", 2)
        OT = [sb(cx, st, "OT", [65, 512], F32) for _ in range(3)]
        OTB = S.bufs("OT", 3)
        rs4 = sb(cx, st, "rs4", [128, 4], F32)
        ri4 = sb(cx, st, "ri4", [128, 4], F32)
        fac = sb(cx, st, "fac", [128, 4], F32)
        tmp = sb(cx, st, "tmp", [128, 4, 64], F32)
        oacc = sb(cx, st, "oacc", [128, 4, 64], F32)
        ob = [sb(cx, st, "ob", [128, 4, 64], BF16) for _ in range(2)]
        CMB, OBB = S.buf("comb"), S.bufs("ob", 2)
        bank = [ps(cx, st, "bk", [128, 512], F32) for _ in range(8)]
        BK = S.bufs("bk", 8)
        SS = [0, 1, 7]
        B_PT, B_OC, B_OS, B_OW, B_CT = 2, 3, 4, 5, 6
        ptps = bank[B_PT][:].bitcast(BF16)
        MT = S.bufs("mt", 2)
        ecnt = [0]

        def masked_tile(kT, vT, KB, VB, kt, q2, mask_ap_fn, mask_bufs, o_bank, first, last):
            e = ecnt[0] % 3
            ecnt[0] += 1
            sb_ = SS[e]
            S.op("pe", lambda sb_=sb_, kt=kt, q2=q2: nc.tensor.matmul(
                bank[sb_][:, :], kT[:, kt * 128:(kt + 1) * 128], qt[q2][:, :, :].rearrange("p h q -> p (h q)"),
                start=True, stop=True), reads=[KB, QTB[q2]], writes=[BK[sb_]])
            dbg_stop(44)
            S.op("act", lambda e=e, sb_=sb_: nc.scalar.activation(out=E[e][:], in_=bank[sb_][:], func=AF.Exp, scale=0.125),
                 reads=[BK[sb_]], writes=[EB[e]])
            dbg_stop(45)
            S.op("dve", lambda e=e: nc.vector.tensor_tensor(
                out=PM[e][:].rearrange("p (c q) -> p c q", c=4), in0=E[e][:].rearrange("p (c q) -> p c q", c=4),
                in1=mask_ap_fn().unsqueeze(1).broadcast_to([128, 4, 128]), op=ALU.mult),
                reads=[EB[e]] + mask_bufs, writes=[PMB[e]])
            dbg_stop(46)
            S.op("pe", lambda e=e, kt=kt, first=first, last=last: nc.tensor.matmul(
                bank[o_bank][0:65, :], vT[:, kt, :], PM[e][:], start=first, stop=last),
                reads=[VB, PMB[e]], writes=[BK[o_bank]])
            dbg_stop(47)

        try:
            ucnt = 0
            for g in range(4):
                for r_ in range(4):
                    S.dma("sp", ks[:].rearrange("p (i r q) -> p i r q", r=4, q=128)[:, :, r_, :],
                          al.feat("KsT", r_, g).rearrange("p (i q) -> p i q", q=128), writes=[KS], key="ks")
                    S.dma("sp", kw[:].rearrange("p (i r q) -> p i r q", r=4, q=128)[:, :, r_, :],
                          al.feat("KwT", r_, g).rearrange("p (i q) -> p i q", q=128), writes=[KW], key="kw")
                    for hf in range(2):
                        isl = slice(hf * NB // 2, (hf + 1) * NB // 2)
                        S.dma("sp", vs[:, :, 0:64].rearrange("q (i r) d -> q i r d", r=4)[:, isl, r_, :],
                              al.tok("Vs", r_, hf)[:, 64 * g:64 * g + 64].rearrange("(i q) d -> q i d", q=128),
                              writes=[VS], key="vs")
                        S.dma("sp", vw[:, :, 0:64].rearrange("q (i r) d -> q i r d", r=4)[:, isl, r_, :],
                              al.tok("Vw", r_, hf)[:, 64 * g:64 * g + 64].rearrange("(i q) d -> q i d", q=128),
                              writes=[VW], key="vw")
                if DBG == 2:
                    S.flush()
                    return
                for i in range(NB):
                    q2 = ucnt % 2
                    ucnt += 1
                    S.dma("sp", qt[q2][:], QT_d[:, 4 * g:4 * g + 4, i * 128:(i + 1) * 128], writes=[QTB[q2]], key="qt%d" % q2)
                    S.dma("sp", gts[q2][:], gates_d[i * 128:(i + 1) * 128, 12 * g:12 * g + 12], writes=[GTS[q2]], key="gts%d" % q2)
                    ncols = 32 * i + 31
                    ctiles = [(c0, min(c0 + 512, ncols)) for c0 in range(0, ncols, 512)]
                    m0 = ncols - 33
                    for cb in range(4):
                        pf = Pf[cb % 2]
                        PFB = PF[cb % 2]
                        for ci, (c0, c1) in enumerate(ctiles):
                            ov0, ov1 = max(c0, m0), c1
                            has_mask = ov1 > ov0
                            S.op("pe", lambda cb=cb, ci=ci, c0=c0, c1=c1, q2=q2, has_mask=has_mask, g=g: nc.tensor.matmul(
                                bank[ci][:, 0:c1 - c0], qt[q2][:, cb, :], kcmpT[:, g, c0:c1],
                                start=True, stop=(not has_mask)), reads=[QTB[q2], KCMP], writes=[BK[ci]])
                            if has_mask:
                                S.op("pe", lambda ci=ci, c0=c0, ov0=ov0, ov1=ov1, m0=m0: nc.tensor.matmul(
                                    bank[ci][:, ov0 - c0:ov1 - c0], ident[:], cmpmask[:, ov0 - m0:ov1 - m0], start=False, stop=True),
                                    reads=[IB, TB], writes=[BK[ci]])
                            S.op("dve", lambda ci=ci, c0=c0, c1=c1: nc.vector.reduce_max(
                                out=smx[:, ci:ci + 1], in_=bank[ci][:, 0:c1 - c0], axis=AX.X), reads=[BK[ci]], writes=[SMX])
                        if len(ctiles) == 2:
                            S.op("dve", lambda: nc.vector.tensor_tensor(out=smx[:, 0:1], in0=smx[:, 0:1], in1=smx[:, 1:2], op=ALU.max),
                                 reads=[SMX], writes=[SMX])
                        S.op("dve", lambda: nc.vector.tensor_scalar(out=snb[:], in0=smx[:, 0:1], scalar1=-1000.0, scalar2=-0.125,
                                                                    op0=ALU.max, op1=ALU.mult), reads=[SMX], writes=[SMX])
                        S.op("dve", lambda: nc.vector.memset(rsum[:, 2:4], 0.0), reads=[SMX], writes=[SMX])
                        for ci, (c0, c1) in enumerate(ctiles):
                            S.op("act", lambda ci=ci, c0=c0, c1=c1, pf=pf: nc.scalar.activation(
                                out=pf[:, c0:c1], in_=bank[ci][:, 0:c1 - c0], func=AF.Exp, bias=snb[:, 0:1], scale=0.125,
                                accum_out=rsum[:, 2 + ci:3 + ci]), reads=[BK[ci], SMX], writes=[PFB, SMX])
                        if len(ctiles) == 2:
                            S.op("dve", lambda: nc.vector.tensor_tensor(out=rsum[:, 2:3], in0=rsum[:, 2:3], in1=rsum[:, 3:4], op=ALU.add),
                                 reads=[SMX], writes=[SMX])
                        S.op("dve", lambda: nc.vector.tensor_scalar(out=rsum[:, 0:1], in0=rsum[:, 2:3], scalar1=1e-30, scalar2=None,
                                                                    op0=ALU.max), reads=[SMX], writes=[SMX])
                        S.op("dve", lambda: nc.vector.reciprocal(out=rinv[:], in_=rsum[:, 0:1]), reads=[SMX], writes=[SMX])
                        if cb == 0:
                            S.op("dve", lambda pf=pf, ncols=ncols: nc.vector.tensor_scalar(
                                out=pns[:, 1:1 + ncols], in0=pf[:, 0:ncols], scalar1=rinv[:, 0:1], scalar2=None, op0=ALU.mult),
                                reads=[PFB, SMX], writes=[PNS])
                        else:
                            S.op("dve", lambda pf=pf, ncols=ncols: nc.vector.scalar_tensor_tensor(
                                out=pns[:, 1:1 + ncols], in0=pf[:, 0:ncols], scalar=rinv[:, 0:1], in1=pns[:, 1:1 + ncols],
                                op0=ALU.mult, op1=ALU.add), reads=[PFB, SMX, PNS], writes=[PNS])
                        S.op("pool", lambda cb=cb, pf=pf, ncols=ncols: nc.gpsimd.tensor_copy(out=Pb[cb][:, 0:ncols], in_=pf[:, 0:ncols]),
                             reads=[PFB], writes=[PBB[cb]])
                        if g > 0 and i == 0:
                            S.op("pool", lambda cb=cb, ncols=ncols: nc.gpsimd.memset(Pb[cb][:, ncols:NCP], 0.0), writes=[PBB[cb]])
                    if g > 0 and i == 0:
                        S.op("pool", lambda ncols=ncols: nc.gpsimd.memset(pns[:, 1 + ncols:NCP + 8], 0.0), reads=[PNS], writes=[PNS])
                    nnt = (ncols + 127) // 128
                    for nt in range(nnt):
                        for cb in range(4):
                            S.op("pe", lambda nt=nt, cb=cb: nc.tensor.transpose(
                                out=ptps[:, cb * 128:(cb + 1) * 128], in_=Pb[cb][:, nt * 128:(nt + 1) * 128], identity=ident[:]),
                                reads=[PBB[cb], IB], writes=[BK[B_PT]])
                        S.op("act", lambda nt=nt: nc.scalar.copy(out=PTs[nt % 2][:], in_=ptps[:, 0:512]),
                             reads=[BK[B_PT]], writes=[PTS[nt % 2]])
                        S.op("pe", lambda nt=nt, nnt=nnt, g=g: nc.tensor.matmul(
                            bank[B_OC][0:65, :], vcmp[:, nt, g, :], PTs[nt % 2][:], start=(nt == 0), stop=(nt == nnt - 1)),
                            reads=[VCMP, PTS[nt % 2]], writes=[BK[B_OC]])
                    if DBG == 3:
                        S.flush()
                        return
                    nm = 8 * i + 8
                    Wd = max(nm, 16)
                    S.op("dve", lambda nm=nm: nc.vector.tensor_reduce(
                        out=imp4[:, 0:nm], in_=pns[:, 0:4 * nm].rearrange("p (m r) -> p m r", r=4), axis=AX.X, op=ALU.add),
                        reads=[PNS], writes=[IMP])
                    S.op("dve", lambda nm=nm: nc.vector.tensor_tensor(
                        out=imp[:, 0:nm], in0=imp4[:, 0:nm], in1=pns[:, 4:4 * nm + 1:4], op=ALU.add), reads=[PNS, IMP], writes=[IMP])
                    r0 = max(8 * i - 1, 0)
                    tcol = r0 - (8 * i - 1)
                    S.op("dve", lambda r0=r0, nm=nm, tcol=tcol: nc.vector.tensor_tensor(
                        out=imp[:, r0:nm], in0=imp[:, r0:nm], in1=keep[:, tcol:9], op=ALU.mult), reads=[IMP, TB], writes=[IMP])
                    S.op("dve", lambda r0=r0, nm=nm, tcol=tcol: nc.vector.tensor_tensor(
                        out=imp[:, r0:nm], in0=imp[:, r0:nm], in1=addt[:, tcol:9], op=ALU.add), reads=[IMP, TB], writes=[IMP])
                    S.op("dve", lambda: nc.vector.memset(imp[:, 0:1], 3e9), reads=[IMP], writes=[IMP])
                    if nm < 16:
                        S.op("dve", lambda nm=nm: nc.vector.memset(imp[:, nm:16], -1e9), reads=[IMP], writes=[IMP])
                    S.op("dve", lambda Wd=Wd: nc.vector.max(out=m8a[:], in_=imp[:, 0:Wd]), reads=[IMP], writes=[IMP])
                    S.op("dve", lambda Wd=Wd: nc.vector.match_replace(out=work[:, 0:Wd], in_to_replace=m8a[:], in_values=imp[:, 0:Wd],
                                                                        imm_value=-3e38), reads=[IMP], writes=[IMP])
                    S.op("dve", lambda Wd=Wd: nc.vector.max(out=m8b[:], in_=work[:, 0:Wd]), reads=[IMP], writes=[IMP])
                    S.op("dve", lambda: nc.vector.tensor_reduce(out=thr[:], in_=m8b[:], axis=AX.X, op=ALU.min), reads=[IMP], writes=[IMP])
                    S.op("dve", lambda Wd=Wd: nc.vector.tensor_scalar(out=msk[:, 0:Wd], in0=imp[:, 0:Wd], scalar1=thr[:, 0:1], scalar2=None,
                                                                       op0=ALU.is_ge), reads=[IMP], writes=[MSK])
                    if DBG == 4:
                        S.flush()
                        return
                    nkt = 4 * i + 4
                    S.op("pool", lambda nm=nm: nc.gpsimd.tensor_copy(
                        out=mskx[:, 0:nm * 64].rearrange("q (m u) -> q m u", u=64),
                        in_=msk[:, 0:nm].unsqueeze(2).broadcast_to([128, nm, 64])), reads=[MSK], writes=[MSKX])
                    dbg_stop(41)
                    for kt in range(nkt):
                        ms = kt % 2
                        mt_ap = ptps[:, 512 + ms * 128:512 + (ms + 1) * 128]
                        S.op("pe", lambda kt=kt, mt_ap=mt_ap: nc.tensor.transpose(
                            out=mt_ap, in_=mskx[:, kt * 128:(kt + 1) * 128], identity=ident[:]),
                            reads=[MSKX, IB], writes=[MT[ms]])
                        dbg_stop(42)
                        if kt >= 4 * i:
                            S.op("dve", lambda kt=kt, ms=ms, mt_ap=mt_ap, i=i: nc.vector.tensor_tensor(
                                out=cm[ms][:], in0=mt_ap, in1=caus[:, kt - 4 * i, :], op=ALU.mult),
                                reads=[MT[ms], TB], writes=[CM[ms]])
                            dbg_stop(43)
                            masked_tile(ks, vs, KS, VS, kt, q2, (lambda ms=ms: cm[ms][:]), [CM[ms]], B_OS, kt == 0, kt == nkt - 1)
                        else:
                            masked_tile(ks, vs, KS, VS, kt, q2, (lambda mt_ap=mt_ap: mt_ap), [MT[ms]], B_OS, kt == 0, kt == nkt - 1)
                    if DBG == 5:
                        S.flush()
                        return
                    wl = [kr for kr in range(-4, 4) if 4 * i + kr >= 0]
                    for idx, kr in enumerate(wl):
                        masked_tile(kw, vw, KW, VW, 4 * i + kr, q2, (lambda kr=kr: wmask[:, kr + 4, :]), [TB], B_OW,
                                    idx == 0, idx == len(wl) - 1)
                    if DBG == 6:
                        S.flush()
                        return
                    gv = gts[q2][:].rearrange("q (h t) -> q h t", t=3)
                    for br, ob_ in enumerate((B_OC, B_OS, B_OW)):
                        S.op("act", lambda br=br, ob_=ob_: nc.scalar.copy(out=OT[br][:], in_=bank[ob_][0:65, :]),
                             reads=[BK[ob_]], writes=[OTB[br]])
                        for cb in range(4):
                            S.op("pe", lambda br=br, cb=cb: nc.tensor.transpose(
                                out=bank[B_CT][:, cb * 65:(cb + 1) * 65], in_=OT[br][0:65, cb * 128:(cb + 1) * 128],
                                identity=identf[0:65, 0:65]), reads=[OTB[br], IB], writes=[BK[B_CT]])
                        ctv = bank[B_CT][:, 0:260].rearrange("q (c e) -> q c e", e=65)
                        S.op("dve", lambda ctv=ctv: nc.vector.tensor_scalar(out=rs4[:], in0=ctv[:, :, 64], scalar1=1e-30, scalar2=None,
                                                                            op0=ALU.max), reads=[BK[B_CT]], writes=[CMB])
                        S.op("dve", lambda: nc.vector.reciprocal(out=ri4[:], in_=rs4[:]), reads=[CMB], writes=[CMB])
                        S.op("dve", lambda br=br, gv=gv: nc.vector.tensor_tensor(
                            out=fac[:], in0=ri4[:], in1=gv[:, :, br], op=ALU.mult), reads=[CMB, GTS[q2]], writes=[CMB])
                        dst = oacc if br == 0 else tmp
                        S.op("dve", lambda ctv=ctv, dst=dst: nc.vector.tensor_tensor(
                            out=dst[:], in0=ctv[:, :, 0:64], in1=fac[:].unsqueeze(2).broadcast_to([128, 4, 64]), op=ALU.mult),
                            reads=[BK[B_CT], CMB], writes=[CMB])
                        if br > 0:
                            S.op("dve", lambda: nc.vector.tensor_tensor(out=oacc[:], in0=oacc[:], in1=tmp[:], op=ALU.add),
                                 reads=[CMB], writes=[CMB])
                    S.op("pool", lambda q2=q2: nc.gpsimd.tensor_copy(out=ob[q2][:], in_=oacc[:]), reads=[CMB], writes=[OBB[q2]])
                    S.dma("sp", O_d[i * 128:(i + 1) * 128, 256 * g:256 * (g + 1)], ob[q2][:].rearrange("q h d -> q (h d)"),
                          reads=[OBB[q2]], key="ob%d" % q2)
        except StopBuild:
            pass
        S.flush()


def phase_proj(cx, a_d, xin, xout, w_d, g_d, b_d, NT, c_res):
    nc, S = cx.nc, cx.S
    KC = 8
    eps = LN_EPS / (ALPHA * ALPHA)
    with ExitStack() as st:
        wo = sb(cx, st, "wo", [128, KC, D], BF16)
        WO = S.buf("wo")
        for k in range(KC):
            S.dma("pool", wo[:, k, :], w_d[k * 128:(k + 1) * 128, :], writes=[WO], key="wo")
        gb, GB = load_gb(cx, st, g_d, b_d, "gbp")
        ident, IB = make_ident(cx, st)
        at = [sb(cx, st, "at", [128, D], BF16) for _ in range(2)]
        AT = S.bufs("at", 2)
        xs = [sb(cx, st, "xs", [128, D], F32) for _ in range(2)]
        XS = S.bufs("xs", 2)
        aT = [sb(cx, st, "aT", [128, KC, 128], BF16) for _ in range(2)]
        ATT = S.bufs("aT", 2)
        z = [sb(cx, st, "z", [128, D], F32) for _ in range(2)]
        Z = S.bufs("z", 2)
        st_t = [sb(cx, st, "st", [128, 2, 6], F32) for _ in range(2)]
        mv_t = [sb(cx, st, "mv", [128, 2], F32) for _ in range(2)]
        sd_t = [sb(cx, st, "sd", [128, 1], F32) for _ in range(2)]
        rs_t = [sb(cx, st, "rs", [128, 1], F32) for _ in range(2)]
        STB = S.bufs("stb", 2)
        tp = [ps(cx, st, "tp", [128, D], BF16) for _ in range(2)]
        TP = S.bufs("tp", 2)
        op_ = [ps(cx, st, "o", [128, 512], F32) for _ in range(4)]
        OP = S.bufs("o", 4)
        nchunk = NT // 128

        def load(c):
            S.dma("sp", at[c % 2][:], a_d[c * 128:(c + 1) * 128, :], writes=[AT[c % 2]], key="at%d" % (c % 2))
            S.dma("sp", xs[c % 2][:], xin[c * 128:(c + 1) * 128, :], writes=[XS[c % 2]], key="xsp%d" % (c % 2))

        load(0)
        for c in range(nchunk):
            c2 = c % 2
            if c + 1 < nchunk:
                load(c + 1)
            for k in range(KC):
                S.op("pe", lambda k=k, c2=c2: nc.tensor.transpose(out=tp[c2][:, k * 128:(k + 1) * 128],
                                                                   in_=at[c2][:, k * 128:(k + 1) * 128], identity=ident[:]),
                     reads=[AT[c2], IB], writes=[TP[c2]])
            S.op("act", lambda c2=c2: nc.scalar.copy(out=aT[c2][:], in_=tp[c2][:].rearrange("p (k t) -> p k t", k=KC)),
                 reads=[TP[c2]], writes=[ATT[c2]])
            for n in range(2):
                o_ = 2 * c2 + n
                for k in range(KC):
                    S.op("pe", lambda n=n, k=k, c2=c2, o_=o_: nc.tensor.matmul(
                        op_[o_][:], aT[c2][:, k, :], wo[:, k, n * 512:(n + 1) * 512], start=(k == 0), stop=(k == KC - 1)),
                        reads=[ATT[c2], WO], writes=[OP[o_]])
                S.op("dve", lambda n=n, c2=c2, o_=o_: nc.vector.scalar_tensor_tensor(
                    out=z[c2][:, n * 512:(n + 1) * 512], in0=op_[o_][:], scalar=c_res, in1=xs[c2][:, n * 512:(n + 1) * 512],
                    op0=ALU.mult, op1=ALU.add), reads=[OP[o_], XS[c2]], writes=[Z[c2]])
            ln_epilogue(cx, z[c2], Z[c2], gb, GB, st_t[c2], mv_t[c2], sd_t[c2], rs_t[c2], STB[c2], eps)
            S.dma("sp", xout[c * 128:(c + 1) * 128, :], z[c2][:], reads=[Z[c2]], key="zp%d" % c2)
        S.flush()


def build_test_ffn(NT):
    nc = bass.Bass("TRN2", target_bir_lowering=False)
    with ExitStack() as stack:
        cx = Ctx(nc, stack)
        x = nc.dram_tensor("x", [NT, D], F32, kind="ExternalInput").ap()
        w_in = nc.dram_tensor("w_in", [D, 2 * DFF], F32, kind="ExternalInput").ap()
        w_out = nc.dram_tensor("w_out", [DFF, D], F32, kind="ExternalInput").ap()
        g = nc.dram_tensor("g", [D], F32, kind="ExternalInput").ap()
        b = nc.dram_tensor("b", [D], F32, kind="ExternalInput").ap()
        cx.ident_d = nc.dram_tensor("ident", [128, 128], F32, kind="ExternalInput").ap()
        y = nc.dram_tensor("y", [NT, D], F32, kind="ExternalOutput").ap()
        phase_ffn(cx, x, y, w_in, w_out, g, b, NT, "f")
        print("instructions:", cx.S.n_inst)
    return nc


def build_test_gmlp(NT):
    nc = bass.Bass("TRN2", target_bir_lowering=False)
    with ExitStack() as stack:
        cx = Ctx(nc, stack)
        x = nc.dram_tensor("x", [NT, D], F32, kind="ExternalInput").ap()
        w_in = nc.dram_tensor("w_in", [D, 2 * GW], F32, kind="ExternalInput").ap()
        lng = nc.dram_tensor("lng", [GW], F32, kind="ExternalInput").ap()
        lnb = nc.dram_tensor("lnb", [GW], F32, kind="ExternalInput").ap()
        ws = nc.dram_tensor("ws", [16, 128, 128], F32, kind="ExternalInput").ap()
        bs = nc.dram_tensor("bs", [16, 128], F32, kind="ExternalInput").ap()
        w_out = nc.dram_tensor("w_out", [GW, D], F32, kind="ExternalInput").ap()
        g = nc.dram_tensor("g", [D], F32, kind="ExternalInput").ap()
        b = nc.dram_tensor("b", [D], F32, kind="ExternalInput").ap()
        cx.ident_d = nc.dram_tensor("ident", [128, 128], F32, kind="ExternalInput").ap()
        cx.tril_d = nc.dram_tensor("tril", [128, 128], F32, kind="ExternalInput").ap()
        uT_d = nc.dram_tensor("uT_d", [GW, NT], BF16, kind="Internal").ap()
        vln_d = nc.dram_tensor("vln_d", [NT, GW], BF16, kind="Internal").ap()
        y = nc.dram_tensor("y", [NT, D], F32, kind="ExternalOutput").ap()
        phase_g1(cx, x, uT_d, vln_d, w_in, lng, lnb, NT)
        phase_g2(cx, x, y, uT_d, vln_d, ws, bs, w_out, g, b, NT)
        print("instructions:", cx.S.n_inst)
    return nc


def nsa_dram(nc, NT, kind_local="Internal"):
    d = {}
    d["QT"] = nc.dram_tensor("QT_d", [64, 16, NT], BF16, kind=kind_local).ap()
    d["KsT"] = nc.dram_tensor("KsT_d", [64, 4, NT], BF16, kind=kind_local).ap()
    d["KwT"] = nc.dram_tensor("KwT_d", [64, 4, NT], BF16, kind=kind_local).ap()
    d["KcT"] = nc.dram_tensor("KcT_d", [64, 4, NT], BF16, kind=kind_local).ap()
    d["VcT"] = nc.dram_tensor("VcT_d", [64, 4, NT], BF16, kind=kind_local).ap()
    d["Vs"] = nc.dram_tensor("Vs_d", [NT, 256], BF16, kind=kind_local).ap()
    d["Vw"] = nc.dram_tensor("Vw_d", [NT, 256], BF16, kind=kind_local).ap()
    d["gates"] = nc.dram_tensor("gates_d", [NT, 48], F32, kind=kind_local).ap()
    return d


def local_from_dict(d):
    return NsaLocal(lambda k, g: d[k][:, g, :], lambda k, t0: d[k][t0:t0 + 128, :], d["QT"], d["gates"])


def gathered_from_dict(al, NT):
    return NsaGathered(lambda k, r, g: al[k][r][:, g, :], lambda k, r, hf: al[k][r][hf * NT // 2:(hf + 1) * NT // 2, :])


def build_test_n1(NT):
    nc = bass.Bass("TRN2", target_bir_lowering=False)
    with ExitStack() as stack:
        cx = Ctx(nc, stack)
        x = nc.dram_tensor("x", [NT, D], F32, kind="ExternalInput").ap()
        w_in = nc.dram_tensor("w_in", [D, NSA_COLS], F32, kind="ExternalInput").ap()
        cos_d = nc.dram_tensor("cos", [128, NT], F32, kind="ExternalInput").ap()
        sin_d = nc.dram_tensor("sin", [128, NT], F32, kind="ExternalInput").ap()
        cx.ident_d = nc.dram_tensor("ident", [128, 128], F32, kind="ExternalInput").ap()
        d = nsa_dram(nc, NT, "ExternalOutput")
        phase_n1(cx, x, w_in, cos_d, sin_d, local_from_dict(d), NT)
        print("instructions:", cx.S.n_inst)
    return nc


def rope_tables(pos):
    half = HD // 2
    freq = (10000.0 ** (-np.arange(half, dtype=np.float32) / half)).astype(np.float32)
    ang = pos.astype(np.float32)[None, :] * freq[:, None]
    c = np.cos(ang).astype(np.float32)
    s_ = np.sin(ang).astype(np.float32)
    return np.ascontiguousarray(np.tile(c, (4, 1))), np.ascontiguousarray(np.tile(s_, (4, 1)))


def attn_tables(r, NB):
    NT = NB * 128
    SEQ = 4 * NT
    NCP = SEQ // 16
    q = np.arange(128)
    t = {}
    pos = ((4 * np.arange(NB)[:, None] + r) * 128 + q[None, :]).reshape(-1)
    t["cos"], t["sin"] = rope_tables(pos)
    cpos = 16 * np.arange(NCP) + 31
    t["cosc"], t["sinc"] = rope_tables(cpos)
    m = np.arange(-2, 31)
    vis = m[None, :] <= (8 * r + np.floor((q[:, None] - 31) / 16.0))
    t["cmpmask"] = np.where(vis, 0.0, NEGM).astype(np.float32)
    mrel = np.arange(-1, 8)[None, :]
    cur = (2 * r + (q >= 64).astype(np.int64))[:, None]
    keep = np.ones((128, 9), np.float32)
    addt = np.zeros((128, 9), np.float32)
    fut = mrel > cur
    keep[fut] = 0.0
    addt[fut] = -1e9
    c1 = mrel == cur - 1
    keep[c1] = 0.0
    addt[c1] = 1e9
    c0 = mrel == cur
    keep[c0] = 0.0
    addt[c0] = 2e9
    t["keep"], t["addt"] = keep, addt
    k = np.arange(128)[:, None, None]
    kr = np.arange(4)[None, :, None]
    qq = q[None, None, :]
    t["caus"] = ((128 * (kr - r) + k) <= qq).astype(np.float32)
    kr8 = np.arange(-4, 4)[None, :, None]
    dl = r - kr8
    wm = np.where(dl == 0, k <= qq, np.where((dl >= 1) & (dl <= 3), True, np.where(dl == 4, k > qq, False)))
    t["wmask"] = np.broadcast_to(wm, (128, 8, 128)).astype(np.float32)
    t["ident"] = np.eye(128, dtype=np.float32)
    t["tril"] = np.tril(np.ones((128, 128), np.float32))
    return t


def declare_tables(cx, nc, NB):
    NT = NB * 128
    NCP = 4 * NT // 16
    cx.ident_d = nc.dram_tensor("ident", [128, 128], F32, kind="ExternalInput").ap()
    cx.tril_d = nc.dram_tensor("tril", [128, 128], F32, kind="ExternalInput").ap()
    cx.cos_d = nc.dram_tensor("cos", [128, NT], F32, kind="ExternalInput").ap()
    cx.sin_d = nc.dram_tensor("sin", [128, NT], F32, kind="ExternalInput").ap()
    cx.cosc_d = nc.dram_tensor("cosc", [128, NCP], F32, kind="ExternalInput").ap()
    cx.sinc_d = nc.dram_tensor("sinc", [128, NCP], F32, kind="ExternalInput").ap()
    cx.cmpmask_d = nc.dram_tensor("cmpmask", [128, 33], F32, kind="ExternalInput").ap()
    cx.keep_d = nc.dram_tensor("keep", [128, 9], F32, kind="ExternalInput").ap()
    cx.addt_d = nc.dram_tensor("addt", [128, 9], F32, kind="ExternalInput").ap()
    cx.caus_d = nc.dram_tensor("caus", [128, 4, 128], F32, kind="ExternalInput").ap()
    cx.wmask_d = nc.dram_tensor("wmask", [128, 8, 128], F32, kind="ExternalInput").ap()


def gathered_dram(nc, NT, kind):
    al = {}
    al["KsT"] = nc.dram_tensor("KsT_all", [4, 64, 4, NT], BF16, kind=kind).ap()
    al["KwT"] = nc.dram_tensor("KwT_all", [4, 64, 4, NT], BF16, kind=kind).ap()
    al["KcT"] = nc.dram_tensor("KcT_all", [4, 64, 4, NT], BF16, kind=kind).ap()
    al["VcT"] = nc.dram_tensor("VcT_all", [4, 64, 4, NT], BF16, kind=kind).ap()
    al["Vs"] = nc.dram_tensor("Vs_all", [4, NT, 256], BF16, kind=kind).ap()
    al["Vw"] = nc.dram_tensor("Vw_all", [4, NT, 256], BF16, kind=kind).ap()
    return al


def build_test_attn(NB):
    NT = NB * 128
    nc = bass.Bass("TRN2", target_bir_lowering=False)
    with ExitStack() as stack:
        cx = Ctx(nc, stack)
        declare_tables(cx, nc, NB)
        QT = nc.dram_tensor("QT_d", [64, 16, NT], BF16, kind="ExternalInput").ap()
        gates = nc.dram_tensor("gates_d", [NT, 48], F32, kind="ExternalInput").ap()
        al = gathered_dram(nc, NT, "ExternalInput")
        w1k = nc.dram_tensor("w1k", [32, 64, 256], F32, kind="ExternalInput").ap()
        w2k = nc.dram_tensor("w2k", [256, 64], F32, kind="ExternalInput").ap()
        pek = nc.dram_tensor("pek", [32, 64], F32, kind="ExternalInput").ap()
        w1v = nc.dram_tensor("w1v", [32, 64, 256], F32, kind="ExternalInput").ap()
        w2v = nc.dram_tensor("w2v", [256, 64], F32, kind="ExternalInput").ap()
        pev = nc.dram_tensor("pev", [32, 64], F32, kind="ExternalInput").ap()
        O_d = nc.dram_tensor("O_d", [NT, 1024], BF16, kind="ExternalOutput").ap()
        phase_attn(cx, gathered_from_dict(al, NT), QT, gates, O_d, w1k, w2k, pek, w1v, w2v, pev, NB)
        print("instructions:", cx.S.n_inst)
    return nc


L0_W = ["l0_ffn1_w_in", "l0_ffn1_w_out", "l0_ln1_g", "l0_ln1_b", "l0_gm_w_in", "l0_gm_ln_g", "l0_gm_ln_b", "l0_gm_w_s",
        "l0_gm_b_s", "l0_gm_w_out", "l0_ln2_g", "l0_ln2_b", "l0_ffn2_w_in", "l0_ffn2_w_out", "l0_ln3_g", "l0_ln3_b"]
L1A_W = ["l1_ffn1_w_in", "l1_ffn1_w_out", "l1_ln1_g", "l1_ln1_b", "l1_nsa_w_in"]
L1B_W = ["l1_nsa_cmp_pe_k", "l1_nsa_cmp_w1_k", "l1_nsa_cmp_w2_k", "l1_nsa_cmp_pe_v", "l1_nsa_cmp_w1_v", "l1_nsa_cmp_w2_v",
         "l1_nsa_w_out", "l1_ln2_g", "l1_ln2_b", "l1_ffn2_w_in", "l1_ffn2_w_out", "l1_ln3_g", "l1_ln3_b"]
W_SHAPES = {
    "ffn1_w_in": [D, 2 * DFF], "ffn2_w_in": [D, 2 * DFF], "ffn1_w_out": [DFF, D], "ffn2_w_out": [DFF, D],
    "gm_w_in": [D, 2 * GW], "gm_ln_g": [GW], "gm_ln_b": [GW], "gm_w_s": [16, 128, 128], "gm_b_s": [16, 128], "gm_w_out": [GW, D],
    "nsa_w_in": [D, NSA_COLS], "nsa_cmp_pe_k": [32, 64], "nsa_cmp_w1_k": [32, 64, 256], "nsa_cmp_w2_k": [256, 64],
    "nsa_cmp_pe_v": [32, 64], "nsa_cmp_w1_v": [32, 64, 256], "nsa_cmp_w2_v": [256, 64], "nsa_w_out": [D, D],
}


def wshape(name):
    base = name[3:]
    if base in W_SHAPES:
        return W_SHAPES[base]
    return [D]


def declare_w(nc, names):
    return {n: nc.dram_tensor(n, wshape(n), F32, kind="ExternalInput").ap() for n in names}


def emit_part_a(cx, nc, w, x, NT, xa, xb, uT_d, vln_d, nd):
    phase_ffn(cx, x, xa, w["l0_ffn1_w_in"], w["l0_ffn1_w_out"], w["l0_ln1_g"], w["l0_ln1_b"], NT, "a")
    phase_g1(cx, xa, uT_d, vln_d, w["l0_gm_w_in"], w["l0_gm_ln_g"], w["l0_gm_ln_b"], NT)
    phase_g2(cx, xa, xb, uT_d, vln_d, w["l0_gm_w_s"], w["l0_gm_b_s"], w["l0_gm_w_out"], w["l0_ln2_g"], w["l0_ln2_b"], NT)
    phase_ffn(cx, xb, xa, w["l0_ffn2_w_in"], w["l0_ffn2_w_out"], w["l0_ln3_g"], w["l0_ln3_b"], NT, "b")
    phase_ffn(cx, xa, xb, w["l1_ffn1_w_in"], w["l1_ffn1_w_out"], w["l1_ln1_g"], w["l1_ln1_b"], NT, "c")
    phase_n1(cx, xb, w["l1_nsa_w_in"], cx.cos_d, cx.sin_d, nd, NT)
    return xb


def emit_part_b(cx, nc, w, xmid, y, NT, NB, al, QT, gates, O_d, xa):
    phase_attn(cx, al, QT, gates, O_d, w["l1_nsa_cmp_w1_k"], w["l1_nsa_cmp_w2_k"], w["l1_nsa_cmp_pe_k"],
               w["l1_nsa_cmp_w1_v"], w["l1_nsa_cmp_w2_v"], w["l1_nsa_cmp_pe_v"], NB)
    phase_proj(cx, O_d, xmid, xa, w["l1_nsa_w_out"], w["l1_ln2_g"], w["l1_ln2_b"], NT, 1.0 / ALPHA)
    phase_ffn(cx, xa, y, w["l1_ffn2_w_in"], w["l1_ffn2_w_out"], w["l1_ln3_g"], w["l1_ln3_b"], NT, "d")


def build_a(NB):
    NT = NB * 128
    nc = bass.Bass("TRN2", target_bir_lowering=False)
    with ExitStack() as stack:
        cx = Ctx(nc, stack)
        declare_tables(cx, nc, NB)
        w = declare_w(nc, L0_W + L1A_W)
        x = nc.dram_tensor("x", [NT, D], F32, kind="ExternalInput").ap()
        xa = nc.dram_tensor("xa", [NT, D], F32, kind="Internal").ap()
        xb = nc.dram_tensor("xmid", [NT, D], F32, kind="ExternalOutput").ap()
        uT_d = nc.dram_tensor("uT_d", [GW, NT], BF16, kind="Internal").ap()
        vln_d = nc.dram_tensor("vln_d", [NT, GW], BF16, kind="Internal").ap()
        nd = local_from_dict(nsa_dram(nc, NT, "ExternalOutput"))
        emit_part_a(cx, nc, w, x, NT, xa, xb, uT_d, vln_d, nd)
        print("part A instructions:", cx.S.n_inst)
    return nc


def build_b(NB):
    NT = NB * 128
    nc = bass.Bass("TRN2", target_bir_lowering=False)
    with ExitStack() as stack:
        cx = Ctx(nc, stack)
        declare_tables(cx, nc, NB)
        w = declare_w(nc, L1B_W)
        xmid = nc.dram_tensor("xmid", [NT, D], F32, kind="ExternalInput").ap()
        QT = nc.dram_tensor("QT_d", [64, 16, NT], BF16, kind="ExternalInput").ap()
        gates = nc.dram_tensor("gates_d", [NT, 48], F32, kind="ExternalInput").ap()
        al = gathered_dram(nc, NT, "ExternalInput")
        O_d = nc.dram_tensor("O_d", [NT, D], BF16, kind="Internal").ap()
        xa = nc.dram_tensor("xa", [NT, D], F32, kind="Internal").ap()
        y = nc.dram_tensor("y", [NT, D], F32, kind="ExternalOutput").ap()
        emit_part_b(cx, nc, w, xmid, y, NT, NB, gathered_from_dict(al, NT), QT, gates, O_d, xa)
        print("part B instructions:", cx.S.n_inst)
    return nc


TABLE_KEYS = ("ident", "tril", "cos", "sin", "cosc", "sinc", "cmpmask", "keep", "addt", "caus", "wmask")
GATHER_KEYS = ("KsT", "KwT", "KcT", "VcT", "Vs", "Vw")


def run_unfused(inputs, NB):
    NT = NB * 128
    x = np.asarray(inputs["x"], dtype=np.float32)
    B = x.shape[0]
    tabs = [attn_tables(c % 4, NB) for c in range(NCORES)]
    wts = {k: np.ascontiguousarray(np.asarray(v, dtype=np.float32)) for k, v in inputs.items() if k != "x"}
    in_a = []
    for c in range(NCORES):
        b, r = c // 4, c % 4
        m = {k: tabs[c][k] for k in TABLE_KEYS}
        m["x"] = np.ascontiguousarray(x[b].reshape(NB, 4, 128, D)[:, r].reshape(NT, D))
        for k in L0_W + L1A_W:
            m[k] = wts[k]
        in_a.append(m)
    res_a = run_bass_kernel_spmd(build_a(NB), in_a, core_ids=list(range(NCORES))).results
    in_b = []
    for c in range(NCORES):
        b = c // 4
        m = {k: tabs[c][k] for k in TABLE_KEYS}
        m["xmid"] = res_a[c]["xmid"]
        m["QT_d"] = res_a[c]["QT_d"]
        m["gates_d"] = res_a[c]["gates_d"]
        for k in GATHER_KEYS:
            m[k + "_all"] = np.ascontiguousarray(np.stack([np.asarray(res_a[4 * b + rr][k + "_d"]) for rr in range(4)]))
        for k in L1B_W:
            m[k] = wts[k]
        in_b.append(m)
    res_b = run_bass_kernel_spmd(build_b(NB), in_b, core_ids=list(range(NCORES))).results
    out = np.zeros((B, 4 * NT, D), np.float32)
    for c in range(NCORES):
        b, r = c // 4, c % 4
        out[b].reshape(NB, 4, 128, D)[:, r] = np.asarray(res_b[c]["y"]).reshape(NB, 128, D)
    return out


def build_fused(NB):
    NT = NB * 128
    nc = bass.Bass("TRN2", target_bir_lowering=False)
    with ExitStack() as stack:
        cx = Ctx(nc, stack)
        declare_tables(cx, nc, NB)
        w = declare_w(nc, L0_W + L1A_W + L1B_W)
        x = nc.dram_tensor("x", [NT, D], F32, kind="ExternalInput").ap()
        y = nc.dram_tensor("y", [NT, D], F32, kind="ExternalOutput").ap()
        xa = nc.dram_tensor("xa", [NT, D], F32, kind="Internal").ap()
        xb = nc.dram_tensor("xb", [NT, D], F32, kind="Internal").ap()
        uT_d = nc.dram_tensor("uT_d", [GW, NT], BF16, kind="Internal").ap()
        vln_d = nc.dram_tensor("vln_d", [NT, GW], BF16, kind="Internal").ap()
        O_d = nc.dram_tensor("O_d", [NT, D], BF16, kind="Internal").ap()
        QT = nc.dram_tensor("QT_d", [64, 16, NT], BF16, kind="Internal").ap()
        gates = nc.dram_tensor("gates_d", [NT, 48], F32, kind="Internal").ap()
        loc, gat = {}, {}
        for k in GATHER_KEYS:
            for hf in range(2):
                loc[k, hf] = nc.dram_tensor("%s_loc%d" % (k, hf), [128, NT], BF16, kind="Internal").ap()
                gat[k, hf] = nc.dram_tensor("%s_gat%d" % (k, hf), [4 * 128, NT], BF16, kind="Internal").ap()
        tokv = lambda a: a.rearrange("r (x c) -> (r x) c", c=256)
        H = NT // 2
        nd = NsaLocal(lambda k, g: loc[k, g // 2][(g % 2) * 64:(g % 2) * 64 + 64, :],
                      lambda k, t0: tokv(loc[k, t0 // H])[t0 % H:t0 % H + 128, :], QT, gates)
        al = NsaGathered(lambda k, r, g: gat[k, g // 2][r * 128 + (g % 2) * 64:r * 128 + (g % 2) * 64 + 64, :],
                         lambda k, r, hf: tokv(gat[k, hf][r * 128:(r + 1) * 128, :]))
        xmid = emit_part_a(cx, nc, w, x, NT, xa, xb, uT_d, vln_d, nd)
        for k in GATHER_KEYS:
            for hf in range(2):
                cx.S.cc("AllGather", [[0, 1, 2, 3], [4, 5, 6, 7]], loc[k, hf], gat[k, hf], key="cc_%s%d" % (k, hf))
        cx.S.flush()
        emit_part_b(cx, nc, w, xmid, y, NT, NB, al, QT, gates, O_d, xa)
        print("fused instructions:", cx.S.n_inst)
    return nc


def run_fused(inputs, NB):
    NT = NB * 128
    x = np.asarray(inputs["x"], dtype=np.float32)
    B = x.shape[0]
    wts = {k: np.ascontiguousarray(np.asarray(v, dtype=np.float32)) for k, v in inputs.items() if k != "x"}
    in_maps = []
    for c in range(NCORES):
        b, r = c // 4, c % 4
        tabs = attn_tables(r, NB)
        m = {k: tabs[k] for k in TABLE_KEYS}
        m["x"] = np.ascontiguousarray(x[b].reshape(NB, 4, 128, D)[:, r].reshape(NT, D))
        for k in L0_W + L1A_W + L1B_W:
            m[k] = wts[k]
        in_maps.append(m)
    res = run_bass_kernel_spmd(build_fused(NB), in_maps, core_ids=list(range(NCORES))).results
    out = np.zeros((B, 4 * NT, D), np.float32)
    for c in range(NCORES):
        b, r = c // 4, c % 4
        out[b].reshape(NB, 4, 128, D)[:, r] = np.asarray(res[c]["y"]).reshape(NB, 128, D)
    return out


def kernel(**inputs):
    return run_fused(inputs, 32)
```

```python
import math
from contextlib import ExitStack

import numpy as np
import concourse.bass as bass
import concourse.mybir as mybir
from concourse.bass_utils import run_bass_kernel_spmd

F32 = mybir.dt.float32
BF16 = mybir.dt.bfloat16
AF = mybir.ActivationFunctionType
ALU = mybir.AluOpType
AX = mybir.AxisListType

D = 1024
DFF = 2816
DEPTH = 2
ALPHA = (2 * DEPTH) ** 0.25
LN_EPS = 1e-5
NCORES = 8


class Buf:
    __slots__ = ("name", "writers", "dma_writers", "readers", "dma_readers")

    def __init__(self, name):
        self.name = name
        self.writers = {}
        self.dma_writers = []
        self.readers = {}
        self.dma_readers = []


class Op:
    __slots__ = ("eng", "fn", "deps", "is_dma", "dsem", "dcount", "signal", "sigval", "emitted")

    def __init__(self, eng, fn, is_dma):
        self.eng = eng
        self.fn = fn
        self.deps = []
        self.is_dma = is_dma
        self.dsem = None
        self.dcount = 0
        self.signal = False
        self.sigval = 0
        self.emitted = False


class Sched:
    def __init__(self, nc, stack):
        self.nc = nc
        self.engs = {"pe": nc.tensor, "act": nc.scalar, "dve": nc.vector, "pool": nc.gpsimd, "sp": nc.sync}
        self.esem = {e: stack.enter_context(nc.semaphore("es_" + e)) for e in self.engs}
        self.stack = stack
        self.pending = []
        self.sigcount = {e: 0 for e in self.engs}
        self.waited = {e: {} for e in self.engs}
        self.dsems = {}
        self.n_inst = 0

    def buf(self, name):
        return Buf(name)

    def bufs(self, name, n):
        return [Buf("%s%d" % (name, i)) for i in range(n)]

    def _add_dep(self, op, p):
        if p is op:
            return
        if (not p.is_dma) and (not op.is_dma) and p.eng == op.eng and op.eng == "pe":
            return
        op.deps.append(p)

    def _track(self, op, reads, writes):
        for b in reads:
            for p in b.writers.values():
                self._add_dep(op, p)
            for p in b.dma_writers:
                self._add_dep(op, p)
        for b in writes:
            if b.readers or b.dma_readers:
                for p in b.readers.values():
                    self._add_dep(op, p)
                for p in b.dma_readers:
                    self._add_dep(op, p)
                for p in b.writers.values():
                    self._add_dep(op, p)
                for p in b.dma_writers:
                    self._add_dep(op, p)
                b.readers = {}
                b.dma_readers = []
                b.writers = {}
                b.dma_writers = []
            else:
                for e, p in b.writers.items():
                    if e != op.eng or op.is_dma:
                        self._add_dep(op, p)
                for p in b.dma_writers:
                    self._add_dep(op, p)
        for b in reads:
            if op.is_dma:
                b.dma_readers.append(op)
            else:
                b.readers[op.eng] = op
        for b in writes:
            if op.is_dma:
                b.dma_writers.append(op)
            else:
                b.writers[op.eng] = op

    def op(self, eng, fn, reads=(), writes=()):
        o = Op(eng, fn, False)
        self._track(o, reads, writes)
        self.pending.append(o)
        return o

    def dma(self, eng, out, in_, reads=(), writes=(), key=None, slow=False):
        assert key is not None
        if key not in self.dsems:
            self.dsems[key] = [self.stack.enter_context(self.nc.semaphore("ds_" + key)), 0, 16]
        ent = self.dsems[key]
        ent[1] += 1
        if slow:
            o = Op(eng, (lambda: self.engs[eng].dma_start(out=out, in_=in_, allow_slow_non_contiguous=True)), True)
        else:
            o = Op(eng, (lambda: self.engs[eng].dma_start(out=out, in_=in_)), True)
        o.dsem = key
        o.dcount = ent[1]
        self._track(o, reads, writes)
        self.pending.append(o)
        return o

    def cc(self, kind, groups, in_ap, out_ap, reads=(), writes=(), key=None):
        assert key not in self.dsems
        self.dsems[key] = [self.stack.enter_context(self.nc.semaphore("ds_" + key)), 1, 1]
        o = Op("pool", (lambda: self.nc.gpsimd.collective_compute(kind, ALU.bypass, replica_groups=groups,
                                                                  ins=[in_ap.opt()], outs=[out_ap.opt()])), True)
        o.dsem = key
        o.dcount = 1
        self._track(o, reads, writes)
        self.pending.append(o)
        return o

    def _wait(self, eng, semkey, sem, val):
        w = self.waited[eng]
        if w.get(semkey, 0) >= val:
            return
        w[semkey] = val
        self.engs[eng].wait_ge(sem, val)
        self.n_inst += 1

    def flush(self):
        for o in self.pending:
            o.deps = [p for p in o.deps if not p.emitted]
            for p in o.deps:
                if not p.is_dma:
                    p.signal = True
        last = {}
        for o in self.pending:
            if not o.is_dma:
                last[o.eng] = o
        for o in last.values():
            o.signal = True
        for o in self.pending:
            for p in o.deps:
                assert p.emitted, "dependency on later op"
                if p.is_dma:
                    self._wait(o.eng, "d_" + p.dsem, self.dsems[p.dsem][0], self.dsems[p.dsem][2] * p.dcount)
                else:
                    self._wait(o.eng, "e_" + p.eng, self.esem[p.eng], p.sigval)
            ins = o.fn()
            self.n_inst += 1
            if o.is_dma:
                if self.dsems[o.dsem][2] == 16:
                    ins.then_inc(self.dsems[o.dsem][0], 16)
                else:
                    ins.then_inc(self.dsems[o.dsem][0])
            elif o.signal:
                self.sigcount[o.eng] += 1
                o.sigval = self.sigcount[o.eng]
                ins.then_inc(self.esem[o.eng], 1)
            o.emitted = True
            o.fn = None
        self.pending = []
        self.barrier()

    def barrier(self):
        for e in self.engs:
            for e2 in self.engs:
                if e2 != e and self.sigcount[e2] > 0:
                    self._wait(e, "e_" + e2, self.esem[e2], self.sigcount[e2])
            for key, (sem, cnt, unit) in self.dsems.items():
                if cnt > 0:
                    self._wait(e, "d_" + key, sem, unit * cnt)


class Ctx:
    def __init__(self, nc, stack):
        self.nc = nc
        self.stack = stack
        self.S = Sched(nc, stack)
        self.uid = 0

    def name(self, s):
        self.uid += 1
        return "%s_%d" % (s, self.uid)


def sb(cx, st, name, shape, dt):
    return st.enter_context(cx.nc.sbuf_tensor(cx.name(name), shape, dt))


def ps(cx, st, name, shape, dt):
    return st.enter_context(cx.nc.psum_tensor(cx.name(name), shape, dt))


def make_ident(cx, st):
    nc, S = cx.nc, cx.S
    ident = sb(cx, st, "ident", [128, 128], BF16)
    IB = S.buf("ident")
    S.dma("pool", ident[:], cx.ident_d[:, :], writes=[IB], key="ident")
    return ident, IB


def ln_epilogue(cx, z, ZB, gb, GB, st_t, mv_t, sd_t, rs_t, SB_, eps):
    nc, S = cx.nc, cx.S
    S.op("dve", lambda: nc.vector.bn_stats(out=st_t[:, 0, :], in_=z[:, 0:512]), reads=[ZB], writes=[SB_])
    S.op("dve", lambda: nc.vector.bn_stats(out=st_t[:, 1, :], in_=z[:, 512:1024]), reads=[ZB], writes=[SB_])
    S.op("dve", lambda: nc.vector.bn_aggr(out=mv_t[:], in_=st_t[:]), reads=[SB_], writes=[SB_])
    S.op("act", lambda: nc.scalar.activation(out=sd_t[:], in_=mv_t[:, 1:2], func=AF.Sqrt, bias=eps, scale=1.0),
         reads=[SB_], writes=[SB_])
    S.op("dve", lambda: nc.vector.reciprocal(out=rs_t[:], in_=sd_t[:]), reads=[SB_], writes=[SB_])
    S.op("dve", lambda: nc.vector.tensor_scalar(out=z[:], in0=z[:], scalar1=mv_t[:, 0:1], scalar2=rs_t[:, 0:1],
                                                op0=ALU.subtract, op1=ALU.mult), reads=[ZB, SB_], writes=[ZB])
    S.op("pool", lambda: nc.gpsimd.tensor_tensor(out=z[:], in0=z[:], in1=gb[:, 0, :], op=ALU.mult),
         reads=[ZB, GB], writes=[ZB])
    S.op("pool", lambda: nc.gpsimd.tensor_tensor(out=z[:], in0=z[:], in1=gb[:, 1, :], op=ALU.add),
         reads=[ZB, GB], writes=[ZB])


def load_gb(cx, st, g_d, b_d, name):
    nc, S = cx.nc, cx.S
    gb = sb(cx, st, name, [128, 2, D], F32)
    GB = S.buf(name)
    S.dma("sp", gb[:, 0, :], g_d.partition_broadcast(128), writes=[GB], key=name)
    S.dma("sp", gb[:, 1, :], b_d.partition_broadcast(128), writes=[GB], key=name)
    return gb, GB


def emit_xT(cx, xs_t, XSb, xbf, XBF, tp, TP, xT_t, XTb, ident, IB, NS):
    nc, S = cx.nc, cx.S
    KC = D // 128
    S.op("pool", lambda: nc.gpsimd.tensor_copy(out=xbf[:], in_=xs_t[:]), reads=[XSb], writes=[XBF])
    for s in range(NS):
        b = s % len(tp)
        for k in range(KC):
            S.op("pe", lambda s=s, k=k, b=b: nc.tensor.transpose(out=tp[b][:, k * 128:(k + 1) * 128],
                                                                  in_=xbf[:, s, k * 128:(k + 1) * 128], identity=ident[:]),
                 reads=[XBF, IB], writes=[TP[b]])
        S.op("dve", lambda s=s, b=b: nc.vector.tensor_copy(
            out=xT_t[:, :, s * 128:(s + 1) * 128], in_=tp[b][:].rearrange("p (k t) -> p k t", k=KC)),
            reads=[TP[b]], writes=[XTb])


def phase_ffn(cx, xin, xout, w_in_d, w_out_d, g_d, b_d, NT, tag):
    nc, S = cx.nc, cx.S
    TT = 256
    NS = TT // 128
    KC = D // 128
    MC = DFF // 128
    c_res = 0.5 / ALPHA
    eps = LN_EPS / (ALPHA * ALPHA)
    with ExitStack() as st:
        w1 = sb(cx, st, "w1", [128, KC, 2 * DFF], BF16)
        w2 = sb(cx, st, "w2", [128, MC, D], BF16)
        W1, W2 = S.buf("W1"), S.buf("W2")
        for k in range(KC):
            S.dma("pool", w1[:, k, :], w_in_d[k * 128:(k + 1) * 128, :], writes=[W1], key="w1")
        for m in range(MC):
            S.dma("pool", w2[:, m, :], w_out_d[m * 128:(m + 1) * 128, :], writes=[W2], key="w2")
        gb, GB = load_gb(cx, st, g_d, b_d, "gb")
        ident, IB = make_ident(cx, st)
        xs = [sb(cx, st, "xs", [128, NS, D], F32) for _ in range(2)]
        XS = S.bufs("xs", 2)
        xbf = sb(cx, st, "xbf", [128, NS, D], BF16)
        XBF = S.buf("xbf")
        xT = [sb(cx, st, "xT", [128, KC, TT], BF16) for _ in range(2)]
        XT = S.bufs("xT", 2)
        hT = [sb(cx, st, "hT", [128, MC, TT], BF16) for _ in range(2)]
        HT = S.bufs("hT", 2)
        sg = [sb(cx, st, "sg", [128, TT], BF16) for _ in range(2)]
        SG = S.bufs("sg", 2)
        z = [sb(cx, st, "z", [128, D], F32) for _ in range(2)]
        Z = S.bufs("z", 2)
        st_t = [sb(cx, st, "st", [128, 2, 6], F32) for _ in range(2)]
        mv_t = [sb(cx, st, "mv", [128, 2], F32) for _ in range(2)]
        sd_t = [sb(cx, st, "sd", [128, 1], F32) for _ in range(2)]
        rs_t = [sb(cx, st, "rs", [128, 1], F32) for _ in range(2)]
        STB = S.bufs("stb", 2)
        tp = [ps(cx, st, "tp", [128, D], BF16) for _ in range(2)]
        TP = S.bufs("tp", 2)
        gu = [ps(cx, st, "gu", [128, 512], F32) for _ in range(4)]
        GU = S.bufs("gu", 4)
        op_ = [ps(cx, st, "o", [128, 512], F32) for _ in range(2)]
        OP = S.bufs("o", 2)

        ntile = NT // TT
        xin_v = xin.rearrange("(t s p) d -> t p s d", s=NS, p=128)
        xout_v = xout.rearrange("(t s p) d -> t s p d", s=NS, p=128)

        def load(t):
            S.dma("sp", xs[t % 2][:], xin_v[t], writes=[XS[t % 2]], key="xs%d" % (t % 2))

        load(0)
        for t in range(ntile):
            b2 = t % 2
            if t + 1 < ntile:
                load(t + 1)
            emit_xT(cx, xs[b2], XS[b2], xbf, XBF, tp, TP, xT[b2], XT[b2], ident, IB, NS)
            for m in range(MC):
                g_ps, u_ps = gu[2 * (m % 2)], gu[2 * (m % 2) + 1]
                GP, UP = GU[2 * (m % 2)], GU[2 * (m % 2) + 1]
                for k in range(KC):
                    S.op("pe", lambda m=m, k=k, g_ps=g_ps, b2=b2: nc.tensor.matmul(
                        g_ps[:, 0:TT], w1[:, k, m * 128:(m + 1) * 128], xT[b2][:, k, :], start=(k == 0), stop=(k == KC - 1)),
                        reads=[W1, XT[b2]], writes=[GP])
                for k in range(KC):
                    S.op("pe", lambda m=m, k=k, u_ps=u_ps, b2=b2: nc.tensor.matmul(
                        u_ps[:, 0:TT], w1[:, k, DFF + m * 128:DFF + (m + 1) * 128], xT[b2][:, k, :], start=(k == 0), stop=(k == KC - 1)),
                        reads=[W1, XT[b2]], writes=[UP])
                S.op("act", lambda m=m, g_ps=g_ps: nc.scalar.activation(out=sg[m % 2][:], in_=g_ps[:, 0:TT], func=AF.Silu),
                     reads=[GP], writes=[SG[m % 2]])
                S.op("dve", lambda m=m, u_ps=u_ps, b2=b2: nc.vector.tensor_tensor(
                    out=hT[b2][:, m, :], in0=sg[m % 2][:], in1=u_ps[:, 0:TT], op=ALU.mult),
                    reads=[SG[m % 2], UP], writes=[HT[b2]])
            for s in range(NS):
                for n in range(2):
                    for m in range(MC):
                        S.op("pe", lambda s=s, n=n, m=m, b2=b2: nc.tensor.matmul(
                            op_[n][:], hT[b2][:, m, s * 128:(s + 1) * 128], w2[:, m, n * 512:(n + 1) * 512],
                            start=(m == 0), stop=(m == MC - 1)), reads=[HT[b2], W2], writes=[OP[n]])
                    S.op("dve", lambda s=s, n=n, b2=b2: nc.vector.scalar_tensor_tensor(
                        out=z[s][:, n * 512:(n + 1) * 512], in0=op_[n][:], scalar=c_res, in1=xs[b2][:, s, n * 512:(n + 1) * 512],
                        op0=ALU.mult, op1=ALU.add), reads=[OP[n], XS[b2]], writes=[Z[s]])
                ln_epilogue(cx, z[s], Z[s], gb, GB, st_t[s], mv_t[s], sd_t[s], rs_t[s], STB[s], eps)
                S.dma("sp", xout_v[t, s], z[s][:], reads=[Z[s]], key="zst%d" % s)
        S.flush()


GW = 3072


def ln_stats(cx, src, SRC, nchunk, st_t, mv_t, sd_t, rs_t, SB_, eps):
    nc, S = cx.nc, cx.S
    for c in range(nchunk):
        S.op("dve", lambda c=c: nc.vector.bn_stats(out=st_t[:, c, :], in_=src[:, c * 512:(c + 1) * 512]),
             reads=[SRC], writes=[SB_])
    S.op("dve", lambda: nc.vector.bn_aggr(out=mv_t[:], in_=st_t[:]), reads=[SB_], writes=[SB_])
    S.op("act", lambda: nc.scalar.activation(out=sd_t[:], in_=mv_t[:, 1:2], func=AF.Sqrt, bias=eps, scale=1.0),
         reads=[SB_], writes=[SB_])
    S.op("dve", lambda: nc.vector.reciprocal(out=rs_t[:], in_=sd_t[:]), reads=[SB_], writes=[SB_])


def phase_g1(cx, xin, uT_d, vln_d, w_in_d, lng_d, lnb_d, NT):
    nc, S = cx.nc, cx.S
    TT, NS, KC, JC = 256, 2, 8, 24
    with ExitStack() as st:
        wg = sb(cx, st, "wg", [128, KC, 2 * GW], BF16)
        WG = S.buf("WG")
        for k in range(KC):
            S.dma("pool", wg[:, k, :], w_in_d[k * 128:(k + 1) * 128, :], writes=[WG], key="wg")
        gbv = sb(cx, st, "gbv", [128, 2, GW], F32)
        GBV = S.buf("gbv")
        S.dma("sp", gbv[:, 0, :], lng_d.partition_broadcast(128), writes=[GBV], key="gbv")
        S.dma("sp", gbv[:, 1, :], lnb_d.partition_broadcast(128), writes=[GBV], key="gbv")
        ident, IB = make_ident(cx, st)
        xs = [sb(cx, st, "xs", [128, NS, D], F32) for _ in range(2)]
        XS = S.bufs("xs", 2)
        xbf = sb(cx, st, "xbf", [128, NS, D], BF16)
        XBF = S.buf("xbf")
        xT = [sb(cx, st, "xT", [128, KC, TT], BF16) for _ in range(2)]
        XT = S.bufs("xT", 2)
        ust = [sb(cx, st, "ust", [128, 4, TT], BF16) for _ in range(2)]
        UST = S.bufs("ust", 2)
        v = [sb(cx, st, "v", [128, GW], F32) for _ in range(2)]
        V = S.bufs("v", 2)
        vln = [sb(cx, st, "vln", [128, GW], BF16) for _ in range(2)]
        VLN = S.bufs("vln", 2)
        st_t = [sb(cx, st, "st", [128, 6, 6], F32) for _ in range(2)]
        mv_t = [sb(cx, st, "mv", [128, 2], F32) for _ in range(2)]
        sd_t = [sb(cx, st, "sd", [128, 1], F32) for _ in range(2)]
        rs_t = [sb(cx, st, "rs", [128, 1], F32) for _ in range(2)]
        STB = S.bufs("stb", 2)
        tp = [ps(cx, st, "tp", [128, D], BF16) for _ in range(2)]
        TP = S.bufs("tp", 2)
        pu = [ps(cx, st, "pu", [128, 512], F32) for _ in range(2)]
        PU = S.bufs("pu", 2)
        pv = [ps(cx, st, "pv", [128, 512], F32) for _ in range(3)]
        PV = S.bufs("pv", 3)
        ntile = NT // TT
        xin_v = xin.rearrange("(t s p) d -> t p s d", s=NS, p=128)
        uT_v = uT_d.rearrange("(j p) n -> p j n", p=128)

        def load(t):
            S.dma("sp", xs[t % 2][:], xin_v[t], writes=[XS[t % 2]], key="xs%d" % (t % 2))

        load(0)
        for t in range(ntile):
            b2 = t % 2
            if t + 1 < ntile:
                load(t + 1)
            emit_xT(cx, xs[b2], XS[b2], xbf, XBF, tp, TP, xT[b2], XT[b2], ident, IB, NS)
            for j in range(JC):
                q = (j // 4) % 2
                for k in range(KC):
                    S.op("pe", lambda j=j, k=k, b2=b2: nc.tensor.matmul(
                        pu[j % 2][:, 0:TT], wg[:, k, j * 128:(j + 1) * 128], xT[b2][:, k, :],
                        start=(k == 0), stop=(k == KC - 1)), reads=[WG, XT[b2]], writes=[PU[j % 2]])
                S.op("act", lambda j=j, q=q: nc.scalar.activation(out=ust[q][:, j % 4, :], in_=pu[j % 2][:, 0:TT],
                                                                    func=AF.Gelu_apprx_tanh),
                     reads=[PU[j % 2]], writes=[UST[q]])
                if j % 4 == 3:
                    S.dma("sp", uT_v[:, j - 3:j + 1, t * TT:(t + 1) * TT], ust[q][:], reads=[UST[q]], key="ust%d" % q)
            for s in range(NS):
                for n in range(6):
                    for k in range(KC):
                        S.op("pe", lambda s=s, n=n, k=k, b2=b2: nc.tensor.matmul(
                            pv[n % 3][:], xT[b2][:, k, s * 128:(s + 1) * 128], wg[:, k, GW + n * 512:GW + (n + 1) * 512],
                            start=(k == 0), stop=(k == KC - 1)), reads=[WG, XT[b2]], writes=[PV[n % 3]])
                    S.op("act", lambda s=s, n=n: nc.scalar.activation(out=v[s][:, n * 512:(n + 1) * 512], in_=pv[n % 3][:],
                                                                        func=AF.Gelu_apprx_tanh),
                         reads=[PV[n % 3]], writes=[V[s]])
                ln_stats(cx, v[s], V[s], 6, st_t[s], mv_t[s], sd_t[s], rs_t[s], STB[s], LN_EPS)
                S.op("dve", lambda s=s: nc.vector.tensor_scalar(out=v[s][:], in0=v[s][:], scalar1=mv_t[s][:, 0:1],
                                                                scalar2=rs_t[s][:, 0:1], op0=ALU.subtract, op1=ALU.mult),
                     reads=[V[s], STB[s]], writes=[V[s]])
                S.op("pool", lambda s=s: nc.gpsimd.tensor_tensor(out=v[s][:], in0=v[s][:], in1=gbv[:, 0, :], op=ALU.mult),
                     reads=[V[s], GBV], writes=[V[s]])
                S.op("pool", lambda s=s: nc.gpsimd.tensor_tensor(out=vln[s][:], in0=v[s][:], in1=gbv[:, 1, :], op=ALU.add),
                     reads=[V[s], GBV], writes=[VLN[s]])
                S.dma("sp", vln_d[t * TT + s * 128:t * TT + (s + 1) * 128, :], vln[s][:], reads=[VLN[s]], key="vln%d" % s)
        S.flush()


def phase_g2(cx, xin, xout, uT_d, vln_d, ws_d, bs_d, w_out_d, g_d, b_d, NT):
    nc, S = cx.nc, cx.S
    JC = 24
    c_res = 1.0 / ALPHA
    eps = LN_EPS / (ALPHA * ALPHA)
    with ExitStack() as st:
        w2 = sb(cx, st, "w2g", [128, JC, D], BF16)
        W2 = S.buf("W2g")
        for j in range(JC):
            S.dma("pool", w2[:, j, :], w_out_d[j * 128:(j + 1) * 128, :], writes=[W2], key="w2g")
        gb, GB = load_gb(cx, st, g_d, b_d, "gb2")
        ident, IB = make_ident(cx, st)
        WT = sb(cx, st, "WT", [128, 16, 128], BF16)
        WTB = S.buf("WT")
        bhi = sb(cx, st, "bhi", [1, 2048], BF16)
        blo = sb(cx, st, "blo", [1, 2048], BF16)
        ones = sb(cx, st, "ones", [1, 128], BF16)
        BB = S.buf("bias")
        with ExitStack() as st2:
            wsf = sb(cx, st2, "wsf", [128, 16, 128], F32)
            WSF = S.buf("wsf")
            S.dma("sp", wsf[:], ws_d.rearrange("g t s -> t g s"), writes=[WSF], key="wsf")
            tril = sb(cx, st2, "tril", [128, 128], F32)
            TR = S.buf("tril")
            S.dma("sp", tril[:], cx.tril_d[:, :], writes=[TR], key="tril")
            wsm = sb(cx, st2, "wsm", [128, 16, 128], BF16)
            WSM = S.buf("wsm")
            S.op("dve", lambda: nc.vector.tensor_tensor(out=wsm[:], in0=wsf[:], in1=tril[:].unsqueeze(1).broadcast_to([128, 16, 128]),
                                                        op=ALU.mult), reads=[WSF, TR], writes=[WSM])
            tpw = [ps(cx, st2, "tpw", [128, D], BF16) for _ in range(2)]
            TPW = S.bufs("tpw", 2)
            for g in range(16):
                S.op("pe", lambda g=g: nc.tensor.transpose(out=tpw[g // 8][:, (g % 8) * 128:(g % 8 + 1) * 128],
                                                           in_=wsm[:, g, :], identity=ident[:]),
                     reads=[WSM, IB], writes=[TPW[g // 8]])
            for h in range(2):
                S.op("dve", lambda h=h: nc.vector.tensor_copy(out=WT[:, h * 8:(h + 1) * 8, :],
                                                              in_=tpw[h][:].rearrange("p (g t) -> p g t", g=8)),
                     reads=[TPW[h]], writes=[WTB])
            bsf = sb(cx, st2, "bsf", [1, 2048], F32)
            BSF = S.buf("bsf")
            S.dma("sp", bsf[:], bs_d.rearrange("g t -> (g t)").unsqueeze(0), writes=[BSF], key="bsf")
            S.op("dve", lambda: nc.vector.tensor_copy(out=bhi[:], in_=bsf[:]), reads=[BSF], writes=[BB])
            S.op("dve", lambda: nc.vector.tensor_tensor(out=blo[:], in0=bsf[:], in1=bhi[:], op=ALU.subtract),
                 reads=[BSF, BB], writes=[BB])
            S.op("dve", lambda: nc.vector.memset(ones[:], 1.0), writes=[BB])
            S.flush()
        ut = [sb(cx, st, "ut", [128, JC, 256], BF16) for _ in range(2)]
        UT = S.bufs("ut", 2)
        vl = [sb(cx, st, "vl", [128, GW], BF16) for _ in range(2)]
        VL = S.bufs("vl", 2)
        xs = [sb(cx, st, "xs", [128, D], F32) for _ in range(2)]
        XS = S.bufs("xs", 2)
        uv = [sb(cx, st, "uv", [128, JC, 128], BF16) for _ in range(2)]
        UV = S.bufs("uv", 2)
        z = [sb(cx, st, "z", [128, D], F32) for _ in range(2)]
        Z = S.bufs("z", 2)
        st_t = [sb(cx, st, "st", [128, 2, 6], F32) for _ in range(2)]
        mv_t = [sb(cx, st, "mv", [128, 2], F32) for _ in range(2)]
        sd_t = [sb(cx, st, "sd", [128, 1], F32) for _ in range(2)]
        rs_t = [sb(cx, st, "rs", [128, 1], F32) for _ in range(2)]
        STB = S.bufs("stb", 2)
        P = [ps(cx, st, "P", [128, 512], F32) for _ in range(6)]
        PB = S.bufs("P", 6)
        op_ = [ps(cx, st, "o", [128, 512], F32) for _ in range(2)]
        OP = S.bufs("o", 2)
        nchunk = NT // 128
        uT_v = uT_d.rearrange("(j p) n -> p j n", p=128)
        pieces = []
        for g in range(16):
            a = g // 2
            if g % 2 == 0:
                pieces.append((g, 192 * g, 192 * g + 128, 3 * a, 0, 128))
                pieces.append((g, 192 * g + 128, 192 * g + 192, 3 * a + 1, 0, 64))
            else:
                pieces.append((g, 192 * g, 192 * g + 64, 3 * a + 1, 64, 128))
                pieces.append((g, 192 * g + 64, 192 * g + 192, 3 * a + 2, 0, 128))

        def load_u(tt):
            S.dma("sp", ut[tt % 2][:], uT_v[:, :, tt * 256:(tt + 1) * 256], writes=[UT[tt % 2]], key="ut%d" % (tt % 2))

        def load_c(c):
            S.dma("sp", vl[c % 2][:], vln_d[c * 128:(c + 1) * 128, :], writes=[VL[c % 2]], key="vl%d" % (c % 2))
            S.dma("sp", xs[c % 2][:], xin[c * 128:(c + 1) * 128, :], writes=[XS[c % 2]], key="xsg%d" % (c % 2))

        load_u(0)
        load_c(0)
        for c in range(nchunk):
            c2 = c % 2
            tt, cs = c // 2, c % 2
            if c + 1 < nchunk:
                load_c(c + 1)
                if (c + 1) % 2 == 0:
                    load_u((c + 1) // 2)
            for (g, f0, f1, j, r0, r1) in pieces:
                M = f1 - f0
                out_ap = lambda j=j, r0=r0, r1=r1: P[j // 4][r0:r1, (j % 4) * 128:(j % 4 + 1) * 128]
                S.op("pe", lambda g=g, f0=f0, f1=f1, out_ap=out_ap, c2=c2: nc.tensor.matmul(
                    out_ap(), vl[c2][:, f0:f1], WT[:, g, :], start=True, stop=False),
                    reads=[VL[c2], WTB], writes=[PB[j // 4]])
                S.op("pe", lambda g=g, M=M, out_ap=out_ap: nc.tensor.matmul(
                    out_ap(), ones[0:1, 0:M], bhi[0:1, g * 128:(g + 1) * 128], start=False, stop=False),
                    reads=[BB], writes=[PB[j // 4]])
                S.op("pe", lambda g=g, M=M, out_ap=out_ap: nc.tensor.matmul(
                    out_ap(), ones[0:1, 0:M], blo[0:1, g * 128:(g + 1) * 128], start=False, stop=True),
                    reads=[BB], writes=[PB[j // 4]])
            for jj in range(6):
                S.op("dve", lambda jj=jj, c2=c2, tt=tt, cs=cs: nc.vector.tensor_tensor(
                    out=uv[c2][:, 4 * jj:4 * jj + 4, :], in0=P[jj][:].rearrange("p (j t) -> p j t", j=4),
                    in1=ut[tt % 2][:, 4 * jj:4 * jj + 4, cs * 128:(cs + 1) * 128], op=ALU.mult),
                    reads=[PB[jj], UT[tt % 2]], writes=[UV[c2]])
            for n in range(2):
                for j in range(JC):
                    S.op("pe", lambda n=n, j=j, c2=c2: nc.tensor.matmul(
                        op_[n][:], uv[c2][:, j, :], w2[:, j, n * 512:(n + 1) * 512], start=(j == 0), stop=(j == JC - 1)),
                        reads=[UV[c2], W2], writes=[OP[n]])
                S.op("dve", lambda n=n, c2=c2: nc.vector.scalar_tensor_tensor(
                    out=z[c2][:, n * 512:(n + 1) * 512], in0=op_[n][:], scalar=c_res, in1=xs[c2][:, n * 512:(n + 1) * 512],
                    op0=ALU.mult, op1=ALU.add), reads=[OP[n], XS[c2]], writes=[Z[c2]])
            ln_epilogue(cx, z[c2], Z[c2], gb, GB, st_t[c2], mv_t[c2], sd_t[c2], rs_t[c2], STB[c2], eps)
            S.dma("sp", xout[c * 128:(c + 1) * 128, :], z[c2][:], reads=[Z[c2]], key="zg%d" % c2)
        S.flush()


class NsaLocal:
    def __init__(self, feat, tok, QT, gates):
        self.feat, self.tok, self.QT, self.gates = feat, tok, QT, gates


class NsaGathered:
    def __init__(self, feat, tok):
        self.feat, self.tok = feat, tok


NSA_COLS = 2608
HD = 64


def phase_n1(cx, xin, w_in_d, cos_d, sin_d, nl, NT):
    nc, S = cx.nc, cx.S
    TT, NS, KC = 512, 4, 8
    with ExitStack() as st:
        w = sb(cx, st, "wn", [128, KC, NSA_COLS], BF16)
        W = S.buf("wn")
        for k in range(KC):
            S.dma("pool", w[:, k, :], w_in_d[k * 128:(k + 1) * 128, :], writes=[W], key="wn")
        wqr = sb(cx, st, "wqr", [128, KC, 1024], BF16)
        wkr = sb(cx, st, "wkr", [128, KC, 512], BF16)
        WR = S.buf("wrot")

        def rot(dst, src):
            sv = src.rearrange("p k (b t f) -> p k b t f", t=2, f=32)
            dv = dst.rearrange("p k (b t f) -> p k b t f", t=2, f=32)
            for k in range(KC):
                S.op("dve", lambda k=k: nc.vector.tensor_scalar(out=dv[:, k, :, 0, :], in0=sv[:, k, :, 1, :], scalar1=-1.0,
                                                                scalar2=None, op0=ALU.mult), reads=[W, WR], writes=[WR])
                S.op("dve", lambda k=k: nc.vector.tensor_copy(out=dv[:, k, :, 1, :], in_=sv[:, k, :, 0, :]),
                     reads=[W, WR], writes=[WR])

        rot(wqr[:], w[:, :, 0:1024])
        rot(wkr[:, :, 0:256], w[:, :, 1536:1792])
        rot(wkr[:, :, 256:512], w[:, :, 2048:2304])
        ident, IB = make_ident(cx, st)
        xs = [sb(cx, st, "xs", [128, NS, D], F32) for _ in range(2)]
        XS = S.bufs("xs", 2)
        xbf = sb(cx, st, "xbf", [128, NS, D], BF16)
        XBF = S.buf("xbf")
        xT = [sb(cx, st, "xT", [128, KC, TT], BF16) for _ in range(2)]
        XT = S.bufs("xT", 2)
        cs = [sb(cx, st, "cs", [128, 2, TT], F32) for _ in range(2)]
        CS = S.bufs("cs", 2)
        t1 = [sb(cx, st, "t1", [128, TT], F32) for _ in range(2)]
        t2 = [sb(cx, st, "t2", [128, TT], F32) for _ in range(2)]
        T1 = S.bufs("t1", 2)
        T2 = S.bufs("t2", 2)
        og = [sb(cx, st, "og", [128, TT], BF16) for _ in range(2)]
        OG = S.bufs("og", 2)
        vt = [sb(cx, st, "vt", [128, 512], BF16) for _ in range(2)]
        VT = S.bufs("vt", 2)
        gt = [sb(cx, st, "gt", [128, 48], F32) for _ in range(2)]
        GT = S.bufs("gt", 2)
        tp = [ps(cx, st, "tp", [128, D], BF16) for _ in range(2)]
        TP = S.bufs("tp", 2)
        pa = [ps(cx, st, "pa", [128, 512], F32) for _ in range(2)]
        PA = S.bufs("pa", 2)
        pb = [ps(cx, st, "pb", [128, 512], F32) for _ in range(2)]
        PB = S.bufs("pb", 2)
        pt = [ps(cx, st, "pt", [128, 512], F32) for _ in range(2)]
        PT = S.bufs("pt", 2)
        ntile = NT // TT
        xin_v = xin.rearrange("(t s p) d -> t p s d", s=NS, p=128)

        def load(t):
            S.dma("sp", xs[t % 2][:], xin_v[t], writes=[XS[t % 2]], key="xs%d" % (t % 2))
            S.dma("sp", cs[t % 2][:, 0, :], cos_d[:, t * TT:(t + 1) * TT], writes=[CS[t % 2]], key="cs%d" % (t % 2))
            S.dma("sp", cs[t % 2][:, 1, :], sin_d[:, t * TT:(t + 1) * TT], writes=[CS[t % 2]], key="cs%d" % (t % 2))

        units = []
        qf = lambda h: nl.QT[:, h, :]
        for a in range(8):
            units.append((qf, a, w, wqr, a * 128, a * 128))
        for a in range(2):
            units.append(((lambda g: nl.feat("KsT", g)), a, w, wkr, 1536 + a * 128, a * 128))
        for a in range(2):
            units.append(((lambda g: nl.feat("KwT", g)), a, w, wkr, 2048 + a * 128, 256 + a * 128))
        kcf = lambda g: nl.feat("KcT", g)
        vcf = lambda g: nl.feat("VcT", g)
        plain = [(kcf, 0, 1024), (kcf, 1, 1152), (vcf, 0, 1280), (vcf, 1, 1408)]
        load(0)
        cnt = 0
        for t in range(ntile):
            b2 = t % 2
            if t + 1 < ntile:
                load(t + 1)
            emit_xT(cx, xs[b2], XS[b2], xbf, XBF, tp, TP, xT[b2], XT[b2], ident, IB, NS)
            tok = slice(t * TT, (t + 1) * TT)
            for (dst, di, wa, wb, ca, cb) in units:
                q = cnt % 2
                cnt += 1
                for k in range(KC):
                    S.op("pe", lambda k=k, q=q, wa=wa, ca=ca, b2=b2: nc.tensor.matmul(
                        pa[q][:], wa[:, k, ca:ca + 128], xT[b2][:, k, :], start=(k == 0), stop=(k == KC - 1)),
                        reads=[W, WR, XT[b2]], writes=[PA[q]])
                for k in range(KC):
                    S.op("pe", lambda k=k, q=q, wb=wb, cb=cb, b2=b2: nc.tensor.matmul(
                        pb[q][:], wb[:, k, cb:cb + 128], xT[b2][:, k, :], start=(k == 0), stop=(k == KC - 1)),
                        reads=[WR, XT[b2]], writes=[PB[q]])
                S.op("dve", lambda q=q, b2=b2: nc.vector.tensor_tensor(out=t1[q][:], in0=pa[q][:], in1=cs[b2][:, 0, :], op=ALU.mult),
                     reads=[PA[q], CS[b2]], writes=[T1[q]])
                S.op("dve", lambda q=q, b2=b2: nc.vector.tensor_tensor(out=t2[q][:], in0=pb[q][:], in1=cs[b2][:, 1, :], op=ALU.mult),
                     reads=[PB[q], CS[b2]], writes=[T2[q]])
                S.op("pool", lambda q=q: nc.gpsimd.tensor_tensor(out=og[q][:], in0=t1[q][:], in1=t2[q][:], op=ALU.add),
                     reads=[T1[q], T2[q]], writes=[OG[q]])
                for h in range(2):
                    S.dma("sp", dst(2 * di + h)[:, tok], og[q][64 * h:64 * h + 64, :], reads=[OG[q]], key="og%d" % q)
            for (dst, di, c0) in plain:
                q = cnt % 2
                cnt += 1
                for k in range(KC):
                    S.op("pe", lambda k=k, q=q, c0=c0, b2=b2: nc.tensor.matmul(
                        pa[q][:], w[:, k, c0:c0 + 128], xT[b2][:, k, :], start=(k == 0), stop=(k == KC - 1)),
                        reads=[W, XT[b2]], writes=[PA[q]])
                S.op("act", lambda q=q: nc.scalar.copy(out=og[q][:], in_=pa[q][:]), reads=[PA[q]], writes=[OG[q]])
                for h in range(2):
                    S.dma("sp", dst(2 * di + h)[:, tok], og[q][64 * h:64 * h + 64, :], reads=[OG[q]], key="og%d" % q)
            for s in range(NS):
                q = s % 2
                t0 = t * TT + s * 128
                rows = slice(t0, t0 + 128)
                for k in range(KC):
                    S.op("pe", lambda k=k, q=q, s=s, b2=b2: nc.tensor.matmul(
                        pt[q][:, 0:256], xT[b2][:, k, s * 128:(s + 1) * 128], w[:, k, 1792:2048],
                        start=(k == 0), stop=(k == KC - 1)), reads=[W, XT[b2]], writes=[PT[q]])
                S.op("act", lambda q=q: nc.scalar.copy(out=vt[q][:, 0:256], in_=pt[q][:, 0:256]), reads=[PT[q]], writes=[VT[q]])
                S.dma("sp", nl.tok("Vs", t0), vt[q][:, 0:256], reads=[VT[q]], key="vt%d" % q)
                for k in range(KC):
                    S.op("pe", lambda k=k, q=q, s=s, b2=b2: nc.tensor.matmul(
                        pt[q][:, 0:304], xT[b2][:, k, s * 128:(s + 1) * 128], w[:, k, 2304:2608],
                        start=(k == 0), stop=(k == KC - 1)), reads=[W, XT[b2]], writes=[PT[q]])
                S.op("act", lambda q=q: nc.scalar.copy(out=vt[q][:, 256:512], in_=pt[q][:, 0:256]), reads=[PT[q]], writes=[VT[q]])
                S.op("act", lambda q=q: nc.scalar.activation(out=gt[q][:], in_=pt[q][:, 256:304], func=AF.Sigmoid),
                     reads=[PT[q]], writes=[GT[q]])
                S.dma("sp", nl.tok("Vw", t0), vt[q][:, 256:512], reads=[VT[q]], key="vt%d" % q)
                S.dma("sp", nl.gates[rows, :], gt[q][:], reads=[GT[q]], key="gt%d" % q)
        S.flush()


NEGM = -30000.0
MTS = 2
PIPE_DEPTH = 2
DBG = 0


class StopBuild(Exception):
    pass


def dbg_stop(n):
    if DBG == n:
        raise StopBuild()


def phase_attn(cx, al, QT_d, gates_d, O_d, w1k_d, w2k_d, pek_d, w1v_d, w2v_d, pev_d, NB):
    nc, S = cx.nc, cx.S
    NT = NB * 128
    SEQ = 4 * NT
    KT = 4 * NB
    NCMP = SEQ // 16 - 1
    NCP = SEQ // 16
    with ExitStack() as st:
        ident, IB = make_ident(cx, st)
        identf = sb(cx, st, "identf", [128, 128], F32)
        S.dma("sp", identf[:], cx.ident_d[:, :], writes=[IB], key="identf")
        kcmpT = sb(cx, st, "kcmpT", [64, 4, NCP], BF16)
        vcmp = sb(cx, st, "vcmp", [128, NCP // 128, 4, 65], BF16)
        KCMP = S.buf("kcmp")
        VCMP = S.buf("vcmp")
        S.op("pool", lambda: nc.gpsimd.memset(vcmp[:], 0.0), writes=[VCMP])
        S.op("pool", lambda: nc.gpsimd.memset(kcmpT[:], 0.0), writes=[KCMP])
        with ExitStack() as st2:
            w1s = sb(cx, st2, "w1s", [64, 32, 256], BF16)
            w2d = sb(cx, st2, "w2d", [128, 2, 64], BF16)
            w2r = sb(cx, st2, "w2r", [128, 2, 64], BF16)
            peT = sb(cx, st2, "peT", [64, 32], BF16)
            kc = sb(cx, st2, "kc", [64, SEQ], BF16)
            hT = sb(cx, st2, "hTc", [128, 2, NCP], BF16)
            cbias = sb(cx, st2, "cbias", [128, 2], F32)
            ccs = sb(cx, st2, "ccs", [64, 2, NCP], F32)
            t1 = sb(cx, st2, "ct1", [128, 512], F32)
            t2 = sb(cx, st2, "ct2", [128, 512], F32)
            W1S, W2D, PET, KC, HT, CB, CCS, T1, T2 = [S.buf(n) for n in ("w1s", "w2d", "peT", "kc", "hTc", "cb", "ccs", "ct1", "ct2")]
            ph = [ps(cx, st2, "ph", [128, 512], F32) for _ in range(2)]
            PH = S.bufs("ph", 2)
            pc = ps(cx, st2, "pc", [128, 512], F32)
            PC = S.buf("pc")
            pA = ps(cx, st2, "pA", [128, 512], F32)
            pB = ps(cx, st2, "pB", [128, 512], F32)
            PA_, PB_ = S.buf("pA"), S.buf("pB")
            S.dma("sp", ccs[:, 0, :], cx.cosc_d[0:64, :], writes=[CCS], key="ccs")
            S.dma("sp", ccs[:, 1, :], cx.sinc_d[0:64, :], writes=[CCS], key="ccs")
            ntiles = [(n0, min(n0 + 512, NCMP)) for n0 in range(0, NCMP, 512)]
            for which in range(2):
                w1_d, w2_d, pe_d = (w1k_d, w2k_d, pek_d) if which == 0 else (w1v_d, w2v_d, pev_d)
                src_k = "KcT" if which == 0 else "VcT"
                S.dma("pool", w1s[:, :, :], w1_d.rearrange("l d h -> d l h"), writes=[W1S], key="w1s")
                S.dma("pool", peT[:, :], pe_d.rearrange("l d -> d l"), writes=[PET], key="peT", slow=True)
                for hc in range(2):
                    S.dma("pool", w2d[:, hc, :], w2_d[hc * 128:(hc + 1) * 128, :], writes=[W2D], key="w2d")
                if which == 0:
                    S.op("dve", lambda: nc.vector.tensor_scalar(out=w2r[:, :, 0:32], in0=w2d[:, :, 32:64], scalar1=-1.0,
                                                                scalar2=None, op0=ALU.mult), reads=[W2D], writes=[W2D])
                    S.op("dve", lambda: nc.vector.tensor_copy(out=w2r[:, :, 32:64], in_=w2d[:, :, 0:32]),
                         reads=[W2D], writes=[W2D])
                for hc in range(2):
                    for l in range(32):
                        S.op("pe", lambda hc=hc, l=l: nc.tensor.matmul(
                            pc[:, hc:hc + 1], w1s[:, l, hc * 128:(hc + 1) * 128], peT[:, l:l + 1],
                            start=(l == 0), stop=(l == 31)), reads=[W1S, PET], writes=[PC])
                S.op("dve", lambda: nc.vector.tensor_copy(out=cbias[:], in_=pc[:, 0:2]), reads=[PC], writes=[CB])
                for g in range(4):
                    for r_ in range(4):
                        S.dma("sp", kc[:, :].rearrange("p (i r q) -> p i r q", r=4, q=128)[:, :, r_, :],
                              al.feat(src_k, r_, g).rearrange("p (i q) -> p i q", q=128), writes=[KC], key="kc")
                    for hc in range(2):
                        for (n0, n1) in ntiles:
                            cnt = n1 - n0
                            q = (hc + (n0 // 512)) % 2
                            for l in range(32):
                                S.op("pe", lambda hc=hc, l=l, n0=n0, cnt=cnt, q=q: nc.tensor.matmul(
                                    ph[q][:, 0:cnt], w1s[:, l, hc * 128:(hc + 1) * 128],
                                    kc[:, 16 * n0 + l:16 * n0 + l + 16 * (cnt - 1) + 1:16],
                                    start=(l == 0), stop=(l == 31)), reads=[W1S, KC], writes=[PH[q]])
                            S.op("act", lambda hc=hc, n0=n0, cnt=cnt, q=q: nc.scalar.activation(
                                out=hT[:, hc, n0:n0 + cnt], in_=ph[q][:, 0:cnt], func=AF.Gelu_apprx_tanh,
                                bias=cbias[:, hc:hc + 1], scale=1.0), reads=[PH[q], CB], writes=[HT])
                    if which == 0:
                        for (n0, n1) in ntiles:
                            cnt = n1 - n0
                            for hc in range(2):
                                S.op("pe", lambda hc=hc, n0=n0, cnt=cnt: nc.tensor.matmul(
                                    pA[0:64, 0:cnt], w2d[:, hc, :], hT[:, hc, n0:n0 + cnt], start=(hc == 0), stop=(hc == 1)),
                                    reads=[W2D, HT], writes=[PA_])
                            for hc in range(2):
                                S.op("pe", lambda hc=hc, n0=n0, cnt=cnt: nc.tensor.matmul(
                                    pB[0:64, 0:cnt], w2r[:, hc, :], hT[:, hc, n0:n0 + cnt], start=(hc == 0), stop=(hc == 1)),
                                    reads=[W2D, HT], writes=[PB_])
                            S.op("dve", lambda n0=n0, cnt=cnt: nc.vector.tensor_tensor(
                                out=t1[0:64, 0:cnt], in0=pA[0:64, 0:cnt], in1=ccs[:, 0, n0:n0 + cnt], op=ALU.mult),
                                reads=[PA_, CCS], writes=[T1])
                            S.op("dve", lambda n0=n0, cnt=cnt: nc.vector.tensor_tensor(
                                out=t2[0:64, 0:cnt], in0=pB[0:64, 0:cnt], in1=ccs[:, 1, n0:n0 + cnt], op=ALU.mult),
                                reads=[PB_, CCS], writes=[T2])
                            S.op("pool", lambda n0=n0, cnt=cnt, g=g: nc.gpsimd.tensor_tensor(
                                out=kcmpT[:, g, n0:n0 + cnt], in0=t1[0:64, 0:cnt], in1=t2[0:64, 0:cnt], op=ALU.add),
                                reads=[T1, T2], writes=[KCMP])
                    else:
                        for nt in range((NCMP + 127) // 128):
                            c0 = nt * 128
                            cnt = min(128, NCMP - c0)
                            for hc in range(2):
                                S.op("pe", lambda hc=hc, c0=c0, cnt=cnt: nc.tensor.matmul(
                                    pA[0:cnt, 0:64], hT[:, hc, c0:c0 + cnt], w2d[:, hc, 0:64], start=(hc == 0), stop=(hc == 1)),
                                    reads=[W2D, HT], writes=[PA_])
                            S.op("act", lambda nt=nt, cnt=cnt, g=g: nc.scalar.copy(out=vcmp[0:cnt, nt, g, 0:64], in_=pA[0:cnt, 0:64]),
                                 reads=[PA_], writes=[VCMP])
            S.op("pool", lambda: nc.gpsimd.memset(vcmp[:, :, :, 64:65], 1.0), reads=[VCMP], writes=[VCMP])
            S.flush()
        if DBG == 1:
            return
        ks = sb(cx, st, "ks", [64, SEQ], BF16)
        kw = sb(cx, st, "kw", [64, SEQ], BF16)
        vs = sb(cx, st, "vs", [128, KT, 65], BF16)
        vw = sb(cx, st, "vw", [128, KT, 65], BF16)
        KS, KW, VS, VW = S.buf("ks"), S.buf("kw"), S.buf("vs"), S.buf("vw")
        S.op("pool", lambda: nc.gpsimd.memset(vs[:, :, 64:65], 1.0), writes=[VS])
        S.op("pool", lambda: nc.gpsimd.memset(vw[:, :, 64:65], 1.0), writes=[VW])
        cmpmask = sb(cx, st, "cmpmask", [128, 33], BF16)
        keep = sb(cx, st, "keep", [128, 9], F32)
        addt = sb(cx, st, "addt", [128, 9], F32)
        caus = sb(cx, st, "caus", [128, 4, 128], BF16)
        wmask = sb(cx, st, "wmask", [128, 8, 128], BF16)
        TB = S.buf("tables")
        S.dma("pool", cmpmask[:], cx.cmpmask_d[:, :], writes=[TB], key="tb")
        S.dma("sp", keep[:], cx.keep_d[:, :], writes=[TB], key="tb")
        S.dma("sp", addt[:], cx.addt_d[:, :], writes=[TB], key="tb")
        S.dma("pool", caus[:], cx.caus_d[:, :, :], writes=[TB], key="tb")
        S.dma("pool", wmask[:], cx.wmask_d[:, :, :], writes=[TB], key="tb")
        qt = [sb(cx, st, "qt", [64, 4, 128], BF16) for _ in range(2)]
        QTB = S.bufs("qt", 2)
        gts = [sb(cx, st, "gts", [128, 12], F32) for _ in range(2)]
        GTS = S.bufs("gts", 2)
        Pf = [sb(cx, st, "Pf", [128, NCP], F32) for _ in range(2)]
        PF = S.bufs("Pf", 2)
        Pb = [sb(cx, st, "Pb", [128, NCP], BF16) for _ in range(4)]
        PBB = S.bufs("Pb", 4)
        for cb in range(4):
            S.op("pool", lambda cb=cb: nc.gpsimd.memset(Pb[cb][:], 0.0), writes=[PBB[cb]])
        PTs = [sb(cx, st, "PTs", [128, 512], BF16) for _ in range(2)]
        PTS = S.bufs("PTs", 2)
        pns = sb(cx, st, "pns", [128, NCP + 8], F32)
        PNS = S.buf("pns")
        S.op("pool", lambda: nc.gpsimd.memset(pns[:], 0.0), writes=[PNS])
        NMX = max(8 * NB, 16)
        imp = sb(cx, st, "imp", [128, NMX], F32)
        imp4 = sb(cx, st, "imp4", [128, NMX], F32)
        work = sb(cx, st, "work", [128, NMX], F32)
        msk = sb(cx, st, "msk", [128, NMX], BF16)
        m8a = sb(cx, st, "m8a", [128, 8], F32)
        m8b = sb(cx, st, "m8b", [128, 8], F32)
        thr = sb(cx, st, "thr", [128, 1], F32)
        IMP, MSK = S.buf("imp"), S.buf("msk")
        mskx = sb(cx, st, "mskx", [128, NMX * 64], BF16)
        MSKX = S.buf("mskx")
        smx = sb(cx, st, "smx", [128, 4], F32)
        snb = sb(cx, st, "snb", [128, 1], F32)
        rsum = sb(cx, st, "rsum", [128, 4], F32)
        rinv = sb(cx, st, "rinv", [128, 1], F32)
        SMX = S.buf("smx")
        E = [sb(cx, st, "E", [128, 512], BF16) for _ in range(3)]
        EB = S.bufs("E", 3)
        PM = [sb(cx, st, "PM", [128, 512], BF16) for _ in range(3)]
        PMB = S.bufs("PM", 3)
        cm = [sb(cx, st, "cm", [128, 128], BF16) for _ in range(4)]
        CM = S.bufs("cm", 4)
        OT = [sb(cx, st, "OT", [65, 512], F32) for _ in range(3)]
        OTB = S.bufs("OT", 3)
        rs4 = sb(cx, st, "rs4", [128, 4], F32)
        ri4 = sb(cx, st, "ri4", [128, 4], F32)
        fac = sb(cx, st, "fac", [128, 4], F32)
        tmp = sb(cx, st, "tmp", [128, 4, 64], F32)
        oacc = sb(cx, st, "oacc", [128, 4, 64], F32)
        ob = [sb(cx, st, "ob", [128, 4, 64], BF16) for _ in range(2)]
        CMB, OBB = S.buf("comb"), S.bufs("ob", 2)
        bank = [ps(cx, st, "bk", [128, 512], F32) for _ in range(8)]
        BK = S.bufs("bk", 8)
        SS = [0, 1, 7]
        B_PT, B_OC, B_OS, B_OW, B_CT = 2, 3, 4, 5, 6
        ptps = bank[B_PT][:].bitcast(BF16)
        MT = [BK[B_PT], BK[B_CT]]
        ctps = bank[B_CT][:].bitcast(BF16)
        ecnt = [0]

        def job_front(j):
            e = ecnt[0] % 3
            ecnt[0] += 1
            j["e"] = e
            sb_ = SS[e]
            kT, KB, kt, q2 = j["kT"], j["KB"], j["kt"], j["q2"]
            if j["pre"] is not None:
                j["pre"]()
            S.op("pe", lambda sb_=sb_, kt=kt, q2=q2, kT=kT: nc.tensor.matmul(
                bank[sb_][:, :], kT[:, kt * 128:(kt + 1) * 128], qt[q2][:, :, :].rearrange("p h q -> p (h q)"),
                start=True, stop=True), reads=[KB, QTB[q2]], writes=[BK[sb_]])
            S.op("act", lambda e=e, sb_=sb_: nc.scalar.activation(out=E[e][:], in_=bank[sb_][:], func=AF.Exp, scale=0.125),
                 reads=[BK[sb_]], writes=[EB[e]])
            mfn = j["mask"]
            S.op("dve", lambda e=e, mfn=mfn: nc.vector.tensor_tensor(
                out=PM[e][:].rearrange("p (c q) -> p c q", c=4), in0=E[e][:].rearrange("p (c q) -> p c q", c=4),
                in1=mfn().unsqueeze(1).broadcast_to([128, 4, 128]), op=ALU.mult),
                reads=[EB[e]] + j["mbufs"], writes=[PMB[e]])

        def job_back(j):
            e, kt, vT, VB, o_bank, first, last = j["e"], j["kt"], j["vT"], j["VB"], j["o_bank"], j["first"], j["last"]
            S.op("pe", lambda e=e, kt=kt, first=first, last=last, vT=vT, o_bank=o_bank: nc.tensor.matmul(
                bank[o_bank][0:65, :], vT[:, kt, :], PM[e][:], start=first, stop=last),
                reads=[VB, PMB[e]], writes=[BK[o_bank]])

        def run_jobs(jobs, depth=None):
            depth = PIPE_DEPTH if depth is None else depth
            n = len(jobs)
            for step in range(n + depth):
                if step < n:
                    job_front(jobs[step])
                if step >= depth:
                    job_back(jobs[step - depth])

        try:
            ucnt = 0
            for g in range(4):
                for r_ in range(4):
                    S.dma("sp", ks[:].rearrange("p (i r q) -> p i r q", r=4, q=128)[:, :, r_, :],
                          al.feat("KsT", r_, g).rearrange("p (i q) -> p i q", q=128), writes=[KS], key="ks")
                    S.dma("sp", kw[:].rearrange("p (i r q) -> p i r q", r=4, q=128)[:, :, r_, :],
                          al.feat("KwT", r_, g).rearrange("p (i q) -> p i q", q=128), writes=[KW], key="kw")
                    for hf in range(2):
                        isl = slice(hf * NB // 2, (hf + 1) * NB // 2)
                        S.dma("sp", vs[:, :, 0:64].rearrange("q (i r) d -> q i r d", r=4)[:, isl, r_, :],
                              al.tok("Vs", r_, hf)[:, 64 * g:64 * g + 64].rearrange("(i q) d -> q i d", q=128),
                              writes=[VS], key="vs")
                        S.dma("sp", vw[:, :, 0:64].rearrange("q (i r) d -> q i r d", r=4)[:, isl, r_, :],
                              al.tok("Vw", r_, hf)[:, 64 * g:64 * g + 64].rearrange("(i q) d -> q i d", q=128),
                              writes=[VW], key="vw")
                if DBG == 2:
                    S.flush()
                    return
                for i in range(NB):
                    q2 = ucnt % 2
                    ucnt += 1
                    S.dma("sp", qt[q2][:], QT_d[:, 4 * g:4 * g + 4, i * 128:(i + 1) * 128], writes=[QTB[q2]], key="qt%d" % q2)
                    S.dma("sp", gts[q2][:], gates_d[i * 128:(i + 1) * 128, 12 * g:12 * g + 12], writes=[GTS[q2]], key="gts%d" % q2)
                    ncols = 32 * i + 31
                    ctiles = [(c0, min(c0 + 512, ncols)) for c0 in range(0, ncols, 512)]
                    m0 = ncols - 33
                    for cb in range(4):
                        pf = Pf[cb % 2]
                        PFB = PF[cb % 2]
                        for ci, (c0, c1) in enumerate(ctiles):
                            ov0, ov1 = max(c0, m0), c1
                            has_mask = ov1 > ov0
                            S.op("pe", lambda cb=cb, ci=ci, c0=c0, c1=c1, q2=q2, has_mask=has_mask, g=g: nc.tensor.matmul(
                                bank[ci][:, 0:c1 - c0], qt[q2][:, cb, :], kcmpT[:, g, c0:c1],
                                start=True, stop=(not has_mask)), reads=[QTB[q2], KCMP], writes=[BK[ci]])
                            if has_mask:
                                S.op("pe", lambda ci=ci, c0=c0, ov0=ov0, ov1=ov1, m0=m0: nc.tensor.matmul(
                                    bank[ci][:, ov0 - c0:ov1 - c0], ident[:], cmpmask[:, ov0 - m0:ov1 - m0], start=False, stop=True),
                                    reads=[IB, TB], writes=[BK[ci]])
                            S.op("dve", lambda ci=ci, c0=c0, c1=c1: nc.vector.reduce_max(
                                out=smx[:, ci:ci + 1], in_=bank[ci][:, 0:c1 - c0], axis=AX.X), reads=[BK[ci]], writes=[SMX])
                        if len(ctiles) == 2:
                            S.op("dve", lambda: nc.vector.tensor_tensor(out=smx[:, 0:1], in0=smx[:, 0:1], in1=smx[:, 1:2], op=ALU.max),
                                 reads=[SMX], writes=[SMX])
                        S.op("dve", lambda: nc.vector.tensor_scalar(out=snb[:], in0=smx[:, 0:1], scalar1=-1000.0, scalar2=-0.125,
                                                                    op0=ALU.max, op1=ALU.mult), reads=[SMX], writes=[SMX])
                        S.op("dve", lambda: nc.vector.memset(rsum[:, 2:4], 0.0), reads=[SMX], writes=[SMX])
                        for ci, (c0, c1) in enumerate(ctiles):
                            S.op("act", lambda ci=ci, c0=c0, c1=c1, pf=pf: nc.scalar.activation(
                                out=pf[:, c0:c1], in_=bank[ci][:, 0:c1 - c0], func=AF.Exp, bias=snb[:, 0:1], scale=0.125,
                                accum_out=rsum[:, 2 + ci:3 + ci]), reads=[BK[ci], SMX], writes=[PFB, SMX])
                        if len(ctiles) == 2:
                            S.op("dve", lambda: nc.vector.tensor_tensor(out=rsum[:, 2:3], in0=rsum[:, 2:3], in1=rsum[:, 3:4], op=ALU.add),
                                 reads=[SMX], writes=[SMX])
                        S.op("dve", lambda: nc.vector.tensor_scalar(out=rsum[:, 0:1], in0=rsum[:, 2:3], scalar1=1e-30, scalar2=None,
                                                                    op0=ALU.max), reads=[SMX], writes=[SMX])
                        S.op("dve", lambda: nc.vector.reciprocal(out=rinv[:], in_=rsum[:, 0:1]), reads=[SMX], writes=[SMX])
                        if cb == 0:
                            S.op("dve", lambda pf=pf, ncols=ncols: nc.vector.tensor_scalar(
                                out=pns[:, 1:1 + ncols], in0=pf[:, 0:ncols], scalar1=rinv[:, 0:1], scalar2=None, op0=ALU.mult),
                                reads=[PFB, SMX], writes=[PNS])
                        else:
                            S.op("dve", lambda pf=pf, ncols=ncols: nc.vector.scalar_tensor_tensor(
                                out=pns[:, 1:1 + ncols], in0=pf[:, 0:ncols], scalar=rinv[:, 0:1], in1=pns[:, 1:1 + ncols],
                                op0=ALU.mult, op1=ALU.add), reads=[PFB, SMX, PNS], writes=[PNS])
                        S.op("pool", lambda cb=cb, pf=pf, ncols=ncols: nc.gpsimd.tensor_copy(out=Pb[cb][:, 0:ncols], in_=pf[:, 0:ncols]),
                             reads=[PFB], writes=[PBB[cb]])
                        if g > 0 and i == 0:
                            S.op("pool", lambda cb=cb, ncols=ncols: nc.gpsimd.memset(Pb[cb][:, ncols:NCP], 0.0), writes=[PBB[cb]])
                    if g > 0 and i == 0:
                        S.op("pool", lambda ncols=ncols: nc.gpsimd.memset(pns[:, 1 + ncols:NCP + 8], 0.0), reads=[PNS], writes=[PNS])
                    nnt = (ncols + 127) // 128
                    for nt in range(nnt):
                        for cb in range(4):
                            S.op("pe", lambda nt=nt, cb=cb: nc.tensor.transpose(
                                out=ptps[:, cb * 128:(cb + 1) * 128], in_=Pb[cb][:, nt * 128:(nt + 1) * 128], identity=ident[:]),
                                reads=[PBB[cb], IB], writes=[BK[B_PT]])
                        S.op("act", lambda nt=nt: nc.scalar.copy(out=PTs[nt % 2][:], in_=ptps[:, 0:512]),
                             reads=[BK[B_PT]], writes=[PTS[nt % 2]])
                        S.op("pe", lambda nt=nt, nnt=nnt, g=g: nc.tensor.matmul(
                            bank[B_OC][0:65, :], vcmp[:, nt, g, :], PTs[nt % 2][:], start=(nt == 0), stop=(nt == nnt - 1)),
                            reads=[VCMP, PTS[nt % 2]], writes=[BK[B_OC]])
                    if DBG == 3:
                        S.flush()
                        return
                    nm = 8 * i + 8
                    Wd = max(nm, 16)
                    S.op("dve", lambda nm=nm: nc.vector.tensor_reduce(
                        out=imp4[:, 0:nm], in_=pns[:, 0:4 * nm].rearrange("p (m r) -> p m r", r=4), axis=AX.X, op=ALU.add),
                        reads=[PNS], writes=[IMP])
                    S.op("dve", lambda nm=nm: nc.vector.tensor_tensor(
                        out=imp[:, 0:nm], in0=imp4[:, 0:nm], in1=pns[:, 4:4 * nm + 1:4], op=ALU.add), reads=[PNS, IMP], writes=[IMP])
                    r0 = max(8 * i - 1, 0)
                    tcol = r0 - (8 * i - 1)
                    S.op("dve", lambda r0=r0, nm=nm, tcol=tcol: nc.vector.tensor_tensor(
                        out=imp[:, r0:nm], in0=imp[:, r0:nm], in1=keep[:, tcol:9], op=ALU.mult), reads=[IMP, TB], writes=[IMP])
                    S.op("dve", lambda r0=r0, nm=nm, tcol=tcol: nc.vector.tensor_tensor(
                        out=imp[:, r0:nm], in0=imp[:, r0:nm], in1=addt[:, tcol:9], op=ALU.add), reads=[IMP, TB], writes=[IMP])
                    S.op("dve", lambda: nc.vector.memset(imp[:, 0:1], 3e9), reads=[IMP], writes=[IMP])
                    if nm < 16:
                        S.op("dve", lambda nm=nm: nc.vector.memset(imp[:, nm:16], -1e9), reads=[IMP], writes=[IMP])
                    S.op("dve", lambda Wd=Wd: nc.vector.max(out=m8a[:], in_=imp[:, 0:Wd]), reads=[IMP], writes=[IMP])
                    S.op("dve", lambda Wd=Wd: nc.vector.match_replace(out=work[:, 0:Wd], in_to_replace=m8a[:], in_values=imp[:, 0:Wd],
                                                                        imm_value=-3e38), reads=[IMP], writes=[IMP])
                    S.op("dve", lambda Wd=Wd: nc.vector.max(out=m8b[:], in_=work[:, 0:Wd]), reads=[IMP], writes=[IMP])
                    S.op("dve", lambda: nc.vector.tensor_reduce(out=thr[:], in_=m8b[:], axis=AX.X, op=ALU.min), reads=[IMP], writes=[IMP])
                    S.op("dve", lambda Wd=Wd: nc.vector.tensor_scalar(out=msk[:, 0:Wd], in0=imp[:, 0:Wd], scalar1=thr[:, 0:1], scalar2=None,
                                                                       op0=ALU.is_ge), reads=[IMP], writes=[MSK])
                    if DBG == 4:
                        S.flush()
                        return
                    nkt = 4 * i + 4
                    S.op("pool", lambda nm=nm: nc.gpsimd.tensor_copy(
                        out=mskx[:, 0:nm * 64].rearrange("q (m u) -> q m u", u=64),
                        in_=msk[:, 0:nm].unsqueeze(2).broadcast_to([128, nm, 64])), reads=[MSK], writes=[MSKX])
                    jobs = []
                    for kt in range(nkt):
                        ms = kt % 2
                        mt_ap = ptps[:, 512:640] if ms == 0 else ctps[:, 0:128]

                        def pre(kt=kt, ms=ms, mt_ap=mt_ap, i=i):
                            S.op("pe", lambda: nc.tensor.transpose(out=mt_ap, in_=mskx[:, kt * 128:(kt + 1) * 128], identity=ident[:]),
                                 reads=[MSKX, IB], writes=[MT[ms]])
                            if kt >= 4 * i:
                                S.op("dve", lambda: nc.vector.tensor_tensor(out=cm[ms][:], in0=mt_ap, in1=caus[:, kt - 4 * i, :],
                                                                            op=ALU.mult), reads=[MT[ms], TB], writes=[CM[ms]])
                        if kt >= 4 * i:
                            mfn, mb = (lambda ms=ms: cm[ms][:]), [CM[ms]]
                        else:
                            mfn, mb = (lambda mt_ap=mt_ap: mt_ap), [MT[ms]]
                        jobs.append(dict(kT=ks, vT=vs, KB=KS, VB=VS, kt=kt, q2=q2, pre=pre, mask=mfn, mbufs=mb,
                                         o_bank=B_OS, first=(kt == 0), last=(kt == nkt - 1)))
                    wl = [kr for kr in range(-4, 4) if 4 * i + kr >= 0]
                    for idx, kr in enumerate(wl):
                        jobs.append(dict(kT=kw, vT=vw, KB=KW, VB=VW, kt=4 * i + kr, q2=q2, pre=None,
                                         mask=(lambda kr=kr: wmask[:, kr + 4, :]), mbufs=[TB],
                                         o_bank=B_OW, first=(idx == 0), last=(idx == len(wl) - 1)))
                    run_jobs(jobs)
                    if DBG == 6:
                        S.flush()
                        return
                    gv = gts[q2][:].rearrange("q (h t) -> q h t", t=3)
                    for br, ob_ in enumerate((B_OC, B_OS, B_OW)):
                        S.op("act", lambda br=br, ob_=ob_: nc.scalar.copy(out=OT[br][:], in_=bank[ob_][0:65, :]),
                             reads=[BK[ob_]], writes=[OTB[br]])
                        for cb in range(4):
                            S.op("pe", lambda br=br, cb=cb: nc.tensor.transpose(
                                out=bank[B_CT][:, cb * 65:(cb + 1) * 65], in_=OT[br][0:65, cb * 128:(cb + 1) * 128],
                                identity=identf[0:65, 0:65]), reads=[OTB[br], IB], writes=[BK[B_CT]])
                        ctv = bank[B_CT][:, 0:260].rearrange("q (c e) -> q c e", e=65)
                        S.op("dve", lambda ctv=ctv: nc.vector.tensor_scalar(out=rs4[:], in0=ctv[:, :, 64], scalar1=1e-30, scalar2=None,
                                                                            op0=ALU.max), reads=[BK[B_CT]], writes=[CMB])
                        S.op("dve", lambda: nc.vector.reciprocal(out=ri4[:], in_=rs4[:]), reads=[CMB], writes=[CMB])
                        S.op("dve", lambda br=br, gv=gv: nc.vector.tensor_tensor(
                            out=fac[:], in0=ri4[:], in1=gv[:, :, br], op=ALU.mult), reads=[CMB, GTS[q2]], writes=[CMB])
                        dst = oacc if br == 0 else tmp
                        S.op("dve", lambda ctv=ctv, dst=dst: nc.vector.tensor_tensor(
                            out=dst[:], in0=ctv[:, :, 0:64], in1=fac[:].unsqueeze(2).broadcast_to([128, 4, 64]), op=ALU.mult),
                            reads=[BK[B_CT], CMB], writes=[CMB])
                        if br > 0:
                            S.op("dve", lambda: nc.vector.tensor_tensor(out=oacc[:], in0=oacc[:], in1=tmp[:], op=ALU.add),
                                 reads=[CMB], writes=[CMB])
                    S.op("pool", lambda q2=q2: nc.gpsimd.tensor_copy(out=ob[q2][:], in_=oacc[:]), reads=[CMB], writes=[OBB[q2]])
                    S.dma("sp", O_d[i * 128:(i + 1) * 128, 256 * g:256 * (g + 1)], ob[q2][:].rearrange("q h d -> q (h d)"),
                          reads=[OBB[q2]], key="ob%d" % q2)
        except StopBuild:
            pass
        S.flush()


def phase_proj(cx, a_d, xin, xout, w_d, g_d, b_d, NT, c_res):
    nc, S = cx.nc, cx.S
    KC = 8
    eps = LN_EPS / (ALPHA * ALPHA)
    with ExitStack() as st:
        wo = sb(cx, st, "wo", [128, KC, D], BF16)
        WO = S.buf("wo")
        for k in range(KC):
            S.dma("pool", wo[:, k, :], w_d[k * 128:(k + 1) * 128, :], writes=[WO], key="wo")
        gb, GB = load_gb(cx, st, g_d, b_d, "gbp")
        ident, IB = make_ident(cx, st)
        at = [sb(cx, st, "at", [128, D], BF16) for _ in range(2)]
        AT = S.bufs("at", 2)
        xs = [sb(cx, st, "xs", [128, D], F32) for _ in range(2)]
        XS = S.bufs("xs", 2)
        aT = [sb(cx, st, "aT", [128, KC, 128], BF16) for _ in range(2)]
        ATT = S.bufs("aT", 2)
        z = [sb(cx, st, "z", [128, D], F32) for _ in range(2)]
        Z = S.bufs("z", 2)
        st_t = [sb(cx, st, "st", [128, 2, 6], F32) for _ in range(2)]
        mv_t = [sb(cx, st, "mv", [128, 2], F32) for _ in range(2)]
        sd_t = [sb(cx, st, "sd", [128, 1], F32) for _ in range(2)]
        rs_t = [sb(cx, st, "rs", [128, 1], F32) for _ in range(2)]
        STB = S.bufs("stb", 2)
        tp = [ps(cx, st, "tp", [128, D], BF16) for _ in range(2)]
        TP = S.bufs("tp", 2)
        op_ = [ps(cx, st, "o", [128, 512], F32) for _ in range(4)]
        OP = S.bufs("o", 4)
        nchunk = NT // 128

        def load(c):
            S.dma("sp", at[c % 2][:], a_d[c * 128:(c + 1) * 128, :], writes=[AT[c % 2]], key="at%d" % (c % 2))
            S.dma("sp", xs[c % 2][:], xin[c * 128:(c + 1) * 128, :], writes=[XS[c % 2]], key="xsp%d" % (c % 2))

        load(0)
        for c in range(nchunk):
            c2 = c % 2
            if c + 1 < nchunk:
                load(c + 1)
            for k in range(KC):
                S.op("pe", lambda k=k, c2=c2: nc.tensor.transpose(out=tp[c2][:, k * 128:(k + 1) * 128],
                                                                   in_=at[c2][:, k * 128:(k + 1) * 128], identity=ident[:]),
                     reads=[AT[c2], IB], writes=[TP[c2]])
            S.op("act", lambda c2=c2: nc.scalar.copy(out=aT[c2][:], in_=tp[c2][:].rearrange("p (k t) -> p k t", k=KC)),
                 reads=[TP[c2]], writes=[ATT[c2]])
            for n in range(2):
                o_ = 2 * c2 + n
                for k in range(KC):
                    S.op("pe", lambda n=n, k=k, c2=c2, o_=o_: nc.tensor.matmul(
                        op_[o_][:], aT[c2][:, k, :], wo[:, k, n * 512:(n + 1) * 512], start=(k == 0), stop=(k == KC - 1)),
                        reads=[ATT[c2], WO], writes=[OP[o_]])
                S.op("dve", lambda n=n, c2=c2, o_=o_: nc.vector.scalar_tensor_tensor(
                    out=z[c2][:, n * 512:(n + 1) * 512], in0=op_[o_][:], scalar=c_res, in1=xs[c2][:, n * 512:(n + 1) * 512],
                    op0=ALU.mult, op1=ALU.add), reads=[OP[o_], XS[c2]], writes=[Z[c2]])
            ln_epilogue(cx, z[c2], Z[c2], gb, GB, st_t[c2], mv_t[c2], sd_t[c2], rs_t[c2], STB[c2], eps)
            S.dma("sp", xout[c * 128:(c + 1) * 128, :], z[c2][:], reads=[Z[c2]], key="zp%d" % c2)
        S.flush()


def build_test_ffn(NT):
    nc = bass.Bass("TRN2", target_bir_lowering=False)
    with ExitStack() as stack:
        cx = Ctx(nc, stack)
        x = nc.dram_tensor("x", [NT, D], F32, kind="ExternalInput").ap()
        w_in = nc.dram_tensor("w_in", [D, 2 * DFF], F32, kind="ExternalInput").ap()
        w_out = nc.dram_tensor("w_out", [DFF, D], F32, kind="ExternalInput").ap()
        g = nc.dram_tensor("g", [D], F32, kind="ExternalInput").ap()
        b = nc.dram_tensor("b", [D], F32, kind="ExternalInput").ap()
        cx.ident_d = nc.dram_tensor("ident", [128, 128], F32, kind="ExternalInput").ap()
        y = nc.dram_tensor("y", [NT, D], F32, kind="ExternalOutput").ap()
        phase_ffn(cx, x, y, w_in, w_out, g, b, NT, "f")
        print("instructions:", cx.S.n_inst)
    return nc


def build_test_gmlp(NT):
    nc = bass.Bass("TRN2", target_bir_lowering=False)
    with ExitStack() as stack:
        cx = Ctx(nc, stack)
        x = nc.dram_tensor("x", [NT, D], F32, kind="ExternalInput").ap()
        w_in = nc.dram_tensor("w_in", [D, 2 * GW], F32, kind="ExternalInput").ap()
        lng = nc.dram_tensor("lng", [GW], F32, kind="ExternalInput").ap()
        lnb = nc.dram_tensor("lnb", [GW], F32, kind="ExternalInput").ap()
        ws = nc.dram_tensor("ws", [16, 128, 128], F32, kind="ExternalInput").ap()
        bs = nc.dram_tensor("bs", [16, 128], F32, kind="ExternalInput").ap()
        w_out = nc.dram_tensor("w_out", [GW, D], F32, kind="ExternalInput").ap()
        g = nc.dram_tensor("g", [D], F32, kind="ExternalInput").ap()
        b = nc.dram_tensor("b", [D], F32, kind="ExternalInput").ap()
        cx.ident_d = nc.dram_tensor("ident", [128, 128], F32, kind="ExternalInput").ap()
        cx.tril_d = nc.dram_tensor("tril", [128, 128], F32, kind="ExternalInput").ap()
        uT_d = nc.dram_tensor("uT_d", [GW, NT], BF16, kind="Internal").ap()
        vln_d = nc.dram_tensor("vln_d", [NT, GW], BF16, kind="Internal").ap()
        y = nc.dram_tensor("y", [NT, D], F32, kind="ExternalOutput").ap()
        phase_g1(cx, x, uT_d, vln_d, w_in, lng, lnb, NT)
        phase_g2(cx, x, y, uT_d, vln_d, ws, bs, w_out, g, b, NT)
        print("instructions:", cx.S.n_inst)
    return nc


def nsa_dram(nc, NT, kind_local="Internal"):
    d = {}
    d["QT"] = nc.dram_tensor("QT_d", [64, 16, NT], BF16, kind=kind_local).ap()
    d["KsT"] = nc.dram_tensor("KsT_d", [64, 4, NT], BF16, kind=kind_local).ap()
    d["KwT"] = nc.dram_tensor("KwT_d", [64, 4, NT], BF16, kind=kind_local).ap()
    d["KcT"] = nc.dram_tensor("KcT_d", [64, 4, NT], BF16, kind=kind_local).ap()
    d["VcT"] = nc.dram_tensor("VcT_d", [64, 4, NT], BF16, kind=kind_local).ap()
    d["Vs"] = nc.dram_tensor("Vs_d", [NT, 256], BF16, kind=kind_local).ap()
    d["Vw"] = nc.dram_tensor("Vw_d", [NT, 256], BF16, kind=kind_local).ap()
    d["gates"] = nc.dram_tensor("gates_d", [NT, 48], F32, kind=kind_local).ap()
    return d


def local_from_dict(d):
    return NsaLocal(lambda k, g: d[k][:, g, :], lambda k, t0: d[k][t0:t0 + 128, :], d["QT"], d["gates"])


def gathered_from_dict(al, NT):
    return NsaGathered(lambda k, r, g: al[k][r][:, g, :], lambda k, r, hf: al[k][r][hf * NT // 2:(hf + 1) * NT // 2, :])


def build_test_n1(NT):
    nc = bass.Bass("TRN2", target_bir_lowering=False)
    with ExitStack() as stack:
        cx = Ctx(nc, stack)
        x = nc.dram_tensor("x", [NT, D], F32, kind="ExternalInput").ap()
        w_in = nc.dram_tensor("w_in", [D, NSA_COLS], F32, kind="ExternalInput").ap()
        cos_d = nc.dram_tensor("cos", [128, NT], F32, kind="ExternalInput").ap()
        sin_d = nc.dram_tensor("sin", [128, NT], F32, kind="ExternalInput").ap()
        cx.ident_d = nc.dram_tensor("ident", [128, 128], F32, kind="ExternalInput").ap()
        d = nsa_dram(nc, NT, "ExternalOutput")
        phase_n1(cx, x, w_in, cos_d, sin_d, local_from_dict(d), NT)
        print("instructions:", cx.S.n_inst)
    return nc


def rope_tables(pos):
    half = HD // 2
    freq = (10000.0 ** (-np.arange(half, dtype=np.float32) / half)).astype(np.float32)
    ang = pos.astype(np.float32)[None, :] * freq[:, None]
    c = np.cos(ang).astype(np.float32)
    s_ = np.sin(ang).astype(np.float32)
    return np.ascontiguousarray(np.tile(c, (4, 1))), np.ascontiguousarray(np.tile(s_, (4, 1)))


def attn_tables(r, NB):
    NT = NB * 128
    SEQ = 4 * NT
    NCP = SEQ // 16
    q = np.arange(128)
    t = {}
    pos = ((4 * np.arange(NB)[:, None] + r) * 128 + q[None, :]).reshape(-1)
    t["cos"], t["sin"] = rope_tables(pos)
    cpos = 16 * np.arange(NCP) + 31
    t["cosc"], t["sinc"] = rope_tables(cpos)
    m = np.arange(-2, 31)
    vis = m[None, :] <= (8 * r + np.floor((q[:, None] - 31) / 16.0))
    t["cmpmask"] = np.where(vis, 0.0, NEGM).astype(np.float32)
    mrel = np.arange(-1, 8)[None, :]
    cur = (2 * r + (q >= 64).astype(np.int64))[:, None]
    keep = np.ones((128, 9), np.float32)
    addt = np.zeros((128, 9), np.float32)
    fut = mrel > cur
    keep[fut] = 0.0
    addt[fut] = -1e9
    c1 = mrel == cur - 1
    keep[c1] = 0.0
    addt[c1] = 1e9
    c0 = mrel == cur
    keep[c0] = 0.0
    addt[c0] = 2e9
    t["keep"], t["addt"] = keep, addt
    k = np.arange(128)[:, None, None]
    kr = np.arange(4)[None, :, None]
    qq = q[None, None, :]
    t["caus"] = ((128 * (kr - r) + k) <= qq).astype(np.float32)
    kr8 = np.arange(-4, 4)[None, :, None]
    dl = r - kr8
    wm = np.where(dl == 0, k <= qq, np.where((dl >= 1) & (dl <= 3), True, np.where(dl == 4, k > qq, False)))
    t["wmask"] = np.broadcast_to(wm, (128, 8, 128)).astype(np.float32)
    t["ident"] = np.eye(128, dtype=np.float32)
    t["tril"] = np.tril(np.ones((128, 128), np.float32))
    return t


def declare_tables(cx, nc, NB):
    NT = NB * 128
    NCP = 4 * NT // 16
    cx.ident_d = nc.dram_tensor("ident", [128, 128], F32, kind="ExternalInput").ap()
    cx.tril_d = nc.dram_tensor("tril", [128, 128], F32, kind="ExternalInput").ap()
    cx.cos_d = nc.dram_tensor("cos", [128, NT], F32, kind="ExternalInput").ap()
    cx.sin_d = nc.dram_tensor("sin", [128, NT], F32, kind="ExternalInput").ap()
    cx.cosc_d = nc.dram_tensor("cosc", [128, NCP], F32, kind="ExternalInput").ap()
    cx.sinc_d = nc.dram_tensor("sinc", [128, NCP], F32, kind="ExternalInput").ap()
    cx.cmpmask_d = nc.dram_tensor("cmpmask", [128, 33], F32, kind="ExternalInput").ap()
    cx.keep_d = nc.dram_tensor("keep", [128, 9], F32, kind="ExternalInput").ap()
    cx.addt_d = nc.dram_tensor("addt", [128, 9], F32, kind="ExternalInput").ap()
    cx.caus_d = nc.dram_tensor("caus", [128, 4, 128], F32, kind="ExternalInput").ap()
    cx.wmask_d = nc.dram_tensor("wmask", [128, 8, 128], F32, kind="ExternalInput").ap()


def gathered_dram(nc, NT, kind):
    al = {}
    al["KsT"] = nc.dram_tensor("KsT_all", [4, 64, 4, NT], BF16, kind=kind).ap()
    al["KwT"] = nc.dram_tensor("KwT_all", [4, 64, 4, NT], BF16, kind=kind).ap()
    al["KcT"] = nc.dram_tensor("KcT_all", [4, 64, 4, NT], BF16, kind=kind).ap()
    al["VcT"] = nc.dram_tensor("VcT_all", [4, 64, 4, NT], BF16, kind=kind).ap()
    al["Vs"] = nc.dram_tensor("Vs_all", [4, NT, 256], BF16, kind=kind).ap()
    al["Vw"] = nc.dram_tensor("Vw_all", [4, NT, 256], BF16, kind=kind).ap()
    return al


def build_test_attn(NB):
    NT = NB * 128
    nc = bass.Bass("TRN2", target_bir_lowering=False)
    with ExitStack() as stack:
        cx = Ctx(nc, stack)
        declare_tables(cx, nc, NB)
        QT = nc.dram_tensor("QT_d", [64, 16, NT], BF16, kind="ExternalInput").ap()
        gates = nc.dram_tensor("gates_d", [NT, 48], F32, kind="ExternalInput").ap()
        al = gathered_dram(nc, NT, "ExternalInput")
        w1k = nc.dram_tensor("w1k", [32, 64, 256], F32, kind="ExternalInput").ap()
        w2k = nc.dram_tensor("w2k", [256, 64], F32, kind="ExternalInput").ap()
        pek = nc.dram_tensor("pek", [32, 64], F32, kind="ExternalInput").ap()
        w1v = nc.dram_tensor("w1v", [32, 64, 256], F32, kind="ExternalInput").ap()
        w2v = nc.dram_tensor("w2v", [256, 64], F32, kind="ExternalInput").ap()
        pev = nc.dram_tensor("pev", [32, 64], F32, kind="ExternalInput").ap()
        O_d = nc.dram_tensor("O_d", [NT, 1024], BF16, kind="ExternalOutput").ap()
        phase_attn(cx, gathered_from_dict(al, NT), QT, gates, O_d, w1k, w2k, pek, w1v, w2v, pev, NB)
        print("instructions:", cx.S.n_inst)
    return nc


L0_W = ["l0_ffn1_w_in", "l0_ffn1_w_out", "l0_ln1_g", "l0_ln1_b", "l0_gm_w_in", "l0_gm_ln_g", "l0_gm_ln_b", "l0_gm_w_s",
        "l0_gm_b_s", "l0_gm_w_out", "l0_ln2_g", "l0_ln2_b", "l0_ffn2_w_in", "l0_ffn2_w_out", "l0_ln3_g", "l0_ln3_b"]
L1A_W = ["l1_ffn1_w_in", "l1_ffn1_w_out", "l1_ln1_g", "l1_ln1_b", "l1_nsa_w_in"]
L1B_W = ["l1_nsa_cmp_pe_k", "l1_nsa_cmp_w1_k", "l1_nsa_cmp_w2_k", "l1_nsa_cmp_pe_v", "l1_nsa_cmp_w1_v", "l1_nsa_cmp_w2_v",
         "l1_nsa_w_out", "l1_ln2_g", "l1_ln2_b", "l1_ffn2_w_in", "l1_ffn2_w_out", "l1_ln3_g", "l1_ln3_b"]
W_SHAPES = {
    "ffn1_w_in": [D, 2 * DFF], "ffn2_w_in": [D, 2 * DFF], "ffn1_w_out": [DFF, D], "ffn2_w_out": [DFF, D],
    "gm_w_in": [D, 2 * GW], "gm_ln_g": [GW], "gm_ln_b": [GW], "gm_w_s": [16, 128, 128], "gm_b_s": [16, 128], "gm_w_out": [GW, D],
    "nsa_w_in": [D, NSA_COLS], "nsa_cmp_pe_k": [32, 64], "nsa_cmp_w1_k": [32, 64, 256], "nsa_cmp_w2_k": [256, 64],
    "nsa_cmp_pe_v": [32, 64], "nsa_cmp_w1_v": [32, 64, 256], "nsa_cmp_w2_v": [256, 64], "nsa_w_out": [D, D],
}


def wshape(name):
    base = name[3:]
    if base in W_SHAPES:
        return W_SHAPES[base]
    return [D]


def declare_w(nc, names):
    return {n: nc.dram_tensor(n, wshape(n), F32, kind="ExternalInput").ap() for n in names}


def emit_part_a(cx, nc, w, x, NT, xa, xb, uT_d, vln_d, nd):
    phase_ffn(cx, x, xa, w["l0_ffn1_w_in"], w["l0_ffn1_w_out"], w["l0_ln1_g"], w["l0_ln1_b"], NT, "a")
    phase_g1(cx, xa, uT_d, vln_d, w["l0_gm_w_in"], w["l0_gm_ln_g"], w["l0_gm_ln_b"], NT)
    phase_g2(cx, xa, xb, uT_d, vln_d, w["l0_gm_w_s"], w["l0_gm_b_s"], w["l0_gm_w_out"], w["l0_ln2_g"], w["l0_ln2_b"], NT)
    phase_ffn(cx, xb, xa, w["l0_ffn2_w_in"], w["l0_ffn2_w_out"], w["l0_ln3_g"], w["l0_ln3_b"], NT, "b")
    phase_ffn(cx, xa, xb, w["l1_ffn1_w_in"], w["l1_ffn1_w_out"], w["l1_ln1_g"], w["l1_ln1_b"], NT, "c")
    phase_n1(cx, xb, w["l1_nsa_w_in"], cx.cos_d, cx.sin_d, nd, NT)
    return xb


def emit_part_b(cx, nc, w, xmid, y, NT, NB, al, QT, gates, O_d, xa):
    phase_attn(cx, al, QT, gates, O_d, w["l1_nsa_cmp_w1_k"], w["l1_nsa_cmp_w2_k"], w["l1_nsa_cmp_pe_k"],
               w["l1_nsa_cmp_w1_v"], w["l1_nsa_cmp_w2_v"], w["l1_nsa_cmp_pe_v"], NB)
    phase_proj(cx, O_d, xmid, xa, w["l1_nsa_w_out"], w["l1_ln2_g"], w["l1_ln2_b"], NT, 1.0 / ALPHA)
    phase_ffn(cx, xa, y, w["l1_ffn2_w_in"], w["l1_ffn2_w_out"], w["l1_ln3_g"], w["l1_ln3_b"], NT, "d")


def build_a(NB):
    NT = NB * 128
    nc = bass.Bass("TRN2", target_bir_lowering=False)
    with ExitStack() as stack:
        cx = Ctx(nc, stack)
        declare_tables(cx, nc, NB)
        w = declare_w(nc, L0_W + L1A_W)
        x = nc.dram_tensor("x", [NT, D], F32, kind="ExternalInput").ap()
        xa = nc.dram_tensor("xa", [NT, D], F32, kind="Internal").ap()
        xb = nc.dram_tensor("xmid", [NT, D], F32, kind="ExternalOutput").ap()
        uT_d = nc.dram_tensor("uT_d", [GW, NT], BF16, kind="Internal").ap()
        vln_d = nc.dram_tensor("vln_d", [NT, GW], BF16, kind="Internal").ap()
        nd = local_from_dict(nsa_dram(nc, NT, "ExternalOutput"))
        emit_part_a(cx, nc, w, x, NT, xa, xb, uT_d, vln_d, nd)
        print("part A instructions:", cx.S.n_inst)
    return nc


def build_b(NB):
    NT = NB * 128
    nc = bass.Bass("TRN2", target_bir_lowering=False)
    with ExitStack() as stack:
        cx = Ctx(nc, stack)
        declare_tables(cx, nc, NB)
        w = declare_w(nc, L1B_W)
        xmid = nc.dram_tensor("xmid", [NT, D], F32, kind="ExternalInput").ap()
        QT = nc.dram_tensor("QT_d", [64, 16, NT], BF16, kind="ExternalInput").ap()
        gates = nc.dram_tensor("gates_d", [NT, 48], F32, kind="ExternalInput").ap()
        al = gathered_dram(nc, NT, "ExternalInput")
        O_d = nc.dram_tensor("O_d", [NT, D], BF16, kind="Internal").ap()
        xa = nc.dram_tensor("xa", [NT, D], F32, kind="Internal").ap()
        y = nc.dram_tensor("y", [NT, D], F32, kind="ExternalOutput").ap()
        emit_part_b(cx, nc, w, xmid, y, NT, NB, gathered_from_dict(al, NT), QT, gates, O_d, xa)
        print("part B instructions:", cx.S.n_inst)
    return nc


TABLE_KEYS = ("ident", "tril", "cos", "sin", "cosc", "sinc", "cmpmask", "keep", "addt", "caus", "wmask")
GATHER_KEYS = ("KsT", "KwT", "KcT", "VcT", "Vs", "Vw")


def run_unfused(inputs, NB):
    NT = NB * 128
    x = np.asarray(inputs["x"], dtype=np.float32)
    B = x.shape[0]
    tabs = [attn_tables(c % 4, NB) for c in range(NCORES)]
    wts = {k: np.ascontiguousarray(np.asarray(v, dtype=np.float32)) for k, v in inputs.items() if k != "x"}
    in_a = []
    for c in range(NCORES):
        b, r = c // 4, c % 4
        m = {k: tabs[c][k] for k in TABLE_KEYS}
        m["x"] = np.ascontiguousarray(x[b].reshape(NB, 4, 128, D)[:, r].reshape(NT, D))
        for k in L0_W + L1A_W:
            m[k] = wts[k]
        in_a.append(m)
    res_a = run_bass_kernel_spmd(build_a(NB), in_a, core_ids=list(range(NCORES))).results
    in_b = []
    for c in range(NCORES):
        b = c // 4
        m = {k: tabs[c][k] for k in TABLE_KEYS}
        m["xmid"] = res_a[c]["xmid"]
        m["QT_d"] = res_a[c]["QT_d"]
        m["gates_d"] = res_a[c]["gates_d"]
        for k in GATHER_KEYS:
            m[k + "_all"] = np.ascontiguousarray(np.stack([np.asarray(res_a[4 * b + rr][k + "_d"]) for rr in range(4)]))
        for k in L1B_W:
            m[k] = wts[k]
        in_b.append(m)
    res_b = run_bass_kernel_spmd(build_b(NB), in_b, core_ids=list(range(NCORES))).results
    out = np.zeros((B, 4 * NT, D), np.float32)
    for c in range(NCORES):
        b, r = c // 4, c % 4
        out[b].reshape(NB, 4, 128, D)[:, r] = np.asarray(res_b[c]["y"]).reshape(NB, 128, D)
    return out


def build_fused(NB):
    NT = NB * 128
    nc = bass.Bass("TRN2", target_bir_lowering=False)
    with ExitStack() as stack:
        cx = Ctx(nc, stack)
        declare_tables(cx, nc, NB)
        w = declare_w(nc, L0_W + L1A_W + L1B_W)
        x = nc.dram_tensor("x", [NT, D], F32, kind="ExternalInput").ap()
        y = nc.dram_tensor("y", [NT, D], F32, kind="ExternalOutput").ap()
        xa = nc.dram_tensor("xa", [NT, D], F32, kind="Internal").ap()
        xb = nc.dram_tensor("xb", [NT, D], F32, kind="Internal").ap()
        uT_d = nc.dram_tensor("uT_d", [GW, NT], BF16, kind="Internal").ap()
        vln_d = nc.dram_tensor("vln_d", [NT, GW], BF16, kind="Internal").ap()
        O_d = nc.dram_tensor("O_d", [NT, D], BF16, kind="Internal").ap()
        QT = nc.dram_tensor("QT_d", [64, 16, NT], BF16, kind="Internal").ap()
        gates = nc.dram_tensor("gates_d", [NT, 48], F32, kind="Internal").ap()
        loc, gat = {}, {}
        for k in GATHER_KEYS:
            for hf in range(2):
                loc[k, hf] = nc.dram_tensor("%s_loc%d" % (k, hf), [128, NT], BF16, kind="Internal").ap()
                gat[k, hf] = nc.dram_tensor("%s_gat%d" % (k, hf), [4 * 128, NT], BF16, kind="Internal").ap()
        tokv = lambda a: a.rearrange("r (x c) -> (r x) c", c=256)
        H = NT // 2
        nd = NsaLocal(lambda k, g: loc[k, g // 2][(g % 2) * 64:(g % 2) * 64 + 64, :],
                      lambda k, t0: tokv(loc[k, t0 // H])[t0 % H:t0 % H + 128, :], QT, gates)
        al = NsaGathered(lambda k, r, g: gat[k, g // 2][r * 128 + (g % 2) * 64:r * 128 + (g % 2) * 64 + 64, :],
                         lambda k, r, hf: tokv(gat[k, hf][r * 128:(r + 1) * 128, :]))
        xmid = emit_part_a(cx, nc, w, x, NT, xa, xb, uT_d, vln_d, nd)
        for k in GATHER_KEYS:
            for hf in range(2):
                cx.S.cc("AllGather", [[0, 1, 2, 3], [4, 5, 6, 7]], loc[k, hf], gat[k, hf], key="cc_%s%d" % (k, hf))
        cx.S.flush()
        emit_part_b(cx, nc, w, xmid, y, NT, NB, al, QT, gates, O_d, xa)
        print("fused instructions:", cx.S.n_inst)
    return nc


def run_fused(inputs, NB):
    NT = NB * 128
    x = np.asarray(inputs["x"], dtype=np.float32)
    B = x.shape[0]
    wts = {k: np.ascontiguousarray(np.asarray(v, dtype=np.float32)) for k, v in inputs.items() if k != "x"}
    in_maps = []
    for c in range(NCORES):
        b, r = c // 4, c % 4
        tabs = attn_tables(r, NB)
        m = {k: tabs[k] for k in TABLE_KEYS}
        m["x"] = np.ascontiguousarray(x[b].reshape(NB, 4, 128, D)[:, r].reshape(NT, D))
        for k in L0_W + L1A_W + L1B_W:
            m[k] = wts[k]
        in_maps.append(m)
    res = run_bass_kernel_spmd(build_fused(NB), in_maps, core_ids=list(range(NCORES))).results
    out = np.zeros((B, 4 * NT, D), np.float32)
    for c in range(NCORES):
        b, r = c // 4, c % 4
        out[b].reshape(NB, 4, 128, D)[:, r] = np.asarray(res[c]["y"]).reshape(NB, 128, D)
    return out


def kernel(**inputs):
    return run_fused(inputs, 32)
```

```python
import math
from contextlib import ExitStack

import numpy as np
import concourse.bass as bass
import concourse.mybir as mybir
from concourse.bass_utils import run_bass_kernel_spmd

F32 = mybir.dt.float32
BF16 = mybir.dt.bfloat16
AF = mybir.ActivationFunctionType
ALU = mybir.AluOpType
AX = mybir.AxisListType

D = 1024
DFF = 2816
DEPTH = 2
ALPHA = (2 * DEPTH) ** 0.25
LN_EPS = 1e-5
NCORES = 8


class Buf:
    __slots__ = ("name", "writers", "dma_writers", "readers", "dma_readers")

    def __init__(self, name):
        self.name = name
        self.writers = {}
        self.dma_writers = []
        self.readers = {}
        self.dma_readers = []


class Op:
    __slots__ = ("eng", "fn", "deps", "is_dma", "dsem", "dcount", "signal", "sigval", "emitted")

    def __init__(self, eng, fn, is_dma):
        self.eng = eng
        self.fn = fn
        self.deps = []
        self.is_dma = is_dma
        self.dsem = None
        self.dcount = 0
        self.signal = False
        self.sigval = 0
        self.emitted = False


class Sched:
    def __init__(self, nc, stack):
        self.nc = nc
        self.engs = {"pe": nc.tensor, "act": nc.scalar, "dve": nc.vector, "pool": nc.gpsimd, "sp": nc.sync}
        self.esem = {e: stack.enter_context(nc.semaphore("es_" + e)) for e in self.engs}
        self.stack = stack
        self.pending = []
        self.sigcount = {e: 0 for e in self.engs}
        self.waited = {e: {} for e in self.engs}
        self.dsems = {}
        self.n_inst = 0

    def buf(self, name):
        return Buf(name)

    def bufs(self, name, n):
        return [Buf("%s%d" % (name, i)) for i in range(n)]

    def _add_dep(self, op, p):
        if p is op:
            return
        if (not p.is_dma) and (not op.is_dma) and p.eng == op.eng and op.eng == "pe":
            return
        op.deps.append(p)

    def _track(self, op, reads, writes):
        for b in reads:
            for p in b.writers.values():
                self._add_dep(op, p)
            for p in b.dma_writers:
                self._add_dep(op, p)
        for b in writes:
            if b.readers or b.dma_readers:
                for p in b.readers.values():
                    self._add_dep(op, p)
                for p in b.dma_readers:
                    self._add_dep(op, p)
                for p in b.writers.values():
                    self._add_dep(op, p)
                for p in b.dma_writers:
                    self._add_dep(op, p)
                b.readers = {}
                b.dma_readers = []
                b.writers = {}
                b.dma_writers = []
            else:
                for e, p in b.writers.items():
                    if e != op.eng or op.is_dma:
                        self._add_dep(op, p)
                for p in b.dma_writers:
                    self._add_dep(op, p)
        for b in reads:
            if op.is_dma:
                b.dma_readers.append(op)
            else:
                b.readers[op.eng] = op
        for b in writes:
            if op.is_dma:
                b.dma_writers.append(op)
            else:
                b.writers[op.eng] = op

    def op(self, eng, fn, reads=(), writes=()):
        o = Op(eng, fn, False)
        self._track(o, reads, writes)
        self.pending.append(o)
        return o

    def dma(self, eng, out, in_, reads=(), writes=(), key=None, slow=False):
        assert key is not None
        if key not in self.dsems:
            self.dsems[key] = [self.stack.enter_context(self.nc.semaphore("ds_" + key)), 0, 16]
        ent = self.dsems[key]
        ent[1] += 1
        if slow:
            o = Op(eng, (lambda: self.engs[eng].dma_start(out=out, in_=in_, allow_slow_non_contiguous=True)), True)
        else:
            o = Op(eng, (lambda: self.engs[eng].dma_start(out=out, in_=in_)), True)
        o.dsem = key
        o.dcount = ent[1]
        self._track(o, reads, writes)
        self.pending.append(o)
        return o

    def cc(self, kind, groups, in_ap, out_ap, reads=(), writes=(), key=None):
        assert key not in self.dsems
        self.dsems[key] = [self.stack.enter_context(self.nc.semaphore("ds_" + key)), 1, 1]
        o = Op("pool", (lambda: self.nc.gpsimd.collective_compute(kind, ALU.bypass, replica_groups=groups,
                                                                  ins=[in_ap.opt()], outs=[out_ap.opt()])), True)
        o.dsem = key
        o.dcount = 1
        self._track(o, reads, writes)
        self.pending.append(o)
        return o

    def _wait(self, eng, semkey, sem, val):
        w = self.waited[eng]
        if w.get(semkey, 0) >= val:
            return
        w[semkey] = val
        self.engs[eng].wait_ge(sem, val)
        self.n_inst += 1

    def flush(self):
        for o in self.pending:
            o.deps = [p for p in o.deps if not p.emitted]
            for p in o.deps:
                if not p.is_dma:
                    p.signal = True
        last = {}
        for o in self.pending:
            if not o.is_dma:
                last[o.eng] = o
        for o in last.values():
            o.signal = True
        for o in self.pending:
            for p in o.deps:
                assert p.emitted, "dependency on later op"
                if p.is_dma:
                    self._wait(o.eng, "d_" + p.dsem, self.dsems[p.dsem][0], self.dsems[p.dsem][2] * p.dcount)
                else:
                    self._wait(o.eng, "e_" + p.eng, self.esem[p.eng], p.sigval)
            ins = o.fn()
            self.n_inst += 1
            if o.is_dma:
                if self.dsems[o.dsem][2] == 16:
                    ins.then_inc(self.dsems[o.dsem][0], 16)
                else:
                    ins.then_inc(self.dsems[o.dsem][0])
            elif o.signal:
                self.sigcount[o.eng] += 1
                o.sigval = self.sigcount[o.eng]
                ins.then_inc(self.esem[o.eng], 1)
            o.emitted = True
            o.fn = None
        self.pending = []
        self.barrier()

    def barrier(self):
        for e in self.engs:
            for e2 in self.engs:
                if e2 != e and self.sigcount[e2] > 0:
                    self._wait(e, "e_" + e2, self.esem[e2], self.sigcount[e2])
            for key, (sem, cnt, unit) in self.dsems.items():
                if cnt > 0:
                    self._wait(e, "d_" + key, sem, unit * cnt)


class Ctx:
    def __init__(self, nc, stack):
        self.nc = nc
        self.stack = stack
        self.S = Sched(nc, stack)
        self.uid = 0

    def name(self, s):
        self.uid += 1
        return "%s_%d" % (s, self.uid)


def sb(cx, st, name, shape, dt):
    return st.enter_context(cx.nc.sbuf_tensor(cx.name(name), shape, dt))


def ps(cx, st, name, shape, dt):
    return st.enter_context(cx.nc.psum_tensor(cx.name(name), shape, dt))


def make_ident(cx, st):
    nc, S = cx.nc, cx.S
    ident = sb(cx, st, "ident", [128, 128], BF16)
    IB = S.buf("ident")
    S.dma("pool", ident[:], cx.ident_d[:, :], writes=[IB], key="ident")
    return ident, IB


def ln_epilogue(cx, z, ZB, gb, GB, st_t, mv_t, sd_t, rs_t, SB_, eps):
    nc, S = cx.nc, cx.S
    S.op("dve", lambda: nc.vector.bn_stats(out=st_t[:, 0, :], in_=z[:, 0:512]), reads=[ZB], writes=[SB_])
    S.op("dve", lambda: nc.vector.bn_stats(out=st_t[:, 1, :], in_=z[:, 512:1024]), reads=[ZB], writes=[SB_])
    S.op("dve", lambda: nc.vector.bn_aggr(out=mv_t[:], in_=st_t[:]), reads=[SB_], writes=[SB_])
    S.op("act", lambda: nc.scalar.activation(out=sd_t[:], in_=mv_t[:, 1:2], func=AF.Sqrt, bias=eps, scale=1.0),
         reads=[SB_], writes=[SB_])
    S.op("dve", lambda: nc.vector.reciprocal(out=rs_t[:], in_=sd_t[:]), reads=[SB_], writes=[SB_])
    S.op("dve", lambda: nc.vector.tensor_scalar(out=z[:], in0=z[:], scalar1=mv_t[:, 0:1], scalar2=rs_t[:, 0:1],
                                                op0=ALU.subtract, op1=ALU.mult), reads=[ZB, SB_], writes=[ZB])
    S.op("pool", lambda: nc.gpsimd.tensor_tensor(out=z[:], in0=z[:], in1=gb[:, 0, :], op=ALU.mult),
         reads=[ZB, GB], writes=[ZB])
    S.op("pool", lambda: nc.gpsimd.tensor_tensor(out=z[:], in0=z[:], in1=gb[:, 1, :], op=ALU.add),
         reads=[ZB, GB], writes=[ZB])


def load_gb(cx, st, g_d, b_d, name):
    nc, S = cx.nc, cx.S
    gb = sb(cx, st, name, [128, 2, D], F32)
    GB = S.buf(name)
    S.dma("sp", gb[:, 0, :], g_d.partition_broadcast(128), writes=[GB], key=name)
    S.dma("sp", gb[:, 1, :], b_d.partition_broadcast(128), writes=[GB], key=name)
    return gb, GB


def emit_xT(cx, xs_t, XSb, xbf, XBF, tp, TP, xT_t, XTb, ident, IB, NS):
    nc, S = cx.nc, cx.S
    KC = D // 128
    S.op("pool", lambda: nc.gpsimd.tensor_copy(out=xbf[:], in_=xs_t[:]), reads=[XSb], writes=[XBF])
    for s in range(NS):
        b = s % len(tp)
        for k in range(KC):
            S.op("pe", lambda s=s, k=k, b=b: nc.tensor.transpose(out=tp[b][:, k * 128:(k + 1) * 128],
                                                                  in_=xbf[:, s, k * 128:(k + 1) * 128], identity=ident[:]),
                 reads=[XBF, IB], writes=[TP[b]])
        S.op("dve", lambda s=s, b=b: nc.vector.tensor_copy(
            out=xT_t[:, :, s * 128:(s + 1) * 128], in_=tp[b][:].rearrange("p (k t) -> p k t", k=KC)),
            reads=[TP[b]], writes=[XTb])


def phase_ffn(cx, xin, xout, w_in_d, w_out_d, g_d, b_d, NT, tag):
    nc, S = cx.nc, cx.S
    TT = 256
    NS = TT // 128
    KC = D // 128
    MC = DFF // 128
    c_res = 0.5 / ALPHA
    eps = LN_EPS / (ALPHA * ALPHA)
    with ExitStack() as st:
        w1 = sb(cx, st, "w1", [128, KC, 2 * DFF], BF16)
        w2 = sb(cx, st, "w2", [128, MC, D], BF16)
        W1, W2 = S.buf("W1"), S.buf("W2")
        for k in range(KC):
            S.dma("pool", w1[:, k, :], w_in_d[k * 128:(k + 1) * 128, :], writes=[W1], key="w1")
        for m in range(MC):
            S.dma("pool", w2[:, m, :], w_out_d[m * 128:(m + 1) * 128, :], writes=[W2], key="w2")
        gb, GB = load_gb(cx, st, g_d, b_d, "gb")
        ident, IB = make_ident(cx, st)
        xs = [sb(cx, st, "xs", [128, NS, D], F32) for _ in range(2)]
        XS = S.bufs("xs", 2)
        xbf = sb(cx, st, "xbf", [128, NS, D], BF16)
        XBF = S.buf("xbf")
        xT = [sb(cx, st, "xT", [128, KC, TT], BF16) for _ in range(2)]
        XT = S.bufs("xT", 2)
        hT = [sb(cx, st, "hT", [128, MC, TT], BF16) for _ in range(2)]
        HT = S.bufs("hT", 2)
        sg = [sb(cx, st, "sg", [128, TT], BF16) for _ in range(2)]
        SG = S.bufs("sg", 2)
        z = [sb(cx, st, "z", [128, D], F32) for _ in range(2)]
        Z = S.bufs("z", 2)
        st_t = [sb(cx, st, "st", [128, 2, 6], F32) for _ in range(2)]
        mv_t = [sb(cx, st, "mv", [128, 2], F32) for _ in range(2)]
        sd_t = [sb(cx, st, "sd", [128, 1], F32) for _ in range(2)]
        rs_t = [sb(cx, st, "rs", [128, 1], F32) for _ in range(2)]
        STB = S.bufs("stb", 2)
        tp = [ps(cx, st, "tp", [128, D], BF16) for _ in range(2)]
        TP = S.bufs("tp", 2)
        gu = [ps(cx, st, "gu", [128, 512], F32) for _ in range(4)]
        GU = S.bufs("gu", 4)
        op_ = [ps(cx, st, "o", [128, 512], F32) for _ in range(2)]
        OP = S.bufs("o", 2)

        ntile = NT // TT
        xin_v = xin.rearrange("(t s p) d -> t p s d", s=NS, p=128)
        xout_v = xout.rearrange("(t s p) d -> t s p d", s=NS, p=128)

        def load(t):
            S.dma("sp", xs[t % 2][:], xin_v[t], writes=[XS[t % 2]], key="xs%d" % (t % 2))

        load(0)
        for t in range(ntile):
            b2 = t % 2
            if t + 1 < ntile:
                load(t + 1)
            emit_xT(cx, xs[b2], XS[b2], xbf, XBF, tp, TP, xT[b2], XT[b2], ident, IB, NS)
            for m in range(MC):
                g_ps, u_ps = gu[2 * (m % 2)], gu[2 * (m % 2) + 1]
                GP, UP = GU[2 * (m % 2)], GU[2 * (m % 2) + 1]
                for k in range(KC):
                    S.op("pe", lambda m=m, k=k, g_ps=g_ps, b2=b2: nc.tensor.matmul(
                        g_ps[:, 0:TT], w1[:, k, m * 128:(m + 1) * 128], xT[b2][:, k, :], start=(k == 0), stop=(k == KC - 1)),
                        reads=[W1, XT[b2]], writes=[GP])
                for k in range(KC):
                    S.op("pe", lambda m=m, k=k, u_ps=u_ps, b2=b2: nc.tensor.matmul(
                        u_ps[:, 0:TT], w1[:, k, DFF + m * 128:DFF + (m + 1) * 128], xT[b2][:, k, :], start=(k == 0), stop=(k == KC - 1)),
                        reads=[W1, XT[b2]], writes=[UP])
                S.op("act", lambda m=m, g_ps=g_ps: nc.scalar.activation(out=sg[m % 2][:], in_=g_ps[:, 0:TT], func=AF.Silu),
                     reads=[GP], writes=[SG[m % 2]])
                S.op("dve", lambda m=m, u_ps=u_ps, b2=b2: nc.vector.tensor_tensor(
                    out=hT[b2][:, m, :], in0=sg[m % 2][:], in1=u_ps[:, 0:TT], op=ALU.mult),
                    reads=[SG[m % 2], UP], writes=[HT[b2]])
            for s in range(NS):
                for n in range(2):
                    for m in range(MC):
                        S.op("pe", lambda s=s, n=n, m=m, b2=b2: nc.tensor.matmul(
                            op_[n][:], hT[b2][:, m, s * 128:(s + 1) * 128], w2[:, m, n * 512:(n + 1) * 512],
                            start=(m == 0), stop=(m == MC - 1)), reads=[HT[b2], W2], writes=[OP[n]])
                    S.op("dve", lambda s=s, n=n, b2=b2: nc.vector.scalar_tensor_tensor(
                        out=z[s][:, n * 512:(n + 1) * 512], in0=op_[n][:], scalar=c_res, in1=xs[b2][:, s, n * 512:(n + 1) * 512],
                        op0=ALU.mult, op1=ALU.add), reads=[OP[n], XS[b2]], writes=[Z[s]])
                ln_epilogue(cx, z[s], Z[s], gb, GB, st_t[s], mv_t[s], sd_t[s], rs_t[s], STB[s], eps)
                S.dma("sp", xout_v[t, s], z[s][:], reads=[Z[s]], key="zst%d" % s)
        S.flush()


GW = 3072


def ln_stats(cx, src, SRC, nchunk, st_t, mv_t, sd_t, rs_t, SB_, eps):
    nc, S = cx.nc, cx.S
    for c in range(nchunk):
        S.op("dve", lambda c=c: nc.vector.bn_stats(out=st_t[:, c, :], in_=src[:, c * 512:(c + 1) * 512]),
             reads=[SRC], writes=[SB_])
    S.op("dve", lambda: nc.vector.bn_aggr(out=mv_t[:], in_=st_t[:]), reads=[SB_], writes=[SB_])
    S.op("act", lambda: nc.scalar.activation(out=sd_t[:], in_=mv_t[:, 1:2], func=AF.Sqrt, bias=eps, scale=1.0),
         reads=[SB_], writes=[SB_])
    S.op("dve", lambda: nc.vector.reciprocal(out=rs_t[:], in_=sd_t[:]), reads=[SB_], writes=[SB_])


def phase_g1(cx, xin, uT_d, vln_d, w_in_d, lng_d, lnb_d, NT):
    nc, S = cx.nc, cx.S
    TT, NS, KC, JC = 256, 2, 8, 24
    with ExitStack() as st:
        wg = sb(cx, st, "wg", [128, KC, 2 * GW], BF16)
        WG = S.buf("WG")
        for k in range(KC):
            S.dma("pool", wg[:, k, :], w_in_d[k * 128:(k + 1) * 128, :], writes=[WG], key="wg")
        gbv = sb(cx, st, "gbv", [128, 2, GW], F32)
        GBV = S.buf("gbv")
        S.dma("sp", gbv[:, 0, :], lng_d.partition_broadcast(128), writes=[GBV], key="gbv")
        S.dma("sp", gbv[:, 1, :], lnb_d.partition_broadcast(128), writes=[GBV], key="gbv")
        ident, IB = make_ident(cx, st)
        xs = [sb(cx, st, "xs", [128, NS, D], F32) for _ in range(2)]
        XS = S.bufs("xs", 2)
        xbf = sb(cx, st, "xbf", [128, NS, D], BF16)
        XBF = S.buf("xbf")
        xT = [sb(cx, st, "xT", [128, KC, TT], BF16) for _ in range(2)]
        XT = S.bufs("xT", 2)
        ust = [sb(cx, st, "ust", [128, 4, TT], BF16) for _ in range(2)]
        UST = S.bufs("ust", 2)
        v = [sb(cx, st, "v", [128, GW], F32) for _ in range(2)]
        V = S.bufs("v", 2)
        vln = [sb(cx, st, "vln", [128, GW], BF16) for _ in range(2)]
        VLN = S.bufs("vln", 2)
        st_t = [sb(cx, st, "st", [128, 6, 6], F32) for _ in range(2)]
        mv_t = [sb(cx, st, "mv", [128, 2], F32) for _ in range(2)]
        sd_t = [sb(cx, st, "sd", [128, 1], F32) for _ in range(2)]
        rs_t = [sb(cx, st, "rs", [128, 1], F32) for _ in range(2)]
        STB = S.bufs("stb", 2)
        tp = [ps(cx, st, "tp", [128, D], BF16) for _ in range(2)]
        TP = S.bufs("tp", 2)
        pu = [ps(cx, st, "pu", [128, 512], F32) for _ in range(2)]
        PU = S.bufs("pu", 2)
        pv = [ps(cx, st, "pv", [128, 512], F32) for _ in range(3)]
        PV = S.bufs("pv", 3)
        ntile = NT // TT
        xin_v = xin.rearrange("(t s p) d -> t p s d", s=NS, p=128)
        uT_v = uT_d.rearrange("(j p) n -> p j n", p=128)

        def load(t):
            S.dma("sp", xs[t % 2][:], xin_v[t], writes=[XS[t % 2]], key="xs%d" % (t % 2))

        load(0)
        for t in range(ntile):
            b2 = t % 2
            if t + 1 < ntile:
                load(t + 1)
            emit_xT(cx, xs[b2], XS[b2], xbf, XBF, tp, TP, xT[b2], XT[b2], ident, IB, NS)
            for j in range(JC):
                q = (j // 4) % 2
                for k in range(KC):
                    S.op("pe", lambda j=j, k=k, b2=b2: nc.tensor.matmul(
                        pu[j % 2][:, 0:TT], wg[:, k, j * 128:(j + 1) * 128], xT[b2][:, k, :],
                        start=(k == 0), stop=(k == KC - 1)), reads=[WG, XT[b2]], writes=[PU[j % 2]])
                S.op("act", lambda j=j, q=q: nc.scalar.activation(out=ust[q][:, j % 4, :], in_=pu[j % 2][:, 0:TT],
                                                                    func=AF.Gelu_apprx_tanh),
                     reads=[PU[j % 2]], writes=[UST[q]])
                if j % 4 == 3:
                    S.dma("sp", uT_v[:, j - 3:j + 1, t * TT:(t + 1) * TT], ust[q][:], reads=[UST[q]], key="ust%d" % q)
            for s in range(NS):
                for n in range(6):
                    for k in range(KC):
                        S.op("pe", lambda s=s, n=n, k=k, b2=b2: nc.tensor.matmul(
                            pv[n % 3][:], xT[b2][:, k, s * 128:(s + 1) * 128], wg[:, k, GW + n * 512:GW + (n + 1) * 512],
                            start=(k == 0), stop=(k == KC - 1)), reads=[WG, XT[b2]], writes=[PV[n % 3]])
                    S.op("act", lambda s=s, n=n: nc.scalar.activation(out=v[s][:, n * 512:(n + 1) * 512], in_=pv[n % 3][:],
                                                                        func=AF.Gelu_apprx_tanh),
                         reads=[PV[n % 3]], writes=[V[s]])
                ln_stats(cx, v[s], V[s], 6, st_t[s], mv_t[s], sd_t[s], rs_t[s], STB[s], LN_EPS)
                S.op("dve", lambda s=s: nc.vector.tensor_scalar(out=v[s][:], in0=v[s][:], scalar1=mv_t[s][:, 0:1],
                                                                scalar2=rs_t[s][:, 0:1], op0=ALU.subtract, op1=ALU.mult),
                     reads=[V[s], STB[s]], writes=[V[s]])
                S.op("pool", lambda s=s: nc.gpsimd.tensor_tensor(out=v[s][:], in0=v[s][:], in1=gbv[:, 0, :], op=ALU.mult),
                     reads=[V[s], GBV], writes=[V[s]])
                S.op("pool", lambda s=s: nc.gpsimd.tensor_tensor(out=vln[s][:], in0=v[s][:], in1=gbv[:, 1, :], op=ALU.add),
                     reads=[V[s], GBV], writes=[VLN[s]])
                S.dma("sp", vln_d[t * TT + s * 128:t * TT + (s + 1) * 128, :], vln[s][:], reads=[VLN[s]], key="vln%d" % s)
        S.flush()


def phase_g2(cx, xin, xout, uT_d, vln_d, ws_d, bs_d, w_out_d, g_d, b_d, NT):
    nc, S = cx.nc, cx.S
    JC = 24
    c_res = 1.0 / ALPHA
    eps = LN_EPS / (ALPHA * ALPHA)
    with ExitStack() as st:
        w2 = sb(cx, st, "w2g", [128, JC, D], BF16)
        W2 = S.buf("W2g")
        for j in range(JC):
            S.dma("pool", w2[:, j, :], w_out_d[j * 128:(j + 1) * 128, :], writes=[W2], key="w2g")
        gb, GB = load_gb(cx, st, g_d, b_d, "gb2")
        ident, IB = make_ident(cx, st)
        WT = sb(cx, st, "WT", [128, 16, 128], BF16)
        WTB = S.buf("WT")
        bhi = sb(cx, st, "bhi", [1, 2048], BF16)
        blo = sb(cx, st, "blo", [1, 2048], BF16)
        ones = sb(cx, st, "ones", [1, 128], BF16)
        BB = S.buf("bias")
        with ExitStack() as st2:
            wsf = sb(cx, st2, "wsf", [128, 16, 128], F32)
            WSF = S.buf("wsf")
            S.dma("sp", wsf[:], ws_d.rearrange("g t s -> t g s"), writes=[WSF], key="wsf")
            tril = sb(cx, st2, "tril", [128, 128], F32)
            TR = S.buf("tril")
            S.dma("sp", tril[:], cx.tril_d[:, :], writes=[TR], key="tril")
            wsm = sb(cx, st2, "wsm", [128, 16, 128], BF16)
            WSM = S.buf("wsm")
            S.op("dve", lambda: nc.vector.tensor_tensor(out=wsm[:], in0=wsf[:], in1=tril[:].unsqueeze(1).broadcast_to([128, 16, 128]),
                                                        op=ALU.mult), reads=[WSF, TR], writes=[WSM])
            tpw = [ps(cx, st2, "tpw", [128, D], BF16) for _ in range(2)]
            TPW = S.bufs("tpw", 2)
            for g in range(16):
                S.op("pe", lambda g=g: nc.tensor.transpose(out=tpw[g // 8][:, (g % 8) * 128:(g % 8 + 1) * 128],
                                                           in_=wsm[:, g, :], identity=ident[:]),
                     reads=[WSM, IB], writes=[TPW[g // 8]])
            for h in range(2):
                S.op("dve", lambda h=h: nc.vector.tensor_copy(out=WT[:, h * 8:(h + 1) * 8, :],
                                                              in_=tpw[h][:].rearrange("p (g t) -> p g t", g=8)),
                     reads=[TPW[h]], writes=[WTB])
            bsf = sb(cx, st2, "bsf", [1, 2048], F32)
            BSF = S.buf("bsf")
            S.dma("sp", bsf[:], bs_d.rearrange("g t -> (g t)").unsqueeze(0), writes=[BSF], key="bsf")
            S.op("dve", lambda: nc.vector.tensor_copy(out=bhi[:], in_=bsf[:]), reads=[BSF], writes=[BB])
            S.op("dve", lambda: nc.vector.tensor_tensor(out=blo[:], in0=bsf[:], in1=bhi[:], op=ALU.subtract),
                 reads=[BSF, BB], writes=[BB])
            S.op("dve", lambda: nc.vector.memset(ones[:], 1.0), writes=[BB])
            S.flush()
        ut = [sb(cx, st, "ut", [128, JC, 256], BF16) for _ in range(2)]
        UT = S.bufs("ut", 2)
        vl = [sb(cx, st, "vl", [128, GW], BF16) for _ in range(2)]
        VL = S.bufs("vl", 2)
        xs = [sb(cx, st, "xs", [128, D], F32) for _ in range(2)]
        XS = S.bufs("xs", 2)
        uv = [sb(cx, st, "uv", [128, JC, 128], BF16) for _ in range(2)]
        UV = S.bufs("uv", 2)
        z = [sb(cx, st, "z", [128, D], F32) for _ in range(2)]
        Z = S.bufs("z", 2)
        st_t = [sb(cx, st, "st", [128, 2, 6], F32) for _ in range(2)]
        mv_t = [sb(cx, st, "mv", [128, 2], F32) for _ in range(2)]
        sd_t = [sb(cx, st, "sd", [128, 1], F32) for _ in range(2)]
        rs_t = [sb(cx, st, "rs", [128, 1], F32) for _ in range(2)]
        STB = S.bufs("stb", 2)
        P = [ps(cx, st, "P", [128, 512], F32) for _ in range(6)]
        PB = S.bufs("P", 6)
        op_ = [ps(cx, st, "o", [128, 512], F32) for _ in range(2)]
        OP = S.bufs("o", 2)
        nchunk = NT // 128
        uT_v = uT_d.rearrange("(j p) n -> p j n", p=128)
        pieces = []
        for g in range(16):
            a = g // 2
            if g % 2 == 0:
                pieces.append((g, 192 * g, 192 * g + 128, 3 * a, 0, 128))
                pieces.append((g, 192 * g + 128, 192 * g + 192, 3 * a + 1, 0, 64))
            else:
                pieces.append((g, 192 * g, 192 * g + 64, 3 * a + 1, 64, 128))
                pieces.append((g, 192 * g + 64, 192 * g + 192, 3 * a + 2, 0, 128))

        def load_u(tt):
            S.dma("sp", ut[tt % 2][:], uT_v[:, :, tt * 256:(tt + 1) * 256], writes=[UT[tt % 2]], key="ut%d" % (tt % 2))

        def load_c(c):
            S.dma("sp", vl[c % 2][:], vln_d[c * 128:(c + 1) * 128, :], writes=[VL[c % 2]], key="vl%d" % (c % 2))
            S.dma("sp", xs[c % 2][:], xin[c * 128:(c + 1) * 128, :], writes=[XS[c % 2]], key="xsg%d" % (c % 2))

        load_u(0)
        load_c(0)
        for c in range(nchunk):
            c2 = c % 2
            tt, cs = c // 2, c % 2
            if c + 1 < nchunk:
                load_c(c + 1)
                if (c + 1) % 2 == 0:
                    load_u((c + 1) // 2)
            for (g, f0, f1, j, r0, r1) in pieces:
                M = f1 - f0
                out_ap = lambda j=j, r0=r0, r1=r1: P[j // 4][r0:r1, (j % 4) * 128:(j % 4 + 1) * 128]
                S.op("pe", lambda g=g, f0=f0, f1=f1, out_ap=out_ap, c2=c2: nc.tensor.matmul(
                    out_ap(), vl[c2][:, f0:f1], WT[:, g, :], start=True, stop=False),
                    reads=[VL[c2], WTB], writes=[PB[j // 4]])
                S.op("pe", lambda g=g, M=M, out_ap=out_ap: nc.tensor.matmul(
                    out_ap(), ones[0:1, 0:M], bhi[0:1, g * 128:(g + 1) * 128], start=False, stop=False),
                    reads=[BB], writes=[PB[j // 4]])
                S.op("pe", lambda g=g, M=M, out_ap=out_ap: nc.tensor.matmul(
                    out_ap(), ones[0:1, 0:M], blo[0:1, g * 128:(g + 1) * 128], start=False, stop=True),
                    reads=[BB], writes=[PB[j // 4]])
            for jj in range(6):
                S.op("dve", lambda jj=jj, c2=c2, tt=tt, cs=cs: nc.vector.tensor_tensor(
                    out=uv[c2][:, 4 * jj:4 * jj + 4, :], in0=P[jj][:].rearrange("p (j t) -> p j t", j=4),
                    in1=ut[tt % 2][:, 4 * jj:4 * jj + 4, cs * 128:(cs + 1) * 128], op=ALU.mult),
                    reads=[PB[jj], UT[tt % 2]], writes=[UV[c2]])
            for n in range(2):
                for j in range(JC):
                    S.op("pe", lambda n=n, j=j, c2=c2: nc.tensor.matmul(
                        op_[n][:], uv[c2][:, j, :], w2[:, j, n * 512:(n + 1) * 512], start=(j == 0), stop=(j == JC - 1)),
                        reads=[UV[c2], W2], writes=[OP[n]])
                S.op("dve", lambda n=n, c2=c2: nc.vector.scalar_tensor_tensor(
                    out=z[c2][:, n * 512:(n + 1) * 512], in0=op_[n][:], scalar=c_res, in1=xs[c2][:, n * 512:(n + 1) * 512],
                    op0=ALU.mult, op1=ALU.add), reads=[OP[n], XS[c2]], writes=[Z[c2]])
            ln_epilogue(cx, z[c2], Z[c2], gb, GB, st_t[c2], mv_t[c2], sd_t[c2], rs_t[c2], STB[c2], eps)
            S.dma("sp", xout[c * 128:(c + 1) * 128, :], z[c2][:], reads=[Z[c2]], key="zg%d" % c2)
        S.flush()


class NsaLocal:
    def __init__(self, feat, tok, QT, gates):
        self.feat, self.tok, self.QT, self.gates = feat, tok, QT, gates


class NsaGathered:
    def __init__(self, feat, tok):
        self.feat, self.tok = feat, tok


NSA_COLS = 2608
HD = 64


def phase_n1(cx, xin, w_in_d, cos_d, sin_d, nl, NT):
    nc, S = cx.nc, cx.S
    TT, NS, KC = 512, 4, 8
    with ExitStack() as st:
        w = sb(cx, st, "wn", [128, KC, NSA_COLS], BF16)
        W = S.buf("wn")
        for k in range(KC):
            S.dma("pool", w[:, k, :], w_in_d[k * 128:(k + 1) * 128, :], writes=[W], key="wn")
        wqr = sb(cx, st, "wqr", [128, KC, 1024], BF16)
        wkr = sb(cx, st, "wkr", [128, KC, 512], BF16)
        WR = S.buf("wrot")

        def rot(dst, src):
            sv = src.rearrange("p k (b t f) -> p k b t f", t=2, f=32)
            dv = dst.rearrange("p k (b t f) -> p k b t f", t=2, f=32)
            for k in range(KC):
                S.op("dve", lambda k=k: nc.vector.tensor_scalar(out=dv[:, k, :, 0, :], in0=sv[:, k, :, 1, :], scalar1=-1.0,
                                                                scalar2=None, op0=ALU.mult), reads=[W, WR], writes=[WR])
                S.op("dve", lambda k=k: nc.vector.tensor_copy(out=dv[:, k, :, 1, :], in_=sv[:, k, :, 0, :]),
                     reads=[W, WR], writes=[WR])

        rot(wqr[:], w[:, :, 0:1024])
        rot(wkr[:, :, 0:256], w[:, :, 1536:1792])
        rot(wkr[:, :, 256:512], w[:, :, 2048:2304])
        ident, IB = make_ident(cx, st)
        xs = [sb(cx, st, "xs", [128, NS, D], F32) for _ in range(2)]
        XS = S.bufs("xs", 2)
        xbf = sb(cx, st, "xbf", [128, NS, D], BF16)
        XBF = S.buf("xbf")
        xT = [sb(cx, st, "xT", [128, KC, TT], BF16) for _ in range(2)]
        XT = S.bufs("xT", 2)
        cs = [sb(cx, st, "cs", [128, 2, TT], F32) for _ in range(2)]
        CS = S.bufs("cs", 2)
        t1 = [sb(cx, st, "t1", [128, TT], F32) for _ in range(2)]
        t2 = [sb(cx, st, "t2", [128, TT], F32) for _ in range(2)]
        T1 = S.bufs("t1", 2)
        T2 = S.bufs("t2", 2)
        og = [sb(cx, st, "og", [128, TT], BF16) for _ in range(2)]
        OG = S.bufs("og", 2)
        vt = [sb(cx, st, "vt", [128, 512], BF16) for _ in range(2)]
        VT = S.bufs("vt", 2)
        gt = [sb(cx, st, "gt", [128, 48], F32) for _ in range(2)]
        GT = S.bufs("gt", 2)
        tp = [ps(cx, st, "tp", [128, D], BF16) for _ in range(2)]
        TP = S.bufs("tp", 2)
        pa = [ps(cx, st, "pa", [128, 512], F32) for _ in range(2)]
        PA = S.bufs("pa", 2)
        pb = [ps(cx, st, "pb", [128, 512], F32) for _ in range(2)]
        PB = S.bufs("pb", 2)
        pt = [ps(cx, st, "pt", [128, 512], F32) for _ in range(2)]
        PT = S.bufs("pt", 2)
        ntile = NT // TT
        xin_v = xin.rearrange("(t s p) d -> t p s d", s=NS, p=128)

        def load(t):
            S.dma("sp", xs[t % 2][:], xin_v[t], writes=[XS[t % 2]], key="xs%d" % (t % 2))
            S.dma("sp", cs[t % 2][:, 0, :], cos_d[:, t * TT:(t + 1) * TT], writes=[CS[t % 2]], key="cs%d" % (t % 2))
            S.dma("sp", cs[t % 2][:, 1, :], sin_d[:, t * TT:(t + 1) * TT], writes=[CS[t % 2]], key="cs%d" % (t % 2))

        units = []
        qf = lambda h: nl.QT[:, h, :]
        for a in range(8):
            units.append((qf, a, w, wqr, a * 128, a * 128))
        for a in range(2):
            units.append(((lambda g: nl.feat("KsT", g)), a, w, wkr, 1536 + a * 128, a * 128))
        for a in range(2):
            units.append(((lambda g: nl.feat("KwT", g)), a, w, wkr, 2048 + a * 128, 256 + a * 128))
        kcf = lambda g: nl.feat("KcT", g)
        vcf = lambda g: nl.feat("VcT", g)
        plain = [(kcf, 0, 1024), (kcf, 1, 1152), (vcf, 0, 1280), (vcf, 1, 1408)]
        load(0)
        cnt = 0
        for t in range(ntile):
            b2 = t % 2
            if t + 1 < ntile:
                load(t + 1)
            emit_xT(cx, xs[b2], XS[b2], xbf, XBF, tp, TP, xT[b2], XT[b2], ident, IB, NS)
            tok = slice(t * TT, (t + 1) * TT)
            for (dst, di, wa, wb, ca, cb) in units:
                q = cnt % 2
                cnt += 1
                for k in range(KC):
                    S.op("pe", lambda k=k, q=q, wa=wa, ca=ca, b2=b2: nc.tensor.matmul(
                        pa[q][:], wa[:, k, ca:ca + 128], xT[b2][:, k, :], start=(k == 0), stop=(k == KC - 1)),
                        reads=[W, WR, XT[b2]], writes=[PA[q]])
                for k in range(KC):
                    S.op("pe", lambda k=k, q=q, wb=wb, cb=cb, b2=b2: nc.tensor.matmul(
                        pb[q][:], wb[:, k, cb:cb + 128], xT[b2][:, k, :], start=(k == 0), stop=(k == KC - 1)),
                        reads=[WR, XT[b2]], writes=[PB[q]])
                S.op("dve", lambda q=q, b2=b2: nc.vector.tensor_tensor(out=t1[q][:], in0=pa[q][:], in1=cs[b2][:, 0, :], op=ALU.mult),
                     reads=[PA[q], CS[b2]], writes=[T1[q]])
                S.op("dve", lambda q=q, b2=b2: nc.vector.tensor_tensor(out=t2[q][:], in0=pb[q][:], in1=cs[b2][:, 1, :], op=ALU.mult),
                     reads=[PB[q], CS[b2]], writes=[T2[q]])
                S.op("pool", lambda q=q: nc.gpsimd.tensor_tensor(out=og[q][:], in0=t1[q][:], in1=t2[q][:], op=ALU.add),
                     reads=[T1[q], T2[q]], writes=[OG[q]])
                for h in range(2):
                    S.dma("sp", dst(2 * di + h)[:, tok], og[q][64 * h:64 * h + 64, :], reads=[OG[q]], key="og%d" % q)
            for (dst, di, c0) in plain:
                q = cnt % 2
                cnt += 1
                for k in range(KC):
                    S.op("pe", lambda k=k, q=q, c0=c0, b2=b2: nc.tensor.matmul(
                        pa[q][:], w[:, k, c0:c0 + 128], xT[b2][:, k, :], start=(k == 0), stop=(k == KC - 1)),
                        reads=[W, XT[b2]], writes=[PA[q]])
                S.op("act", lambda q=q: nc.scalar.copy(out=og[q][:], in_=pa[q][:]), reads=[PA[q]], writes=[OG[q]])
                for h in range(2):
                    S.dma("sp", dst(2 * di + h)[:, tok], og[q][64 * h:64 * h + 64, :], reads=[OG[q]], key="og%d" % q)
            for s in range(NS):
                q = s % 2
                t0 = t * TT + s * 128
                rows = slice(t0, t0 + 128)
                for k in range(KC):
                    S.op("pe", lambda k=k, q=q, s=s, b2=b2: nc.tensor.matmul(
                        pt[q][:, 0:256], xT[b2][:, k, s * 128:(s + 1) * 128], w[:, k, 1792:2048],
                        start=(k == 0), stop=(k == KC - 1)), reads=[W, XT[b2]], writes=[PT[q]])
                S.op("act", lambda q=q: nc.scalar.copy(out=vt[q][:, 0:256], in_=pt[q][:, 0:256]), reads=[PT[q]], writes=[VT[q]])
                S.dma("sp", nl.tok("Vs", t0), vt[q][:, 0:256], reads=[VT[q]], key="vt%d" % q)
                for k in range(KC):
                    S.op("pe", lambda k=k, q=q, s=s, b2=b2: nc.tensor.matmul(
                        pt[q][:, 0:304], xT[b2][:, k, s * 128:(s + 1) * 128], w[:, k, 2304:2608],
                        start=(k == 0), stop=(k == KC - 1)), reads=[W, XT[b2]], writes=[PT[q]])
                S.op("act", lambda q=q: nc.scalar.copy(out=vt[q][:, 256:512], in_=pt[q][:, 0:256]), reads=[PT[q]], writes=[VT[q]])
                S.op("act", lambda q=q: nc.scalar.activation(out=gt[q][:], in_=pt[q][:, 256:304], func=AF.Sigmoid),
                     reads=[PT[q]], writes=[GT[q]])
                S.dma("sp", nl.tok("Vw", t0), vt[q][:, 256:512], reads=[VT[q]], key="vt%d" % q)
                S.dma("sp", nl.gates[rows, :], gt[q][:], reads=[GT[q]], key="gt%d" % q)
        S.flush()


NEGM = -30000.0
MTS = 2
PIPE_DEPTH = 3
DBG = 0


class StopBuild(Exception):
    pass


def dbg_stop(n):
    if DBG == n:
        raise StopBuild()


def phase_attn(cx, al, QT_d, gates_d, O_d, w1k_d, w2k_d, pek_d, w1v_d, w2v_d, pev_d, NB):
    nc, S = cx.nc, cx.S
    NT = NB * 128
    SEQ = 4 * NT
    KT = 4 * NB
    NCMP = SEQ // 16 - 1
    NCP = SEQ // 16
    with ExitStack() as st:
        ident, IB = make_ident(cx, st)
        identf = sb(cx, st, "identf", [128, 128], F32)
        S.dma("sp", identf[:], cx.ident_d[:, :], writes=[IB], key="identf")
        kcmpT = sb(cx, st, "kcmpT", [64, 4, NCP], BF16)
        vcmp = sb(cx, st, "vcmp", [128, NCP // 128, 4, 65], BF16)
        KCMP = S.buf("kcmp")
        VCMP = S.buf("vcmp")
        S.op("pool", lambda: nc.gpsimd.memset(vcmp[:], 0.0), writes=[VCMP])
        S.op("pool", lambda: nc.gpsimd.memset(kcmpT[:], 0.0), writes=[KCMP])
        with ExitStack() as st2:
            w1s = sb(cx, st2, "w1s", [64, 32, 256], BF16)
            w2d = sb(cx, st2, "w2d", [128, 2, 64], BF16)
            w2r = sb(cx, st2, "w2r", [128, 2, 64], BF16)
            peT = sb(cx, st2, "peT", [64, 32], BF16)
            kc = sb(cx, st2, "kc", [64, SEQ], BF16)
            hT = sb(cx, st2, "hTc", [128, 2, NCP], BF16)
            cbias = sb(cx, st2, "cbias", [128, 2], F32)
            ccs = sb(cx, st2, "ccs", [64, 2, NCP], F32)
            t1 = sb(cx, st2, "ct1", [128, 512], F32)
            t2 = sb(cx, st2, "ct2", [128, 512], F32)
            W1S, W2D, PET, KC, HT, CB, CCS, T1, T2 = [S.buf(n) for n in ("w1s", "w2d", "peT", "kc", "hTc", "cb", "ccs", "ct1", "ct2")]
            ph = [ps(cx, st2, "ph", [128, 512], F32) for _ in range(2)]
            PH = S.bufs("ph", 2)
            pc = ps(cx, st2, "pc", [128, 512], F32)
            PC = S.buf("pc")
            pA = ps(cx, st2, "pA", [128, 512], F32)
            pB = ps(cx, st2, "pB", [128, 512], F32)
            PA_, PB_ = S.buf("pA"), S.buf("pB")
            S.dma("sp", ccs[:, 0, :], cx.cosc_d[0:64, :], writes=[CCS], key="ccs")
            S.dma("sp", ccs[:, 1, :], cx.sinc_d[0:64, :], writes=[CCS], key="ccs")
            ntiles = [(n0, min(n0 + 512, NCMP)) for n0 in range(0, NCMP, 512)]
            for which in range(2):
                w1_d, w2_d, pe_d = (w1k_d, w2k_d, pek_d) if which == 0 else (w1v_d, w2v_d, pev_d)
                src_k = "KcT" if which == 0 else "VcT"
                S.dma("pool", w1s[:, :, :], w1_d.rearrange("l d h -> d l h"), writes=[W1S], key="w1s")
                S.dma("pool", peT[:, :], pe_d.rearrange("l d -> d l"), writes=[PET], key="peT", slow=True)
                for hc in range(2):
                    S.dma("pool", w2d[:, hc, :], w2_d[hc * 128:(hc + 1) * 128, :], writes=[W2D], key="w2d")
                if which == 0:
                    S.op("dve", lambda: nc.vector.tensor_scalar(out=w2r[:, :, 0:32], in0=w2d[:, :, 32:64], scalar1=-1.0,
                                                                scalar2=None, op0=ALU.mult), reads=[W2D], writes=[W2D])
                    S.op("dve", lambda: nc.vector.tensor_copy(out=w2r[:, :, 32:64], in_=w2d[:, :, 0:32]),
                         reads=[W2D], writes=[W2D])
                for hc in range(2):
                    for l in range(32):
                        S.op("pe", lambda hc=hc, l=l: nc.tensor.matmul(
                            pc[:, hc:hc + 1], w1s[:, l, hc * 128:(hc + 1) * 128], peT[:, l:l + 1],
                            start=(l == 0), stop=(l == 31)), reads=[W1S, PET], writes=[PC])
                S.op("dve", lambda: nc.vector.tensor_copy(out=cbias[:], in_=pc[:, 0:2]), reads=[PC], writes=[CB])
                for g in range(4):
                    for r_ in range(4):
                        S.dma("sp", kc[:, :].rearrange("p (i r q) -> p i r q", r=4, q=128)[:, :, r_, :],
                              al.feat(src_k, r_, g).rearrange("p (i q) -> p i q", q=128), writes=[KC], key="kc")
                    for hc in range(2):
                        for (n0, n1) in ntiles:
                            cnt = n1 - n0
                            q = (hc + (n0 // 512)) % 2
                            for l in range(32):
                                S.op("pe", lambda hc=hc, l=l, n0=n0, cnt=cnt, q=q: nc.tensor.matmul(
                                    ph[q][:, 0:cnt], w1s[:, l, hc * 128:(hc + 1) * 128],
                                    kc[:, 16 * n0 + l:16 * n0 + l + 16 * (cnt - 1) + 1:16],
                                    start=(l == 0), stop=(l == 31)), reads=[W1S, KC], writes=[PH[q]])
                            S.op("act", lambda hc=hc, n0=n0, cnt=cnt, q=q: nc.scalar.activation(
                                out=hT[:, hc, n0:n0 + cnt], in_=ph[q][:, 0:cnt], func=AF.Gelu_apprx_tanh,
                                bias=cbias[:, hc:hc + 1], scale=1.0), reads=[PH[q], CB], writes=[HT])
                    if which == 0:
                        for (n0, n1) in ntiles:
                            cnt = n1 - n0
                            for hc in range(2):
                                S.op("pe", lambda hc=hc, n0=n0, cnt=cnt: nc.tensor.matmul(
                                    pA[0:64, 0:cnt], w2d[:, hc, :], hT[:, hc, n0:n0 + cnt], start=(hc == 0), stop=(hc == 1)),
                                    reads=[W2D, HT], writes=[PA_])
                            for hc in range(2):
                                S.op("pe", lambda hc=hc, n0=n0, cnt=cnt: nc.tensor.matmul(
                                    pB[0:64, 0:cnt], w2r[:, hc, :], hT[:, hc, n0:n0 + cnt], start=(hc == 0), stop=(hc == 1)),
                                    reads=[W2D, HT], writes=[PB_])
                            S.op("dve", lambda n0=n0, cnt=cnt: nc.vector.tensor_tensor(
                                out=t1[0:64, 0:cnt], in0=pA[0:64, 0:cnt], in1=ccs[:, 0, n0:n0 + cnt], op=ALU.mult),
                                reads=[PA_, CCS], writes=[T1])
                            S.op("dve", lambda n0=n0, cnt=cnt: nc.vector.tensor_tensor(
                                out=t2[0:64, 0:cnt], in0=pB[0:64, 0:cnt], in1=ccs[:, 1, n0:n0 + cnt], op=ALU.mult),
                                reads=[PB_, CCS], writes=[T2])
                            S.op("pool", lambda n0=n0, cnt=cnt, g=g: nc.gpsimd.tensor_tensor(
                                out=kcmpT[:, g, n0:n0 + cnt], in0=t1[0:64, 0:cnt], in1=t2[0:64, 0:cnt], op=ALU.add),
                                reads=[T1, T2], writes=[KCMP])
                    else:
                        for nt in range((NCMP + 127) // 128):
                            c0 = nt * 128
                            cnt = min(128, NCMP - c0)
                            for hc in range(2):
                                S.op("pe", lambda hc=hc, c0=c0, cnt=cnt: nc.tensor.matmul(
                                    pA[0:cnt, 0:64], hT[:, hc, c0:c0 + cnt], w2d[:, hc, 0:64], start=(hc == 0), stop=(hc == 1)),
                                    reads=[W2D, HT], writes=[PA_])
                            S.op("act", lambda nt=nt, cnt=cnt, g=g: nc.scalar.copy(out=vcmp[0:cnt, nt, g, 0:64], in_=pA[0:cnt, 0:64]),
                                 reads=[PA_], writes=[VCMP])
            S.op("pool", lambda: nc.gpsimd.memset(vcmp[:, :, :, 64:65], 1.0), reads=[VCMP], writes=[VCMP])
            S.flush()
        if DBG == 1:
            return
        ks = sb(cx, st, "ks", [64, SEQ], BF16)
        kw = sb(cx, st, "kw", [64, SEQ], BF16)
        vs = sb(cx, st, "vs", [128, KT, 65], BF16)
        vw = sb(cx, st, "vw", [128, KT, 65], BF16)
        KS, KW, VS, VW = S.buf("ks"), S.buf("kw"), S.buf("vs"), S.buf("vw")
        S.op("pool", lambda: nc.gpsimd.memset(vs[:, :, 64:65], 1.0), writes=[VS])
        S.op("pool", lambda: nc.gpsimd.memset(vw[:, :, 64:65], 1.0), writes=[VW])
        cmpmask = sb(cx, st, "cmpmask", [128, 33], BF16)
        keep = sb(cx, st, "keep", [128, 9], F32)
        addt = sb(cx, st, "addt", [128, 9], F32)
        caus = sb(cx, st, "caus", [128, 4, 128], BF16)
        wmask = sb(cx, st, "wmask", [128, 8, 128], BF16)
        TB = S.buf("tables")
        S.dma("pool", cmpmask[:], cx.cmpmask_d[:, :], writes=[TB], key="tb")
        S.dma("sp", keep[:], cx.keep_d[:, :], writes=[TB], key="tb")
        S.dma("sp", addt[:], cx.addt_d[:, :], writes=[TB], key="tb")
        S.dma("pool", caus[:], cx.caus_d[:, :, :], writes=[TB], key="tb")
        S.dma("pool", wmask[:], cx.wmask_d[:, :, :], writes=[TB], key="tb")
        qt = [sb(cx, st, "qt", [64, 4, 128], BF16) for _ in range(2)]
        QTB = S.bufs("qt", 2)
        gts = [sb(cx, st, "gts", [128, 12], F32) for _ in range(2)]
        GTS = S.bufs("gts", 2)
        Pf = [sb(cx, st, "Pf", [128, NCP], F32) for _ in range(2)]
        PF = S.bufs("Pf", 2)
        Pb = [sb(cx, st, "Pb", [128, NCP], BF16) for _ in range(4)]
        PBB = S.bufs("Pb", 4)
        for cb in range(4):
            S.op("pool", lambda cb=cb: nc.gpsimd.memset(Pb[cb][:], 0.0), writes=[PBB[cb]])
        PTs = [sb(cx, st, "PTs", [128, 512], BF16) for _ in range(2)]
        PTS = S.bufs("PTs", 2)
        pns = sb(cx, st, "pns", [128, NCP + 8], F32)
        PNS = S.buf("pns")
        S.op("pool", lambda: nc.gpsimd.memset(pns[:], 0.0), writes=[PNS])
        NMX = max(8 * NB, 16)
        imp = sb(cx, st, "imp", [128, NMX], F32)
        imp4 = sb(cx, st, "imp4", [128, NMX], F32)
        work = sb(cx, st, "work", [128, NMX], F32)
        msk = sb(cx, st, "msk", [128, NMX], BF16)
        m8a = sb(cx, st, "m8a", [128, 8], F32)
        m8b = sb(cx, st, "m8b", [128, 8], F32)
        thr = sb(cx, st, "thr", [128, 1], F32)
        IMP, MSK = S.buf("imp"), S.buf("msk")
        mskx = sb(cx, st, "mskx", [128, NMX * 64], BF16)
        MSKX = S.buf("mskx")
        smx = sb(cx, st, "smx", [128, 4], F32)
        snb = sb(cx, st, "snb", [128, 1], F32)
        rsum = sb(cx, st, "rsum", [128, 4], F32)
        rinv = sb(cx, st, "rinv", [128, 1], F32)
        SMX = S.buf("smx")
        E = [sb(cx, st, "E", [128, 512], BF16) for _ in range(4)]
        EB = S.bufs("E", 4)
        PM = [sb(cx, st, "PM", [128, 512], BF16) for _ in range(4)]
        PMB = S.bufs("PM", 4)
        cm = [sb(cx, st, "cm", [128, 128], BF16) for _ in range(4)]
        CM = S.bufs("cm", 4)
        OT = [sb(cx, st, "OT", [65, 512], F32) for _ in range(3)]
        OTB = S.bufs("OT", 3)
        rs4 = sb(cx, st, "rs4", [128, 4], F32)
        ri4 = sb(cx, st, "ri4", [128, 4], F32)
        fac = sb(cx, st, "fac", [128, 4], F32)
        tmp = sb(cx, st, "tmp", [128, 4, 64], F32)
        oacc = sb(cx, st, "oacc", [128, 4, 64], F32)
        ob = [sb(cx, st, "ob", [128, 4, 64], BF16) for _ in range(2)]
        CMB, OBB = S.buf("comb"), S.bufs("ob", 2)
        bank = [ps(cx, st, "bk", [128, 512], F32) for _ in range(8)]
        BK = S.bufs("bk", 8)
        SS = [0, 1, 7, 3]
        B_PT, B_OC, B_OS, B_OW, B_CT = 2, 3, 4, 5, 6
        ptps = bank[B_PT][:].bitcast(BF16)
        MT = [BK[B_PT], BK[B_CT]]
        ctps = bank[B_CT][:].bitcast(BF16)
        ecnt = [0]

        def job_front(j):
            e = ecnt[0] % 4
            ecnt[0] += 1
            j["e"] = e
            sb_ = SS[e]
            kT, KB, kt, q2 = j["kT"], j["KB"], j["kt"], j["q2"]
            if j["pre"] is not None:
                j["pre"]()
            S.op("pe", lambda sb_=sb_, kt=kt, q2=q2, kT=kT: nc.tensor.matmul(
                bank[sb_][:, :], kT[:, kt * 128:(kt + 1) * 128], qt[q2][:, :, :].rearrange("p h q -> p (h q)"),
                start=True, stop=True), reads=[KB, QTB[q2]], writes=[BK[sb_]])
            S.op("act", lambda e=e, sb_=sb_: nc.scalar.activation(out=E[e][:], in_=bank[sb_][:], func=AF.Exp, scale=0.125),
                 reads=[BK[sb_]], writes=[EB[e]])
            mfn = j["mask"]
            S.op("dve", lambda e=e, mfn=mfn: nc.vector.tensor_tensor(
                out=PM[e][:].rearrange("p (c q) -> p c q", c=4), in0=E[e][:].rearrange("p (c q) -> p c q", c=4),
                in1=mfn().unsqueeze(1).broadcast_to([128, 4, 128]), op=ALU.mult),
                reads=[EB[e]] + j["mbufs"], writes=[PMB[e]])

        def job_back(j):
            e, kt, vT, VB, o_bank, first, last = j["e"], j["kt"], j["vT"], j["VB"], j["o_bank"], j["first"], j["last"]
            S.op("pe", lambda e=e, kt=kt, first=first, last=last, vT=vT, o_bank=o_bank: nc.tensor.matmul(
                bank[o_bank][0:65, :], vT[:, kt, :], PM[e][:], start=first, stop=last),
                reads=[VB, PMB[e]], writes=[BK[o_bank]])

        def run_jobs(jobs, depth=None):
            depth = PIPE_DEPTH if depth is None else depth
            n = len(jobs)
            for step in range(n + depth):
                if step < n:
                    job_front(jobs[step])
                if step >= depth:
                    job_back(jobs[step - depth])

        try:
            ucnt = 0
            for g in range(4):
                for r_ in range(4):
                    S.dma("sp", ks[:].rearrange("p (i r q) -> p i r q", r=4, q=128)[:, :, r_, :],
                          al.feat("KsT", r_, g).rearrange("p (i q) -> p i q", q=128), writes=[KS], key="ks")
                    S.dma("sp", kw[:].rearrange("p (i r q) -> p i r q", r=4, q=128)[:, :, r_, :],
                          al.feat("KwT", r_, g).rearrange("p (i q) -> p i q", q=128), writes=[KW], key="kw")
                    for hf in range(2):
                        isl = slice(hf * NB // 2, (hf + 1) * NB // 2)
                        S.dma("sp", vs[:, :, 0:64].rearrange("q (i r) d -> q i r d", r=4)[:, isl, r_, :],
                              al.tok("Vs", r_, hf)[:, 64 * g:64 * g + 64].rearrange("(i q) d -> q i d", q=128),
                              writes=[VS], key="vs")
                        S.dma("sp", vw[:, :, 0:64].rearrange("q (i r) d -> q i r d", r=4)[:, isl, r_, :],
                              al.tok("Vw", r_, hf)[:, 64 * g:64 * g + 64].rearrange("(i q) d -> q i d", q=128),
                              writes=[VW], key="vw")
                if DBG == 2:
                    S.flush()
                    return
                for i in range(NB):
                    q2 = ucnt % 2
                    ucnt += 1
                    S.dma("sp", qt[q2][:], QT_d[:, 4 * g:4 * g + 4, i * 128:(i + 1) * 128], writes=[QTB[q2]], key="qt%d" % q2)
                    S.dma("sp", gts[q2][:], gates_d[i * 128:(i + 1) * 128, 12 * g:12 * g + 12], writes=[GTS[q2]], key="gts%d" % q2)
                    ncols = 32 * i + 31
                    ctiles = [(c0, min(c0 + 512, ncols)) for c0 in range(0, ncols, 512)]
                    m0 = ncols - 33
                    for cb in range(4):
                        pf = Pf[cb % 2]
                        PFB = PF[cb % 2]
                        for ci, (c0, c1) in enumerate(ctiles):
                            ov0, ov1 = max(c0, m0), c1
                            has_mask = ov1 > ov0
                            S.op("pe", lambda cb=cb, ci=ci, c0=c0, c1=c1, q2=q2, has_mask=has_mask, g=g: nc.tensor.matmul(
                                bank[ci][:, 0:c1 - c0], qt[q2][:, cb, :], kcmpT[:, g, c0:c1],
                                start=True, stop=(not has_mask)), reads=[QTB[q2], KCMP], writes=[BK[ci]])
                            if has_mask:
                                S.op("pe", lambda ci=ci, c0=c0, ov0=ov0, ov1=ov1, m0=m0: nc.tensor.matmul(
                                    bank[ci][:, ov0 - c0:ov1 - c0], ident[:], cmpmask[:, ov0 - m0:ov1 - m0], start=False, stop=True),
                                    reads=[IB, TB], writes=[BK[ci]])
                            S.op("dve", lambda ci=ci, c0=c0, c1=c1: nc.vector.reduce_max(
                                out=smx[:, ci:ci + 1], in_=bank[ci][:, 0:c1 - c0], axis=AX.X), reads=[BK[ci]], writes=[SMX])
                        if len(ctiles) == 2:
                            S.op("dve", lambda: nc.vector.tensor_tensor(out=smx[:, 0:1], in0=smx[:, 0:1], in1=smx[:, 1:2], op=ALU.max),
                                 reads=[SMX], writes=[SMX])
                        S.op("dve", lambda: nc.vector.tensor_scalar(out=snb[:], in0=smx[:, 0:1], scalar1=-1000.0, scalar2=-0.125,
                                                                    op0=ALU.max, op1=ALU.mult), reads=[SMX], writes=[SMX])
                        S.op("dve", lambda: nc.vector.memset(rsum[:, 2:4], 0.0), reads=[SMX], writes=[SMX])
                        for ci, (c0, c1) in enumerate(ctiles):
                            S.op("act", lambda ci=ci, c0=c0, c1=c1, pf=pf: nc.scalar.activation(
                                out=pf[:, c0:c1], in_=bank[ci][:, 0:c1 - c0], func=AF.Exp, bias=snb[:, 0:1], scale=0.125,
                                accum_out=rsum[:, 2 + ci:3 + ci]), reads=[BK[ci], SMX], writes=[PFB, SMX])
                        if len(ctiles) == 2:
                            S.op("dve", lambda: nc.vector.tensor_tensor(out=rsum[:, 2:3], in0=rsum[:, 2:3], in1=rsum[:, 3:4], op=ALU.add),
                                 reads=[SMX], writes=[SMX])
                        S.op("dve", lambda: nc.vector.tensor_scalar(out=rsum[:, 0:1], in0=rsum[:, 2:3], scalar1=1e-30, scalar2=None,
                                                                    op0=ALU.max), reads=[SMX], writes=[SMX])
                        S.op("dve", lambda: nc.vector.reciprocal(out=rinv[:], in_=rsum[:, 0:1]), reads=[SMX], writes=[SMX])
                        if cb == 0:
                            S.op("dve", lambda pf=pf, ncols=ncols: nc.vector.tensor_scalar(
                                out=pns[:, 1:1 + ncols], in0=pf[:, 0:ncols], scalar1=rinv[:, 0:1], scalar2=None, op0=ALU.mult),
                                reads=[PFB, SMX], writes=[PNS])
                        else:
                            S.op("dve", lambda pf=pf, ncols=ncols: nc.vector.scalar_tensor_tensor(
                                out=pns[:, 1:1 + ncols], in0=pf[:, 0:ncols], scalar=rinv[:, 0:1], in1=pns[:, 1:1 + ncols],
                                op0=ALU.mult, op1=ALU.add), reads=[PFB, SMX, PNS], writes=[PNS])
                        S.op("pool", lambda cb=cb, pf=pf, ncols=ncols: nc.gpsimd.tensor_copy(out=Pb[cb][:, 0:ncols], in_=pf[:, 0:ncols]),
                             reads=[PFB], writes=[PBB[cb]])
                        if g > 0 and i == 0:
                            S.op("pool", lambda cb=cb, ncols=ncols: nc.gpsimd.memset(Pb[cb][:, ncols:NCP], 0.0), writes=[PBB[cb]])
                    if g > 0 and i == 0:
                        S.op("pool", lambda ncols=ncols: nc.gpsimd.memset(pns[:, 1 + ncols:NCP + 8], 0.0), reads=[PNS], writes=[PNS])
                    nnt = (ncols + 127) // 128
                    for nt in range(nnt):
                        for cb in range(4):
                            S.op("pe", lambda nt=nt, cb=cb: nc.tensor.transpose(
                                out=ptps[:, cb * 128:(cb + 1) * 128], in_=Pb[cb][:, nt * 128:(nt + 1) * 128], identity=ident[:]),
                                reads=[PBB[cb], IB], writes=[BK[B_PT]])
                        S.op("act", lambda nt=nt: nc.scalar.copy(out=PTs[nt % 2][:], in_=ptps[:, 0:512]),
                             reads=[BK[B_PT]], writes=[PTS[nt % 2]])
                        S.op("pe", lambda nt=nt, nnt=nnt, g=g: nc.tensor.matmul(
                            bank[B_OC][0:65, :], vcmp[:, nt, g, :], PTs[nt % 2][:], start=(nt == 0), stop=(nt == nnt - 1)),
                            reads=[VCMP, PTS[nt % 2]], writes=[BK[B_OC]])
                    S.op("act", lambda: nc.scalar.copy(out=OT[0][:], in_=bank[B_OC][0:65, :]), reads=[BK[B_OC]], writes=[OTB[0]])
                    if DBG == 3:
                        S.flush()
                        return
                    nm = 8 * i + 8
                    Wd = max(nm, 16)
                    S.op("dve", lambda nm=nm: nc.vector.tensor_reduce(
                        out=imp4[:, 0:nm], in_=pns[:, 0:4 * nm].rearrange("p (m r) -> p m r", r=4), axis=AX.X, op=ALU.add),
                        reads=[PNS], writes=[IMP])
                    S.op("dve", lambda nm=nm: nc.vector.tensor_tensor(
                        out=imp[:, 0:nm], in0=imp4[:, 0:nm], in1=pns[:, 4:4 * nm + 1:4], op=ALU.add), reads=[PNS, IMP], writes=[IMP])
                    r0 = max(8 * i - 1, 0)
                    tcol = r0 - (8 * i - 1)
                    S.op("dve", lambda r0=r0, nm=nm, tcol=tcol: nc.vector.tensor_tensor(
                        out=imp[:, r0:nm], in0=imp[:, r0:nm], in1=keep[:, tcol:9], op=ALU.mult), reads=[IMP, TB], writes=[IMP])
                    S.op("dve", lambda r0=r0, nm=nm, tcol=tcol: nc.vector.tensor_tensor(
                        out=imp[:, r0:nm], in0=imp[:, r0:nm], in1=addt[:, tcol:9], op=ALU.add), reads=[IMP, TB], writes=[IMP])
                    S.op("dve", lambda: nc.vector.memset(imp[:, 0:1], 3e9), reads=[IMP], writes=[IMP])
                    if nm < 16:
                        S.op("dve", lambda nm=nm: nc.vector.memset(imp[:, nm:16], -1e9), reads=[IMP], writes=[IMP])
                    S.op("dve", lambda Wd=Wd: nc.vector.max(out=m8a[:], in_=imp[:, 0:Wd]), reads=[IMP], writes=[IMP])
                    S.op("dve", lambda Wd=Wd: nc.vector.match_replace(out=work[:, 0:Wd], in_to_replace=m8a[:], in_values=imp[:, 0:Wd],
                                                                        imm_value=-3e38), reads=[IMP], writes=[IMP])
                    S.op("dve", lambda Wd=Wd: nc.vector.max(out=m8b[:], in_=work[:, 0:Wd]), reads=[IMP], writes=[IMP])
                    S.op("dve", lambda: nc.vector.tensor_reduce(out=thr[:], in_=m8b[:], axis=AX.X, op=ALU.min), reads=[IMP], writes=[IMP])
                    S.op("dve", lambda Wd=Wd: nc.vector.tensor_scalar(out=msk[:, 0:Wd], in0=imp[:, 0:Wd], scalar1=thr[:, 0:1], scalar2=None,
                                                                       op0=ALU.is_ge), reads=[IMP], writes=[MSK])
                    if DBG == 4:
                        S.flush()
                        return
                    nkt = 4 * i + 4
                    S.op("pool", lambda nm=nm: nc.gpsimd.tensor_copy(
                        out=mskx[:, 0:nm * 64].rearrange("q (m u) -> q m u", u=64),
                        in_=msk[:, 0:nm].unsqueeze(2).broadcast_to([128, nm, 64])), reads=[MSK], writes=[MSKX])
                    jobs = []
                    for kt in range(nkt):
                        ms = kt % 2
                        mt_ap = ptps[:, 512:640] if ms == 0 else ctps[:, 0:128]

                        def pre(kt=kt, ms=ms, mt_ap=mt_ap, i=i):
                            S.op("pe", lambda: nc.tensor.transpose(out=mt_ap, in_=mskx[:, kt * 128:(kt + 1) * 128], identity=ident[:]),
                                 reads=[MSKX, IB], writes=[MT[ms]])
                            if kt >= 4 * i:
                                S.op("dve", lambda: nc.vector.tensor_tensor(out=cm[ms][:], in0=mt_ap, in1=caus[:, kt - 4 * i, :],
                                                                            op=ALU.mult), reads=[MT[ms], TB], writes=[CM[ms]])
                        if kt >= 4 * i:
                            mfn, mb = (lambda ms=ms: cm[ms][:]), [CM[ms]]
                        else:
                            mfn, mb = (lambda mt_ap=mt_ap: mt_ap), [MT[ms]]
                        jobs.append(dict(kT=ks, vT=vs, KB=KS, VB=VS, kt=kt, q2=q2, pre=pre, mask=mfn, mbufs=mb,
                                         o_bank=B_OS, first=(kt == 0), last=(kt == nkt - 1)))
                    wl = [kr for kr in range(-4, 4) if 4 * i + kr >= 0]
                    for idx, kr in enumerate(wl):
                        jobs.append(dict(kT=kw, vT=vw, KB=KW, VB=VW, kt=4 * i + kr, q2=q2, pre=None,
                                         mask=(lambda kr=kr: wmask[:, kr + 4, :]), mbufs=[TB],
                                         o_bank=B_OW, first=(idx == 0), last=(idx == len(wl) - 1)))
                    run_jobs(jobs)
                    if DBG == 6:
                        S.flush()
                        return
                    gv = gts[q2][:].rearrange("q (h t) -> q h t", t=3)
                    for br, ob_ in enumerate((B_OC, B_OS, B_OW)):
                        if br > 0:
                            S.op("act", lambda br=br, ob_=ob_: nc.scalar.copy(out=OT[br][:], in_=bank[ob_][0:65, :]),
                                 reads=[BK[ob_]], writes=[OTB[br]])
                        for cb in range(4):
                            S.op("pe", lambda br=br, cb=cb: nc.tensor.transpose(
                                out=bank[B_CT][:, cb * 65:(cb + 1) * 65], in_=OT[br][0:65, cb * 128:(cb + 1) * 128],
                                identity=identf[0:65, 0:65]), reads=[OTB[br], IB], writes=[BK[B_CT]])
                        ctv = bank[B_CT][:, 0:260].rearrange("q (c e) -> q c e", e=65)
                        S.op("dve", lambda ctv=ctv: nc.vector.tensor_scalar(out=rs4[:], in0=ctv[:, :, 64], scalar1=1e-30, scalar2=None,
                                                                            op0=ALU.max), reads=[BK[B_CT]], writes=[CMB])
                        S.op("dve", lambda: nc.vector.reciprocal(out=ri4[:], in_=rs4[:]), reads=[CMB], writes=[CMB])
                        S.op("dve", lambda br=br, gv=gv: nc.vector.tensor_tensor(
                            out=fac[:], in0=ri4[:], in1=gv[:, :, br], op=ALU.mult), reads=[CMB, GTS[q2]], writes=[CMB])
                        dst = oacc if br == 0 else tmp
                        S.op("dve", lambda ctv=ctv, dst=dst: nc.vector.tensor_tensor(
                            out=dst[:], in0=ctv[:, :, 0:64], in1=fac[:].unsqueeze(2).broadcast_to([128, 4, 64]), op=ALU.mult),
                            reads=[BK[B_CT], CMB], writes=[CMB])
                        if br > 0:
                            S.op("dve", lambda: nc.vector.tensor_tensor(out=oacc[:], in0=oacc[:], in1=tmp[:], op=ALU.add),
                                 reads=[CMB], writes=[CMB])
                    S.op("pool", lambda q2=q2: nc.gpsimd.tensor_copy(out=ob[q2][:], in_=oacc[:]), reads=[CMB], writes=[OBB[q2]])
                    S.dma("sp", O_d[i * 128:(i + 1) * 128, 256 * g:256 * (g + 1)], ob[q2][:].rearrange("q h d -> q (h d)"),
                          reads=[OBB[q2]], key="ob%d" % q2)
        except StopBuild:
            pass
        S.flush()


def phase_proj(cx, a_d, xin, xout, w_d, g_d, b_d, NT, c_res):
    nc, S = cx.nc, cx.S
    KC = 8
    eps = LN_EPS / (ALPHA * ALPHA)
    with ExitStack() as st:
        wo = sb(cx, st, "wo", [128, KC, D], BF16)
        WO = S.buf("wo")
        for k in range(KC):
            S.dma("pool", wo[:, k, :], w_d[k * 128:(k + 1) * 128, :], writes=[WO], key="wo")
        gb, GB = load_gb(cx, st, g_d, b_d, "gbp")
        ident, IB = make_ident(cx, st)
        at = [sb(cx, st, "at", [128, D], BF16) for _ in range(2)]
        AT = S.bufs("at", 2)
        xs = [sb(cx, st, "xs", [128, D], F32) for _ in range(2)]
        XS = S.bufs("xs", 2)
        aT = [sb(cx, st, "aT", [128, KC, 128], BF16) for _ in range(2)]
        ATT = S.bufs("aT", 2)
        z = [sb(cx, st, "z", [128, D], F32) for _ in range(2)]
        Z = S.bufs("z", 2)
        st_t = [sb(cx, st, "st", [128, 2, 6], F32) for _ in range(2)]
        mv_t = [sb(cx, st, "mv", [128, 2], F32) for _ in range(2)]
        sd_t = [sb(cx, st, "sd", [128, 1], F32) for _ in range(2)]
        rs_t = [sb(cx, st, "rs", [128, 1], F32) for _ in range(2)]
        STB = S.bufs("stb", 2)
        tp = [ps(cx, st, "tp", [128, D], BF16) for _ in range(2)]
        TP = S.bufs("tp", 2)
        op_ = [ps(cx, st, "o", [128, 512], F32) for _ in range(4)]
        OP = S.bufs("o", 4)
        nchunk = NT // 128

        def load(c):
            S.dma("sp", at[c % 2][:], a_d[c * 128:(c + 1) * 128, :], writes=[AT[c % 2]], key="at%d" % (c % 2))
            S.dma("sp", xs[c % 2][:], xin[c * 128:(c + 1) * 128, :], writes=[XS[c % 2]], key="xsp%d" % (c % 2))

        load(0)
        for c in range(nchunk):
            c2 = c % 2
            if c + 1 < nchunk:
                load(c + 1)
            for k in range(KC):
                S.op("pe", lambda k=k, c2=c2: nc.tensor.transpose(out=tp[c2][:, k * 128:(k + 1) * 128],
                                                                   in_=at[c2][:, k * 128:(k + 1) * 128], identity=ident[:]),
                     reads=[AT[c2], IB], writes=[TP[c2]])
            S.op("act", lambda c2=c2: nc.scalar.copy(out=aT[c2][:], in_=tp[c2][:].rearrange("p (k t) -> p k t", k=KC)),
                 reads=[TP[c2]], writes=[ATT[c2]])
            for n in range(2):
                o_ = 2 * c2 + n
                for k in range(KC):
                    S.op("pe", lambda n=n, k=k, c2=c2, o_=o_: nc.tensor.matmul(
                        op_[o_][:], aT[c2][:, k, :], wo[:, k, n * 512:(n + 1) * 512], start=(k == 0), stop=(k == KC - 1)),
                        reads=[ATT[c2], WO], writes=[OP[o_]])
                S.op("dve", lambda n=n, c2=c2, o_=o_: nc.vector.scalar_tensor_tensor(
                    out=z[c2][:, n * 512:(n + 1) * 512], in0=op_[o_][:], scalar=c_res, in1=xs[c2][:, n * 512:(n + 1) * 512],
                    op0=ALU.mult, op1=ALU.add), reads=[OP[o_], XS[c2]], writes=[Z[c2]])
            ln_epilogue(cx, z[c2], Z[c2], gb, GB, st_t[c2], mv_t[c2], sd_t[c2], rs_t[c2], STB[c2], eps)
            S.dma("sp", xout[c * 128:(c + 1) * 128, :], z[c2][:], reads=[Z[c2]], key="zp%d" % c2)
        S.flush()


def build_test_ffn(NT):
    nc = bass.Bass("TRN2", target_bir_lowering=False)
    with ExitStack() as stack:
        cx = Ctx(nc, stack)
        x = nc.dram_tensor("x", [NT, D], F32, kind="ExternalInput").ap()
        w_in = nc.dram_tensor("w_in", [D, 2 * DFF], F32, kind="ExternalInput").ap()
        w_out = nc.dram_tensor("w_out", [DFF, D], F32, kind="ExternalInput").ap()
        g = nc.dram_tensor("g", [D], F32, kind="ExternalInput").ap()
        b = nc.dram_tensor("b", [D], F32, kind="ExternalInput").ap()
        cx.ident_d = nc.dram_tensor("ident", [128, 128], F32, kind="ExternalInput").ap()
        y = nc.dram_tensor("y", [NT, D], F32, kind="ExternalOutput").ap()
        phase_ffn(cx, x, y, w_in, w_out, g, b, NT, "f")
        print("instructions:", cx.S.n_inst)
    return nc


def build_test_gmlp(NT):
    nc = bass.Bass("TRN2", target_bir_lowering=False)
    with ExitStack() as stack:
        cx = Ctx(nc, stack)
        x = nc.dram_tensor("x", [NT, D], F32, kind="ExternalInput").ap()
        w_in = nc.dram_tensor("w_in", [D, 2 * GW], F32, kind="ExternalInput").ap()
        lng = nc.dram_tensor("lng", [GW], F32, kind="ExternalInput").ap()
        lnb = nc.dram_tensor("lnb", [GW], F32, kind="ExternalInput").ap()
        ws = nc.dram_tensor("ws", [16, 128, 128], F32, kind="ExternalInput").ap()
        bs = nc.dram_tensor("bs", [16, 128], F32, kind="ExternalInput").ap()
        w_out = nc.dram_tensor("w_out", [GW, D], F32, kind="ExternalInput").ap()
        g = nc.dram_tensor("g", [D], F32, kind="ExternalInput").ap()
        b = nc.dram_tensor("b", [D], F32, kind="ExternalInput").ap()
        cx.ident_d = nc.dram_tensor("ident", [128, 128], F32, kind="ExternalInput").ap()
        cx.tril_d = nc.dram_tensor("tril", [128, 128], F32, kind="ExternalInput").ap()
        uT_d = nc.dram_tensor("uT_d", [GW, NT], BF16, kind="Internal").ap()
        vln_d = nc.dram_tensor("vln_d", [NT, GW], BF16, kind="Internal").ap()
        y = nc.dram_tensor("y", [NT, D], F32, kind="ExternalOutput").ap()
        phase_g1(cx, x, uT_d, vln_d, w_in, lng, lnb, NT)
        phase_g2(cx, x, y, uT_d, vln_d, ws, bs, w_out, g, b, NT)
        print("instructions:", cx.S.n_inst)
    return nc


def nsa_dram(nc, NT, kind_local="Internal"):
    d = {}
    d["QT"] = nc.dram_tensor("QT_d", [64, 16, NT], BF16, kind=kind_local).ap()
    d["KsT"] = nc.dram_tensor("KsT_d", [64, 4, NT], BF16, kind=kind_local).ap()
    d["KwT"] = nc.dram_tensor("KwT_d", [64, 4, NT], BF16, kind=kind_local).ap()
    d["KcT"] = nc.dram_tensor("KcT_d", [64, 4, NT], BF16, kind=kind_local).ap()
    d["VcT"] = nc.dram_tensor("VcT_d", [64, 4, NT], BF16, kind=kind_local).ap()
    d["Vs"] = nc.dram_tensor("Vs_d", [NT, 256], BF16, kind=kind_local).ap()
    d["Vw"] = nc.dram_tensor("Vw_d", [NT, 256], BF16, kind=kind_local).ap()
    d["gates"] = nc.dram_tensor("gates_d", [NT, 48], F32, kind=kind_local).ap()
    return d


def local_from_dict(d):
    return NsaLocal(lambda k, g: d[k][:, g, :], lambda k, t0: d[k][t0:t0 + 128, :], d["QT"], d["gates"])


def gathered_from_dict(al, NT):
    return NsaGathered(lambda k, r, g: al[k][r][:, g, :], lambda k, r, hf: al[k][r][hf * NT // 2:(hf + 1) * NT // 2, :])


def build_test_n1(NT):
    nc = bass.Bass("TRN2", target_bir_lowering=False)
    with ExitStack() as stack:
        cx = Ctx(nc, stack)
        x = nc.dram_tensor("x", [NT, D], F32, kind="ExternalInput").ap()
        w_in = nc.dram_tensor("w_in", [D, NSA_COLS], F32, kind="ExternalInput").ap()
        cos_d = nc.dram_tensor("cos", [128, NT], F32, kind="ExternalInput").ap()
        sin_d = nc.dram_tensor("sin", [128, NT], F32, kind="ExternalInput").ap()
        cx.ident_d = nc.dram_tensor("ident", [128, 128], F32, kind="ExternalInput").ap()
        d = nsa_dram(nc, NT, "ExternalOutput")
        phase_n1(cx, x, w_in, cos_d, sin_d, local_from_dict(d), NT)
        print("instructions:", cx.S.n_inst)
    return nc


def rope_tables(pos):
    half = HD // 2
    freq = (10000.0 ** (-np.arange(half, dtype=np.float32) / half)).astype(np.float32)
    ang = pos.astype(np.float32)[None, :] * freq[:, None]
    c = np.cos(ang).astype(np.float32)
    s_ = np.sin(ang).astype(np.float32)
    return np.ascontiguousarray(np.tile(c, (4, 1))), np.ascontiguousarray(np.tile(s_, (4, 1)))


def attn_tables(r, NB):
    NT = NB * 128
    SEQ = 4 * NT
    NCP = SEQ // 16
    q = np.arange(128)
    t = {}
    pos = ((4 * np.arange(NB)[:, None] + r) * 128 + q[None, :]).reshape(-1)
    t["cos"], t["sin"] = rope_tables(pos)
    cpos = 16 * np.arange(NCP) + 31
    t["cosc"], t["sinc"] = rope_tables(cpos)
    m = np.arange(-2, 31)
    vis = m[None, :] <= (8 * r + np.floor((q[:, None] - 31) / 16.0))
    t["cmpmask"] = np.where(vis, 0.0, NEGM).astype(np.float32)
    mrel = np.arange(-1, 8)[None, :]
    cur = (2 * r + (q >= 64).astype(np.int64))[:, None]
    keep = np.ones((128, 9), np.float32)
    addt = np.zeros((128, 9), np.float32)
    fut = mrel > cur
    keep[fut] = 0.0
    addt[fut] = -1e9
    c1 = mrel == cur - 1
    keep[c1] = 0.0
    addt[c1] = 1e9
    c0 = mrel == cur
    keep[c0] = 0.0
    addt[c0] = 2e9
    t["keep"], t["addt"] = keep, addt
    k = np.arange(128)[:, None, None]
    kr = np.arange(4)[None, :, None]
    qq = q[None, None, :]
    t["caus"] = ((128 * (kr - r) + k) <= qq).astype(np.float32)
    kr8 = np.arange(-4, 4)[None, :, None]
    dl = r - kr8
    wm = np.where(dl == 0, k <= qq, np.where((dl >= 1) & (dl <= 3), True, np.where(dl == 4, k > qq, False)))
    t["wmask"] = np.broadcast_to(wm, (128, 8, 128)).astype(np.float32)
    t["ident"] = np.eye(128, dtype=np.float32)
    t["tril"] = np.tril(np.ones((128, 128), np.float32))
    return t


def declare_tables(cx, nc, NB):
    NT = NB * 128
    NCP = 4 * NT // 16
    cx.ident_d = nc.dram_tensor("ident", [128, 128], F32, kind="ExternalInput").ap()
    cx.tril_d = nc.dram_tensor("tril", [128, 128], F32, kind="ExternalInput").ap()
    cx.cos_d = nc.dram_tensor("cos", [128, NT], F32, kind="ExternalInput").ap()
    cx.sin_d = nc.dram_tensor("sin", [128, NT], F32, kind="ExternalInput").ap()
    cx.cosc_d = nc.dram_tensor("cosc", [128, NCP], F32, kind="ExternalInput").ap()
    cx.sinc_d = nc.dram_tensor("sinc", [128, NCP], F32, kind="ExternalInput").ap()
    cx.cmpmask_d = nc.dram_tensor("cmpmask", [128, 33], F32, kind="ExternalInput").ap()
    cx.keep_d = nc.dram_tensor("keep", [128, 9], F32, kind="ExternalInput").ap()
    cx.addt_d = nc.dram_tensor("addt", [128, 9], F32, kind="ExternalInput").ap()
    cx.caus_d = nc.dram_tensor("caus", [128, 4, 128], F32, kind="ExternalInput").ap()
    cx.wmask_d = nc.dram_tensor("wmask", [128, 8, 128], F32, kind="ExternalInput").ap()


def gathered_dram(nc, NT, kind):
    al = {}
    al["KsT"] = nc.dram_tensor("KsT_all", [4, 64, 4, NT], BF16, kind=kind).ap()
    al["KwT"] = nc.dram_tensor("KwT_all", [4, 64, 4, NT], BF16, kind=kind).ap()
    al["KcT"] = nc.dram_tensor("KcT_all", [4, 64, 4, NT], BF16, kind=kind).ap()
    al["VcT"] = nc.dram_tensor("VcT_all", [4, 64, 4, NT], BF16, kind=kind).ap()
    al["Vs"] = nc.dram_tensor("Vs_all", [4, NT, 256], BF16, kind=kind).ap()
    al["Vw"] = nc.dram_tensor("Vw_all", [4, NT, 256], BF16, kind=kind).ap()
    return al


def build_test_attn(NB):
    NT = NB * 128
    nc = bass.Bass("TRN2", target_bir_lowering=False)
    with ExitStack() as stack:
        cx = Ctx(nc, stack)
        declare_tables(cx, nc, NB)
        QT = nc.dram_tensor("QT_d", [64, 16, NT], BF16, kind="ExternalInput").ap()
        gates = nc.dram_tensor("gates_d", [NT, 48], F32, kind="ExternalInput").ap()
        al = gathered_dram(nc, NT, "ExternalInput")
        w1k = nc.dram_tensor("w1k", [32, 64, 256], F32, kind="ExternalInput").ap()
        w2k = nc.dram_tensor("w2k", [256, 64], F32, kind="ExternalInput").ap()
        pek = nc.dram_tensor("pek", [32, 64], F32, kind="ExternalInput").ap()
        w1v = nc.dram_tensor("w1v", [32, 64, 256], F32, kind="ExternalInput").ap()
        w2v = nc.dram_tensor("w2v", [256, 64], F32, kind="ExternalInput").ap()
        pev = nc.dram_tensor("pev", [32, 64], F32, kind="ExternalInput").ap()
        O_d = nc.dram_tensor("O_d", [NT, 1024], BF16, kind="ExternalOutput").ap()
        phase_attn(cx, gathered_from_dict(al, NT), QT, gates, O_d, w1k, w2k, pek, w1v, w2v, pev, NB)
        print("instructions:", cx.S.n_inst)
    return nc


L0_W = ["l0_ffn1_w_in", "l0_ffn1_w_out", "l0_ln1_g", "l0_ln1_b", "l0_gm_w_in", "l0_gm_ln_g", "l0_gm_ln_b", "l0_gm_w_s",
        "l0_gm_b_s", "l0_gm_w_out", "l0_ln2_g", "l0_ln2_b", "l0_ffn2_w_in", "l0_ffn2_w_out", "l0_ln3_g", "l0_ln3_b"]
L1A_W = ["l1_ffn1_w_in", "l1_ffn1_w_out", "l1_ln1_g", "l1_ln1_b", "l1_nsa_w_in"]
L1B_W = ["l1_nsa_cmp_pe_k", "l1_nsa_cmp_w1_k", "l1_nsa_cmp_w2_k", "l1_nsa_cmp_pe_v", "l1_nsa_cmp_w1_v", "l1_nsa_cmp_w2_v",
         "l1_nsa_w_out", "l1_ln2_g", "l1_ln2_b", "l1_ffn2_w_in", "l1_ffn2_w_out", "l1_ln3_g", "l1_ln3_b"]
W_SHAPES = {
    "ffn1_w_in": [D, 2 * DFF], "ffn2_w_in": [D, 2 * DFF], "ffn1_w_out": [DFF, D], "ffn2_w_out": [DFF, D],
    "gm_w_in": [D, 2 * GW], "gm_ln_g": [GW], "gm_ln_b": [GW], "gm_w_s": [16, 128, 128], "gm_b_s": [16, 128], "gm_w_out": [GW, D],
    "nsa_w_in": [D, NSA_COLS], "nsa_cmp_pe_k": [32, 64], "nsa_cmp_w1_k": [32, 64, 256], "nsa_cmp_w2_k": [256, 64],
    "nsa_cmp_pe_v": [32, 64], "nsa_cmp_w1_v": [32, 64, 256], "nsa_cmp_w2_v": [256, 64], "nsa_w_out": [D, D],
}


def wshape(name):
    base = name[3:]
    if base in W_SHAPES:
        return W_SHAPES[base]
    return [D]


def declare_w(nc, names):
    return {n: nc.dram_tensor(n, wshape(n), F32, kind="ExternalInput").ap() for n in names}


def emit_part_a(cx, nc, w, x, NT, xa, xb, uT_d, vln_d, nd):
    phase_ffn(cx, x, xa, w["l0_ffn1_w_in"], w["l0_ffn1_w_out"], w["l0_ln1_g"], w["l0_ln1_b"], NT, "a")
    phase_g1(cx, xa, uT_d, vln_d, w["l0_gm_w_in"], w["l0_gm_ln_g"], w["l0_gm_ln_b"], NT)
    phase_g2(cx, xa, xb, uT_d, vln_d, w["l0_gm_w_s"], w["l0_gm_b_s"], w["l0_gm_w_out"], w["l0_ln2_g"], w["l0_ln2_b"], NT)
    phase_ffn(cx, xb, xa, w["l0_ffn2_w_in"], w["l0_ffn2_w_out"], w["l0_ln3_g"], w["l0_ln3_b"], NT, "b")
    phase_ffn(cx, xa, xb, w["l1_ffn1_w_in"], w["l1_ffn1_w_out"], w["l1_ln1_g"], w["l1_ln1_b"], NT, "c")
    phase_n1(cx, xb, w["l1_nsa_w_in"], cx.cos_d, cx.sin_d, nd, NT)
    return xb


def emit_part_b(cx, nc, w, xmid, y, NT, NB, al, QT, gates, O_d, xa):
    phase_attn(cx, al, QT, gates, O_d, w["l1_nsa_cmp_w1_k"], w["l1_nsa_cmp_w2_k"], w["l1_nsa_cmp_pe_k"],
               w["l1_nsa_cmp_w1_v"], w["l1_nsa_cmp_w2_v"], w["l1_nsa_cmp_pe_v"], NB)
    phase_proj(cx, O_d, xmid, xa, w["l1_nsa_w_out"], w["l1_ln2_g"], w["l1_ln2_b"], NT, 1.0 / ALPHA)
    phase_ffn(cx, xa, y, w["l1_ffn2_w_in"], w["l1_ffn2_w_out"], w["l1_ln3_g"], w["l1_ln3_b"], NT, "d")


def build_a(NB):
    NT = NB * 128
    nc = bass.Bass("TRN2", target_bir_lowering=False)
    with ExitStack() as stack:
        cx = Ctx(nc, stack)
        declare_tables(cx, nc, NB)
        w = declare_w(nc, L0_W + L1A_W)
        x = nc.dram_tensor("x", [NT, D], F32, kind="ExternalInput").ap()
        xa = nc.dram_tensor("xa", [NT, D], F32, kind="Internal").ap()
        xb = nc.dram_tensor("xmid", [NT, D], F32, kind="ExternalOutput").ap()
        uT_d = nc.dram_tensor("uT_d", [GW, NT], BF16, kind="Internal").ap()
        vln_d = nc.dram_tensor("vln_d", [NT, GW], BF16, kind="Internal").ap()
        nd = local_from_dict(nsa_dram(nc, NT, "ExternalOutput"))
        emit_part_a(cx, nc, w, x, NT, xa, xb, uT_d, vln_d, nd)
        print("part A instructions:", cx.S.n_inst)
    return nc


def build_b(NB):
    NT = NB * 128
    nc = bass.Bass("TRN2", target_bir_lowering=False)
    with ExitStack() as stack:
        cx = Ctx(nc, stack)
        declare_tables(cx, nc, NB)
        w = declare_w(nc, L1B_W)
        xmid = nc.dram_tensor("xmid", [NT, D], F32, kind="ExternalInput").ap()
        QT = nc.dram_tensor("QT_d", [64, 16, NT], BF16, kind="ExternalInput").ap()
        gates = nc.dram_tensor("gates_d", [NT, 48], F32, kind="ExternalInput").ap()
        al = gathered_dram(nc, NT, "ExternalInput")
        O_d = nc.dram_tensor("O_d", [NT, D], BF16, kind="Internal").ap()
        xa = nc.dram_tensor("xa", [NT, D], F32, kind="Internal").ap()
        y = nc.dram_tensor("y", [NT, D], F32, kind="ExternalOutput").ap()
        emit_part_b(cx, nc, w, xmid, y, NT, NB, gathered_from_dict(al, NT), QT, gates, O_d, xa)
        print("part B instructions:", cx.S.n_inst)
    return nc


TABLE_KEYS = ("ident", "tril", "cos", "sin", "cosc", "sinc", "cmpmask", "keep", "addt", "caus", "wmask")
GATHER_KEYS = ("KsT", "KwT", "KcT", "VcT", "Vs", "Vw")


def run_unfused(inputs, NB):
    NT = NB * 128
    x = np.asarray(inputs["x"], dtype=np.float32)
    B = x.shape[0]
    tabs = [attn_tables(c % 4, NB) for c in range(NCORES)]
    wts = {k: np.ascontiguousarray(np.asarray(v, dtype=np.float32)) for k, v in inputs.items() if k != "x"}
    in_a = []
    for c in range(NCORES):
        b, r = c // 4, c % 4
        m = {k: tabs[c][k] for k in TABLE_KEYS}
        m["x"] = np.ascontiguousarray(x[b].reshape(NB, 4, 128, D)[:, r].reshape(NT, D))
        for k in L0_W + L1A_W:
            m[k] = wts[k]
        in_a.append(m)
    res_a = run_bass_kernel_spmd(build_a(NB), in_a, core_ids=list(range(NCORES))).results
    in_b = []
    for c in range(NCORES):
        b = c // 4
        m = {k: tabs[c][k] for k in TABLE_KEYS}
        m["xmid"] = res_a[c]["xmid"]
        m["QT_d"] = res_a[c]["QT_d"]
        m["gates_d"] = res_a[c]["gates_d"]
        for k in GATHER_KEYS:
            m[k + "_all"] = np.ascontiguousarray(np.stack([np.asarray(res_a[4 * b + rr][k + "_d"]) for rr in range(4)]))
        for k in L1B_W:
            m[k] = wts[k]
        in_b.append(m)
    res_b = run_bass_kernel_spmd(build_b(NB), in_b, core_ids=list(range(NCORES))).results
    out = np.zeros((B, 4 * NT, D), np.float32)
    for c in range(NCORES):
        b, r = c // 4, c % 4
        out[b].reshape(NB, 4, 128, D)[:, r] = np.asarray(res_b[c]["y"]).reshape(NB, 128, D)
    return out


def build_fused(NB):
    NT = NB * 128
    nc = bass.Bass("TRN2", target_bir_lowering=False)
    with ExitStack() as stack:
        cx = Ctx(nc, stack)
        declare_tables(cx, nc, NB)
        w = declare_w(nc, L0_W + L1A_W + L1B_W)
        x = nc.dram_tensor("x", [NT, D], F32, kind="ExternalInput").ap()
        y = nc.dram_tensor("y", [NT, D], F32, kind="ExternalOutput").ap()
        xa = nc.dram_tensor("xa", [NT, D], F32, kind="Internal").ap()
        xb = nc.dram_tensor("xb", [NT, D], F32, kind="Internal").ap()
        uT_d = nc.dram_tensor("uT_d", [GW, NT], BF16, kind="Internal").ap()
        vln_d = nc.dram_tensor("vln_d", [NT, GW], BF16, kind="Internal").ap()
        O_d = nc.dram_tensor("O_d", [NT, D], BF16, kind="Internal").ap()
        QT = nc.dram_tensor("QT_d", [64, 16, NT], BF16, kind="Internal").ap()
        gates = nc.dram_tensor("gates_d", [NT, 48], F32, kind="Internal").ap()
        loc, gat = {}, {}
        for k in GATHER_KEYS:
            for hf in range(2):
                loc[k, hf] = nc.dram_tensor("%s_loc%d" % (k, hf), [128, NT], BF16, kind="Internal").ap()
                gat[k, hf] = nc.dram_tensor("%s_gat%d" % (k, hf), [4 * 128, NT], BF16, kind="Internal").ap()
        tokv = lambda a: a.rearrange("r (x c) -> (r x) c", c=256)
        H = NT // 2
        nd = NsaLocal(lambda k, g: loc[k, g // 2][(g % 2) * 64:(g % 2) * 64 + 64, :],
                      lambda k, t0: tokv(loc[k, t0 // H])[t0 % H:t0 % H + 128, :], QT, gates)
        al = NsaGathered(lambda k, r, g: gat[k, g // 2][r * 128 + (g % 2) * 64:r * 128 + (g % 2) * 64 + 64, :],
                         lambda k, r, hf: tokv(gat[k, hf][r * 128:(r + 1) * 128, :]))
        xmid = emit_part_a(cx, nc, w, x, NT, xa, xb, uT_d, vln_d, nd)
        for k in ("KcT", "VcT"):
            for hf in range(2):
                cx.S.cc("AllGather", [[0, 1, 2, 3], [4, 5, 6, 7]], loc[k, hf], gat[k, hf], key="cc_%s%d" % (k, hf))
        cx.S.flush()
        for k in ("KsT", "KwT", "Vs", "Vw"):
            for hf in range(2):
                cx.S.cc("AllGather", [[0, 1, 2, 3], [4, 5, 6, 7]], loc[k, hf], gat[k, hf], key="cc_%s%d" % (k, hf))
        emit_part_b(cx, nc, w, xmid, y, NT, NB, al, QT, gates, O_d, xa)
        print("fused instructions:", cx.S.n_inst)
    return nc


def run_fused(inputs, NB):
    NT = NB * 128
    x = np.asarray(inputs["x"], dtype=np.float32)
    B = x.shape[0]
    wts = {k: np.ascontiguousarray(np.asarray(v, dtype=np.float32)) for k, v in inputs.items() if k != "x"}
    in_maps = []
    for c in range(NCORES):
        b, r = c // 4, c % 4
        tabs = attn_tables(r, NB)
        m = {k: tabs[k] for k in TABLE_KEYS}
        m["x"] = np.ascontiguousarray(x[b].reshape(NB, 4, 128, D)[:, r].reshape(NT, D))
        for k in L0_W + L1A_W + L1B_W:
            m[k] = wts[k]
        in_maps.append(m)
    res = run_bass_kernel_spmd(build_fused(NB), in_maps, core_ids=list(range(NCORES))).results
    out = np.zeros((B, 4 * NT, D), np.float32)
    for c in range(NCORES):
        b, r = c // 4, c % 4
        out[b].reshape(NB, 4, 128, D)[:, r] = np.asarray(res[c]["y"]).reshape(NB, 128, D)
    return out


def kernel(**inputs):
    return run_fused(inputs, 32)
```

```python
import math
from contextlib import ExitStack

import numpy as np
import concourse.bass as bass
import concourse.mybir as mybir
from concourse.bass_utils import run_bass_kernel_spmd

F32 = mybir.dt.float32
BF16 = mybir.dt.bfloat16
AF = mybir.ActivationFunctionType
ALU = mybir.AluOpType
AX = mybir.AxisListType

D = 1024
DFF = 2816
DEPTH = 2
ALPHA = (2 * DEPTH) ** 0.25
LN_EPS = 1e-5
NCORES = 8


class Buf:
    __slots__ = ("name", "writers", "dma_writers", "readers", "dma_readers")

    def __init__(self, name):
        self.name = name
        self.writers = {}
        self.dma_writers = []
        self.readers = {}
        self.dma_readers = []


class Op:
    __slots__ = ("eng", "fn", "deps", "is_dma", "dsem", "dcount", "signal", "sigval", "emitted")

    def __init__(self, eng, fn, is_dma):
        self.eng = eng
        self.fn = fn
        self.deps = []
        self.is_dma = is_dma
        self.dsem = None
        self.dcount = 0
        self.signal = False
        self.sigval = 0
        self.emitted = False


class Sched:
    def __init__(self, nc, stack):
        self.nc = nc
        self.engs = {"pe": nc.tensor, "act": nc.scalar, "dve": nc.vector, "pool": nc.gpsimd, "sp": nc.sync}
        self.esem = {e: stack.enter_context(nc.semaphore("es_" + e)) for e in self.engs}
        self.stack = stack
        self.pending = []
        self.sigcount = {e: 0 for e in self.engs}
        self.waited = {e: {} for e in self.engs}
        self.dsems = {}
        self.n_inst = 0

    def buf(self, name):
        return Buf(name)

    def bufs(self, name, n):
        return [Buf("%s%d" % (name, i)) for i in range(n)]

    def _add_dep(self, op, p):
        if p is op:
            return
        if (not p.is_dma) and (not op.is_dma) and p.eng == op.eng and op.eng == "pe":
            return
        op.deps.append(p)

    def _track(self, op, reads, writes):
        for b in reads:
            for p in b.writers.values():
                self._add_dep(op, p)
            for p in b.dma_writers:
                self._add_dep(op, p)
        for b in writes:
            if b.readers or b.dma_readers:
                for p in b.readers.values():
                    self._add_dep(op, p)
                for p in b.dma_readers:
                    self._add_dep(op, p)
                for p in b.writers.values():
                    self._add_dep(op, p)
                for p in b.dma_writers:
                    self._add_dep(op, p)
                b.readers = {}
                b.dma_readers = []
                b.writers = {}
                b.dma_writers = []
            else:
                for e, p in b.writers.items():
                    if e != op.eng or op.is_dma:
                        self._add_dep(op, p)
                for p in b.dma_writers:
                    self._add_dep(op, p)
        for b in reads:
            if op.is_dma:
                b.dma_readers.append(op)
            else:
                b.readers[op.eng] = op
        for b in writes:
            if op.is_dma:
                b.dma_writers.append(op)
            else:
                b.writers[op.eng] = op

    def op(self, eng, fn, reads=(), writes=()):
        o = Op(eng, fn, False)
        self._track(o, reads, writes)
        self.pending.append(o)
        return o

    def dma(self, eng, out, in_, reads=(), writes=(), key=None, slow=False):
        assert key is not None
        if key not in self.dsems:
            self.dsems[key] = [self.stack.enter_context(self.nc.semaphore("ds_" + key)), 0, 16]
        ent = self.dsems[key]
        ent[1] += 1
        if slow:
            o = Op(eng, (lambda: self.engs[eng].dma_start(out=out, in_=in_, allow_slow_non_contiguous=True)), True)
        else:
            o = Op(eng, (lambda: self.engs[eng].dma_start(out=out, in_=in_)), True)
        o.dsem = key
        o.dcount = ent[1]
        self._track(o, reads, writes)
        self.pending.append(o)
        return o

    def cc(self, kind, groups, in_ap, out_ap, reads=(), writes=(), key=None):
        assert key not in self.dsems
        self.dsems[key] = [self.stack.enter_context(self.nc.semaphore("ds_" + key)), 1, 1]
        o = Op("pool", (lambda: self.nc.gpsimd.collective_compute(kind, ALU.bypass, replica_groups=groups,
                                                                  ins=[in_ap.opt()], outs=[out_ap.opt()])), True)
        o.dsem = key
        o.dcount = 1
        self._track(o, reads, writes)
        self.pending.append(o)
        return o

    def _wait(self, eng, semkey, sem, val):
        w = self.waited[eng]
        if w.get(semkey, 0) >= val:
            return
        w[semkey] = val
        self.engs[eng].wait_ge(sem, val)
        self.n_inst += 1

    def flush(self):
        for o in self.pending:
            o.deps = [p for p in o.deps if not p.emitted]
            for p in o.deps:
                if not p.is_dma:
                    p.signal = True
        last = {}
        for o in self.pending:
            if not o.is_dma:
                last[o.eng] = o
        for o in last.values():
            o.signal = True
        for o in self.pending:
            for p in o.deps:
                assert p.emitted, "dependency on later op"
                if p.is_dma:
                    self._wait(o.eng, "d_" + p.dsem, self.dsems[p.dsem][0], self.dsems[p.dsem][2] * p.dcount)
                else:
                    self._wait(o.eng, "e_" + p.eng, self.esem[p.eng], p.sigval)
            ins = o.fn()
            self.n_inst += 1
            if o.is_dma:
                if self.dsems[o.dsem][2] == 16:
                    ins.then_inc(self.dsems[o.dsem][0], 16)
                else:
                    ins.then_inc(self.dsems[o.dsem][0])
            elif o.signal:
                self.sigcount[o.eng] += 1
                o.sigval = self.sigcount[o.eng]
                ins.then_inc(self.esem[o.eng], 1)
            o.emitted = True
            o.fn = None
        self.pending = []
        self.barrier()

    def barrier(self):
        for e in self.engs:
            for e2 in self.engs:
                if e2 != e and self.sigcount[e2] > 0:
                    self._wait(e, "e_" + e2, self.esem[e2], self.sigcount[e2])
            for key, (sem, cnt, unit) in self.dsems.items():
                if cnt > 0:
                    self._wait(e, "d_" + key, sem, unit * cnt)


class Ctx:
    def __init__(self, nc, stack):
        self.nc = nc
        self.stack = stack
        self.S = Sched(nc, stack)
        self.uid = 0

    def name(self, s):
        self.uid += 1
        return "%s_%d" % (s, self.uid)


def sb(cx, st, name, shape, dt):
    return st.enter_context(cx.nc.sbuf_tensor(cx.name(name), shape, dt))


def ps(cx, st, name, shape, dt):
    return st.enter_context(cx.nc.psum_tensor(cx.name(name), shape, dt))


def make_ident(cx, st):
    nc, S = cx.nc, cx.S
    ident = sb(cx, st, "ident", [128, 128], BF16)
    IB = S.buf("ident")
    S.dma("pool", ident[:], cx.ident_d[:, :], writes=[IB], key="ident")
    return ident, IB


def ln_epilogue(cx, z, ZB, gb, GB, st_t, mv_t, sd_t, rs_t, SB_, eps):
    nc, S = cx.nc, cx.S
    S.op("dve", lambda: nc.vector.bn_stats(out=st_t[:, 0, :], in_=z[:, 0:512]), reads=[ZB], writes=[SB_])
    S.op("dve", lambda: nc.vector.bn_stats(out=st_t[:, 1, :], in_=z[:, 512:1024]), reads=[ZB], writes=[SB_])
    S.op("dve", lambda: nc.vector.bn_aggr(out=mv_t[:], in_=st_t[:]), reads=[SB_], writes=[SB_])
    S.op("act", lambda: nc.scalar.activation(out=sd_t[:], in_=mv_t[:, 1:2], func=AF.Sqrt, bias=eps, scale=1.0),
         reads=[SB_], writes=[SB_])
    S.op("dve", lambda: nc.vector.reciprocal(out=rs_t[:], in_=sd_t[:]), reads=[SB_], writes=[SB_])
    S.op("dve", lambda: nc.vector.tensor_scalar(out=z[:], in0=z[:], scalar1=mv_t[:, 0:1], scalar2=rs_t[:, 0:1],
                                                op0=ALU.subtract, op1=ALU.mult), reads=[ZB, SB_], writes=[ZB])
    S.op("pool", lambda: nc.gpsimd.tensor_tensor(out=z[:], in0=z[:], in1=gb[:, 0, :], op=ALU.mult),
         reads=[ZB, GB], writes=[ZB])
    S.op("pool", lambda: nc.gpsimd.tensor_tensor(out=z[:], in0=z[:], in1=gb[:, 1, :], op=ALU.add),
         reads=[ZB, GB], writes=[ZB])


def load_gb(cx, st, g_d, b_d, name):
    nc, S = cx.nc, cx.S
    gb = sb(cx, st, name, [128, 2, D], F32)
    GB = S.buf(name)
    S.dma("sp", gb[:, 0, :], g_d.partition_broadcast(128), writes=[GB], key=name)
    S.dma("sp", gb[:, 1, :], b_d.partition_broadcast(128), writes=[GB], key=name)
    return gb, GB


def emit_xT(cx, xs_t, XSb, xbf, XBF, tp, TP, xT_t, XTb, ident, IB, NS):
    nc, S = cx.nc, cx.S
    KC = D // 128
    S.op("pool", lambda: nc.gpsimd.tensor_copy(out=xbf[:], in_=xs_t[:]), reads=[XSb], writes=[XBF])
    for s in range(NS):
        b = s % len(tp)
        for k in range(KC):
            S.op("pe", lambda s=s, k=k, b=b: nc.tensor.transpose(out=tp[b][:, k * 128:(k + 1) * 128],
                                                                  in_=xbf[:, s, k * 128:(k + 1) * 128], identity=ident[:]),
                 reads=[XBF, IB], writes=[TP[b]])
        S.op("dve", lambda s=s, b=b: nc.vector.tensor_copy(
            out=xT_t[:, :, s * 128:(s + 1) * 128], in_=tp[b][:].rearrange("p (k t) -> p k t", k=KC)),
            reads=[TP[b]], writes=[XTb])


def phase_ffn(cx, xin, xout, w_in_d, w_out_d, g_d, b_d, NT, tag):
    nc, S = cx.nc, cx.S
    TT = 256
    NS = TT // 128
    KC = D // 128
    MC = DFF // 128
    c_res = 0.5 / ALPHA
    eps = LN_EPS / (ALPHA * ALPHA)
    with ExitStack() as st:
        w1 = sb(cx, st, "w1", [128, KC, 2 * DFF], BF16)
        w2 = sb(cx, st, "w2", [128, MC, D], BF16)
        W1, W2 = S.buf("W1"), S.buf("W2")
        for k in range(KC):
            S.dma("pool", w1[:, k, :], w_in_d[k * 128:(k + 1) * 128, :], writes=[W1], key="w1")
        for m in range(MC):
            S.dma("pool", w2[:, m, :], w_out_d[m * 128:(m + 1) * 128, :], writes=[W2], key="w2")
        gb, GB = load_gb(cx, st, g_d, b_d, "gb")
        ident, IB = make_ident(cx, st)
        xs = [sb(cx, st, "xs", [128, NS, D], F32) for _ in range(2)]
        XS = S.bufs("xs", 2)
        xbf = sb(cx, st, "xbf", [128, NS, D], BF16)
        XBF = S.buf("xbf")
        xT = [sb(cx, st, "xT", [128, KC, TT], BF16) for _ in range(2)]
        XT = S.bufs("xT", 2)
        hT = [sb(cx, st, "hT", [128, MC, TT], BF16) for _ in range(2)]
        HT = S.bufs("hT", 2)
        sg = [sb(cx, st, "sg", [128, TT], BF16) for _ in range(2)]
        SG = S.bufs("sg", 2)
        z = [sb(cx, st, "z", [128, D], F32) for _ in range(2)]
        Z = S.bufs("z", 2)
        st_t = [sb(cx, st, "st", [128, 2, 6], F32) for _ in range(2)]
        mv_t = [sb(cx, st, "mv", [128, 2], F32) for _ in range(2)]
        sd_t = [sb(cx, st, "sd", [128, 1], F32) for _ in range(2)]
        rs_t = [sb(cx, st, "rs", [128, 1], F32) for _ in range(2)]
        STB = S.bufs("stb", 2)
        tp = [ps(cx, st, "tp", [128, D], BF16) for _ in range(2)]
        TP = S.bufs("tp", 2)
        gu = [ps(cx, st, "gu", [128, 512], F32) for _ in range(4)]
        GU = S.bufs("gu", 4)
        op_ = [ps(cx, st, "o", [128, 512], F32) for _ in range(2)]
        OP = S.bufs("o", 2)

        ntile = NT // TT
        xin_v = xin.rearrange("(t s p) d -> t p s d", s=NS, p=128)
        xout_v = xout.rearrange("(t s p) d -> t s p d", s=NS, p=128)

        def load(t):
            S.dma("sp", xs[t % 2][:], xin_v[t], writes=[XS[t % 2]], key="xs%d" % (t % 2))

        load(0)
        for t in range(ntile):
            b2 = t % 2
            if t + 1 < ntile:
                load(t + 1)
            emit_xT(cx, xs[b2], XS[b2], xbf, XBF, tp, TP, xT[b2], XT[b2], ident, IB, NS)
            for m in range(MC):
                g_ps, u_ps = gu[2 * (m % 2)], gu[2 * (m % 2) + 1]
                GP, UP = GU[2 * (m % 2)], GU[2 * (m % 2) + 1]
                for k in range(KC):
                    S.op("pe", lambda m=m, k=k, g_ps=g_ps, b2=b2: nc.tensor.matmul(
                        g_ps[:, 0:TT], w1[:, k, m * 128:(m + 1) * 128], xT[b2][:, k, :], start=(k == 0), stop=(k == KC - 1)),
                        reads=[W1, XT[b2]], writes=[GP])
                for k in range(KC):
                    S.op("pe", lambda m=m, k=k, u_ps=u_ps, b2=b2: nc.tensor.matmul(
                        u_ps[:, 0:TT], w1[:, k, DFF + m * 128:DFF + (m + 1) * 128], xT[b2][:, k, :], start=(k == 0), stop=(k == KC - 1)),
                        reads=[W1, XT[b2]], writes=[UP])
                S.op("act", lambda m=m, g_ps=g_ps: nc.scalar.activation(out=sg[m % 2][:], in_=g_ps[:, 0:TT], func=AF.Silu),
                     reads=[GP], writes=[SG[m % 2]])
                S.op("dve", lambda m=m, u_ps=u_ps, b2=b2: nc.vector.tensor_tensor(
                    out=hT[b2][:, m, :], in0=sg[m % 2][:], in1=u_ps[:, 0:TT], op=ALU.mult),
                    reads=[SG[m % 2], UP], writes=[HT[b2]])
            for s in range(NS):
                for n in range(2):
                    for m in range(MC):
                        S.op("pe", lambda s=s, n=n, m=m, b2=b2: nc.tensor.matmul(
                            op_[n][:], hT[b2][:, m, s * 128:(s + 1) * 128], w2[:, m, n * 512:(n + 1) * 512],
                            start=(m == 0), stop=(m == MC - 1)), reads=[HT[b2], W2], writes=[OP[n]])
                    S.op("dve", lambda s=s, n=n, b2=b2: nc.vector.scalar_tensor_tensor(
                        out=z[s][:, n * 512:(n + 1) * 512], in0=op_[n][:], scalar=c_res, in1=xs[b2][:, s, n * 512:(n + 1) * 512],
                        op0=ALU.mult, op1=ALU.add), reads=[OP[n], XS[b2]], writes=[Z[s]])
                ln_epilogue(cx, z[s], Z[s], gb, GB, st_t[s], mv_t[s], sd_t[s], rs_t[s], STB[s], eps)
                S.dma("sp", xout_v[t, s], z[s][:], reads=[Z[s]], key="zst%d" % s)
        S.flush()


GW = 3072


def ln_stats(cx, src, SRC, nchunk, st_t, mv_t, sd_t, rs_t, SB_, eps):
    nc, S = cx.nc, cx.S
    for c in range(nchunk):
        S.op("dve", lambda c=c: nc.vector.bn_stats(out=st_t[:, c, :], in_=src[:, c * 512:(c + 1) * 512]),
             reads=[SRC], writes=[SB_])
    S.op("dve", lambda: nc.vector.bn_aggr(out=mv_t[:], in_=st_t[:]), reads=[SB_], writes=[SB_])
    S.op("act", lambda: nc.scalar.activation(out=sd_t[:], in_=mv_t[:, 1:2], func=AF.Sqrt, bias=eps, scale=1.0),
         reads=[SB_], writes=[SB_])
    S.op("dve", lambda: nc.vector.reciprocal(out=rs_t[:], in_=sd_t[:]), reads=[SB_], writes=[SB_])


def phase_g1(cx, xin, uT_d, vln_d, w_in_d, lng_d, lnb_d, NT):
    nc, S = cx.nc, cx.S
    TT, NS, KC, JC = 256, 2, 8, 24
    with ExitStack() as st:
        wg = sb(cx, st, "wg", [128, KC, 2 * GW], BF16)
        WG = S.buf("WG")
        for k in range(KC):
            S.dma("pool", wg[:, k, :], w_in_d[k * 128:(k + 1) * 128, :], writes=[WG], key="wg")
        gbv = sb(cx, st, "gbv", [128, 2, GW], F32)
        GBV = S.buf("gbv")
        S.dma("sp", gbv[:, 0, :], lng_d.partition_broadcast(128), writes=[GBV], key="gbv")
        S.dma("sp", gbv[:, 1, :], lnb_d.partition_broadcast(128), writes=[GBV], key="gbv")
        ident, IB = make_ident(cx, st)
        xs = [sb(cx, st, "xs", [128, NS, D], F32) for _ in range(2)]
        XS = S.bufs("xs", 2)
        xbf = sb(cx, st, "xbf", [128, NS, D], BF16)
        XBF = S.buf("xbf")
        xT = [sb(cx, st, "xT", [128, KC, TT], BF16) for _ in range(2)]
        XT = S.bufs("xT", 2)
        ust = [sb(cx, st, "ust", [128, 4, TT], BF16) for _ in range(2)]
        UST = S.bufs("ust", 2)
        v = [sb(cx, st, "v", [128, GW], F32) for _ in range(2)]
        V = S.bufs("v", 2)
        vln = [sb(cx, st, "vln", [128, GW], BF16) for _ in range(2)]
        VLN = S.bufs("vln", 2)
        st_t = [sb(cx, st, "st", [128, 6, 6], F32) for _ in range(2)]
        mv_t = [sb(cx, st, "mv", [128, 2], F32) for _ in range(2)]
        sd_t = [sb(cx, st, "sd", [128, 1], F32) for _ in range(2)]
        rs_t = [sb(cx, st, "rs", [128, 1], F32) for _ in range(2)]
        STB = S.bufs("stb", 2)
        tp = [ps(cx, st, "tp", [128, D], BF16) for _ in range(2)]
        TP = S.bufs("tp", 2)
        pu = [ps(cx, st, "pu", [128, 512], F32) for _ in range(2)]
        PU = S.bufs("pu", 2)
        pv = [ps(cx, st, "pv", [128, 512], F32) for _ in range(3)]
        PV = S.bufs("pv", 3)
        ntile = NT // TT
        xin_v = xin.rearrange("(t s p) d -> t p s d", s=NS, p=128)
        uT_v = uT_d.rearrange("(j p) n -> p j n", p=128)

        def load(t):
            S.dma("sp", xs[t % 2][:], xin_v[t], writes=[XS[t % 2]], key="xs%d" % (t % 2))

        load(0)
        for t in range(ntile):
            b2 = t % 2
            if t + 1 < ntile:
                load(t + 1)
            emit_xT(cx, xs[b2], XS[b2], xbf, XBF, tp, TP, xT[b2], XT[b2], ident, IB, NS)
            for j in range(JC):
                q = (j // 4) % 2
                for k in range(KC):
                    S.op("pe", lambda j=j, k=k, b2=b2: nc.tensor.matmul(
                        pu[j % 2][:, 0:TT], wg[:, k, j * 128:(j + 1) * 128], xT[b2][:, k, :],
                        start=(k == 0), stop=(k == KC - 1)), reads=[WG, XT[b2]], writes=[PU[j % 2]])
                S.op("act", lambda j=j, q=q: nc.scalar.activation(out=ust[q][:, j % 4, :], in_=pu[j % 2][:, 0:TT],
                                                                    func=AF.Gelu_apprx_tanh),
                     reads=[PU[j % 2]], writes=[UST[q]])
                if j % 4 == 3:
                    S.dma("sp", uT_v[:, j - 3:j + 1, t * TT:(t + 1) * TT], ust[q][:], reads=[UST[q]], key="ust%d" % q)
            for s in range(NS):
                for n in range(6):
                    for k in range(KC):
                        S.op("pe", lambda s=s, n=n, k=k, b2=b2: nc.tensor.matmul(
                            pv[n % 3][:], xT[b2][:, k, s * 128:(s + 1) * 128], wg[:, k, GW + n * 512:GW + (n + 1) * 512],
                            start=(k == 0), stop=(k == KC - 1)), reads=[WG, XT[b2]], writes=[PV[n % 3]])
                    S.op("act", lambda s=s, n=n: nc.scalar.activation(out=v[s][:, n * 512:(n + 1) * 512], in_=pv[n % 3][:],
                                                                        func=AF.Gelu_apprx_tanh),
                         reads=[PV[n % 3]], writes=[V[s]])
                ln_stats(cx, v[s], V[s], 6, st_t[s], mv_t[s], sd_t[s], rs_t[s], STB[s], LN_EPS)
                S.op("dve", lambda s=s: nc.vector.tensor_scalar(out=v[s][:], in0=v[s][:], scalar1=mv_t[s][:, 0:1],
                                                                scalar2=rs_t[s][:, 0:1], op0=ALU.subtract, op1=ALU.mult),
                     reads=[V[s], STB[s]], writes=[V[s]])
                S.op("pool", lambda s=s: nc.gpsimd.tensor_tensor(out=v[s][:], in0=v[s][:], in1=gbv[:, 0, :], op=ALU.mult),
                     reads=[V[s], GBV], writes=[V[s]])
                S.op("pool", lambda s=s: nc.gpsimd.tensor_tensor(out=vln[s][:], in0=v[s][:], in1=gbv[:, 1, :], op=ALU.add),
                     reads=[V[s], GBV], writes=[VLN[s]])
                S.dma("sp", vln_d[t * TT + s * 128:t * TT + (s + 1) * 128, :], vln[s][:], reads=[VLN[s]], key="vln%d" % s)
        S.flush()


def phase_g2(cx, xin, xout, uT_d, vln_d, ws_d, bs_d, w_out_d, g_d, b_d, NT):
    nc, S = cx.nc, cx.S
    JC = 24
    c_res = 1.0 / ALPHA
    eps = LN_EPS / (ALPHA * ALPHA)
    with ExitStack() as st:
        w2 = sb(cx, st, "w2g", [128, JC, D], BF16)
        W2 = S.buf("W2g")
        for j in range(JC):
            S.dma("pool", w2[:, j, :], w_out_d[j * 128:(j + 1) * 128, :], writes=[W2], key="w2g")
        gb, GB = load_gb(cx, st, g_d, b_d, "gb2")
        ident, IB = make_ident(cx, st)
        WT = sb(cx, st, "WT", [128, 16, 128], BF16)
        WTB = S.buf("WT")
        bhi = sb(cx, st, "bhi", [1, 2048], BF16)
        blo = sb(cx, st, "blo", [1, 2048], BF16)
        ones = sb(cx, st, "ones", [1, 128], BF16)
        BB = S.buf("bias")
        with ExitStack() as st2:
            wsf = sb(cx, st2, "wsf", [128, 16, 128], F32)
            WSF = S.buf("wsf")
            S.dma("sp", wsf[:], ws_d.rearrange("g t s -> t g s"), writes=[WSF], key="wsf")
            tril = sb(cx, st2, "tril", [128, 128], F32)
            TR = S.buf("tril")
            S.dma("sp", tril[:], cx.tril_d[:, :], writes=[TR], key="tril")
            wsm = sb(cx, st2, "wsm", [128, 16, 128], BF16)
            WSM = S.buf("wsm")
            S.op("dve", lambda: nc.vector.tensor_tensor(out=wsm[:], in0=wsf[:], in1=tril[:].unsqueeze(1).broadcast_to([128, 16, 128]),
                                                        op=ALU.mult), reads=[WSF, TR], writes=[WSM])
            tpw = [ps(cx, st2, "tpw", [128, D], BF16) for _ in range(2)]
            TPW = S.bufs("tpw", 2)
            for g in range(16):
                S.op("pe", lambda g=g: nc.tensor.transpose(out=tpw[g // 8][:, (g % 8) * 128:(g % 8 + 1) * 128],
                                                           in_=wsm[:, g, :], identity=ident[:]),
                     reads=[WSM, IB], writes=[TPW[g // 8]])
            for h in range(2):
                S.op("dve", lambda h=h: nc.vector.tensor_copy(out=WT[:, h * 8:(h + 1) * 8, :],
                                                              in_=tpw[h][:].rearrange("p (g t) -> p g t", g=8)),
                     reads=[TPW[h]], writes=[WTB])
            bsf = sb(cx, st2, "bsf", [1, 2048], F32)
            BSF = S.buf("bsf")
            S.dma("sp", bsf[:], bs_d.rearrange("g t -> (g t)").unsqueeze(0), writes=[BSF], key="bsf")
            S.op("dve", lambda: nc.vector.tensor_copy(out=bhi[:], in_=bsf[:]), reads=[BSF], writes=[BB])
            S.op("dve", lambda: nc.vector.tensor_tensor(out=blo[:], in0=bsf[:], in1=bhi[:], op=ALU.subtract),
                 reads=[BSF, BB], writes=[BB])
            S.op("dve", lambda: nc.vector.memset(ones[:], 1.0), writes=[BB])
            S.flush()
        ut = [sb(cx, st, "ut", [128, JC, 256], BF16) for _ in range(2)]
        UT = S.bufs("ut", 2)
        vl = [sb(cx, st, "vl", [128, GW], BF16) for _ in range(2)]
        VL = S.bufs("vl", 2)
        xs = [sb(cx, st, "xs", [128, D], F32) for _ in range(2)]
        XS = S.bufs("xs", 2)
        uv = [sb(cx, st, "uv", [128, JC, 128], BF16) for _ in range(2)]
        UV = S.bufs("uv", 2)
        z = [sb(cx, st, "z", [128, D], F32) for _ in range(2)]
        Z = S.bufs("z", 2)
        st_t = [sb(cx, st, "st", [128, 2, 6], F32) for _ in range(2)]
        mv_t = [sb(cx, st, "mv", [128, 2], F32) for _ in range(2)]
        sd_t = [sb(cx, st, "sd", [128, 1], F32) for _ in range(2)]
        rs_t = [sb(cx, st, "rs", [128, 1], F32) for _ in range(2)]
        STB = S.bufs("stb", 2)
        P = [ps(cx, st, "P", [128, 512], F32) for _ in range(6)]
        PB = S.bufs("P", 6)
        op_ = [ps(cx, st, "o", [128, 512], F32) for _ in range(2)]
        OP = S.bufs("o", 2)
        nchunk = NT // 128
        uT_v = uT_d.rearrange("(j p) n -> p j n", p=128)
        pieces = []
        for g in range(16):
            a = g // 2
            if g % 2 == 0:
                pieces.append((g, 192 * g, 192 * g + 128, 3 * a, 0, 128))
                pieces.append((g, 192 * g + 128, 192 * g + 192, 3 * a + 1, 0, 64))
            else:
                pieces.append((g, 192 * g, 192 * g + 64, 3 * a + 1, 64, 128))
                pieces.append((g, 192 * g + 64, 192 * g + 192, 3 * a + 2, 0, 128))

        def load_u(tt):
            S.dma("sp", ut[tt % 2][:], uT_v[:, :, tt * 256:(tt + 1) * 256], writes=[UT[tt % 2]], key="ut%d" % (tt % 2))

        def load_c(c):
            S.dma("sp", vl[c % 2][:], vln_d[c * 128:(c + 1) * 128, :], writes=[VL[c % 2]], key="vl%d" % (c % 2))
            S.dma("sp", xs[c % 2][:], xin[c * 128:(c + 1) * 128, :], writes=[XS[c % 2]], key="xsg%d" % (c % 2))

        load_u(0)
        load_c(0)
        for c in range(nchunk):
            c2 = c % 2
            tt, cs = c // 2, c % 2
            if c + 1 < nchunk:
                load_c(c + 1)
                if (c + 1) % 2 == 0:
                    load_u((c + 1) // 2)
            for (g, f0, f1, j, r0, r1) in pieces:
                M = f1 - f0
                out_ap = lambda j=j, r0=r0, r1=r1: P[j // 4][r0:r1, (j % 4) * 128:(j % 4 + 1) * 128]
                S.op("pe", lambda g=g, f0=f0, f1=f1, out_ap=out_ap, c2=c2: nc.tensor.matmul(
                    out_ap(), vl[c2][:, f0:f1], WT[:, g, :], start=True, stop=False),
                    reads=[VL[c2], WTB], writes=[PB[j // 4]])
                S.op("pe", lambda g=g, M=M, out_ap=out_ap: nc.tensor.matmul(
                    out_ap(), ones[0:1, 0:M], bhi[0:1, g * 128:(g + 1) * 128], start=False, stop=False),
                    reads=[BB], writes=[PB[j // 4]])
                S.op("pe", lambda g=g, M=M, out_ap=out_ap: nc.tensor.matmul(
                    out_ap(), ones[0:1, 0:M], blo[0:1, g * 128:(g + 1) * 128], start=False, stop=True),
                    reads=[BB], writes=[PB[j // 4]])
            for jj in range(6):
                S.op("dve", lambda jj=jj, c2=c2, tt=tt, cs=cs: nc.vector.tensor_tensor(
                    out=uv[c2][:, 4 * jj:4 * jj + 4, :], in0=P[jj][:].rearrange("p (j t) -> p j t", j=4),
                    in1=ut[tt % 2][:, 4 * jj:4 * jj + 4, cs * 128:(cs + 1) * 128], op=ALU.mult),
                    reads=[PB[jj], UT[tt % 2]], writes=[UV[c2]])
            for n in range(2):
                for j in range(JC):
                    S.op("pe", lambda n=n, j=j, c2=c2: nc.tensor.matmul(
                        op_[n][:], uv[c2][:, j, :], w2[:, j, n * 512:(n + 1) * 512], start=(j == 0), stop=(j == JC - 1)),
                        reads=[UV[c2], W2], writes=[OP[n]])
                S.op("dve", lambda n=n, c2=c2: nc.vector.scalar_tensor_tensor(
                    out=z[c2][:, n * 512:(n + 1) * 512], in0=op_[n][:], scalar=c_res, in1=xs[c2][:, n * 512:(n + 1) * 512],
                    op0=ALU.mult, op1=ALU.add), reads=[OP[n], XS[c2]], writes=[Z[c2]])
            ln_epilogue(cx, z[c2], Z[c2], gb, GB, st_t[c2], mv_t[c2], sd_t[c2], rs_t[c2], STB[c2], eps)
            S.dma("sp", xout[c * 128:(c + 1) * 128, :], z[c2][:], reads=[Z[c2]], key="zg%d" % c2)
        S.flush()


class NsaLocal:
    def __init__(self, feat, tok, QT, gates):
        self.feat, self.tok, self.QT, self.gates = feat, tok, QT, gates


class NsaGathered:
    def __init__(self, feat, tok):
        self.feat, self.tok = feat, tok


NSA_COLS = 2608
HD = 64


def phase_n1(cx, xin, w_in_d, cos_d, sin_d, nl, NT):
    nc, S = cx.nc, cx.S
    TT, NS, KC = 512, 4, 8
    with ExitStack() as st:
        w = sb(cx, st, "wn", [128, KC, NSA_COLS], BF16)
        W = S.buf("wn")
        for k in range(KC):
            S.dma("pool", w[:, k, :], w_in_d[k * 128:(k + 1) * 128, :], writes=[W], key="wn")
        wqr = sb(cx, st, "wqr", [128, KC, 1024], BF16)
        wkr = sb(cx, st, "wkr", [128, KC, 512], BF16)
        WR = S.buf("wrot")

        def rot(dst, src):
            sv = src.rearrange("p k (b t f) -> p k b t f", t=2, f=32)
            dv = dst.rearrange("p k (b t f) -> p k b t f", t=2, f=32)
            for k in range(KC):
                S.op("dve", lambda k=k: nc.vector.tensor_scalar(out=dv[:, k, :, 0, :], in0=sv[:, k, :, 1, :], scalar1=-1.0,
                                                                scalar2=None, op0=ALU.mult), reads=[W, WR], writes=[WR])
                S.op("dve", lambda k=k: nc.vector.tensor_copy(out=dv[:, k, :, 1, :], in_=sv[:, k, :, 0, :]),
                     reads=[W, WR], writes=[WR])

        rot(wqr[:], w[:, :, 0:1024])
        rot(wkr[:, :, 0:256], w[:, :, 1536:1792])
        rot(wkr[:, :, 256:512], w[:, :, 2048:2304])
        ident, IB = make_ident(cx, st)
        xs = [sb(cx, st, "xs", [128, NS, D], F32) for _ in range(2)]
        XS = S.bufs("xs", 2)
        xbf = sb(cx, st, "xbf", [128, NS, D], BF16)
        XBF = S.buf("xbf")
        xT = [sb(cx, st, "xT", [128, KC, TT], BF16) for _ in range(2)]
        XT = S.bufs("xT", 2)
        cs = [sb(cx, st, "cs", [128, 2, TT], F32) for _ in range(2)]
        CS = S.bufs("cs", 2)
        t1 = [sb(cx, st, "t1", [128, TT], F32) for _ in range(2)]
        t2 = [sb(cx, st, "t2", [128, TT], F32) for _ in range(2)]
        T1 = S.bufs("t1", 2)
        T2 = S.bufs("t2", 2)
        og = [sb(cx, st, "og", [128, TT], BF16) for _ in range(2)]
        OG = S.bufs("og", 2)
        vt = [sb(cx, st, "vt", [128, 512], BF16) for _ in range(2)]
        VT = S.bufs("vt", 2)
        gt = [sb(cx, st, "gt", [128, 48], F32) for _ in range(2)]
        GT = S.bufs("gt", 2)
        tp = [ps(cx, st, "tp", [128, D], BF16) for _ in range(2)]
        TP = S.bufs("tp", 2)
        pa = [ps(cx, st, "pa", [128, 512], F32) for _ in range(2)]
        PA = S.bufs("pa", 2)
        pb = [ps(cx, st, "pb", [128, 512], F32) for _ in range(2)]
        PB = S.bufs("pb", 2)
        pt = [ps(cx, st, "pt", [128, 512], F32) for _ in range(2)]
        PT = S.bufs("pt", 2)
        ntile = NT // TT
        xin_v = xin.rearrange("(t s p) d -> t p s d", s=NS, p=128)

        def load(t):
            S.dma("sp", xs[t % 2][:], xin_v[t], writes=[XS[t % 2]], key="xs%d" % (t % 2))
            S.dma("sp", cs[t % 2][:, 0, :], cos_d[:, t * TT:(t + 1) * TT], writes=[CS[t % 2]], key="cs%d" % (t % 2))
            S.dma("sp", cs[t % 2][:, 1, :], sin_d[:, t * TT:(t + 1) * TT], writes=[CS[t % 2]], key="cs%d" % (t % 2))

        units = []
        qf = lambda h: nl.QT[:, h, :]
        for a in range(8):
            units.append((qf, a, w, wqr, a * 128, a * 128))
        for a in range(2):
            units.append(((lambda g: nl.feat("KsT", g)), a, w, wkr, 1536 + a * 128, a * 128))
        for a in range(2):
            units.append(((lambda g: nl.feat("KwT", g)), a, w, wkr, 2048 + a * 128, 256 + a * 128))
        kcf = lambda g: nl.feat("KcT", g)
        vcf = lambda g: nl.feat("VcT", g)
        plain = [(kcf, 0, 1024), (kcf, 1, 1152), (vcf, 0, 1280), (vcf, 1, 1408)]
        load(0)
        cnt = 0
        for t in range(ntile):
            b2 = t % 2
            if t + 1 < ntile:
                load(t + 1)
            emit_xT(cx, xs[b2], XS[b2], xbf, XBF, tp, TP, xT[b2], XT[b2], ident, IB, NS)
            tok = slice(t * TT, (t + 1) * TT)
            for (dst, di, wa, wb, ca, cb) in units:
                q = cnt % 2
                cnt += 1
                for k in range(KC):
                    S.op("pe", lambda k=k, q=q, wa=wa, ca=ca, b2=b2: nc.tensor.matmul(
                        pa[q][:], wa[:, k, ca:ca + 128], xT[b2][:, k, :], start=(k == 0), stop=(k == KC - 1)),
                        reads=[W, WR, XT[b2]], writes=[PA[q]])
                for k in range(KC):
                    S.op("pe", lambda k=k, q=q, wb=wb, cb=cb, b2=b2: nc.tensor.matmul(
                        pb[q][:], wb[:, k, cb:cb + 128], xT[b2][:, k, :], start=(k == 0), stop=(k == KC - 1)),
                        reads=[WR, XT[b2]], writes=[PB[q]])
                S.op("dve", lambda q=q, b2=b2: nc.vector.tensor_tensor(out=t1[q][:], in0=pa[q][:], in1=cs[b2][:, 0, :], op=ALU.mult),
                     reads=[PA[q], CS[b2]], writes=[T1[q]])
                S.op("dve", lambda q=q, b2=b2: nc.vector.tensor_tensor(out=t2[q][:], in0=pb[q][:], in1=cs[b2][:, 1, :], op=ALU.mult),
                     reads=[PB[q], CS[b2]], writes=[T2[q]])
                S.op("pool", lambda q=q: nc.gpsimd.tensor_tensor(out=og[q][:], in0=t1[q][:], in1=t2[q][:], op=ALU.add),
                     reads=[T1[q], T2[q]], writes=[OG[q]])
                for h in range(2):
                    S.dma("sp", dst(2 * di + h)[:, tok], og[q][64 * h:64 * h + 64, :], reads=[OG[q]], key="og%d" % q)
            for (dst, di, c0) in plain:
                q = cnt % 2
                cnt += 1
                for k in range(KC):
                    S.op("pe", lambda k=k, q=q, c0=c0, b2=b2: nc.tensor.matmul(
                        pa[q][:], w[:, k, c0:c0 + 128], xT[b2][:, k, :], start=(k == 0), stop=(k == KC - 1)),
                        reads=[W, XT[b2]], writes=[PA[q]])
                S.op("act", lambda q=q: nc.scalar.copy(out=og[q][:], in_=pa[q][:]), reads=[PA[q]], writes=[OG[q]])
                for h in range(2):
                    S.dma("sp", dst(2 * di + h)[:, tok], og[q][64 * h:64 * h + 64, :], reads=[OG[q]], key="og%d" % q)
            for s in range(NS):
                q = s % 2
                t0 = t * TT + s * 128
                rows = slice(t0, t0 + 128)
                for k in range(KC):
                    S.op("pe", lambda k=k, q=q, s=s, b2=b2: nc.tensor.matmul(
                        pt[q][:, 0:256], xT[b2][:, k, s * 128:(s + 1) * 128], w[:, k, 1792:2048],
                        start=(k == 0), stop=(k == KC - 1)), reads=[W, XT[b2]], writes=[PT[q]])
                S.op("act", lambda q=q: nc.scalar.copy(out=vt[q][:, 0:256], in_=pt[q][:, 0:256]), reads=[PT[q]], writes=[VT[q]])
                S.dma("sp", nl.tok("Vs", t0), vt[q][:, 0:256], reads=[VT[q]], key="vt%d" % q)
                for k in range(KC):
                    S.op("pe", lambda k=k, q=q, s=s, b2=b2: nc.tensor.matmul(
                        pt[q][:, 0:304], xT[b2][:, k, s * 128:(s + 1) * 128], w[:, k, 2304:2608],
                        start=(k == 0), stop=(k == KC - 1)), reads=[W, XT[b2]], writes=[PT[q]])
                S.op("act", lambda q=q: nc.scalar.copy(out=vt[q][:, 256:512], in_=pt[q][:, 0:256]), reads=[PT[q]], writes=[VT[q]])
                S.op("act", lambda q=q: nc.scalar.activation(out=gt[q][:], in_=pt[q][:, 256:304], func=AF.Sigmoid),
                     reads=[PT[q]], writes=[GT[q]])
                S.dma("sp", nl.tok("Vw", t0), vt[q][:, 256:512], reads=[VT[q]], key="vt%d" % q)
                S.dma("sp", nl.gates[rows, :], gt[q][:], reads=[GT[q]], key="gt%d" % q)
        S.flush()


NEGM = -30000.0
MTS = 2
PIPE_DEPTH = 5
DBG = 0


class StopBuild(Exception):
    pass


def dbg_stop(n):
    if DBG == n:
        raise StopBuild()


def phase_attn(cx, al, QT_d, gates_d, O_d, w1k_d, w2k_d, pek_d, w1v_d, w2v_d, pev_d, NB):
    nc, S = cx.nc, cx.S
    NT = NB * 128
    SEQ = 4 * NT
    KT = 4 * NB
    NCMP = SEQ // 16 - 1
    NCP = SEQ // 16
    with ExitStack() as st:
        ident, IB = make_ident(cx, st)
        identf = sb(cx, st, "identf", [128, 128], F32)
        S.dma("sp", identf[:], cx.ident_d[:, :], writes=[IB], key="identf")
        kcmpT = sb(cx, st, "kcmpT", [64, 4, NCP], BF16)
        vcmp = sb(cx, st, "vcmp", [128, NCP // 128, 4, 65], BF16)
        KCMP = S.buf("kcmp")
        VCMP = S.buf("vcmp")
        S.op("pool", lambda: nc.gpsimd.memset(vcmp[:], 0.0), writes=[VCMP])
        S.op("pool", lambda: nc.gpsimd.memset(kcmpT[:], 0.0), writes=[KCMP])
        with ExitStack() as st2:
            w1s = sb(cx, st2, "w1s", [64, 32, 256], BF16)
            w2d = sb(cx, st2, "w2d", [128, 2, 64], BF16)
            w2r = sb(cx, st2, "w2r", [128, 2, 64], BF16)
            peT = sb(cx, st2, "peT", [64, 32], BF16)
            kc = sb(cx, st2, "kc", [64, SEQ], BF16)
            hT = sb(cx, st2, "hTc", [128, 2, NCP], BF16)
            cbias = sb(cx, st2, "cbias", [128, 2], F32)
            ccs = sb(cx, st2, "ccs", [64, 2, NCP], F32)
            t1 = sb(cx, st2, "ct1", [128, 512], F32)
            t2 = sb(cx, st2, "ct2", [128, 512], F32)
            W1S, W2D, PET, KC, HT, CB, CCS, T1, T2 = [S.buf(n) for n in ("w1s", "w2d", "peT", "kc", "hTc", "cb", "ccs", "ct1", "ct2")]
            ph = [ps(cx, st2, "ph", [128, 512], F32) for _ in range(2)]
            PH = S.bufs("ph", 2)
            pc = ps(cx, st2, "pc", [128, 512], F32)
            PC = S.buf("pc")
            pA = ps(cx, st2, "pA", [128, 512], F32)
            pB = ps(cx, st2, "pB", [128, 512], F32)
            PA_, PB_ = S.buf("pA"), S.buf("pB")
            S.dma("sp", ccs[:, 0, :], cx.cosc_d[0:64, :], writes=[CCS], key="ccs")
            S.dma("sp", ccs[:, 1, :], cx.sinc_d[0:64, :], writes=[CCS], key="ccs")
            ntiles = [(n0, min(n0 + 512, NCMP)) for n0 in range(0, NCMP, 512)]
            for which in range(2):
                w1_d, w2_d, pe_d = (w1k_d, w2k_d, pek_d) if which == 0 else (w1v_d, w2v_d, pev_d)
                src_k = "KcT" if which == 0 else "VcT"
                S.dma("pool", w1s[:, :, :], w1_d.rearrange("l d h -> d l h"), writes=[W1S], key="w1s")
                S.dma("pool", peT[:, :], pe_d.rearrange("l d -> d l"), writes=[PET], key="peT", slow=True)
                for hc in range(2):
                    S.dma("pool", w2d[:, hc, :], w2_d[hc * 128:(hc + 1) * 128, :], writes=[W2D], key="w2d")
                if which == 0:
                    S.op("dve", lambda: nc.vector.tensor_scalar(out=w2r[:, :, 0:32], in0=w2d[:, :, 32:64], scalar1=-1.0,
                                                                scalar2=None, op0=ALU.mult), reads=[W2D], writes=[W2D])
                    S.op("dve", lambda: nc.vector.tensor_copy(out=w2r[:, :, 32:64], in_=w2d[:, :, 0:32]),
                         reads=[W2D], writes=[W2D])
                for hc in range(2):
                    for l in range(32):
                        S.op("pe", lambda hc=hc, l=l: nc.tensor.matmul(
                            pc[:, hc:hc + 1], w1s[:, l, hc * 128:(hc + 1) * 128], peT[:, l:l + 1],
                            start=(l == 0), stop=(l == 31)), reads=[W1S, PET], writes=[PC])
                S.op("dve", lambda: nc.vector.tensor_copy(out=cbias[:], in_=pc[:, 0:2]), reads=[PC], writes=[CB])
                for g in range(4):
                    for r_ in range(4):
                        S.dma("sp", kc[:, :].rearrange("p (i r q) -> p i r q", r=4, q=128)[:, :, r_, :],
                              al.feat(src_k, r_, g).rearrange("p (i q) -> p i q", q=128), writes=[KC], key="kc")
                    for hc in range(2):
                        for (n0, n1) in ntiles:
                            cnt = n1 - n0
                            q = (hc + (n0 // 512)) % 2
                            for l in range(32):
                                S.op("pe", lambda hc=hc, l=l, n0=n0, cnt=cnt, q=q: nc.tensor.matmul(
                                    ph[q][:, 0:cnt], w1s[:, l, hc * 128:(hc + 1) * 128],
                                    kc[:, 16 * n0 + l:16 * n0 + l + 16 * (cnt - 1) + 1:16],
                                    start=(l == 0), stop=(l == 31)), reads=[W1S, KC], writes=[PH[q]])
                            S.op("act", lambda hc=hc, n0=n0, cnt=cnt, q=q: nc.scalar.activation(
                                out=hT[:, hc, n0:n0 + cnt], in_=ph[q][:, 0:cnt], func=AF.Gelu_apprx_tanh,
                                bias=cbias[:, hc:hc + 1], scale=1.0), reads=[PH[q], CB], writes=[HT])
                    if which == 0:
                        for (n0, n1) in ntiles:
                            cnt = n1 - n0
                            for hc in range(2):
                                S.op("pe", lambda hc=hc, n0=n0, cnt=cnt: nc.tensor.matmul(
                                    pA[0:64, 0:cnt], w2d[:, hc, :], hT[:, hc, n0:n0 + cnt], start=(hc == 0), stop=(hc == 1)),
                                    reads=[W2D, HT], writes=[PA_])
                            for hc in range(2):
                                S.op("pe", lambda hc=hc, n0=n0, cnt=cnt: nc.tensor.matmul(
                                    pB[0:64, 0:cnt], w2r[:, hc, :], hT[:, hc, n0:n0 + cnt], start=(hc == 0), stop=(hc == 1)),
                                    reads=[W2D, HT], writes=[PB_])
                            S.op("dve", lambda n0=n0, cnt=cnt: nc.vector.tensor_tensor(
                                out=t1[0:64, 0:cnt], in0=pA[0:64, 0:cnt], in1=ccs[:, 0, n0:n0 + cnt], op=ALU.mult),
                                reads=[PA_, CCS], writes=[T1])
                            S.op("dve", lambda n0=n0, cnt=cnt: nc.vector.tensor_tensor(
                                out=t2[0:64, 0:cnt], in0=pB[0:64, 0:cnt], in1=ccs[:, 1, n0:n0 + cnt], op=ALU.mult),
                                reads=[PB_, CCS], writes=[T2])
                            S.op("pool", lambda n0=n0, cnt=cnt, g=g: nc.gpsimd.tensor_tensor(
                                out=kcmpT[:, g, n0:n0 + cnt], in0=t1[0:64, 0:cnt], in1=t2[0:64, 0:cnt], op=ALU.add),
                                reads=[T1, T2], writes=[KCMP])
                    else:
                        for nt in range((NCMP + 127) // 128):
                            c0 = nt * 128
                            cnt = min(128, NCMP - c0)
                            for hc in range(2):
                                S.op("pe", lambda hc=hc, c0=c0, cnt=cnt: nc.tensor.matmul(
                                    pA[0:cnt, 0:64], hT[:, hc, c0:c0 + cnt], w2d[:, hc, 0:64], start=(hc == 0), stop=(hc == 1)),
                                    reads=[W2D, HT], writes=[PA_])
                            S.op("act", lambda nt=nt, cnt=cnt, g=g: nc.scalar.copy(out=vcmp[0:cnt, nt, g, 0:64], in_=pA[0:cnt, 0:64]),
                                 reads=[PA_], writes=[VCMP])
            S.op("pool", lambda: nc.gpsimd.memset(vcmp[:, :, :, 64:65], 1.0), reads=[VCMP], writes=[VCMP])
            S.flush()
        if DBG == 1:
            return
        ks = sb(cx, st, "ks", [64, SEQ], BF16)
        kw = sb(cx, st, "kw", [64, SEQ], BF16)
        vs = sb(cx, st, "vs", [128, KT, 65], BF16)
        vw = sb(cx, st, "vw", [128, KT, 65], BF16)
        KS, KW, VS, VW = S.buf("ks"), S.buf("kw"), S.buf("vs"), S.buf("vw")
        S.op("pool", lambda: nc.gpsimd.memset(vs[:, :, 64:65], 1.0), writes=[VS])
        S.op("pool", lambda: nc.gpsimd.memset(vw[:, :, 64:65], 1.0), writes=[VW])
        cmpmask = sb(cx, st, "cmpmask", [128, 33], BF16)
        keep = sb(cx, st, "keep", [128, 9], F32)
        addt = sb(cx, st, "addt", [128, 9], F32)
        caus = sb(cx, st, "caus", [128, 4, 128], BF16)
        wmask = sb(cx, st, "wmask", [128, 8, 128], BF16)
        TB = S.buf("tables")
        S.dma("pool", cmpmask[:], cx.cmpmask_d[:, :], writes=[TB], key="tb")
        S.dma("sp", keep[:], cx.keep_d[:, :], writes=[TB], key="tb")
        S.dma("sp", addt[:], cx.addt_d[:, :], writes=[TB], key="tb")
        S.dma("pool", caus[:], cx.caus_d[:, :, :], writes=[TB], key="tb")
        S.dma("pool", wmask[:], cx.wmask_d[:, :, :], writes=[TB], key="tb")
        qt = [sb(cx, st, "qt", [64, 4, 128], BF16) for _ in range(2)]
        QTB = S.bufs("qt", 2)
        gts = [sb(cx, st, "gts", [128, 12], F32) for _ in range(2)]
        GTS = S.bufs("gts", 2)
        Pf = [sb(cx, st, "Pf", [128, NCP], F32) for _ in range(2)]
        PF = S.bufs("Pf", 2)
        Pb = [sb(cx, st, "Pb", [128, NCP], BF16) for _ in range(4)]
        PBB = S.bufs("Pb", 4)
        for cb in range(4):
            S.op("pool", lambda cb=cb: nc.gpsimd.memset(Pb[cb][:], 0.0), writes=[PBB[cb]])
        PTs = [sb(cx, st, "PTs", [128, 512], BF16) for _ in range(2)]
        PTS = S.bufs("PTs", 2)
        pns = sb(cx, st, "pns", [128, NCP + 8], F32)
        PNS = S.buf("pns")
        S.op("pool", lambda: nc.gpsimd.memset(pns[:], 0.0), writes=[PNS])
        NMX = max(8 * NB, 16)
        imp = sb(cx, st, "imp", [128, NMX], F32)
        imp4 = sb(cx, st, "imp4", [128, NMX], F32)
        work = sb(cx, st, "work", [128, NMX], F32)
        msk = sb(cx, st, "msk", [128, ((NMX + 127) // 128) * 128], BF16)
        m8a = sb(cx, st, "m8a", [128, 8], F32)
        m8b = sb(cx, st, "m8b", [128, 8], F32)
        thr = sb(cx, st, "thr", [128, 1], F32)
        IMP, MSK = S.buf("imp"), S.buf("msk")
        ebig = sb(cx, st, "ebig", [128, 8192], BF16)
        S.dma("pool", ebig[:], cx.ebig_d[:, :], writes=[TB], key="tb")
        mskT = sb(cx, st, "mskT", [128, 2, 128], BF16)
        MSKX = S.buf("mskT")
        S.op("dve", lambda: nc.vector.memset(msk[:], 0.0), writes=[MSK])
        smx = sb(cx, st, "smx", [128, 4], F32)
        snb = sb(cx, st, "snb", [128, 1], F32)
        rsum = sb(cx, st, "rsum", [128, 4], F32)
        rinv = sb(cx, st, "rinv", [128, 1], F32)
        SMX = S.buf("smx")
        E = [sb(cx, st, "E", [128, 512], BF16) for _ in range(6)]
        EB = S.bufs("E", 6)
        PM = [sb(cx, st, "PM", [128, 512], BF16) for _ in range(6)]
        PMB = S.bufs("PM", 6)
        cm = [sb(cx, st, "cm", [128, 128], BF16) for _ in range(4)]
        CM = S.bufs("cm", 4)
        OT = [sb(cx, st, "OT", [65, 512], F32) for _ in range(3)]
        OTB = S.bufs("OT", 3)
        rs4 = sb(cx, st, "rs4", [128, 4], F32)
        ri4 = sb(cx, st, "ri4", [128, 4], F32)
        fac = sb(cx, st, "fac", [128, 4], F32)
        tmp = sb(cx, st, "tmp", [128, 4, 64], F32)
        oacc = sb(cx, st, "oacc", [128, 4, 64], F32)
        ob = [sb(cx, st, "ob", [128, 4, 64], BF16) for _ in range(2)]
        CMB, OBB = S.buf("comb"), S.bufs("ob", 2)
        bank = [ps(cx, st, "bk", [128, 512], F32) for _ in range(8)]
        BK = S.bufs("bk", 8)
        SS = [0, 1, 7, 3]
        B_PT, B_OC, B_OS, B_OW, B_CT = 2, 3, 4, 5, 6
        ptps = bank[B_PT][:].bitcast(BF16)
        MT = [BK[B_PT], BK[B_CT]]
        ctps = bank[B_CT][:].bitcast(BF16)
        ecnt = [0]

        def job_front(j):
            e = ecnt[0] % 6
            sb_ = SS[ecnt[0] % 4]
            ecnt[0] += 1
            j["e"] = e
            kT, KB, kt, q2 = j["kT"], j["KB"], j["kt"], j["q2"]
            if j["pre"] is not None:
                j["pre"]()
            S.op("pe", lambda sb_=sb_, kt=kt, q2=q2, kT=kT: nc.tensor.matmul(
                bank[sb_][:, :], kT[:, kt * 128:(kt + 1) * 128], qt[q2][:, :, :].rearrange("p h q -> p (h q)"),
                start=True, stop=True), reads=[KB, QTB[q2]], writes=[BK[sb_]])
            S.op("act", lambda e=e, sb_=sb_: nc.scalar.activation(out=E[e][:], in_=bank[sb_][:], func=AF.Exp, scale=0.125),
                 reads=[BK[sb_]], writes=[EB[e]])
            mfn = j["mask"]
            S.op("dve", lambda e=e, mfn=mfn: nc.vector.tensor_tensor(
                out=PM[e][:].rearrange("p (c q) -> p c q", c=4), in0=E[e][:].rearrange("p (c q) -> p c q", c=4),
                in1=mfn().unsqueeze(1).broadcast_to([128, 4, 128]), op=ALU.mult),
                reads=[EB[e]] + j["mbufs"], writes=[PMB[e]])

        def job_back(j):
            e, kt, vT, VB, o_bank, first, last = j["e"], j["kt"], j["vT"], j["VB"], j["o_bank"], j["first"], j["last"]
            S.op("pe", lambda e=e, kt=kt, first=first, last=last, vT=vT, o_bank=o_bank: nc.tensor.matmul(
                bank[o_bank][0:65, :], vT[:, kt, :], PM[e][:], start=first, stop=last),
                reads=[VB, PMB[e]], writes=[BK[o_bank]])

        def run_jobs(jobs, depth=None):
            depth = PIPE_DEPTH if depth is None else depth
            n = len(jobs)
            for step in range(n + depth):
                if step < n:
                    job_front(jobs[step])
                if step >= depth:
                    job_back(jobs[step - depth])

        try:
            ucnt = 0
            for g in range(4):
                for r_ in range(4):
                    S.dma("sp", ks[:].rearrange("p (i r q) -> p i r q", r=4, q=128)[:, :, r_, :],
                          al.feat("KsT", r_, g).rearrange("p (i q) -> p i q", q=128), writes=[KS], key="ks")
                    S.dma("sp", kw[:].rearrange("p (i r q) -> p i r q", r=4, q=128)[:, :, r_, :],
                          al.feat("KwT", r_, g).rearrange("p (i q) -> p i q", q=128), writes=[KW], key="kw")
                    for hf in range(2):
                        isl = slice(hf * NB // 2, (hf + 1) * NB // 2)
                        S.dma("sp", vs[:, :, 0:64].rearrange("q (i r) d -> q i r d", r=4)[:, isl, r_, :],
                              al.tok("Vs", r_, hf)[:, 64 * g:64 * g + 64].rearrange("(i q) d -> q i d", q=128),
                              writes=[VS], key="vs")
                        S.dma("sp", vw[:, :, 0:64].rearrange("q (i r) d -> q i r d", r=4)[:, isl, r_, :],
                              al.tok("Vw", r_, hf)[:, 64 * g:64 * g + 64].rearrange("(i q) d -> q i d", q=128),
                              writes=[VW], key="vw")
                if DBG == 2:
                    S.flush()
                    return
                for i in range(NB):
                    q2 = ucnt % 2
                    ucnt += 1
                    S.dma("sp", qt[q2][:], QT_d[:, 4 * g:4 * g + 4, i * 128:(i + 1) * 128], writes=[QTB[q2]], key="qt%d" % q2)
                    S.dma("sp", gts[q2][:], gates_d[i * 128:(i + 1) * 128, 12 * g:12 * g + 12], writes=[GTS[q2]], key="gts%d" % q2)
                    ncols = 32 * i + 31
                    ctiles = [(c0, min(c0 + 512, ncols)) for c0 in range(0, ncols, 512)]
                    m0 = ncols - 33
                    for cb in range(4):
                        pf = Pf[cb % 2]
                        PFB = PF[cb % 2]
                        for ci, (c0, c1) in enumerate(ctiles):
                            ov0, ov1 = max(c0, m0), c1
                            has_mask = ov1 > ov0
                            S.op("pe", lambda cb=cb, ci=ci, c0=c0, c1=c1, q2=q2, has_mask=has_mask, g=g: nc.tensor.matmul(
                                bank[ci][:, 0:c1 - c0], qt[q2][:, cb, :], kcmpT[:, g, c0:c1],
                                start=True, stop=(not has_mask)), reads=[QTB[q2], KCMP], writes=[BK[ci]])
                            if has_mask:
                                S.op("pe", lambda ci=ci, c0=c0, ov0=ov0, ov1=ov1, m0=m0: nc.tensor.matmul(
                                    bank[ci][:, ov0 - c0:ov1 - c0], ident[:], cmpmask[:, ov0 - m0:ov1 - m0], start=False, stop=True),
                                    reads=[IB, TB], writes=[BK[ci]])
                            S.op("dve", lambda ci=ci, c0=c0, c1=c1: nc.vector.reduce_max(
                                out=smx[:, ci:ci + 1], in_=bank[ci][:, 0:c1 - c0], axis=AX.X), reads=[BK[ci]], writes=[SMX])
                        if len(ctiles) == 2:
                            S.op("dve", lambda: nc.vector.tensor_tensor(out=smx[:, 0:1], in0=smx[:, 0:1], in1=smx[:, 1:2], op=ALU.max),
                                 reads=[SMX], writes=[SMX])
                        S.op("dve", lambda: nc.vector.tensor_scalar(out=snb[:], in0=smx[:, 0:1], scalar1=-1000.0, scalar2=-0.125,
                                                                    op0=ALU.max, op1=ALU.mult), reads=[SMX], writes=[SMX])
                        S.op("dve", lambda: nc.vector.memset(rsum[:, 2:4], 0.0), reads=[SMX], writes=[SMX])
                        for ci, (c0, c1) in enumerate(ctiles):
                            S.op("act", lambda ci=ci, c0=c0, c1=c1, pf=pf: nc.scalar.activation(
                                out=pf[:, c0:c1], in_=bank[ci][:, 0:c1 - c0], func=AF.Exp, bias=snb[:, 0:1], scale=0.125,
                                accum_out=rsum[:, 2 + ci:3 + ci]), reads=[BK[ci], SMX], writes=[PFB, SMX])
                        if len(ctiles) == 2:
                            S.op("dve", lambda: nc.vector.tensor_tensor(out=rsum[:, 2:3], in0=rsum[:, 2:3], in1=rsum[:, 3:4], op=ALU.add),
                                 reads=[SMX], writes=[SMX])
                        S.op("dve", lambda: nc.vector.tensor_scalar(out=rsum[:, 0:1], in0=rsum[:, 2:3], scalar1=1e-30, scalar2=None,
                                                                    op0=ALU.max), reads=[SMX], writes=[SMX])
                        S.op("dve", lambda: nc.vector.reciprocal(out=rinv[:], in_=rsum[:, 0:1]), reads=[SMX], writes=[SMX])
                        if cb == 0:
                            S.op("dve", lambda pf=pf, ncols=ncols: nc.vector.tensor_scalar(
                                out=pns[:, 1:1 + ncols], in0=pf[:, 0:ncols], scalar1=rinv[:, 0:1], scalar2=None, op0=ALU.mult),
                                reads=[PFB, SMX], writes=[PNS])
                        else:
                            S.op("dve", lambda pf=pf, ncols=ncols: nc.vector.scalar_tensor_tensor(
                                out=pns[:, 1:1 + ncols], in0=pf[:, 0:ncols], scalar=rinv[:, 0:1], in1=pns[:, 1:1 + ncols],
                                op0=ALU.mult, op1=ALU.add), reads=[PFB, SMX, PNS], writes=[PNS])
                        S.op("dve", lambda cb=cb, pf=pf, ncols=ncols: nc.vector.tensor_copy(out=Pb[cb][:, 0:ncols], in_=pf[:, 0:ncols]),
                             reads=[PFB], writes=[PBB[cb]])
                        if g > 0 and i == 0:
                            S.op("pool", lambda cb=cb, ncols=ncols: nc.gpsimd.memset(Pb[cb][:, ncols:NCP], 0.0), writes=[PBB[cb]])
                    if g > 0 and i == 0:
                        S.op("pool", lambda ncols=ncols: nc.gpsimd.memset(pns[:, 1 + ncols:NCP + 8], 0.0), reads=[PNS], writes=[PNS])
                    nnt = (ncols + 127) // 128
                    for nt in range(nnt):
                        pv_, PVB_ = (ptps, BK[B_PT]) if nt % 2 == 0 else (ctps, BK[B_CT])
                        for cb in range(4):
                            S.op("pe", lambda nt=nt, cb=cb, pv_=pv_: nc.tensor.transpose(
                                out=pv_[:, cb * 128:(cb + 1) * 128], in_=Pb[cb][:, nt * 128:(nt + 1) * 128], identity=ident[:]),
                                reads=[PBB[cb], IB], writes=[PVB_])
                        S.op("act", lambda nt=nt, pv_=pv_: nc.scalar.copy(out=PTs[nt % 2][:], in_=pv_[:, 0:512]),
                             reads=[PVB_], writes=[PTS[nt % 2]])
                        S.op("pe", lambda nt=nt, nnt=nnt, g=g: nc.tensor.matmul(
                            bank[B_OC][0:65, :], vcmp[:, nt, g, :], PTs[nt % 2][:], start=(nt == 0), stop=(nt == nnt - 1)),
                            reads=[VCMP, PTS[nt % 2]], writes=[BK[B_OC]])
                    S.op("act", lambda: nc.scalar.copy(out=OT[0][:], in_=bank[B_OC][0:65, :]), reads=[BK[B_OC]], writes=[OTB[0]])
                    if DBG == 3:
                        S.flush()
                        return
                    nm = 8 * i + 8
                    Wd = max(nm, 16)
                    S.op("dve", lambda nm=nm: nc.vector.tensor_reduce(
                        out=imp4[:, 0:nm], in_=pns[:, 0:4 * nm].rearrange("p (m r) -> p m r", r=4), axis=AX.X, op=ALU.add),
                        reads=[PNS], writes=[IMP])
                    S.op("dve", lambda nm=nm: nc.vector.tensor_tensor(
                        out=imp[:, 0:nm], in0=imp4[:, 0:nm], in1=pns[:, 4:4 * nm + 1:4], op=ALU.add), reads=[PNS, IMP], writes=[IMP])
                    r0 = max(8 * i - 1, 0)
                    tcol = r0 - (8 * i - 1)
                    S.op("dve", lambda r0=r0, nm=nm, tcol=tcol: nc.vector.tensor_tensor(
                        out=imp[:, r0:nm], in0=imp[:, r0:nm], in1=keep[:, tcol:9], op=ALU.mult), reads=[IMP, TB], writes=[IMP])
                    S.op("dve", lambda r0=r0, nm=nm, tcol=tcol: nc.vector.tensor_tensor(
                        out=imp[:, r0:nm], in0=imp[:, r0:nm], in1=addt[:, tcol:9], op=ALU.add), reads=[IMP, TB], writes=[IMP])
                    S.op("dve", lambda: nc.vector.memset(imp[:, 0:1], 3e9), reads=[IMP], writes=[IMP])
                    if nm < 16:
                        S.op("dve", lambda nm=nm: nc.vector.memset(imp[:, nm:16], -1e9), reads=[IMP], writes=[IMP])
                    S.op("dve", lambda Wd=Wd: nc.vector.max(out=m8a[:], in_=imp[:, 0:Wd]), reads=[IMP], writes=[IMP])
                    S.op("dve", lambda Wd=Wd: nc.vector.match_replace(out=work[:, 0:Wd], in_to_replace=m8a[:], in_values=imp[:, 0:Wd],
                                                                        imm_value=-3e38), reads=[IMP], writes=[IMP])
                    S.op("dve", lambda Wd=Wd: nc.vector.max(out=m8b[:], in_=work[:, 0:Wd]), reads=[IMP], writes=[IMP])
                    S.op("dve", lambda: nc.vector.tensor_reduce(out=thr[:], in_=m8b[:], axis=AX.X, op=ALU.min), reads=[IMP], writes=[IMP])
                    S.op("dve", lambda Wd=Wd: nc.vector.tensor_scalar(out=msk[:, 0:Wd], in0=imp[:, 0:Wd], scalar1=thr[:, 0:1], scalar2=None,
                                                                       op0=ALU.is_ge), reads=[IMP], writes=[MSK])
                    if DBG == 4:
                        S.flush()
                        return
                    nkt = 4 * i + 4
                    for mt in range((nm + 127) // 128):
                        S.op("pe", lambda mt=mt: nc.tensor.transpose(out=ptps[:, mt * 128:(mt + 1) * 128],
                                                                     in_=msk[:, mt * 128:(mt + 1) * 128], identity=ident[:]),
                             reads=[MSK, IB], writes=[BK[B_PT]])
                    for mt in range((nm + 127) // 128):
                        S.op("act", lambda mt=mt: nc.scalar.copy(out=mskT[:, mt, :], in_=ptps[:, mt * 128:(mt + 1) * 128]),
                             reads=[BK[B_PT]], writes=[MSKX])
                    jobs = []
                    for kt in range(nkt):
                        ms = kt % 2
                        mt_ap = bank[B_PT][:, 256:384] if ms == 0 else bank[B_CT][:, 0:128]

                        def pre(kt=kt, ms=ms, mt_ap=mt_ap, i=i):
                            mo = (2 * kt) % 128
                            S.op("pe", lambda: nc.tensor.matmul(mt_ap, ebig[:, mo * 64:mo * 64 + 128], mskT[:, (2 * kt) // 128, :],
                                                                start=True, stop=True), reads=[MSKX, TB], writes=[MT[ms]])
                            if kt >= 4 * i:
                                S.op("dve", lambda: nc.vector.tensor_tensor(out=cm[ms][:], in0=mt_ap, in1=caus[:, kt - 4 * i, :],
                                                                            op=ALU.mult), reads=[MT[ms], TB], writes=[CM[ms]])
                        if kt >= 4 * i:
                            mfn, mb = (lambda ms=ms: cm[ms][:]), [CM[ms]]
                        else:
                            mfn, mb = (lambda mt_ap=mt_ap: mt_ap), [MT[ms]]
                        jobs.append(dict(kT=ks, vT=vs, KB=KS, VB=VS, kt=kt, q2=q2, pre=pre, mask=mfn, mbufs=mb,
                                         o_bank=B_OS, first=(kt == 0), last=(kt == nkt - 1)))
                    wl = [kr for kr in range(-4, 4) if 4 * i + kr >= 0]
                    for idx, kr in enumerate(wl):
                        jobs.append(dict(kT=kw, vT=vw, KB=KW, VB=VW, kt=4 * i + kr, q2=q2, pre=None,
                                         mask=(lambda kr=kr: wmask[:, kr + 4, :]), mbufs=[TB],
                                         o_bank=B_OW, first=(idx == 0), last=(idx == len(wl) - 1)))
                    run_jobs(jobs)
                    if DBG == 6:
                        S.flush()
                        return
                    gv = gts[q2][:].rearrange("q (h t) -> q h t", t=3)
                    for br, ob_ in enumerate((B_OC, B_OS, B_OW)):
                        if br > 0:
                            S.op("act", lambda br=br, ob_=ob_: nc.scalar.copy(out=OT[br][:], in_=bank[ob_][0:65, :]),
                                 reads=[BK[ob_]], writes=[OTB[br]])
                        for cb in range(4):
                            S.op("pe", lambda br=br, cb=cb: nc.tensor.transpose(
                                out=bank[B_CT][:, cb * 65:(cb + 1) * 65], in_=OT[br][0:65, cb * 128:(cb + 1) * 128],
                                identity=identf[0:65, 0:65]), reads=[OTB[br], IB], writes=[BK[B_CT]])
                        ctv = bank[B_CT][:, 0:260].rearrange("q (c e) -> q c e", e=65)
                        S.op("dve", lambda ctv=ctv: nc.vector.tensor_scalar(out=rs4[:], in0=ctv[:, :, 64], scalar1=1e-30, scalar2=None,
                                                                            op0=ALU.max), reads=[BK[B_CT]], writes=[CMB])
                        S.op("dve", lambda: nc.vector.reciprocal(out=ri4[:], in_=rs4[:]), reads=[CMB], writes=[CMB])
                        S.op("dve", lambda br=br, gv=gv: nc.vector.tensor_tensor(
                            out=fac[:], in0=ri4[:], in1=gv[:, :, br], op=ALU.mult), reads=[CMB, GTS[q2]], writes=[CMB])
                        dst = oacc if br == 0 else tmp
                        S.op("dve", lambda ctv=ctv, dst=dst: nc.vector.tensor_tensor(
                            out=dst[:], in0=ctv[:, :, 0:64], in1=fac[:].unsqueeze(2).broadcast_to([128, 4, 64]), op=ALU.mult),
                            reads=[BK[B_CT], CMB], writes=[CMB])
                        if br > 0:
                            S.op("dve", lambda: nc.vector.tensor_tensor(out=oacc[:], in0=oacc[:], in1=tmp[:], op=ALU.add),
                                 reads=[CMB], writes=[CMB])
                    S.op("pool", lambda q2=q2: nc.gpsimd.tensor_copy(out=ob[q2][:], in_=oacc[:]), reads=[CMB], writes=[OBB[q2]])
                    S.dma("sp", O_d[i * 128:(i + 1) * 128, 256 * g:256 * (g + 1)], ob[q2][:].rearrange("q h d -> q (h d)"),
                          reads=[OBB[q2]], key="ob%d" % q2)
        except StopBuild:
            pass
        S.flush()


def phase_proj(cx, a_d, xin, xout, w_d, g_d, b_d, NT, c_res):
    nc, S = cx.nc, cx.S
    KC = 8
    eps = LN_EPS / (ALPHA * ALPHA)
    with ExitStack() as st:
        wo = sb(cx, st, "wo", [128, KC, D], BF16)
        WO = S.buf("wo")
        for k in range(KC):
            S.dma("pool", wo[:, k, :], w_d[k * 128:(k + 1) * 128, :], writes=[WO], key="wo")
        gb, GB = load_gb(cx, st, g_d, b_d, "gbp")
        ident, IB = make_ident(cx, st)
        at = [sb(cx, st, "at", [128, D], BF16) for _ in range(2)]
        AT = S.bufs("at", 2)
        xs = [sb(cx, st, "xs", [128, D], F32) for _ in range(2)]
        XS = S.bufs("xs", 2)
        aT = [sb(cx, st, "aT", [128, KC, 128], BF16) for _ in range(2)]
        ATT = S.bufs("aT", 2)
        z = [sb(cx, st, "z", [128, D], F32) for _ in range(2)]
        Z = S.bufs("z", 2)
        st_t = [sb(cx, st, "st", [128, 2, 6], F32) for _ in range(2)]
        mv_t = [sb(cx, st, "mv", [128, 2], F32) for _ in range(2)]
        sd_t = [sb(cx, st, "sd", [128, 1], F32) for _ in range(2)]
        rs_t = [sb(cx, st, "rs", [128, 1], F32) for _ in range(2)]
        STB = S.bufs("stb", 2)
        tp = [ps(cx, st, "tp", [128, D], BF16) for _ in range(2)]
        TP = S.bufs("tp", 2)
        op_ = [ps(cx, st, "o", [128, 512], F32) for _ in range(4)]
        OP = S.bufs("o", 4)
        nchunk = NT // 128

        def load(c):
            S.dma("sp", at[c % 2][:], a_d[c * 128:(c + 1) * 128, :], writes=[AT[c % 2]], key="at%d" % (c % 2))
            S.dma("sp", xs[c % 2][:], xin[c * 128:(c + 1) * 128, :], writes=[XS[c % 2]], key="xsp%d" % (c % 2))

        load(0)
        for c in range(nchunk):
            c2 = c % 2
            if c + 1 < nchunk:
                load(c + 1)
            for k in range(KC):
                S.op("pe", lambda k=k, c2=c2: nc.tensor.transpose(out=tp[c2][:, k * 128:(k + 1) * 128],
                                                                   in_=at[c2][:, k * 128:(k + 1) * 128], identity=ident[:]),
                     reads=[AT[c2], IB], writes=[TP[c2]])
            S.op("act", lambda c2=c2: nc.scalar.copy(out=aT[c2][:], in_=tp[c2][:].rearrange("p (k t) -> p k t", k=KC)),
                 reads=[TP[c2]], writes=[ATT[c2]])
            for n in range(2):
                o_ = 2 * c2 + n
                for k in range(KC):
                    S.op("pe", lambda n=n, k=k, c2=c2, o_=o_: nc.tensor.matmul(
                        op_[o_][:], aT[c2][:, k, :], wo[:, k, n * 512:(n + 1) * 512], start=(k == 0), stop=(k == KC - 1)),
                        reads=[ATT[c2], WO], writes=[OP[o_]])
                S.op("dve", lambda n=n, c2=c2, o_=o_: nc.vector.scalar_tensor_tensor(
                    out=z[c2][:, n * 512:(n + 1) * 512], in0=op_[o_][:], scalar=c_res, in1=xs[c2][:, n * 512:(n + 1) * 512],
                    op0=ALU.mult, op1=ALU.add), reads=[OP[o_], XS[c2]], writes=[Z[c2]])
            ln_epilogue(cx, z[c2], Z[c2], gb, GB, st_t[c2], mv_t[c2], sd_t[c2], rs_t[c2], STB[c2], eps)
            S.dma("sp", xout[c * 128:(c + 1) * 128, :], z[c2][:], reads=[Z[c2]], key="zp%d" % c2)
        S.flush()


def build_test_ffn(NT):
    nc = bass.Bass("TRN2", target_bir_lowering=False)
    with ExitStack() as stack:
        cx = Ctx(nc, stack)
        x = nc.dram_tensor("x", [NT, D], F32, kind="ExternalInput").ap()
        w_in = nc.dram_tensor("w_in", [D, 2 * DFF], F32, kind="ExternalInput").ap()
        w_out = nc.dram_tensor("w_out", [DFF, D], F32, kind="ExternalInput").ap()
        g = nc.dram_tensor("g", [D], F32, kind="ExternalInput").ap()
        b = nc.dram_tensor("b", [D], F32, kind="ExternalInput").ap()
        cx.ident_d = nc.dram_tensor("ident", [128, 128], F32, kind="ExternalInput").ap()
        y = nc.dram_tensor("y", [NT, D], F32, kind="ExternalOutput").ap()
        phase_ffn(cx, x, y, w_in, w_out, g, b, NT, "f")
        print("instructions:", cx.S.n_inst)
    return nc


def build_test_gmlp(NT):
    nc = bass.Bass("TRN2", target_bir_lowering=False)
    with ExitStack() as stack:
        cx = Ctx(nc, stack)
        x = nc.dram_tensor("x", [NT, D], F32, kind="ExternalInput").ap()
        w_in = nc.dram_tensor("w_in", [D, 2 * GW], F32, kind="ExternalInput").ap()
        lng = nc.dram_tensor("lng", [GW], F32, kind="ExternalInput").ap()
        lnb = nc.dram_tensor("lnb", [GW], F32, kind="ExternalInput").ap()
        ws = nc.dram_tensor("ws", [16, 128, 128], F32, kind="ExternalInput").ap()
        bs = nc.dram_tensor("bs", [16, 128], F32, kind="ExternalInput").ap()
        w_out = nc.dram_tensor("w_out", [GW, D], F32, kind="ExternalInput").ap()
        g = nc.dram_tensor("g", [D], F32, kind="ExternalInput").ap()
        b = nc.dram_tensor("b", [D], F32, kind="ExternalInput").ap()
        cx.ident_d = nc.dram_tensor("ident", [128, 128], F32, kind="ExternalInput").ap()
        cx.tril_d = nc.dram_tensor("tril", [128, 128], F32, kind="ExternalInput").ap()
        uT_d = nc.dram_tensor("uT_d", [GW, NT], BF16, kind="Internal").ap()
        vln_d = nc.dram_tensor("vln_d", [NT, GW], BF16, kind="Internal").ap()
        y = nc.dram_tensor("y", [NT, D], F32, kind="ExternalOutput").ap()
        phase_g1(cx, x, uT_d, vln_d, w_in, lng, lnb, NT)
        phase_g2(cx, x, y, uT_d, vln_d, ws, bs, w_out, g, b, NT)
        print("instructions:", cx.S.n_inst)
    return nc


def nsa_dram(nc, NT, kind_local="Internal"):
    d = {}
    d["QT"] = nc.dram_tensor("QT_d", [64, 16, NT], BF16, kind=kind_local).ap()
    d["KsT"] = nc.dram_tensor("KsT_d", [64, 4, NT], BF16, kind=kind_local).ap()
    d["KwT"] = nc.dram_tensor("KwT_d", [64, 4, NT], BF16, kind=kind_local).ap()
    d["KcT"] = nc.dram_tensor("KcT_d", [64, 4, NT], BF16, kind=kind_local).ap()
    d["VcT"] = nc.dram_tensor("VcT_d", [64, 4, NT], BF16, kind=kind_local).ap()
    d["Vs"] = nc.dram_tensor("Vs_d", [NT, 256], BF16, kind=kind_local).ap()
    d["Vw"] = nc.dram_tensor("Vw_d", [NT, 256], BF16, kind=kind_local).ap()
    d["gates"] = nc.dram_tensor("gates_d", [NT, 48], F32, kind=kind_local).ap()
    return d


def local_from_dict(d):
    return NsaLocal(lambda k, g: d[k][:, g, :], lambda k, t0: d[k][t0:t0 + 128, :], d["QT"], d["gates"])


def gathered_from_dict(al, NT):
    return NsaGathered(lambda k, r, g: al[k][r][:, g, :], lambda k, r, hf: al[k][r][hf * NT // 2:(hf + 1) * NT // 2, :])


def build_test_n1(NT):
    nc = bass.Bass("TRN2", target_bir_lowering=False)
    with ExitStack() as stack:
        cx = Ctx(nc, stack)
        x = nc.dram_tensor("x", [NT, D], F32, kind="ExternalInput").ap()
        w_in = nc.dram_tensor("w_in", [D, NSA_COLS], F32, kind="ExternalInput").ap()
        cos_d = nc.dram_tensor("cos", [128, NT], F32, kind="ExternalInput").ap()
        sin_d = nc.dram_tensor("sin", [128, NT], F32, kind="ExternalInput").ap()
        cx.ident_d = nc.dram_tensor("ident", [128, 128], F32, kind="ExternalInput").ap()
        d = nsa_dram(nc, NT, "ExternalOutput")
        phase_n1(cx, x, w_in, cos_d, sin_d, local_from_dict(d), NT)
        print("instructions:", cx.S.n_inst)
    return nc


def rope_tables(pos):
    half = HD // 2
    freq = (10000.0 ** (-np.arange(half, dtype=np.float32) / half)).astype(np.float32)
    ang = pos.astype(np.float32)[None, :] * freq[:, None]
    c = np.cos(ang).astype(np.float32)
    s_ = np.sin(ang).astype(np.float32)
    return np.ascontiguousarray(np.tile(c, (4, 1))), np.ascontiguousarray(np.tile(s_, (4, 1)))


def attn_tables(r, NB):
    NT = NB * 128
    SEQ = 4 * NT
    NCP = SEQ // 16
    q = np.arange(128)
    t = {}
    pos = ((4 * np.arange(NB)[:, None] + r) * 128 + q[None, :]).reshape(-1)
    t["cos"], t["sin"] = rope_tables(pos)
    cpos = 16 * np.arange(NCP) + 31
    t["cosc"], t["sinc"] = rope_tables(cpos)
    m = np.arange(-2, 31)
    vis = m[None, :] <= (8 * r + np.floor((q[:, None] - 31) / 16.0))
    t["cmpmask"] = np.where(vis, 0.0, NEGM).astype(np.float32)
    mrel = np.arange(-1, 8)[None, :]
    cur = (2 * r + (q >= 64).astype(np.int64))[:, None]
    keep = np.ones((128, 9), np.float32)
    addt = np.zeros((128, 9), np.float32)
    fut = mrel > cur
    keep[fut] = 0.0
    addt[fut] = -1e9
    c1 = mrel == cur - 1
    keep[c1] = 0.0
    addt[c1] = 1e9
    c0 = mrel == cur
    keep[c0] = 0.0
    addt[c0] = 2e9
    t["keep"], t["addt"] = keep, addt
    k = np.arange(128)[:, None, None]
    kr = np.arange(4)[None, :, None]
    qq = q[None, None, :]
    t["caus"] = ((128 * (kr - r) + k) <= qq).astype(np.float32)
    kr8 = np.arange(-4, 4)[None, :, None]
    dl = r - kr8
    wm = np.where(dl == 0, k <= qq, np.where((dl >= 1) & (dl <= 3), True, np.where(dl == 4, k > qq, False)))
    t["wmask"] = np.broadcast_to(wm, (128, 8, 128)).astype(np.float32)
    t["ident"] = np.eye(128, dtype=np.float32)
    t["ebig"] = (np.arange(8192)[None, :] // 64 == np.arange(128)[:, None]).astype(np.float32)
    t["tril"] = np.tril(np.ones((128, 128), np.float32))
    return t


def declare_tables(cx, nc, NB):
    NT = NB * 128
    NCP = 4 * NT // 16
    cx.ident_d = nc.dram_tensor("ident", [128, 128], F32, kind="ExternalInput").ap()
    cx.tril_d = nc.dram_tensor("tril", [128, 128], F32, kind="ExternalInput").ap()
    cx.cos_d = nc.dram_tensor("cos", [128, NT], F32, kind="ExternalInput").ap()
    cx.sin_d = nc.dram_tensor("sin", [128, NT], F32, kind="ExternalInput").ap()
    cx.cosc_d = nc.dram_tensor("cosc", [128, NCP], F32, kind="ExternalInput").ap()
    cx.sinc_d = nc.dram_tensor("sinc", [128, NCP], F32, kind="ExternalInput").ap()
    cx.cmpmask_d = nc.dram_tensor("cmpmask", [128, 33], F32, kind="ExternalInput").ap()
    cx.keep_d = nc.dram_tensor("keep", [128, 9], F32, kind="ExternalInput").ap()
    cx.addt_d = nc.dram_tensor("addt", [128, 9], F32, kind="ExternalInput").ap()
    cx.caus_d = nc.dram_tensor("caus", [128, 4, 128], F32, kind="ExternalInput").ap()
    cx.wmask_d = nc.dram_tensor("wmask", [128, 8, 128], F32, kind="ExternalInput").ap()
    cx.ebig_d = nc.dram_tensor("ebig", [128, 8192], F32, kind="ExternalInput").ap()


def gathered_dram(nc, NT, kind):
    al = {}
    al["KsT"] = nc.dram_tensor("KsT_all", [4, 64, 4, NT], BF16, kind=kind).ap()
    al["KwT"] = nc.dram_tensor("KwT_all", [4, 64, 4, NT], BF16, kind=kind).ap()
    al["KcT"] = nc.dram_tensor("KcT_all", [4, 64, 4, NT], BF16, kind=kind).ap()
    al["VcT"] = nc.dram_tensor("VcT_all", [4, 64, 4, NT], BF16, kind=kind).ap()
    al["Vs"] = nc.dram_tensor("Vs_all", [4, NT, 256], BF16, kind=kind).ap()
    al["Vw"] = nc.dram_tensor("Vw_all", [4, NT, 256], BF16, kind=kind).ap()
    return al


def build_test_attn(NB):
    NT = NB * 128
    nc = bass.Bass("TRN2", target_bir_lowering=False)
    with ExitStack() as stack:
        cx = Ctx(nc, stack)
        declare_tables(cx, nc, NB)
        QT = nc.dram_tensor("QT_d", [64, 16, NT], BF16, kind="ExternalInput").ap()
        gates = nc.dram_tensor("gates_d", [NT, 48], F32, kind="ExternalInput").ap()
        al = gathered_dram(nc, NT, "ExternalInput")
        w1k = nc.dram_tensor("w1k", [32, 64, 256], F32, kind="ExternalInput").ap()
        w2k = nc.dram_tensor("w2k", [256, 64], F32, kind="ExternalInput").ap()
        pek = nc.dram_tensor("pek", [32, 64], F32, kind="ExternalInput").ap()
        w1v = nc.dram_tensor("w1v", [32, 64, 256], F32, kind="ExternalInput").ap()
        w2v = nc.dram_tensor("w2v", [256, 64], F32, kind="ExternalInput").ap()
        pev = nc.dram_tensor("pev", [32, 64], F32, kind="ExternalInput").ap()
        O_d = nc.dram_tensor("O_d", [NT, 1024], BF16, kind="ExternalOutput").ap()
        phase_attn(cx, gathered_from_dict(al, NT), QT, gates, O_d, w1k, w2k, pek, w1v, w2v, pev, NB)
        print("instructions:", cx.S.n_inst)
    return nc


L0_W = ["l0_ffn1_w_in", "l0_ffn1_w_out", "l0_ln1_g", "l0_ln1_b", "l0_gm_w_in", "l0_gm_ln_g", "l0_gm_ln_b", "l0_gm_w_s",
        "l0_gm_b_s", "l0_gm_w_out", "l0_ln2_g", "l0_ln2_b", "l0_ffn2_w_in", "l0_ffn2_w_out", "l0_ln3_g", "l0_ln3_b"]
L1A_W = ["l1_ffn1_w_in", "l1_ffn1_w_out", "l1_ln1_g", "l1_ln1_b", "l1_nsa_w_in"]
L1B_W = ["l1_nsa_cmp_pe_k", "l1_nsa_cmp_w1_k", "l1_nsa_cmp_w2_k", "l1_nsa_cmp_pe_v", "l1_nsa_cmp_w1_v", "l1_nsa_cmp_w2_v",
         "l1_nsa_w_out", "l1_ln2_g", "l1_ln2_b", "l1_ffn2_w_in", "l1_ffn2_w_out", "l1_ln3_g", "l1_ln3_b"]
W_SHAPES = {
    "ffn1_w_in": [D, 2 * DFF], "ffn2_w_in": [D, 2 * DFF], "ffn1_w_out": [DFF, D], "ffn2_w_out": [DFF, D],
    "gm_w_in": [D, 2 * GW], "gm_ln_g": [GW], "gm_ln_b": [GW], "gm_w_s": [16, 128, 128], "gm_b_s": [16, 128], "gm_w_out": [GW, D],
    "nsa_w_in": [D, NSA_COLS], "nsa_cmp_pe_k": [32, 64], "nsa_cmp_w1_k": [32, 64, 256], "nsa_cmp_w2_k": [256, 64],
    "nsa_cmp_pe_v": [32, 64], "nsa_cmp_w1_v": [32, 64, 256], "nsa_cmp_w2_v": [256, 64], "nsa_w_out": [D, D],
}


def wshape(name):
    base = name[3:]
    if base in W_SHAPES:
        return W_SHAPES[base]
    return [D]


def declare_w(nc, names):
    return {n: nc.dram_tensor(n, wshape(n), F32, kind="ExternalInput").ap() for n in names}


def emit_part_a(cx, nc, w, x, NT, xa, xb, uT_d, vln_d, nd):
    phase_ffn(cx, x, xa, w["l0_ffn1_w_in"], w["l0_ffn1_w_out"], w["l0_ln1_g"], w["l0_ln1_b"], NT, "a")
    phase_g1(cx, xa, uT_d, vln_d, w["l0_gm_w_in"], w["l0_gm_ln_g"], w["l0_gm_ln_b"], NT)
    phase_g2(cx, xa, xb, uT_d, vln_d, w["l0_gm_w_s"], w["l0_gm_b_s"], w["l0_gm_w_out"], w["l0_ln2_g"], w["l0_ln2_b"], NT)
    phase_ffn(cx, xb, xa, w["l0_ffn2_w_in"], w["l0_ffn2_w_out"], w["l0_ln3_g"], w["l0_ln3_b"], NT, "b")
    phase_ffn(cx, xa, xb, w["l1_ffn1_w_in"], w["l1_ffn1_w_out"], w["l1_ln1_g"], w["l1_ln1_b"], NT, "c")
    phase_n1(cx, xb, w["l1_nsa_w_in"], cx.cos_d, cx.sin_d, nd, NT)
    return xb


def emit_part_b(cx, nc, w, xmid, y, NT, NB, al, QT, gates, O_d, xa):
    phase_attn(cx, al, QT, gates, O_d, w["l1_nsa_cmp_w1_k"], w["l1_nsa_cmp_w2_k"], w["l1_nsa_cmp_pe_k"],
               w["l1_nsa_cmp_w1_v"], w["l1_nsa_cmp_w2_v"], w["l1_nsa_cmp_pe_v"], NB)
    phase_proj(cx, O_d, xmid, xa, w["l1_nsa_w_out"], w["l1_ln2_g"], w["l1_ln2_b"], NT, 1.0 / ALPHA)
    phase_ffn(cx, xa, y, w["l1_ffn2_w_in"], w["l1_ffn2_w_out"], w["l1_ln3_g"], w["l1_ln3_b"], NT, "d")


def build_a(NB):
    NT = NB * 128
    nc = bass.Bass("TRN2", target_bir_lowering=False)
    with ExitStack() as stack:
        cx = Ctx(nc, stack)
        declare_tables(cx, nc, NB)
        w = declare_w(nc, L0_W + L1A_W)
        x = nc.dram_tensor("x", [NT, D], F32, kind="ExternalInput").ap()
        xa = nc.dram_tensor("xa", [NT, D], F32, kind="Internal").ap()
        xb = nc.dram_tensor("xmid", [NT, D], F32, kind="ExternalOutput").ap()
        uT_d = nc.dram_tensor("uT_d", [GW, NT], BF16, kind="Internal").ap()
        vln_d = nc.dram_tensor("vln_d", [NT, GW], BF16, kind="Internal").ap()
        nd = local_from_dict(nsa_dram(nc, NT, "ExternalOutput"))
        emit_part_a(cx, nc, w, x, NT, xa, xb, uT_d, vln_d, nd)
        print("part A instructions:", cx.S.n_inst)
    return nc


def build_b(NB):
    NT = NB * 128
    nc = bass.Bass("TRN2", target_bir_lowering=False)
    with ExitStack() as stack:
        cx = Ctx(nc, stack)
        declare_tables(cx, nc, NB)
        w = declare_w(nc, L1B_W)
        xmid = nc.dram_tensor("xmid", [NT, D], F32, kind="ExternalInput").ap()
        QT = nc.dram_tensor("QT_d", [64, 16, NT], BF16, kind="ExternalInput").ap()
        gates = nc.dram_tensor("gates_d", [NT, 48], F32, kind="ExternalInput").ap()
        al = gathered_dram(nc, NT, "ExternalInput")
        O_d = nc.dram_tensor("O_d", [NT, D], BF16, kind="Internal").ap()
        xa = nc.dram_tensor("xa", [NT, D], F32, kind="Internal").ap()
        y = nc.dram_tensor("y", [NT, D], F32, kind="ExternalOutput").ap()
        emit_part_b(cx, nc, w, xmid, y, NT, NB, gathered_from_dict(al, NT), QT, gates, O_d, xa)
        print("part B instructions:", cx.S.n_inst)
    return nc


TABLE_KEYS = ("ident", "tril", "cos", "sin", "cosc", "sinc", "cmpmask", "keep", "addt", "caus", "wmask", "ebig")
GATHER_KEYS = ("KsT", "KwT", "KcT", "VcT", "Vs", "Vw")


def run_unfused(inputs, NB):
    NT = NB * 128
    x = np.asarray(inputs["x"], dtype=np.float32)
    B = x.shape[0]
    tabs = [attn_tables(c % 4, NB) for c in range(NCORES)]
    wts = {k: np.ascontiguousarray(np.asarray(v, dtype=np.float32)) for k, v in inputs.items() if k != "x"}
    in_a = []
    for c in range(NCORES):
        b, r = c // 4, c % 4
        m = {k: tabs[c][k] for k in TABLE_KEYS}
        m["x"] = np.ascontiguousarray(x[b].reshape(NB, 4, 128, D)[:, r].reshape(NT, D))
        for k in L0_W + L1A_W:
            m[k] = wts[k]
        in_a.append(m)
    res_a = run_bass_kernel_spmd(build_a(NB), in_a, core_ids=list(range(NCORES))).results
    in_b = []
    for c in range(NCORES):
        b = c // 4
        m = {k: tabs[c][k] for k in TABLE_KEYS}
        m["xmid"] = res_a[c]["xmid"]
        m["QT_d"] = res_a[c]["QT_d"]
        m["gates_d"] = res_a[c]["gates_d"]
        for k in GATHER_KEYS:
            m[k + "_all"] = np.ascontiguousarray(np.stack([np.asarray(res_a[4 * b + rr][k + "_d"]) for rr in range(4)]))
        for k in L1B_W:
            m[k] = wts[k]
        in_b.append(m)
    res_b = run_bass_kernel_spmd(build_b(NB), in_b, core_ids=list(range(NCORES))).results
    out = np.zeros((B, 4 * NT, D), np.float32)
    for c in range(NCORES):
        b, r = c // 4, c % 4
        out[b].reshape(NB, 4, 128, D)[:, r] = np.asarray(res_b[c]["y"]).reshape(NB, 128, D)
    return out


def build_fused(NB):
    NT = NB * 128
    nc = bass.Bass("TRN2", target_bir_lowering=False)
    with ExitStack() as stack:
        cx = Ctx(nc, stack)
        declare_tables(cx, nc, NB)
        w = declare_w(nc, L0_W + L1A_W + L1B_W)
        x = nc.dram_tensor("x", [NT, D], F32, kind="ExternalInput").ap()
        y = nc.dram_tensor("y", [NT, D], F32, kind="ExternalOutput").ap()
        xa = nc.dram_tensor("xa", [NT, D], F32, kind="Internal").ap()
        xb = nc.dram_tensor("xb", [NT, D], F32, kind="Internal").ap()
        uT_d = nc.dram_tensor("uT_d", [GW, NT], BF16, kind="Internal").ap()
        vln_d = nc.dram_tensor("vln_d", [NT, GW], BF16, kind="Internal").ap()
        O_d = nc.dram_tensor("O_d", [NT, D], BF16, kind="Internal").ap()
        QT = nc.dram_tensor("QT_d", [64, 16, NT], BF16, kind="Internal").ap()
        gates = nc.dram_tensor("gates_d", [NT, 48], F32, kind="Internal").ap()
        loc, gat = {}, {}
        for k in GATHER_KEYS:
            for hf in range(2):
                loc[k, hf] = nc.dram_tensor("%s_loc%d" % (k, hf), [128, NT], BF16, kind="Internal").ap()
                gat[k, hf] = nc.dram_tensor("%s_gat%d" % (k, hf), [4 * 128, NT], BF16, kind="Internal").ap()
        tokv = lambda a: a.rearrange("r (x c) -> (r x) c", c=256)
        H = NT // 2
        nd = NsaLocal(lambda k, g: loc[k, g // 2][(g % 2) * 64:(g % 2) * 64 + 64, :],
                      lambda k, t0: tokv(loc[k, t0 // H])[t0 % H:t0 % H + 128, :], QT, gates)
        al = NsaGathered(lambda k, r, g: gat[k, g // 2][r * 128 + (g % 2) * 64:r * 128 + (g % 2) * 64 + 64, :],
                         lambda k, r, hf: tokv(gat[k, hf][r * 128:(r + 1) * 128, :]))
        xmid = emit_part_a(cx, nc, w, x, NT, xa, xb, uT_d, vln_d, nd)
        for k in ("KcT", "VcT"):
            for hf in range(2):
                cx.S.cc("AllGather", [[0, 1, 2, 3], [4, 5, 6, 7]], loc[k, hf], gat[k, hf], key="cc_%s%d" % (k, hf))
        cx.S.flush()
        for k in ("KsT", "KwT", "Vs", "Vw"):
            for hf in range(2):
                cx.S.cc("AllGather", [[0, 1, 2, 3], [4, 5, 6, 7]], loc[k, hf], gat[k, hf], key="cc_%s%d" % (k, hf))
        emit_part_b(cx, nc, w, xmid, y, NT, NB, al, QT, gates, O_d, xa)
        print("fused instructions:", cx.S.n_inst)
    return nc


def run_fused(inputs, NB):
    NT = NB * 128
    x = np.asarray(inputs["x"], dtype=np.float32)
    B = x.shape[0]
    wts = {k: np.ascontiguousarray(np.asarray(v, dtype=np.float32)) for k, v in inputs.items() if k != "x"}
    in_maps = []
    for c in range(NCORES):
        b, r = c // 4, c % 4
        tabs = attn_tables(r, NB)
        m = {k: tabs[k] for k in TABLE_KEYS}
        m["x"] = np.ascontiguousarray(x[b].reshape(NB, 4, 128, D)[:, r].reshape(NT, D))
        for k in L0_W + L1A_W + L1B_W:
            m[k] = wts[k]
        in_maps.append(m)
    res = run_bass_kernel_spmd(build_fused(NB), in_maps, core_ids=list(range(NCORES))).results
    out = np.zeros((B, 4 * NT, D), np.float32)
    for c in range(NCORES):
        b, r = c // 4, c % 4
        out[b].reshape(NB, 4, 128, D)[:, r] = np.asarray(res[c]["y"]).reshape(NB, 128, D)
    return out


def kernel(**inputs):
    return run_fused(inputs, 32)
```

```python
import math
from contextlib import ExitStack

import numpy as np
import concourse.bass as bass
import concourse.mybir as mybir
from concourse.bass_utils import run_bass_kernel_spmd

F32 = mybir.dt.float32
BF16 = mybir.dt.bfloat16
AF = mybir.ActivationFunctionType
ALU = mybir.AluOpType
AX = mybir.AxisListType

D = 1024
DFF = 2816
DEPTH = 2
ALPHA = (2 * DEPTH) ** 0.25
LN_EPS = 1e-5
NCORES = 8


class Buf:
    __slots__ = ("name", "writers", "dma_writers", "readers", "dma_readers")

    def __init__(self, name):
        self.name = name
        self.writers = {}
        self.dma_writers = []
        self.readers = {}
        self.dma_readers = []


class Op:
    __slots__ = ("eng", "fn", "deps", "is_dma", "dsem", "dcount", "signal", "sigval", "emitted")

    def __init__(self, eng, fn, is_dma):
        self.eng = eng
        self.fn = fn
        self.deps = []
        self.is_dma = is_dma
        self.dsem = None
        self.dcount = 0
        self.signal = False
        self.sigval = 0
        self.emitted = False


class Sched:
    def __init__(self, nc, stack):
        self.nc = nc
        self.engs = {"pe": nc.tensor, "act": nc.scalar, "dve": nc.vector, "pool": nc.gpsimd, "sp": nc.sync}
        self.esem = {e: stack.enter_context(nc.semaphore("es_" + e)) for e in self.engs}
        self.stack = stack
        self.pending = []
        self.sigcount = {e: 0 for e in self.engs}
        self.waited = {e: {} for e in self.engs}
        self.dsems = {}
        self.n_inst = 0

    def buf(self, name):
        return Buf(name)

    def bufs(self, name, n):
        return [Buf("%s%d" % (name, i)) for i in range(n)]

    def _add_dep(self, op, p):
        if p is op:
            return
        if (not p.is_dma) and (not op.is_dma) and p.eng == op.eng and op.eng == "pe":
            return
        op.deps.append(p)

    def _track(self, op, reads, writes):
        for b in reads:
            for p in b.writers.values():
                self._add_dep(op, p)
            for p in b.dma_writers:
                self._add_dep(op, p)
        for b in writes:
            if b.readers or b.dma_readers:
                for p in b.readers.values():
                    self._add_dep(op, p)
                for p in b.dma_readers:
                    self._add_dep(op, p)
                for p in b.writers.values():
                    self._add_dep(op, p)
                for p in b.dma_writers:
                    self._add_dep(op, p)
                b.readers = {}
                b.dma_readers = []
                b.writers = {}
                b.dma_writers = []
            else:
                for e, p in b.writers.items():
                    if e != op.eng or op.is_dma:
                        self._add_dep(op, p)
                for p in b.dma_writers:
                    self._add_dep(op, p)
        for b in reads:
            if op.is_dma:
                b.dma_readers.append(op)
            else:
                b.readers[op.eng] = op
        for b in writes:
            if op.is_dma:
                b.dma_writers.append(op)
            else:
                b.writers[op.eng] = op

    def op(self, eng, fn, reads=(), writes=()):
        o = Op(eng, fn, False)
        self._track(o, reads, writes)
        self.pending.append(o)
        return o

    def dma(self, eng, out, in_, reads=(), writes=(), key=None, slow=False):
        assert key is not None
        if key not in self.dsems:
            self.dsems[key] = [self.stack.enter_context(self.nc.semaphore("ds_" + key)), 0, 16]
        ent = self.dsems[key]
        ent[1] += 1
        if slow:
            o = Op(eng, (lambda: self.engs[eng].dma_start(out=out, in_=in_, allow_slow_non_contiguous=True)), True)
        else:
            o = Op(eng, (lambda: self.engs[eng].dma_start(out=out, in_=in_)), True)
        o.dsem = key
        o.dcount = ent[1]
        self._track(o, reads, writes)
        self.pending.append(o)
        return o

    def cc(self, kind, groups, in_ap, out_ap, reads=(), writes=(), key=None):
        assert key not in self.dsems
        self.dsems[key] = [self.stack.enter_context(self.nc.semaphore("ds_" + key)), 1, 1]
        o = Op("pool", (lambda: self.nc.gpsimd.collective_compute(kind, ALU.bypass, replica_groups=groups,
                                                                  ins=[in_ap.opt()], outs=[out_ap.opt()])), True)
        o.dsem = key
        o.dcount = 1
        self._track(o, reads, writes)
        self.pending.append(o)
        return o

    def _wait(self, eng, semkey, sem, val):
        w = self.waited[eng]
        if w.get(semkey, 0) >= val:
            return
        w[semkey] = val
        self.engs[eng].wait_ge(sem, val)
        self.n_inst += 1

    def flush(self):
        for o in self.pending:
            o.deps = [p for p in o.deps if not p.emitted]
            for p in o.deps:
                if not p.is_dma:
                    p.signal = True
        last = {}
        for o in self.pending:
            if not o.is_dma:
                last[o.eng] = o
        for o in last.values():
            o.signal = True
        for o in self.pending:
            for p in o.deps:
                assert p.emitted, "dependency on later op"
                if p.is_dma:
                    self._wait(o.eng, "d_" + p.dsem, self.dsems[p.dsem][0], self.dsems[p.dsem][2] * p.dcount)
                else:
                    self._wait(o.eng, "e_" + p.eng, self.esem[p.eng], p.sigval)
            ins = o.fn()
            self.n_inst += 1
            if o.is_dma:
                if self.dsems[o.dsem][2] == 16:
                    ins.then_inc(self.dsems[o.dsem][0], 16)
                else:
                    ins.then_inc(self.dsems[o.dsem][0])
            elif o.signal:
                self.sigcount[o.eng] += 1
                o.sigval = self.sigcount[o.eng]
                ins.then_inc(self.esem[o.eng], 1)
            o.emitted = True
            o.fn = None
        self.pending = []
        self.barrier()

    def barrier(self):
        for e in self.engs:
            for e2 in self.engs:
                if e2 != e and self.sigcount[e2] > 0:
                    self._wait(e, "e_" + e2, self.esem[e2], self.sigcount[e2])
            for key, (sem, cnt, unit) in self.dsems.items():
                if cnt > 0:
                    self._wait(e, "d_" + key, sem, unit * cnt)


class Ctx:
    def __init__(self, nc, stack):
        self.nc = nc
        self.stack = stack
        self.S = Sched(nc, stack)
        self.uid = 0

    def name(self, s):
        self.uid += 1
        return "%s_%d" % (s, self.uid)


def sb(cx, st, name, shape, dt):
    return st.enter_context(cx.nc.sbuf_tensor(cx.name(name), shape, dt))


def ps(cx, st, name, shape, dt):
    return st.enter_context(cx.nc.psum_tensor(cx.name(name), shape, dt))


def make_ident(cx, st):
    nc, S = cx.nc, cx.S
    ident = sb(cx, st, "ident", [128, 128], BF16)
    IB = S.buf("ident")
    S.dma("pool", ident[:], cx.ident_d[:, :], writes=[IB], key="ident")
    return ident, IB


def ln_epilogue(cx, z, ZB, gb, GB, st_t, mv_t, sd_t, rs_t, SB_, eps):
    nc, S = cx.nc, cx.S
    S.op("dve", lambda: nc.vector.bn_stats(out=st_t[:, 0, :], in_=z[:, 0:512]), reads=[ZB], writes=[SB_])
    S.op("dve", lambda: nc.vector.bn_stats(out=st_t[:, 1, :], in_=z[:, 512:1024]), reads=[ZB], writes=[SB_])
    S.op("dve", lambda: nc.vector.bn_aggr(out=mv_t[:], in_=st_t[:]), reads=[SB_], writes=[SB_])
    S.op("act", lambda: nc.scalar.activation(out=sd_t[:], in_=mv_t[:, 1:2], func=AF.Sqrt, bias=eps, scale=1.0),
         reads=[SB_], writes=[SB_])
    S.op("dve", lambda: nc.vector.reciprocal(out=rs_t[:], in_=sd_t[:]), reads=[SB_], writes=[SB_])
    S.op("dve", lambda: nc.vector.tensor_scalar(out=z[:], in0=z[:], scalar1=mv_t[:, 0:1], scalar2=rs_t[:, 0:1],
                                                op0=ALU.subtract, op1=ALU.mult), reads=[ZB, SB_], writes=[ZB])
    S.op("pool", lambda: nc.gpsimd.tensor_tensor(out=z[:], in0=z[:], in1=gb[:, 0, :], op=ALU.mult),
         reads=[ZB, GB], writes=[ZB])
    S.op("pool", lambda: nc.gpsimd.tensor_tensor(out=z[:], in0=z[:], in1=gb[:, 1, :], op=ALU.add),
         reads=[ZB, GB], writes=[ZB])


def load_gb(cx, st, g_d, b_d, name):
    nc, S = cx.nc, cx.S
    gb = sb(cx, st, name, [128, 2, D], F32)
    GB = S.buf(name)
    S.dma("sp", gb[:, 0, :], g_d.partition_broadcast(128), writes=[GB], key=name)
    S.dma("sp", gb[:, 1, :], b_d.partition_broadcast(128), writes=[GB], key=name)
    return gb, GB


def emit_xT(cx, xs_t, XSb, xbf, XBF, tp, TP, xT_t, XTb, ident, IB, NS):
    nc, S = cx.nc, cx.S
    KC = D // 128
    S.op("pool", lambda: nc.gpsimd.tensor_copy(out=xbf[:], in_=xs_t[:]), reads=[XSb], writes=[XBF])
    for s in range(NS):
        b = s % len(tp)
        for k in range(KC):
            S.op("pe", lambda s=s, k=k, b=b: nc.tensor.transpose(out=tp[b][:, k * 128:(k + 1) * 128],
                                                                  in_=xbf[:, s, k * 128:(k + 1) * 128], identity=ident[:]),
                 reads=[XBF, IB], writes=[TP[b]])
        S.op("dve", lambda s=s, b=b: nc.vector.tensor_copy(
            out=xT_t[:, :, s * 128:(s + 1) * 128], in_=tp[b][:].rearrange("p (k t) -> p k t", k=KC)),
            reads=[TP[b]], writes=[XTb])


def phase_ffn(cx, xin, xout, w_in_d, w_out_d, g_d, b_d, NT, tag):
    nc, S = cx.nc, cx.S
    TT = 256
    NS = TT // 128
    KC = D // 128
    MC = DFF // 128
    c_res = 0.5 / ALPHA
    eps = LN_EPS / (ALPHA * ALPHA)
    with ExitStack() as st:
        w1 = sb(cx, st, "w1", [128, KC, 2 * DFF], BF16)
        w2 = sb(cx, st, "w2", [128, MC, D], BF16)
        W1, W2 = S.buf("W1"), S.buf("W2")
        for k in range(KC):
            S.dma("pool", w1[:, k, :], w_in_d[k * 128:(k + 1) * 128, :], writes=[W1], key="w1")
        for m in range(MC):
            S.dma("pool", w2[:, m, :], w_out_d[m * 128:(m + 1) * 128, :], writes=[W2], key="w2")
        gb, GB = load_gb(cx, st, g_d, b_d, "gb")
        ident, IB = make_ident(cx, st)
        xs = [sb(cx, st, "xs", [128, NS, D], F32) for _ in range(2)]
        XS = S.bufs("xs", 2)
        xbf = sb(cx, st, "xbf", [128, NS, D], BF16)
        XBF = S.buf("xbf")
        xT = [sb(cx, st, "xT", [128, KC, TT], BF16) for _ in range(2)]
        XT = S.bufs("xT", 2)
        hT = [sb(cx, st, "hT", [128, MC, TT], BF16) for _ in range(2)]
        HT = S.bufs("hT", 2)
        sg = [sb(cx, st, "sg", [128, TT], BF16) for _ in range(2)]
        SG = S.bufs("sg", 2)
        z = [sb(cx, st, "z", [128, D], F32) for _ in range(2)]
        Z = S.bufs("z", 2)
        st_t = [sb(cx, st, "st", [128, 2, 6], F32) for _ in range(2)]
        mv_t = [sb(cx, st, "mv", [128, 2], F32) for _ in range(2)]
        sd_t = [sb(cx, st, "sd", [128, 1], F32) for _ in range(2)]
        rs_t = [sb(cx, st, "rs", [128, 1], F32) for _ in range(2)]
        STB = S.bufs("stb", 2)
        tp = [ps(cx, st, "tp", [128, D], BF16) for _ in range(2)]
        TP = S.bufs("tp", 2)
        gu = [ps(cx, st, "gu", [128, 512], F32) for _ in range(4)]
        GU = S.bufs("gu", 4)
        op_ = [ps(cx, st, "o", [128, 512], F32) for _ in range(2)]
        OP = S.bufs("o", 2)

        ntile = NT // TT
        xin_v = xin.rearrange("(t s p) d -> t p s d", s=NS, p=128)
        xout_v = xout.rearrange("(t s p) d -> t s p d", s=NS, p=128)

        def load(t):
            S.dma("sp", xs[t % 2][:], xin_v[t], writes=[XS[t % 2]], key="xs%d" % (t % 2))

        load(0)
        for t in range(ntile):
            b2 = t % 2
            if t + 1 < ntile:
                load(t + 1)
            emit_xT(cx, xs[b2], XS[b2], xbf, XBF, tp, TP, xT[b2], XT[b2], ident, IB, NS)
            for m in range(MC):
                g_ps, u_ps = gu[2 * (m % 2)], gu[2 * (m % 2) + 1]
                GP, UP = GU[2 * (m % 2)], GU[2 * (m % 2) + 1]
                for k in range(KC):
                    S.op("pe", lambda m=m, k=k, g_ps=g_ps, b2=b2: nc.tensor.matmul(
                        g_ps[:, 0:TT], w1[:, k, m * 128:(m + 1) * 128], xT[b2][:, k, :], start=(k == 0), stop=(k == KC - 1)),
                        reads=[W1, XT[b2]], writes=[GP])
                for k in range(KC):
                    S.op("pe", lambda m=m, k=k, u_ps=u_ps, b2=b2: nc.tensor.matmul(
                        u_ps[:, 0:TT], w1[:, k, DFF + m * 128:DFF + (m + 1) * 128], xT[b2][:, k, :], start=(k == 0), stop=(k == KC - 1)),
                        reads=[W1, XT[b2]], writes=[UP])
                S.op("act", lambda m=m, g_ps=g_ps: nc.scalar.activation(out=sg[m % 2][:], in_=g_ps[:, 0:TT], func=AF.Silu),
                     reads=[GP], writes=[SG[m % 2]])
                S.op("dve", lambda m=m, u_ps=u_ps, b2=b2: nc.vector.tensor_tensor(
                    out=hT[b2][:, m, :], in0=sg[m % 2][:], in1=u_ps[:, 0:TT], op=ALU.mult),
                    reads=[SG[m % 2], UP], writes=[HT[b2]])
            for s in range(NS):
                for n in range(2):
                    for m in range(MC):
                        S.op("pe", lambda s=s, n=n, m=m, b2=b2: nc.tensor.matmul(
                            op_[n][:], hT[b2][:, m, s * 128:(s + 1) * 128], w2[:, m, n * 512:(n + 1) * 512],
                            start=(m == 0), stop=(m == MC - 1)), reads=[HT[b2], W2], writes=[OP[n]])
                    S.op("dve", lambda s=s, n=n, b2=b2: nc.vector.scalar_tensor_tensor(
                        out=z[s][:, n * 512:(n + 1) * 512], in0=op_[n][:], scalar=c_res, in1=xs[b2][:, s, n * 512:(n + 1) * 512],
                        op0=ALU.mult, op1=ALU.add), reads=[OP[n], XS[b2]], writes=[Z[s]])
                ln_epilogue(cx, z[s], Z[s], gb, GB, st_t[s], mv_t[s], sd_t[s], rs_t[s], STB[s], eps)
                S.dma("sp", xout_v[t, s], z[s][:], reads=[Z[s]], key="zst%d" % s)
        S.flush()


GW = 3072


def ln_stats(cx, src, SRC, nchunk, st_t, mv_t, sd_t, rs_t, SB_, eps):
    nc, S = cx.nc, cx.S
    for c in range(nchunk):
        S.op("dve", lambda c=c: nc.vector.bn_stats(out=st_t[:, c, :], in_=src[:, c * 512:(c + 1) * 512]),
             reads=[SRC], writes=[SB_])
    S.op("dve", lambda: nc.vector.bn_aggr(out=mv_t[:], in_=st_t[:]), reads=[SB_], writes=[SB_])
    S.op("act", lambda: nc.scalar.activation(out=sd_t[:], in_=mv_t[:, 1:2], func=AF.Sqrt, bias=eps, scale=1.0),
         reads=[SB_], writes=[SB_])
    S.op("dve", lambda: nc.vector.reciprocal(out=rs_t[:], in_=sd_t[:]), reads=[SB_], writes=[SB_])


def phase_g1(cx, xin, uT_d, vln_d, w_in_d, lng_d, lnb_d, NT):
    nc, S = cx.nc, cx.S
    TT, NS, KC, JC = 256, 2, 8, 24
    with ExitStack() as st:
        wg = sb(cx, st, "wg", [128, KC, 2 * GW], BF16)
        WG = S.buf("WG")
        for k in range(KC):
            S.dma("pool", wg[:, k, :], w_in_d[k * 128:(k + 1) * 128, :], writes=[WG], key="wg")
        gbv = sb(cx, st, "gbv", [128, 2, GW], F32)
        GBV = S.buf("gbv")
        S.dma("sp", gbv[:, 0, :], lng_d.partition_broadcast(128), writes=[GBV], key="gbv")
        S.dma("sp", gbv[:, 1, :], lnb_d.partition_broadcast(128), writes=[GBV], key="gbv")
        ident, IB = make_ident(cx, st)
        xs = [sb(cx, st, "xs", [128, NS, D], F32) for _ in range(2)]
        XS = S.bufs("xs", 2)
        xbf = sb(cx, st, "xbf", [128, NS, D], BF16)
        XBF = S.buf("xbf")
        xT = [sb(cx, st, "xT", [128, KC, TT], BF16) for _ in range(2)]
        XT = S.bufs("xT", 2)
        ust = [sb(cx, st, "ust", [128, 4, TT], BF16) for _ in range(2)]
        UST = S.bufs("ust", 2)
        v = [sb(cx, st, "v", [128, GW], F32) for _ in range(2)]
        V = S.bufs("v", 2)
        vln = [sb(cx, st, "vln", [128, GW], BF16) for _ in range(2)]
        VLN = S.bufs("vln", 2)
        st_t = [sb(cx, st, "st", [128, 6, 6], F32) for _ in range(2)]
        mv_t = [sb(cx, st, "mv", [128, 2], F32) for _ in range(2)]
        sd_t = [sb(cx, st, "sd", [128, 1], F32) for _ in range(2)]
        rs_t = [sb(cx, st, "rs", [128, 1], F32) for _ in range(2)]
        STB = S.bufs("stb", 2)
        tp = [ps(cx, st, "tp", [128, D], BF16) for _ in range(2)]
        TP = S.bufs("tp", 2)
        pu = [ps(cx, st, "pu", [128, 512], F32) for _ in range(2)]
        PU = S.bufs("pu", 2)
        pv = [ps(cx, st, "pv", [128, 512], F32) for _ in range(3)]
        PV = S.bufs("pv", 3)
        ntile = NT // TT
        xin_v = xin.rearrange("(t s p) d -> t p s d", s=NS, p=128)
        uT_v = uT_d.rearrange("(j p) n -> p j n", p=128)

        def load(t):
            S.dma("sp", xs[t % 2][:], xin_v[t], writes=[XS[t % 2]], key="xs%d" % (t % 2))

        load(0)
        for t in range(ntile):
            b2 = t % 2
            if t + 1 < ntile:
                load(t + 1)
            emit_xT(cx, xs[b2], XS[b2], xbf, XBF, tp, TP, xT[b2], XT[b2], ident, IB, NS)
            for j in range(JC):
                q = (j // 4) % 2
                for k in range(KC):
                    S.op("pe", lambda j=j, k=k, b2=b2: nc.tensor.matmul(
                        pu[j % 2][:, 0:TT], wg[:, k, j * 128:(j + 1) * 128], xT[b2][:, k, :],
                        start=(k == 0), stop=(k == KC - 1)), reads=[WG, XT[b2]], writes=[PU[j % 2]])
                S.op("act", lambda j=j, q=q: nc.scalar.activation(out=ust[q][:, j % 4, :], in_=pu[j % 2][:, 0:TT],
                                                                    func=AF.Gelu_apprx_tanh),
                     reads=[PU[j % 2]], writes=[UST[q]])
                if j % 4 == 3:
                    S.dma("sp", uT_v[:, j - 3:j + 1, t * TT:(t + 1) * TT], ust[q][:], reads=[UST[q]], key="ust%d" % q)
            for s in range(NS):
                for n in range(6):
                    for k in range(KC):
                        S.op("pe", lambda s=s, n=n, k=k, b2=b2: nc.tensor.matmul(
                            pv[n % 3][:], xT[b2][:, k, s * 128:(s + 1) * 128], wg[:, k, GW + n * 512:GW + (n + 1) * 512],
                            start=(k == 0), stop=(k == KC - 1)), reads=[WG, XT[b2]], writes=[PV[n % 3]])
                    S.op("act", lambda s=s, n=n: nc.scalar.activation(out=v[s][:, n * 512:(n + 1) * 512], in_=pv[n % 3][:],
                                                                        func=AF.Gelu_apprx_tanh),
                         reads=[PV[n % 3]], writes=[V[s]])
                ln_stats(cx, v[s], V[s], 6, st_t[s], mv_t[s], sd_t[s], rs_t[s], STB[s], LN_EPS)
                S.op("dve", lambda s=s: nc.vector.tensor_scalar(out=v[s][:], in0=v[s][:], scalar1=mv_t[s][:, 0:1],
                                                                scalar2=rs_t[s][:, 0:1], op0=ALU.subtract, op1=ALU.mult),
                     reads=[V[s], STB[s]], writes=[V[s]])
                S.op("pool", lambda s=s: nc.gpsimd.tensor_tensor(out=v[s][:], in0=v[s][:], in1=gbv[:, 0, :], op=ALU.mult),
                     reads=[V[s], GBV], writes=[V[s]])
                S.op("pool", lambda s=s: nc.gpsimd.tensor_tensor(out=vln[s][:], in0=v[s][:], in1=gbv[:, 1, :], op=ALU.add),
                     reads=[V[s], GBV], writes=[VLN[s]])
                S.dma("sp", vln_d[t * TT + s * 128:t * TT + (s + 1) * 128, :], vln[s][:], reads=[VLN[s]], key="vln%d" % s)
        S.flush()


def phase_g2(cx, xin, xout, uT_d, vln_d, ws_d, bs_d, w_out_d, g_d, b_d, NT):
    nc, S = cx.nc, cx.S
    JC = 24
    c_res = 1.0 / ALPHA
    eps = LN_EPS / (ALPHA * ALPHA)
    with ExitStack() as st:
        w2 = sb(cx, st, "w2g", [128, JC, D], BF16)
        W2 = S.buf("W2g")
        for j in range(JC):
            S.dma("pool", w2[:, j, :], w_out_d[j * 128:(j + 1) * 128, :], writes=[W2], key="w2g")
        gb, GB = load_gb(cx, st, g_d, b_d, "gb2")
        ident, IB = make_ident(cx, st)
        WT = sb(cx, st, "WT", [128, 16, 128], BF16)
        WTB = S.buf("WT")
        bhi = sb(cx, st, "bhi", [1, 2048], BF16)
        blo = sb(cx, st, "blo", [1, 2048], BF16)
        ones = sb(cx, st, "ones", [1, 128], BF16)
        BB = S.buf("bias")
        with ExitStack() as st2:
            wsf = sb(cx, st2, "wsf", [128, 16, 128], F32)
            WSF = S.buf("wsf")
            S.dma("sp", wsf[:], ws_d.rearrange("g t s -> t g s"), writes=[WSF], key="wsf")
            tril = sb(cx, st2, "tril", [128, 128], F32)
            TR = S.buf("tril")
            S.dma("sp", tril[:], cx.tril_d[:, :], writes=[TR], key="tril")
            wsm = sb(cx, st2, "wsm", [128, 16, 128], BF16)
            WSM = S.buf("wsm")
            S.op("dve", lambda: nc.vector.tensor_tensor(out=wsm[:], in0=wsf[:], in1=tril[:].unsqueeze(1).broadcast_to([128, 16, 128]),
                                                        op=ALU.mult), reads=[WSF, TR], writes=[WSM])
            tpw = [ps(cx, st2, "tpw", [128, D], BF16) for _ in range(2)]
            TPW = S.bufs("tpw", 2)
            for g in range(16):
                S.op("pe", lambda g=g: nc.tensor.transpose(out=tpw[g // 8][:, (g % 8) * 128:(g % 8 + 1) * 128],
                                                           in_=wsm[:, g, :], identity=ident[:]),
                     reads=[WSM, IB], writes=[TPW[g // 8]])
            for h in range(2):
                S.op("dve", lambda h=h: nc.vector.tensor_copy(out=WT[:, h * 8:(h + 1) * 8, :],
                                                              in_=tpw[h][:].rearrange("p (g t) -> p g t", g=8)),
                     reads=[TPW[h]], writes=[WTB])
            bsf = sb(cx, st2, "bsf", [1, 2048], F32)
            BSF = S.buf("bsf")
            S.dma("sp", bsf[:], bs_d.rearrange("g t -> (g t)").unsqueeze(0), writes=[BSF], key="bsf")
            S.op("dve", lambda: nc.vector.tensor_copy(out=bhi[:], in_=bsf[:]), reads=[BSF], writes=[BB])
            S.op("dve", lambda: nc.vector.tensor_tensor(out=blo[:], in0=bsf[:], in1=bhi[:], op=ALU.subtract),
                 reads=[BSF, BB], writes=[BB])
            S.op("dve", lambda: nc.vector.memset(ones[:], 1.0), writes=[BB])
            S.flush()
        ut = [sb(cx, st, "ut", [128, JC, 256], BF16) for _ in range(2)]
        UT = S.bufs("ut", 2)
        vl = [sb(cx, st, "vl", [128, GW], BF16) for _ in range(2)]
        VL = S.bufs("vl", 2)
        xs = [sb(cx, st, "xs", [128, D], F32) for _ in range(2)]
        XS = S.bufs("xs", 2)
        uv = [sb(cx, st, "uv", [128, JC, 128], BF16) for _ in range(2)]
        UV = S.bufs("uv", 2)
        z = [sb(cx, st, "z", [128, D], F32) for _ in range(2)]
        Z = S.bufs("z", 2)
        st_t = [sb(cx, st, "st", [128, 2, 6], F32) for _ in range(2)]
        mv_t = [sb(cx, st, "mv", [128, 2], F32) for _ in range(2)]
        sd_t = [sb(cx, st, "sd", [128, 1], F32) for _ in range(2)]
        rs_t = [sb(cx, st, "rs", [128, 1], F32) for _ in range(2)]
        STB = S.bufs("stb", 2)
        P = [ps(cx, st, "P", [128, 512], F32) for _ in range(6)]
        PB = S.bufs("P", 6)
        op_ = [ps(cx, st, "o", [128, 512], F32) for _ in range(2)]
        OP = S.bufs("o", 2)
        nchunk = NT // 128
        uT_v = uT_d.rearrange("(j p) n -> p j n", p=128)
        pieces = []
        for g in range(16):
            a = g // 2
            if g % 2 == 0:
                pieces.append((g, 192 * g, 192 * g + 128, 3 * a, 0, 128))
                pieces.append((g, 192 * g + 128, 192 * g + 192, 3 * a + 1, 0, 64))
            else:
                pieces.append((g, 192 * g, 192 * g + 64, 3 * a + 1, 64, 128))
                pieces.append((g, 192 * g + 64, 192 * g + 192, 3 * a + 2, 0, 128))

        def load_u(tt):
            S.dma("sp", ut[tt % 2][:], uT_v[:, :, tt * 256:(tt + 1) * 256], writes=[UT[tt % 2]], key="ut%d" % (tt % 2))

        def load_c(c):
            S.dma("sp", vl[c % 2][:], vln_d[c * 128:(c + 1) * 128, :], writes=[VL[c % 2]], key="vl%d" % (c % 2))
            S.dma("sp", xs[c % 2][:], xin[c * 128:(c + 1) * 128, :], writes=[XS[c % 2]], key="xsg%d" % (c % 2))

        load_u(0)
        load_c(0)
        for c in range(nchunk):
            c2 = c % 2
            tt, cs = c // 2, c % 2
            if c + 1 < nchunk:
                load_c(c + 1)
                if (c + 1) % 2 == 0:
                    load_u((c + 1) // 2)
            for (g, f0, f1, j, r0, r1) in pieces:
                M = f1 - f0
                out_ap = lambda j=j, r0=r0, r1=r1: P[j // 4][r0:r1, (j % 4) * 128:(j % 4 + 1) * 128]
                S.op("pe", lambda g=g, f0=f0, f1=f1, out_ap=out_ap, c2=c2: nc.tensor.matmul(
                    out_ap(), vl[c2][:, f0:f1], WT[:, g, :], start=True, stop=False),
                    reads=[VL[c2], WTB], writes=[PB[j // 4]])
                S.op("pe", lambda g=g, M=M, out_ap=out_ap: nc.tensor.matmul(
                    out_ap(), ones[0:1, 0:M], bhi[0:1, g * 128:(g + 1) * 128], start=False, stop=False),
                    reads=[BB], writes=[PB[j // 4]])
                S.op("pe", lambda g=g, M=M, out_ap=out_ap: nc.tensor.matmul(
                    out_ap(), ones[0:1, 0:M], blo[0:1, g * 128:(g + 1) * 128], start=False, stop=True),
                    reads=[BB], writes=[PB[j // 4]])
            for jj in range(6):
                S.op("dve", lambda jj=jj, c2=c2, tt=tt, cs=cs: nc.vector.tensor_tensor(
                    out=uv[c2][:, 4 * jj:4 * jj + 4, :], in0=P[jj][:].rearrange("p (j t) -> p j t", j=4),
                    in1=ut[tt % 2][:, 4 * jj:4 * jj + 4, cs * 128:(cs + 1) * 128], op=ALU.mult),
                    reads=[PB[jj], UT[tt % 2]], writes=[UV[c2]])
            for n in range(2):
                for j in range(JC):
                    S.op("pe", lambda n=n, j=j, c2=c2: nc.tensor.matmul(
                        op_[n][:], uv[c2][:, j, :], w2[:, j, n * 512:(n + 1) * 512], start=(j == 0), stop=(j == JC - 1)),
                        reads=[UV[c2], W2], writes=[OP[n]])
                S.op("dve", lambda n=n, c2=c2: nc.vector.scalar_tensor_tensor(
                    out=z[c2][:, n * 512:(n + 1) * 512], in0=op_[n][:], scalar=c_res, in1=xs[c2][:, n * 512:(n + 1) * 512],
                    op0=ALU.mult, op1=ALU.add), reads=[OP[n], XS[c2]], writes=[Z[c2]])
            ln_epilogue(cx, z[c2], Z[c2], gb, GB, st_t[c2], mv_t[c2], sd_t[c2], rs_t[c2], STB[c2], eps)
            S.dma("sp", xout[c * 128:(c + 1) * 128, :], z[c2][:], reads=[Z[c2]], key="zg%d" % c2)
        S.flush()


class NsaLocal:
    def __init__(self, feat, tok, QT, gates):
        self.feat, self.tok, self.QT, self.gates = feat, tok, QT, gates


class NsaGathered:
    def __init__(self, feat, tok):
        self.feat, self.tok = feat, tok


NSA_COLS = 2608
HD = 64


def phase_n1(cx, xin, w_in_d, cos_d, sin_d, nl, NT):
    nc, S = cx.nc, cx.S
    TT, NS, KC = 512, 4, 8
    with ExitStack() as st:
        w = sb(cx, st, "wn", [128, KC, NSA_COLS], BF16)
        W = S.buf("wn")
        for k in range(KC):
            S.dma("pool", w[:, k, :], w_in_d[k * 128:(k + 1) * 128, :], writes=[W], key="wn")
        wqr = sb(cx, st, "wqr", [128, KC, 1024], BF16)
        wkr = sb(cx, st, "wkr", [128, KC, 512], BF16)
        WR = S.buf("wrot")

        def rot(dst, src):
            sv = src.rearrange("p k (b t f) -> p k b t f", t=2, f=32)
            dv = dst.rearrange("p k (b t f) -> p k b t f", t=2, f=32)
            for k in range(KC):
                S.op("dve", lambda k=k: nc.vector.tensor_scalar(out=dv[:, k, :, 0, :], in0=sv[:, k, :, 1, :], scalar1=-1.0,
                                                                scalar2=None, op0=ALU.mult), reads=[W, WR], writes=[WR])
                S.op("dve", lambda k=k: nc.vector.tensor_copy(out=dv[:, k, :, 1, :], in_=sv[:, k, :, 0, :]),
                     reads=[W, WR], writes=[WR])

        rot(wqr[:], w[:, :, 0:1024])
        rot(wkr[:, :, 0:256], w[:, :, 1536:1792])
        rot(wkr[:, :, 256:512], w[:, :, 2048:2304])
        ident, IB = make_ident(cx, st)
        xs = [sb(cx, st, "xs", [128, NS, D], F32) for _ in range(2)]
        XS = S.bufs("xs", 2)
        xbf = sb(cx, st, "xbf", [128, NS, D], BF16)
        XBF = S.buf("xbf")
        xT = [sb(cx, st, "xT", [128, KC, TT], BF16) for _ in range(2)]
        XT = S.bufs("xT", 2)
        cs = [sb(cx, st, "cs", [128, 2, TT], F32) for _ in range(2)]
        CS = S.bufs("cs", 2)
        t1 = [sb(cx, st, "t1", [128, TT], F32) for _ in range(2)]
        t2 = [sb(cx, st, "t2", [128, TT], F32) for _ in range(2)]
        T1 = S.bufs("t1", 2)
        T2 = S.bufs("t2", 2)
        og = [sb(cx, st, "og", [128, TT], BF16) for _ in range(2)]
        OG = S.bufs("og", 2)
        vt = [sb(cx, st, "vt", [128, 512], BF16) for _ in range(2)]
        VT = S.bufs("vt", 2)
        gt = [sb(cx, st, "gt", [128, 48], F32) for _ in range(2)]
        GT = S.bufs("gt", 2)
        tp = [ps(cx, st, "tp", [128, D], BF16) for _ in range(2)]
        TP = S.bufs("tp", 2)
        pa = [ps(cx, st, "pa", [128, 512], F32) for _ in range(2)]
        PA = S.bufs("pa", 2)
        pb = [ps(cx, st, "pb", [128, 512], F32) for _ in range(2)]
        PB = S.bufs("pb", 2)
        pt = [ps(cx, st, "pt", [128, 512], F32) for _ in range(2)]
        PT = S.bufs("pt", 2)
        ntile = NT // TT
        xin_v = xin.rearrange("(t s p) d -> t p s d", s=NS, p=128)

        def load(t):
            S.dma("sp", xs[t % 2][:], xin_v[t], writes=[XS[t % 2]], key="xs%d" % (t % 2))
            S.dma("sp", cs[t % 2][:, 0, :], cos_d[:, t * TT:(t + 1) * TT], writes=[CS[t % 2]], key="cs%d" % (t % 2))
            S.dma("sp", cs[t % 2][:, 1, :], sin_d[:, t * TT:(t + 1) * TT], writes=[CS[t % 2]], key="cs%d" % (t % 2))

        units = []
        qf = lambda h: nl.QT[:, h, :]
        for a in range(8):
            units.append((qf, a, w, wqr, a * 128, a * 128))
        for a in range(2):
            units.append(((lambda g: nl.feat("KsT", g)), a, w, wkr, 1536 + a * 128, a * 128))
        for a in range(2):
            units.append(((lambda g: nl.feat("KwT", g)), a, w, wkr, 2048 + a * 128, 256 + a * 128))
        kcf = lambda g: nl.feat("KcT", g)
        vcf = lambda g: nl.feat("VcT", g)
        plain = [(kcf, 0, 1024), (kcf, 1, 1152), (vcf, 0, 1280), (vcf, 1, 1408)]
        load(0)
        cnt = 0
        for t in range(ntile):
            b2 = t % 2
            if t + 1 < ntile:
                load(t + 1)
            emit_xT(cx, xs[b2], XS[b2], xbf, XBF, tp, TP, xT[b2], XT[b2], ident, IB, NS)
            tok = slice(t * TT, (t + 1) * TT)
            for (dst, di, wa, wb, ca, cb) in units:
                q = cnt % 2
                cnt += 1
                for k in range(KC):
                    S.op("pe", lambda k=k, q=q, wa=wa, ca=ca, b2=b2: nc.tensor.matmul(
                        pa[q][:], wa[:, k, ca:ca + 128], xT[b2][:, k, :], start=(k == 0), stop=(k == KC - 1)),
                        reads=[W, WR, XT[b2]], writes=[PA[q]])
                for k in range(KC):
                    S.op("pe", lambda k=k, q=q, wb=wb, cb=cb, b2=b2: nc.tensor.matmul(
                        pb[q][:], wb[:, k, cb:cb + 128], xT[b2][:, k, :], start=(k == 0), stop=(k == KC - 1)),
                        reads=[WR, XT[b2]], writes=[PB[q]])
                S.op("dve", lambda q=q, b2=b2: nc.vector.tensor_tensor(out=t1[q][:], in0=pa[q][:], in1=cs[b2][:, 0, :], op=ALU.mult),
                     reads=[PA[q], CS[b2]], writes=[T1[q]])
                S.op("dve", lambda q=q, b2=b2: nc.vector.tensor_tensor(out=t2[q][:], in0=pb[q][:], in1=cs[b2][:, 1, :], op=ALU.mult),
                     reads=[PB[q], CS[b2]], writes=[T2[q]])
                S.op("pool", lambda q=q: nc.gpsimd.tensor_tensor(out=og[q][:], in0=t1[q][:], in1=t2[q][:], op=ALU.add),
                     reads=[T1[q], T2[q]], writes=[OG[q]])
                for h in range(2):
                    S.dma("sp", dst(2 * di + h)[:, tok], og[q][64 * h:64 * h + 64, :], reads=[OG[q]], key="og%d" % q)
            for (dst, di, c0) in plain:
                q = cnt % 2
                cnt += 1
                for k in range(KC):
                    S.op("pe", lambda k=k, q=q, c0=c0, b2=b2: nc.tensor.matmul(
                        pa[q][:], w[:, k, c0:c0 + 128], xT[b2][:, k, :], start=(k == 0), stop=(k == KC - 1)),
                        reads=[W, XT[b2]], writes=[PA[q]])
                S.op("act", lambda q=q: nc.scalar.copy(out=og[q][:], in_=pa[q][:]), reads=[PA[q]], writes=[OG[q]])
                for h in range(2):
                    S.dma("sp", dst(2 * di + h)[:, tok], og[q][64 * h:64 * h + 64, :], reads=[OG[q]], key="og%d" % q)
            for s in range(NS):
                q = s % 2
                t0 = t * TT + s * 128
                rows = slice(t0, t0 + 128)
                for k in range(KC):
                    S.op("pe", lambda k=k, q=q, s=s, b2=b2: nc.tensor.matmul(
                        pt[q][:, 0:256], xT[b2][:, k, s * 128:(s + 1) * 128], w[:, k, 1792:2048],
                        start=(k == 0), stop=(k == KC - 1)), reads=[W, XT[b2]], writes=[PT[q]])
                S.op("act", lambda q=q: nc.scalar.copy(out=vt[q][:, 0:256], in_=pt[q][:, 0:256]), reads=[PT[q]], writes=[VT[q]])
                S.dma("sp", nl.tok("Vs", t0), vt[q][:, 0:256], reads=[VT[q]], key="vt%d" % q)
                for k in range(KC):
                    S.op("pe", lambda k=k, q=q, s=s, b2=b2: nc.tensor.matmul(
                        pt[q][:, 0:304], xT[b2][:, k, s * 128:(s + 1) * 128], w[:, k, 2304:2608],
                        start=(k == 0), stop=(k == KC - 1)), reads=[W, XT[b2]], writes=[PT[q]])
                S.op("act", lambda q=q: nc.scalar.copy(out=vt[q][:, 256:512], in_=pt[q][:, 0:256]), reads=[PT[q]], writes=[VT[q]])
                S.op("act", lambda q=q: nc.scalar.activation(out=gt[q][:], in_=pt[q][:, 256:304], func=AF.Sigmoid),
                     reads=[PT[q]], writes=[GT[q]])
                S.dma("sp", nl.tok("Vw", t0), vt[q][:, 256:512], reads=[VT[q]], key="vt%d" % q)
                S.dma("sp", nl.gates[rows, :], gt[q][:], reads=[GT[q]], key="gt%d" % q)
        S.flush()


NEGM = -30000.0
MTS = 2
PIPE_DEPTH = 5
DBG = 0


class StopBuild(Exception):
    pass


def dbg_stop(n):
    if DBG == n:
        raise StopBuild()


def phase_attn(cx, al, QT_d, gates_d, O_d, w1k_d, w2k_d, pek_d, w1v_d, w2v_d, pev_d, NB):
    nc, S = cx.nc, cx.S
    NT = NB * 128
    SEQ = 4 * NT
    KT = 4 * NB
    NCMP = SEQ // 16 - 1
    NCP = SEQ // 16
    with ExitStack() as st:
        ident, IB = make_ident(cx, st)
        identf = sb(cx, st, "identf", [128, 128], F32)
        S.dma("sp", identf[:], cx.ident_d[:, :], writes=[IB], key="identf")
        kcmpT = sb(cx, st, "kcmpT", [64, 4, NCP], BF16)
        vcmp = sb(cx, st, "vcmp", [128, NCP // 128, 4, 65], BF16)
        KCMP = S.buf("kcmp")
        VCMP = S.buf("vcmp")
        S.op("pool", lambda: nc.gpsimd.memset(vcmp[:], 0.0), writes=[VCMP])
        S.op("pool", lambda: nc.gpsimd.memset(kcmpT[:], 0.0), writes=[KCMP])
        with ExitStack() as st2:
            w1s = sb(cx, st2, "w1s", [64, 32, 256], BF16)
            w2d = sb(cx, st2, "w2d", [128, 2, 64], BF16)
            w2r = sb(cx, st2, "w2r", [128, 2, 64], BF16)
            peT = sb(cx, st2, "peT", [64, 32], BF16)
            kc = sb(cx, st2, "kc", [64, SEQ], BF16)
            hT = sb(cx, st2, "hTc", [128, 2, NCP], BF16)
            cbias = sb(cx, st2, "cbias", [128, 2], F32)
            ccs = sb(cx, st2, "ccs", [64, 2, NCP], F32)
            t1 = sb(cx, st2, "ct1", [128, 512], F32)
            t2 = sb(cx, st2, "ct2", [128, 512], F32)
            W1S, W2D, PET, KC, HT, CB, CCS, T1, T2 = [S.buf(n) for n in ("w1s", "w2d", "peT", "kc", "hTc", "cb", "ccs", "ct1", "ct2")]
            ph = [ps(cx, st2, "ph", [128, 512], F32) for _ in range(2)]
            PH = S.bufs("ph", 2)
            pc = ps(cx, st2, "pc", [128, 512], F32)
            PC = S.buf("pc")
            pA = ps(cx, st2, "pA", [128, 512], F32)
            pB = ps(cx, st2, "pB", [128, 512], F32)
            PA_, PB_ = S.buf("pA"), S.buf("pB")
            S.dma("sp", ccs[:, 0, :], cx.cosc_d[0:64, :], writes=[CCS], key="ccs")
            S.dma("sp", ccs[:, 1, :], cx.sinc_d[0:64, :], writes=[CCS], key="ccs")
            ntiles = [(n0, min(n0 + 512, NCMP)) for n0 in range(0, NCMP, 512)]
            for which in range(2):
                w1_d, w2_d, pe_d = (w1k_d, w2k_d, pek_d) if which == 0 else (w1v_d, w2v_d, pev_d)
                src_k = "KcT" if which == 0 else "VcT"
                S.dma("pool", w1s[:, :, :], w1_d.rearrange("l d h -> d l h"), writes=[W1S], key="w1s")
                S.dma("pool", peT[:, :], pe_d.rearrange("l d -> d l"), writes=[PET], key="peT", slow=True)
                for hc in range(2):
                    S.dma("pool", w2d[:, hc, :], w2_d[hc * 128:(hc + 1) * 128, :], writes=[W2D], key="w2d")
                if which == 0:
                    S.op("dve", lambda: nc.vector.tensor_scalar(out=w2r[:, :, 0:32], in0=w2d[:, :, 32:64], scalar1=-1.0,
                                                                scalar2=None, op0=ALU.mult), reads=[W2D], writes=[W2D])
                    S.op("dve", lambda: nc.vector.tensor_copy(out=w2r[:, :, 32:64], in_=w2d[:, :, 0:32]),
                         reads=[W2D], writes=[W2D])
                for hc in range(2):
                    for l in range(32):
                        S.op("pe", lambda hc=hc, l=l: nc.tensor.matmul(
                            pc[:, hc:hc + 1], w1s[:, l, hc * 128:(hc + 1) * 128], peT[:, l:l + 1],
                            start=(l == 0), stop=(l == 31)), reads=[W1S, PET], writes=[PC])
                S.op("dve", lambda: nc.vector.tensor_copy(out=cbias[:], in_=pc[:, 0:2]), reads=[PC], writes=[CB])
                for g in range(4):
                    for r_ in range(4):
                        S.dma("sp", kc[:, :].rearrange("p (i r q) -> p i r q", r=4, q=128)[:, :, r_, :],
                              al.feat(src_k, r_, g).rearrange("p (i q) -> p i q", q=128), writes=[KC], key="kc")
                    for hc in range(2):
                        for (n0, n1) in ntiles:
                            cnt = n1 - n0
                            q = (hc + (n0 // 512)) % 2
                            for l in range(32):
                                S.op("pe", lambda hc=hc, l=l, n0=n0, cnt=cnt, q=q: nc.tensor.matmul(
                                    ph[q][:, 0:cnt], w1s[:, l, hc * 128:(hc + 1) * 128],
                                    kc[:, 16 * n0 + l:16 * n0 + l + 16 * (cnt - 1) + 1:16],
                                    start=(l == 0), stop=(l == 31)), reads=[W1S, KC], writes=[PH[q]])
                            S.op("act", lambda hc=hc, n0=n0, cnt=cnt, q=q: nc.scalar.activation(
                                out=hT[:, hc, n0:n0 + cnt], in_=ph[q][:, 0:cnt], func=AF.Gelu_apprx_tanh,
                                bias=cbias[:, hc:hc + 1], scale=1.0), reads=[PH[q], CB], writes=[HT])
                    if which == 0:
                        for (n0, n1) in ntiles:
                            cnt = n1 - n0
                            for hc in range(2):
                                S.op("pe", lambda hc=hc, n0=n0, cnt=cnt: nc.tensor.matmul(
                                    pA[0:64, 0:cnt], w2d[:, hc, :], hT[:, hc, n0:n0 + cnt], start=(hc == 0), stop=(hc == 1)),
                                    reads=[W2D, HT], writes=[PA_])
                            for hc in range(2):
                                S.op("pe", lambda hc=hc, n0=n0, cnt=cnt: nc.tensor.matmul(
                                    pB[0:64, 0:cnt], w2r[:, hc, :], hT[:, hc, n0:n0 + cnt], start=(hc == 0), stop=(hc == 1)),
                                    reads=[W2D, HT], writes=[PB_])
                            S.op("dve", lambda n0=n0, cnt=cnt: nc.vector.tensor_tensor(
                                out=t1[0:64, 0:cnt], in0=pA[0:64, 0:cnt], in1=ccs[:, 0, n0:n0 + cnt], op=ALU.mult),
                                reads=[PA_, CCS], writes=[T1])
                            S.op("dve", lambda n0=n0, cnt=cnt: nc.vector.tensor_tensor(
                                out=t2[0:64, 0:cnt], in0=pB[0:64, 0:cnt], in1=ccs[:, 1, n0:n0 + cnt], op=ALU.mult),
                                reads=[PB_, CCS], writes=[T2])
                            S.op("pool", lambda n0=n0, cnt=cnt, g=g: nc.gpsimd.tensor_tensor(
                                out=kcmpT[:, g, n0:n0 + cnt], in0=t1[0:64, 0:cnt], in1=t2[0:64, 0:cnt], op=ALU.add),
                                reads=[T1, T2], writes=[KCMP])
                    else:
                        for nt in range((NCMP + 127) // 128):
                            c0 = nt * 128
                            cnt = min(128, NCMP - c0)
                            for hc in range(2):
                                S.op("pe", lambda hc=hc, c0=c0, cnt=cnt: nc.tensor.matmul(
                                    pA[0:cnt, 0:64], hT[:, hc, c0:c0 + cnt], w2d[:, hc, 0:64], start=(hc == 0), stop=(hc == 1)),
                                    reads=[W2D, HT], writes=[PA_])
                            S.op("act", lambda nt=nt, cnt=cnt, g=g: nc.scalar.copy(out=vcmp[0:cnt, nt, g, 0:64], in_=pA[0:cnt, 0:64]),
                                 reads=[PA_], writes=[VCMP])
            S.op("pool", lambda: nc.gpsimd.memset(vcmp[:, :, :, 64:65], 1.0), reads=[VCMP], writes=[VCMP])
            S.flush()
        if DBG == 1:
            return
        ks = sb(cx, st, "ks", [128, SEQ], BF16)
        kw = sb(cx, st, "kw", [128, SEQ], BF16)
        vsf = sb(cx, st, "vs", [128, KT * 65 + 64], BF16)
        vwf = sb(cx, st, "vw", [128, KT * 65 + 64], BF16)
        vs = vsf[:, 0:KT * 65].rearrange("p (k c) -> p k c", c=65)
        vw = vwf[:, 0:KT * 65].rearrange("p (k c) -> p k c", c=65)
        KS, KW, VS, VW = S.buf("ks"), S.buf("kw"), S.buf("vs"), S.buf("vw")
        S.op("pool", lambda: nc.gpsimd.memset(ks[64:128, :], 0.0), writes=[KS])
        S.op("pool", lambda: nc.gpsimd.memset(kw[64:128, :], 0.0), writes=[KW])
        S.op("pool", lambda: nc.gpsimd.memset(vsf[:, :], 0.0), writes=[VS])
        S.op("pool", lambda: nc.gpsimd.memset(vwf[:, :], 0.0), writes=[VW])
        S.op("pool", lambda: nc.gpsimd.memset(vs[:, :, 64:65], 1.0), writes=[VS])
        S.op("pool", lambda: nc.gpsimd.memset(vw[:, :, 64:65], 1.0), writes=[VW])
        cmpmask = sb(cx, st, "cmpmask", [128, 33], BF16)
        keep = sb(cx, st, "keep", [128, 9], F32)
        addt = sb(cx, st, "addt", [128, 9], F32)
        caus = sb(cx, st, "caus", [128, 4, 128], BF16)
        wmask = sb(cx, st, "wmask", [128, 8, 128], BF16)
        TB = S.buf("tables")
        S.dma("pool", cmpmask[:], cx.cmpmask_d[:, :], writes=[TB], key="tb")
        S.dma("sp", keep[:], cx.keep_d[:, :], writes=[TB], key="tb")
        S.dma("sp", addt[:], cx.addt_d[:, :], writes=[TB], key="tb")
        S.dma("pool", caus[:], cx.caus_d[:, :, :], writes=[TB], key="tb")
        S.dma("pool", wmask[:], cx.wmask_d[:, :, :], writes=[TB], key="tb")
        qt = [sb(cx, st, "qt", [128, 4, 128], BF16) for _ in range(2)]
        QTB = S.bufs("qt", 2)
        for q_ in range(2):
            S.op("pool", lambda q_=q_: nc.gpsimd.memset(qt[q_][64:128, :, :], 0.0), writes=[QTB[q_]])
        gts = [sb(cx, st, "gts", [128, 12], F32) for _ in range(2)]
        GTS = S.bufs("gts", 2)
        Pf = [sb(cx, st, "Pf", [128, NCP], F32) for _ in range(2)]
        PF = S.bufs("Pf", 2)
        Pb = [sb(cx, st, "Pb", [128, NCP], BF16) for _ in range(4)]
        PBB = S.bufs("Pb", 4)
        for cb in range(4):
            S.op("pool", lambda cb=cb: nc.gpsimd.memset(Pb[cb][:], 0.0), writes=[PBB[cb]])
        PTs = [sb(cx, st, "PTs", [128, 512], BF16) for _ in range(2)]
        PTS = S.bufs("PTs", 2)
        pns = sb(cx, st, "pns", [128, NCP + 8], F32)
        PNS = S.buf("pns")
        S.op("pool", lambda: nc.gpsimd.memset(pns[:], 0.0), writes=[PNS])
        NMX = max(8 * NB, 16)
        imp = sb(cx, st, "imp", [128, NMX], F32)
        imp4 = sb(cx, st, "imp4", [128, NMX], F32)
        work = sb(cx, st, "work", [128, NMX], F32)
        msk = sb(cx, st, "msk", [128, ((NMX + 127) // 128) * 128], BF16)
        m8a = sb(cx, st, "m8a", [128, 8], F32)
        m8b = sb(cx, st, "m8b", [128, 8], F32)
        thr = sb(cx, st, "thr", [128, 1], F32)
        IMP, MSK = S.buf("imp"), S.buf("msk")
        ebig = sb(cx, st, "ebig", [128, 8192], BF16)
        S.dma("pool", ebig[:], cx.ebig_d[:, :], writes=[TB], key="tb")
        mskT = sb(cx, st, "mskT", [128, 2, 128], BF16)
        MSKX = S.buf("mskT")
        S.op("dve", lambda: nc.vector.memset(msk[:], 0.0), writes=[MSK])
        smx = sb(cx, st, "smx", [128, 4], F32)
        snb = sb(cx, st, "snb", [128, 1], F32)
        rsum = sb(cx, st, "rsum", [128, 4], F32)
        rinv = sb(cx, st, "rinv", [128, 1], F32)
        SMX = S.buf("smx")
        E = [sb(cx, st, "E", [128, 512], BF16) for _ in range(6)]
        EB = S.bufs("E", 6)
        PM = [sb(cx, st, "PM", [128, 512], BF16) for _ in range(6)]
        PMB = S.bufs("PM", 6)
        cm = [sb(cx, st, "cm", [128, 128], BF16) for _ in range(4)]
        CM = S.bufs("cm", 4)
        OT = [sb(cx, st, "OT", [65, 512], F32) for _ in range(3)]
        OTB = S.bufs("OT", 3)
        rs4 = sb(cx, st, "rs4", [128, 4], F32)
        ri4 = sb(cx, st, "ri4", [128, 4], F32)
        fac = sb(cx, st, "fac", [128, 4], F32)
        tmp = sb(cx, st, "tmp", [128, 4, 64], F32)
        oacc = sb(cx, st, "oacc", [128, 4, 64], F32)
        ob = [sb(cx, st, "ob", [128, 4, 64], BF16) for _ in range(2)]
        CMB, OBB = S.buf("comb"), S.bufs("ob", 2)
        bank = [ps(cx, st, "bk", [128, 512], F32) for _ in range(8)]
        BK = S.bufs("bk", 8)
        SS = [0, 1, 7, 3]
        B_PT, B_OC, B_OS, B_OW, B_CT = 2, 3, 4, 5, 6
        ptps = bank[B_PT][:].bitcast(BF16)
        MT = [BK[B_PT], BK[B_CT]]
        ctps = bank[B_CT][:].bitcast(BF16)
        ecnt = [0]

        def job_front(j):
            e = ecnt[0] % 6
            sb_ = SS[ecnt[0] % 4]
            ecnt[0] += 1
            j["e"] = e
            kT, KB, kt, q2 = j["kT"], j["KB"], j["kt"], j["q2"]
            if j["pre"] is not None:
                j["pre"]()
            S.op("pe", lambda sb_=sb_, kt=kt, q2=q2, kT=kT: nc.tensor.matmul(
                bank[sb_][:, :], kT[:, kt * 128:(kt + 1) * 128], qt[q2][:, :, :].rearrange("p h q -> p (h q)"),
                start=True, stop=True), reads=[KB, QTB[q2]], writes=[BK[sb_]])
            S.op("act", lambda e=e, sb_=sb_: nc.scalar.activation(out=E[e][:], in_=bank[sb_][:], func=AF.Exp, scale=0.125),
                 reads=[BK[sb_]], writes=[EB[e]])
            mfn = j["mask"]
            S.op("dve", lambda e=e, mfn=mfn: nc.vector.tensor_tensor(
                out=PM[e][:].rearrange("p (c q) -> p c q", c=4), in0=E[e][:].rearrange("p (c q) -> p c q", c=4),
                in1=mfn().unsqueeze(1).broadcast_to([128, 4, 128]), op=ALU.mult),
                reads=[EB[e]] + j["mbufs"], writes=[PMB[e]])

        def job_back(j):
            e, kt, vT, VB, o_bank, first, last = j["e"], j["kt"], j["vT"], j["VB"], j["o_bank"], j["first"], j["last"]
            S.op("pe", lambda e=e, kt=kt, first=first, last=last, vT=vT, o_bank=o_bank: nc.tensor.matmul(
                bank[o_bank][:, :], vT[:, kt * 65:kt * 65 + 128], PM[e][:], start=first, stop=last),
                reads=[VB, PMB[e]], writes=[BK[o_bank]])

        def run_jobs(jobs, depth=None):
            depth = PIPE_DEPTH if depth is None else depth
            n = len(jobs)
            for step in range(n + depth):
                if step < n:
                    job_front(jobs[step])
                if step >= depth:
                    job_back(jobs[step - depth])

        try:
            ucnt = 0
            for g in range(4):
                for r_ in range(4):
                    S.dma("sp", ks[0:64, :].rearrange("p (i r q) -> p i r q", r=4, q=128)[:, :, r_, :],
                          al.feat("KsT", r_, g).rearrange("p (i q) -> p i q", q=128), writes=[KS], key="ks")
                    S.dma("sp", kw[0:64, :].rearrange("p (i r q) -> p i r q", r=4, q=128)[:, :, r_, :],
                          al.feat("KwT", r_, g).rearrange("p (i q) -> p i q", q=128), writes=[KW], key="kw")
                    for hf in range(2):
                        isl = slice(hf * NB // 2, (hf + 1) * NB // 2)
                        S.dma("sp", vs[:, :, 0:64].rearrange("q (i r) d -> q i r d", r=4)[:, isl, r_, :],
                              al.tok("Vs", r_, hf)[:, 64 * g:64 * g + 64].rearrange("(i q) d -> q i d", q=128),
                              writes=[VS], key="vs")
                        S.dma("sp", vw[:, :, 0:64].rearrange("q (i r) d -> q i r d", r=4)[:, isl, r_, :],
                              al.tok("Vw", r_, hf)[:, 64 * g:64 * g + 64].rearrange("(i q) d -> q i d", q=128),
                              writes=[VW], key="vw")
                if DBG == 2:
                    S.flush()
                    return
                for i in range(NB):
                    q2 = ucnt % 2
                    ucnt += 1
                    S.dma("sp", qt[q2][0:64, :, :], QT_d[:, 4 * g:4 * g + 4, i * 128:(i + 1) * 128], writes=[QTB[q2]], key="qt%d" % q2)
                    S.dma("sp", gts[q2][:], gates_d[i * 128:(i + 1) * 128, 12 * g:12 * g + 12], writes=[GTS[q2]], key="gts%d" % q2)
                    ncols = 32 * i + 31
                    ctiles = [(c0, min(c0 + 512, ncols)) for c0 in range(0, ncols, 512)]
                    m0 = ncols - 33
                    for cb in range(4):
                        pf = Pf[cb % 2]
                        PFB = PF[cb % 2]
                        for ci, (c0, c1) in enumerate(ctiles):
                            ov0, ov1 = max(c0, m0), c1
                            has_mask = ov1 > ov0
                            S.op("pe", lambda cb=cb, ci=ci, c0=c0, c1=c1, q2=q2, has_mask=has_mask, g=g: nc.tensor.matmul(
                                bank[ci][:, 0:c1 - c0], qt[q2][0:64, cb, :], kcmpT[:, g, c0:c1],
                                start=True, stop=(not has_mask)), reads=[QTB[q2], KCMP], writes=[BK[ci]])
                            if has_mask:
                                S.op("pe", lambda ci=ci, c0=c0, ov0=ov0, ov1=ov1, m0=m0: nc.tensor.matmul(
                                    bank[ci][:, ov0 - c0:ov1 - c0], ident[:], cmpmask[:, ov0 - m0:ov1 - m0], start=False, stop=True),
                                    reads=[IB, TB], writes=[BK[ci]])
                            S.op("dve", lambda ci=ci, c0=c0, c1=c1: nc.vector.reduce_max(
                                out=smx[:, ci:ci + 1], in_=bank[ci][:, 0:c1 - c0], axis=AX.X), reads=[BK[ci]], writes=[SMX])
                        if len(ctiles) == 2:
                            S.op("dve", lambda: nc.vector.tensor_tensor(out=smx[:, 0:1], in0=smx[:, 0:1], in1=smx[:, 1:2], op=ALU.max),
                                 reads=[SMX], writes=[SMX])
                        S.op("dve", lambda: nc.vector.tensor_scalar(out=snb[:], in0=smx[:, 0:1], scalar1=-1000.0, scalar2=-0.125,
                                                                    op0=ALU.max, op1=ALU.mult), reads=[SMX], writes=[SMX])
                        S.op("dve", lambda: nc.vector.memset(rsum[:, 2:4], 0.0), reads=[SMX], writes=[SMX])
                        for ci, (c0, c1) in enumerate(ctiles):
                            S.op("act", lambda ci=ci, c0=c0, c1=c1, pf=pf: nc.scalar.activation(
                                out=pf[:, c0:c1], in_=bank[ci][:, 0:c1 - c0], func=AF.Exp, bias=snb[:, 0:1], scale=0.125,
                                accum_out=rsum[:, 2 + ci:3 + ci]), reads=[BK[ci], SMX], writes=[PFB, SMX])
                        if len(ctiles) == 2:
                            S.op("dve", lambda: nc.vector.tensor_tensor(out=rsum[:, 2:3], in0=rsum[:, 2:3], in1=rsum[:, 3:4], op=ALU.add),
                                 reads=[SMX], writes=[SMX])
                        S.op("dve", lambda: nc.vector.tensor_scalar(out=rsum[:, 0:1], in0=rsum[:, 2:3], scalar1=1e-30, scalar2=None,
                                                                    op0=ALU.max), reads=[SMX], writes=[SMX])
                        S.op("dve", lambda: nc.vector.reciprocal(out=rinv[:], in_=rsum[:, 0:1]), reads=[SMX], writes=[SMX])
                        if cb == 0:
                            S.op("dve", lambda pf=pf, ncols=ncols: nc.vector.tensor_scalar(
                                out=pns[:, 1:1 + ncols], in0=pf[:, 0:ncols], scalar1=rinv[:, 0:1], scalar2=None, op0=ALU.mult),
                                reads=[PFB, SMX], writes=[PNS])
                        else:
                            S.op("dve", lambda pf=pf, ncols=ncols: nc.vector.scalar_tensor_tensor(
                                out=pns[:, 1:1 + ncols], in0=pf[:, 0:ncols], scalar=rinv[:, 0:1], in1=pns[:, 1:1 + ncols],
                                op0=ALU.mult, op1=ALU.add), reads=[PFB, SMX, PNS], writes=[PNS])
                        S.op("dve", lambda cb=cb, pf=pf, ncols=ncols: nc.vector.tensor_copy(out=Pb[cb][:, 0:ncols], in_=pf[:, 0:ncols]),
                             reads=[PFB], writes=[PBB[cb]])
                        if g > 0 and i == 0:
                            S.op("pool", lambda cb=cb, ncols=ncols: nc.gpsimd.memset(Pb[cb][:, ncols:NCP], 0.0), writes=[PBB[cb]])
                    if g > 0 and i == 0:
                        S.op("pool", lambda ncols=ncols: nc.gpsimd.memset(pns[:, 1 + ncols:NCP + 8], 0.0), reads=[PNS], writes=[PNS])
                    nnt = (ncols + 127) // 128
                    for nt in range(nnt):
                        pv_, PVB_ = (ptps, BK[B_PT]) if nt % 2 == 0 else (ctps, BK[B_CT])
                        for cb in range(4):
                            S.op("pe", lambda nt=nt, cb=cb, pv_=pv_: nc.tensor.transpose(
                                out=pv_[:, cb * 128:(cb + 1) * 128], in_=Pb[cb][:, nt * 128:(nt + 1) * 128], identity=ident[:]),
                                reads=[PBB[cb], IB], writes=[PVB_])
                        S.op("act", lambda nt=nt, pv_=pv_: nc.scalar.copy(out=PTs[nt % 2][:], in_=pv_[:, 0:512]),
                             reads=[PVB_], writes=[PTS[nt % 2]])
                        S.op("pe", lambda nt=nt, nnt=nnt, g=g: nc.tensor.matmul(
                            bank[B_OC][0:65, :], vcmp[:, nt, g, :], PTs[nt % 2][:], start=(nt == 0), stop=(nt == nnt - 1)),
                            reads=[VCMP, PTS[nt % 2]], writes=[BK[B_OC]])
                    S.op("act", lambda: nc.scalar.copy(out=OT[0][:], in_=bank[B_OC][0:65, :]), reads=[BK[B_OC]], writes=[OTB[0]])
                    if DBG == 3:
                        S.flush()
                        return
                    nm = 8 * i + 8
                    Wd = max(nm, 16)
                    S.op("dve", lambda nm=nm: nc.vector.tensor_reduce(
                        out=imp4[:, 0:nm], in_=pns[:, 0:4 * nm].rearrange("p (m r) -> p m r", r=4), axis=AX.X, op=ALU.add),
                        reads=[PNS], writes=[IMP])
                    S.op("dve", lambda nm=nm: nc.vector.tensor_tensor(
                        out=imp[:, 0:nm], in0=imp4[:, 0:nm], in1=pns[:, 4:4 * nm + 1:4], op=ALU.add), reads=[PNS, IMP], writes=[IMP])
                    r0 = max(8 * i - 1, 0)
                    tcol = r0 - (8 * i - 1)
                    S.op("dve", lambda r0=r0, nm=nm, tcol=tcol: nc.vector.tensor_tensor(
                        out=imp[:, r0:nm], in0=imp[:, r0:nm], in1=keep[:, tcol:9], op=ALU.mult), reads=[IMP, TB], writes=[IMP])
                    S.op("dve", lambda r0=r0, nm=nm, tcol=tcol: nc.vector.tensor_tensor(
                        out=imp[:, r0:nm], in0=imp[:, r0:nm], in1=addt[:, tcol:9], op=ALU.add), reads=[IMP, TB], writes=[IMP])
                    S.op("dve", lambda: nc.vector.memset(imp[:, 0:1], 3e9), reads=[IMP], writes=[IMP])
                    if nm < 16:
                        S.op("dve", lambda nm=nm: nc.vector.memset(imp[:, nm:16], -1e9), reads=[IMP], writes=[IMP])
                    S.op("dve", lambda Wd=Wd: nc.vector.max(out=m8a[:], in_=imp[:, 0:Wd]), reads=[IMP], writes=[IMP])
                    S.op("dve", lambda Wd=Wd: nc.vector.match_replace(out=work[:, 0:Wd], in_to_replace=m8a[:], in_values=imp[:, 0:Wd],
                                                                        imm_value=-3e38), reads=[IMP], writes=[IMP])
                    S.op("dve", lambda Wd=Wd: nc.vector.max(out=m8b[:], in_=work[:, 0:Wd]), reads=[IMP], writes=[IMP])
                    S.op("dve", lambda: nc.vector.tensor_reduce(out=thr[:], in_=m8b[:], axis=AX.X, op=ALU.min), reads=[IMP], writes=[IMP])
                    S.op("dve", lambda Wd=Wd: nc.vector.tensor_scalar(out=msk[:, 0:Wd], in0=imp[:, 0:Wd], scalar1=thr[:, 0:1], scalar2=None,
                                                                       op0=ALU.is_ge), reads=[IMP], writes=[MSK])
                    if DBG == 4:
                        S.flush()
                        return
                    nkt = 4 * i + 4
                    for mt in range((nm + 127) // 128):
                        S.op("pe", lambda mt=mt: nc.tensor.transpose(out=ptps[:, mt * 128:(mt + 1) * 128],
                                                                     in_=msk[:, mt * 128:(mt + 1) * 128], identity=ident[:]),
                             reads=[MSK, IB], writes=[BK[B_PT]])
                    for mt in range((nm + 127) // 128):
                        S.op("act", lambda mt=mt: nc.scalar.copy(out=mskT[:, mt, :], in_=ptps[:, mt * 128:(mt + 1) * 128]),
                             reads=[BK[B_PT]], writes=[MSKX])
                    jobs = []
                    for kt in range(nkt):
                        ms = kt % 2
                        mt_ap = bank[B_PT][:, 256:384] if ms == 0 else bank[B_CT][:, 0:128]

                        def pre(kt=kt, ms=ms, mt_ap=mt_ap, i=i):
                            mo = (2 * kt) % 128
                            S.op("pe", lambda: nc.tensor.matmul(mt_ap, ebig[:, mo * 64:mo * 64 + 128], mskT[:, (2 * kt) // 128, :],
                                                                start=True, stop=True), reads=[MSKX, TB], writes=[MT[ms]])
                            if kt >= 4 * i:
                                S.op("dve", lambda: nc.vector.tensor_tensor(out=cm[ms][:], in0=mt_ap, in1=caus[:, kt - 4 * i, :],
                                                                            op=ALU.mult), reads=[MT[ms], TB], writes=[CM[ms]])
                        if kt >= 4 * i:
                            mfn, mb = (lambda ms=ms: cm[ms][:]), [CM[ms]]
                        else:
                            mfn, mb = (lambda mt_ap=mt_ap: mt_ap), [MT[ms]]
                        jobs.append(dict(kT=ks, vT=vsf, KB=KS, VB=VS, kt=kt, q2=q2, pre=pre, mask=mfn, mbufs=mb,
                                         o_bank=B_OS, first=(kt == 0), last=(kt == nkt - 1)))
                    wl = [kr for kr in range(-4, 4) if 4 * i + kr >= 0]
                    for idx, kr in enumerate(wl):
                        jobs.append(dict(kT=kw, vT=vwf, KB=KW, VB=VW, kt=4 * i + kr, q2=q2, pre=None,
                                         mask=(lambda kr=kr: wmask[:, kr + 4, :]), mbufs=[TB],
                                         o_bank=B_OW, first=(idx == 0), last=(idx == len(wl) - 1)))
                    run_jobs(jobs)
                    if DBG == 6:
                        S.flush()
                        return
                    gv = gts[q2][:].rearrange("q (h t) -> q h t", t=3)
                    for br, ob_ in enumerate((B_OC, B_OS, B_OW)):
                        if br > 0:
                            S.op("act", lambda br=br, ob_=ob_: nc.scalar.copy(out=OT[br][:], in_=bank[ob_][0:65, :]),
                                 reads=[BK[ob_]], writes=[OTB[br]])
                        for cb in range(4):
                            S.op("pe", lambda br=br, cb=cb: nc.tensor.transpose(
                                out=bank[B_CT][:, cb * 65:(cb + 1) * 65], in_=OT[br][0:65, cb * 128:(cb + 1) * 128],
                                identity=identf[0:65, 0:65]), reads=[OTB[br], IB], writes=[BK[B_CT]])
                        ctv = bank[B_CT][:, 0:260].rearrange("q (c e) -> q c e", e=65)
                        S.op("dve", lambda ctv=ctv: nc.vector.tensor_scalar(out=rs4[:], in0=ctv[:, :, 64], scalar1=1e-30, scalar2=None,
                                                                            op0=ALU.max), reads=[BK[B_CT]], writes=[CMB])
                        S.op("dve", lambda: nc.vector.reciprocal(out=ri4[:], in_=rs4[:]), reads=[CMB], writes=[CMB])
                        S.op("dve", lambda br=br, gv=gv: nc.vector.tensor_tensor(
                            out=fac[:], in0=ri4[:], in1=gv[:, :, br], op=ALU.mult), reads=[CMB, GTS[q2]], writes=[CMB])
                        dst = oacc if br == 0 else tmp
                        S.op("dve", lambda ctv=ctv, dst=dst: nc.vector.tensor_tensor(
                            out=dst[:], in0=ctv[:, :, 0:64], in1=fac[:].unsqueeze(2).broadcast_to([128, 4, 64]), op=ALU.mult),
                            reads=[BK[B_CT], CMB], writes=[CMB])
                        if br > 0:
                            S.op("dve", lambda: nc.vector.tensor_tensor(out=oacc[:], in0=oacc[:], in1=tmp[:], op=ALU.add),
                                 reads=[CMB], writes=[CMB])
                    S.op("pool", lambda q2=q2: nc.gpsimd.tensor_copy(out=ob[q2][:], in_=oacc[:]), reads=[CMB], writes=[OBB[q2]])
                    S.dma("sp", O_d[i * 128:(i + 1) * 128, 256 * g:256 * (g + 1)], ob[q2][:].rearrange("q h d -> q (h d)"),
                          reads=[OBB[q2]], key="ob%d" % q2)
        except StopBuild:
            pass
        S.flush()


def phase_proj(cx, a_d, xin, xout, w_d, g_d, b_d, NT, c_res):
    nc, S = cx.nc, cx.S
    KC = 8
    eps = LN_EPS / (ALPHA * ALPHA)
    with ExitStack() as st:
        wo = sb(cx, st, "wo", [128, KC, D], BF16)
        WO = S.buf("wo")
        for k in range(KC):
            S.dma("pool", wo[:, k, :], w_d[k * 128:(k + 1) * 128, :], writes=[WO], key="wo")
        gb, GB = load_gb(cx, st, g_d, b_d, "gbp")
        ident, IB = make_ident(cx, st)
        at = [sb(cx, st, "at", [128, D], BF16) for _ in range(2)]
        AT = S.bufs("at", 2)
        xs = [sb(cx, st, "xs", [128, D], F32) for _ in range(2)]
        XS = S.bufs("xs", 2)
        aT = [sb(cx, st, "aT", [128, KC, 128], BF16) for _ in range(2)]
        ATT = S.bufs("aT", 2)
        z = [sb(cx, st, "z", [128, D], F32) for _ in range(2)]
        Z = S.bufs("z", 2)
        st_t = [sb(cx, st, "st", [128, 2, 6], F32) for _ in range(2)]
        mv_t = [sb(cx, st, "mv", [128, 2], F32) for _ in range(2)]
        sd_t = [sb(cx, st, "sd", [128, 1], F32) for _ in range(2)]
        rs_t = [sb(cx, st, "rs", [128, 1], F32) for _ in range(2)]
        STB = S.bufs("stb", 2)
        tp = [ps(cx, st, "tp", [128, D], BF16) for _ in range(2)]
        TP = S.bufs("tp", 2)
        op_ = [ps(cx, st, "o", [128, 512], F32) for _ in range(4)]
        OP = S.bufs("o", 4)
        nchunk = NT // 128

        def load(c):
            S.dma("sp", at[c % 2][:], a_d[c * 128:(c + 1) * 128, :], writes=[AT[c % 2]], key="at%d" % (c % 2))
            S.dma("sp", xs[c % 2][:], xin[c * 128:(c + 1) * 128, :], writes=[XS[c % 2]], key="xsp%d" % (c % 2))

        load(0)
        for c in range(nchunk):
            c2 = c % 2
            if c + 1 < nchunk:
                load(c + 1)
            for k in range(KC):
                S.op("pe", lambda k=k, c2=c2: nc.tensor.transpose(out=tp[c2][:, k * 128:(k + 1) * 128],
                                                                   in_=at[c2][:, k * 128:(k + 1) * 128], identity=ident[:]),
                     reads=[AT[c2], IB], writes=[TP[c2]])
            S.op("act", lambda c2=c2: nc.scalar.copy(out=aT[c2][:], in_=tp[c2][:].rearrange("p (k t) -> p k t", k=KC)),
                 reads=[TP[c2]], writes=[ATT[c2]])
            for n in range(2):
                o_ = 2 * c2 + n
                for k in range(KC):
                    S.op("pe", lambda n=n, k=k, c2=c2, o_=o_: nc.tensor.matmul(
                        op_[o_][:], aT[c2][:, k, :], wo[:, k, n * 512:(n + 1) * 512], start=(k == 0), stop=(k == KC - 1)),
                        reads=[ATT[c2], WO], writes=[OP[o_]])
                S.op("dve", lambda n=n, c2=c2, o_=o_: nc.vector.scalar_tensor_tensor(
                    out=z[c2][:, n * 512:(n + 1) * 512], in0=op_[o_][:], scalar=c_res, in1=xs[c2][:, n * 512:(n + 1) * 512],
                    op0=ALU.mult, op1=ALU.add), reads=[OP[o_], XS[c2]], writes=[Z[c2]])
            ln_epilogue(cx, z[c2], Z[c2], gb, GB, st_t[c2], mv_t[c2], sd_t[c2], rs_t[c2], STB[c2], eps)
            S.dma("sp", xout[c * 128:(c + 1) * 128, :], z[c2][:], reads=[Z[c2]], key="zp%d" % c2)
        S.flush()


def build_test_ffn(NT):
    nc = bass.Bass("TRN2", target_bir_lowering=False)
    with ExitStack() as stack:
        cx = Ctx(nc, stack)
        x = nc.dram_tensor("x", [NT, D], F32, kind="ExternalInput").ap()
        w_in = nc.dram_tensor("w_in", [D, 2 * DFF], F32, kind="ExternalInput").ap()
        w_out = nc.dram_tensor("w_out", [DFF, D], F32, kind="ExternalInput").ap()
        g = nc.dram_tensor("g", [D], F32, kind="ExternalInput").ap()
        b = nc.dram_tensor("b", [D], F32, kind="ExternalInput").ap()
        cx.ident_d = nc.dram_tensor("ident", [128, 128], F32, kind="ExternalInput").ap()
        y = nc.dram_tensor("y", [NT, D], F32, kind="ExternalOutput").ap()
        phase_ffn(cx, x, y, w_in, w_out, g, b, NT, "f")
        print("instructions:", cx.S.n_inst)
    return nc


def build_test_gmlp(NT):
    nc = bass.Bass("TRN2", target_bir_lowering=False)
    with ExitStack() as stack:
        cx = Ctx(nc, stack)
        x = nc.dram_tensor("x", [NT, D], F32, kind="ExternalInput").ap()
        w_in = nc.dram_tensor("w_in", [D, 2 * GW], F32, kind="ExternalInput").ap()
        lng = nc.dram_tensor("lng", [GW], F32, kind="ExternalInput").ap()
        lnb = nc.dram_tensor("lnb", [GW], F32, kind="ExternalInput").ap()
        ws = nc.dram_tensor("ws", [16, 128, 128], F32, kind="ExternalInput").ap()
        bs = nc.dram_tensor("bs", [16, 128], F32, kind="ExternalInput").ap()
        w_out = nc.dram_tensor("w_out", [GW, D], F32, kind="ExternalInput").ap()
        g = nc.dram_tensor("g", [D], F32, kind="ExternalInput").ap()
        b = nc.dram_tensor("b", [D], F32, kind="ExternalInput").ap()
        cx.ident_d = nc.dram_tensor("ident", [128, 128], F32, kind="ExternalInput").ap()
        cx.tril_d = nc.dram_tensor("tril", [128, 128], F32, kind="ExternalInput").ap()
        uT_d = nc.dram_tensor("uT_d", [GW, NT], BF16, kind="Internal").ap()
        vln_d = nc.dram_tensor("vln_d", [NT, GW], BF16, kind="Internal").ap()
        y = nc.dram_tensor("y", [NT, D], F32, kind="ExternalOutput").ap()
        phase_g1(cx, x, uT_d, vln_d, w_in, lng, lnb, NT)
        phase_g2(cx, x, y, uT_d, vln_d, ws, bs, w_out, g, b, NT)
        print("instructions:", cx.S.n_inst)
    return nc


def nsa_dram(nc, NT, kind_local="Internal"):
    d = {}
    d["QT"] = nc.dram_tensor("QT_d", [64, 16, NT], BF16, kind=kind_local).ap()
    d["KsT"] = nc.dram_tensor("KsT_d", [64, 4, NT], BF16, kind=kind_local).ap()
    d["KwT"] = nc.dram_tensor("KwT_d", [64, 4, NT], BF16, kind=kind_local).ap()
    d["KcT"] = nc.dram_tensor("KcT_d", [64, 4, NT], BF16, kind=kind_local).ap()
    d["VcT"] = nc.dram_tensor("VcT_d", [64, 4, NT], BF16, kind=kind_local).ap()
    d["Vs"] = nc.dram_tensor("Vs_d", [NT, 256], BF16, kind=kind_local).ap()
    d["Vw"] = nc.dram_tensor("Vw_d", [NT, 256], BF16, kind=kind_local).ap()
    d["gates"] = nc.dram_tensor("gates_d", [NT, 48], F32, kind=kind_local).ap()
    return d


def local_from_dict(d):
    return NsaLocal(lambda k, g: d[k][:, g, :], lambda k, t0: d[k][t0:t0 + 128, :], d["QT"], d["gates"])


def gathered_from_dict(al, NT):
    return NsaGathered(lambda k, r, g: al[k][r][:, g, :], lambda k, r, hf: al[k][r][hf * NT // 2:(hf + 1) * NT // 2, :])


def build_test_n1(NT):
    nc = bass.Bass("TRN2", target_bir_lowering=False)
    with ExitStack() as stack:
        cx = Ctx(nc, stack)
        x = nc.dram_tensor("x", [NT, D], F32, kind="ExternalInput").ap()
        w_in = nc.dram_tensor("w_in", [D, NSA_COLS], F32, kind="ExternalInput").ap()
        cos_d = nc.dram_tensor("cos", [128, NT], F32, kind="ExternalInput").ap()
        sin_d = nc.dram_tensor("sin", [128, NT], F32, kind="ExternalInput").ap()
        cx.ident_d = nc.dram_tensor("ident", [128, 128], F32, kind="ExternalInput").ap()
        d = nsa_dram(nc, NT, "ExternalOutput")
        phase_n1(cx, x, w_in, cos_d, sin_d, local_from_dict(d), NT)
        print("instructions:", cx.S.n_inst)
    return nc


def rope_tables(pos):
    half = HD // 2
    freq = (10000.0 ** (-np.arange(half, dtype=np.float32) / half)).astype(np.float32)
    ang = pos.astype(np.float32)[None, :] * freq[:, None]
    c = np.cos(ang).astype(np.float32)
    s_ = np.sin(ang).astype(np.float32)
    return np.ascontiguousarray(np.tile(c, (4, 1))), np.ascontiguousarray(np.tile(s_, (4, 1)))


def attn_tables(r, NB):
    NT = NB * 128
    SEQ = 4 * NT
    NCP = SEQ // 16
    q = np.arange(128)
    t = {}
    pos = ((4 * np.arange(NB)[:, None] + r) * 128 + q[None, :]).reshape(-1)
    t["cos"], t["sin"] = rope_tables(pos)
    cpos = 16 * np.arange(NCP) + 31
    t["cosc"], t["sinc"] = rope_tables(cpos)
    m = np.arange(-2, 31)
    vis = m[None, :] <= (8 * r + np.floor((q[:, None] - 31) / 16.0))
    t["cmpmask"] = np.where(vis, 0.0, NEGM).astype(np.float32)
    mrel = np.arange(-1, 8)[None, :]
    cur = (2 * r + (q >= 64).astype(np.int64))[:, None]
    keep = np.ones((128, 9), np.float32)
    addt = np.zeros((128, 9), np.float32)
    fut = mrel > cur
    keep[fut] = 0.0
    addt[fut] = -1e9
    c1 = mrel == cur - 1
    keep[c1] = 0.0
    addt[c1] = 1e9
    c0 = mrel == cur
    keep[c0] = 0.0
    addt[c0] = 2e9
    t["keep"], t["addt"] = keep, addt
    k = np.arange(128)[:, None, None]
    kr = np.arange(4)[None, :, None]
    qq = q[None, None, :]
    t["caus"] = ((128 * (kr - r) + k) <= qq).astype(np.float32)
    kr8 = np.arange(-4, 4)[None, :, None]
    dl = r - kr8
    wm = np.where(dl == 0, k <= qq, np.where((dl >= 1) & (dl <= 3), True, np.where(dl == 4, k > qq, False)))
    t["wmask"] = np.broadcast_to(wm, (128, 8, 128)).astype(np.float32)
    t["ident"] = np.eye(128, dtype=np.float32)
    t["ebig"] = (np.arange(8192)[None, :] // 64 == np.arange(128)[:, None]).astype(np.float32)
    t["tril"] = np.tril(np.ones((128, 128), np.float32))
    return t


def declare_tables(cx, nc, NB):
    NT = NB * 128
    NCP = 4 * NT // 16
    cx.ident_d = nc.dram_tensor("ident", [128, 128], F32, kind="ExternalInput").ap()
    cx.tril_d = nc.dram_tensor("tril", [128, 128], F32, kind="ExternalInput").ap()
    cx.cos_d = nc.dram_tensor("cos", [128, NT], F32, kind="ExternalInput").ap()
    cx.sin_d = nc.dram_tensor("sin", [128, NT], F32, kind="ExternalInput").ap()
    cx.cosc_d = nc.dram_tensor("cosc", [128, NCP], F32, kind="ExternalInput").ap()
    cx.sinc_d = nc.dram_tensor("sinc", [128, NCP], F32, kind="ExternalInput").ap()
    cx.cmpmask_d = nc.dram_tensor("cmpmask", [128, 33], F32, kind="ExternalInput").ap()
    cx.keep_d = nc.dram_tensor("keep", [128, 9], F32, kind="ExternalInput").ap()
    cx.addt_d = nc.dram_tensor("addt", [128, 9], F32, kind="ExternalInput").ap()
    cx.caus_d = nc.dram_tensor("caus", [128, 4, 128], F32, kind="ExternalInput").ap()
    cx.wmask_d = nc.dram_tensor("wmask", [128, 8, 128], F32, kind="ExternalInput").ap()
    cx.ebig_d = nc.dram_tensor("ebig", [128, 8192], F32, kind="ExternalInput").ap()


def gathered_dram(nc, NT, kind):
    al = {}
    al["KsT"] = nc.dram_tensor("KsT_all", [4, 64, 4, NT], BF16, kind=kind).ap()
    al["KwT"] = nc.dram_tensor("KwT_all", [4, 64, 4, NT], BF16, kind=kind).ap()
    al["KcT"] = nc.dram_tensor("KcT_all", [4, 64, 4, NT], BF16, kind=kind).ap()
    al["VcT"] = nc.dram_tensor("VcT_all", [4, 64, 4, NT], BF16, kind=kind).ap()
    al["Vs"] = nc.dram_tensor("Vs_all", [4, NT, 256], BF16, kind=kind).ap()
    al["Vw"] = nc.dram_tensor("Vw_all", [4, NT, 256], BF16, kind=kind).ap()
    return al


def build_test_attn(NB):
    NT = NB * 128
    nc = bass.Bass("TRN2", target_bir_lowering=False)
    with ExitStack() as stack:
        cx = Ctx(nc, stack)
        declare_tables(cx, nc, NB)
        QT = nc.dram_tensor("QT_d", [64, 16, NT], BF16, kind="ExternalInput").ap()
        gates = nc.dram_tensor("gates_d", [NT, 48], F32, kind="ExternalInput").ap()
        al = gathered_dram(nc, NT, "ExternalInput")
        w1k = nc.dram_tensor("w1k", [32, 64, 256], F32, kind="ExternalInput").ap()
        w2k = nc.dram_tensor("w2k", [256, 64], F32, kind="ExternalInput").ap()
        pek = nc.dram_tensor("pek", [32, 64], F32, kind="ExternalInput").ap()
        w1v = nc.dram_tensor("w1v", [32, 64, 256], F32, kind="ExternalInput").ap()
        w2v = nc.dram_tensor("w2v", [256, 64], F32, kind="ExternalInput").ap()
        pev = nc.dram_tensor("pev", [32, 64], F32, kind="ExternalInput").ap()
        O_d = nc.dram_tensor("O_d", [NT, 1024], BF16, kind="ExternalOutput").ap()
        phase_attn(cx, gathered_from_dict(al, NT), QT, gates, O_d, w1k, w2k, pek, w1v, w2v, pev, NB)
        print("instructions:", cx.S.n_inst)
    return nc


L0_W = ["l0_ffn1_w_in", "l0_ffn1_w_out", "l0_ln1_g", "l0_ln1_b", "l0_gm_w_in", "l0_gm_ln_g", "l0_gm_ln_b", "l0_gm_w_s",
        "l0_gm_b_s", "l0_gm_w_out", "l0_ln2_g", "l0_ln2_b", "l0_ffn2_w_in", "l0_ffn2_w_out", "l0_ln3_g", "l0_ln3_b"]
L1A_W = ["l1_ffn1_w_in", "l1_ffn1_w_out", "l1_ln1_g", "l1_ln1_b", "l1_nsa_w_in"]
L1B_W = ["l1_nsa_cmp_pe_k", "l1_nsa_cmp_w1_k", "l1_nsa_cmp_w2_k", "l1_nsa_cmp_pe_v", "l1_nsa_cmp_w1_v", "l1_nsa_cmp_w2_v",
         "l1_nsa_w_out", "l1_ln2_g", "l1_ln2_b", "l1_ffn2_w_in", "l1_ffn2_w_out", "l1_ln3_g", "l1_ln3_b"]
W_SHAPES = {
    "ffn1_w_in": [D, 2 * DFF], "ffn2_w_in": [D, 2 * DFF], "ffn1_w_out": [DFF, D], "ffn2_w_out": [DFF, D],
    "gm_w_in": [D, 2 * GW], "gm_ln_g": [GW], "gm_ln_b": [GW], "gm_w_s": [16, 128, 128], "gm_b_s": [16, 128], "gm_w_out": [GW, D],
    "nsa_w_in": [D, NSA_COLS], "nsa_cmp_pe_k": [32, 64], "nsa_cmp_w1_k": [32, 64, 256], "nsa_cmp_w2_k": [256, 64],
    "nsa_cmp_pe_v": [32, 64], "nsa_cmp_w1_v": [32, 64, 256], "nsa_cmp_w2_v": [256, 64], "nsa_w_out": [D, D],
}


def wshape(name):
    base = name[3:]
    if base in W_SHAPES:
        return W_SHAPES[base]
    return [D]


def declare_w(nc, names):
    return {n: nc.dram_tensor(n, wshape(n), F32, kind="ExternalInput").ap() for n in names}


def emit_part_a(cx, nc, w, x, NT, xa, xb, uT_d, vln_d, nd):
    phase_ffn(cx, x, xa, w["l0_ffn1_w_in"], w["l0_ffn1_w_out"], w["l0_ln1_g"], w["l0_ln1_b"], NT, "a")
    phase_g1(cx, xa, uT_d, vln_d, w["l0_gm_w_in"], w["l0_gm_ln_g"], w["l0_gm_ln_b"], NT)
    phase_g2(cx, xa, xb, uT_d, vln_d, w["l0_gm_w_s"], w["l0_gm_b_s"], w["l0_gm_w_out"], w["l0_ln2_g"], w["l0_ln2_b"], NT)
    phase_ffn(cx, xb, xa, w["l0_ffn2_w_in"], w["l0_ffn2_w_out"], w["l0_ln3_g"], w["l0_ln3_b"], NT, "b")
    phase_ffn(cx, xa, xb, w["l1_ffn1_w_in"], w["l1_ffn1_w_out"], w["l1_ln1_g"], w["l1_ln1_b"], NT, "c")
    phase_n1(cx, xb, w["l1_nsa_w_in"], cx.cos_d, cx.sin_d, nd, NT)
    return xb


def emit_part_b(cx, nc, w, xmid, y, NT, NB, al, QT, gates, O_d, xa):
    phase_attn(cx, al, QT, gates, O_d, w["l1_nsa_cmp_w1_k"], w["l1_nsa_cmp_w2_k"], w["l1_nsa_cmp_pe_k"],
               w["l1_nsa_cmp_w1_v"], w["l1_nsa_cmp_w2_v"], w["l1_nsa_cmp_pe_v"], NB)
    phase_proj(cx, O_d, xmid, xa, w["l1_nsa_w_out"], w["l1_ln2_g"], w["l1_ln2_b"], NT, 1.0 / ALPHA)
    phase_ffn(cx, xa, y, w["l1_ffn2_w_in"], w["l1_ffn2_w_out"], w["l1_ln3_g"], w["l1_ln3_b"], NT, "d")


def build_a(NB):
    NT = NB * 128
    nc = bass.Bass("TRN2", target_bir_lowering=False)
    with ExitStack() as stack:
        cx = Ctx(nc, stack)
        declare_tables(cx, nc, NB)
        w = declare_w(nc, L0_W + L1A_W)
        x = nc.dram_tensor("x", [NT, D], F32, kind="ExternalInput").ap()
        xa = nc.dram_tensor("xa", [NT, D], F32, kind="Internal").ap()
        xb = nc.dram_tensor("xmid", [NT, D], F32, kind="ExternalOutput").ap()
        uT_d = nc.dram_tensor("uT_d", [GW, NT], BF16, kind="Internal").ap()
        vln_d = nc.dram_tensor("vln_d", [NT, GW], BF16, kind="Internal").ap()
        nd = local_from_dict(nsa_dram(nc, NT, "ExternalOutput"))
        emit_part_a(cx, nc, w, x, NT, xa, xb, uT_d, vln_d, nd)
        print("part A instructions:", cx.S.n_inst)
    return nc


def build_b(NB):
    NT = NB * 128
    nc = bass.Bass("TRN2", target_bir_lowering=False)
    with ExitStack() as stack:
        cx = Ctx(nc, stack)
        declare_tables(cx, nc, NB)
        w = declare_w(nc, L1B_W)
        xmid = nc.dram_tensor("xmid", [NT, D], F32, kind="ExternalInput").ap()
        QT = nc.dram_tensor("QT_d", [64, 16, NT], BF16, kind="ExternalInput").ap()
        gates = nc.dram_tensor("gates_d", [NT, 48], F32, kind="ExternalInput").ap()
        al = gathered_dram(nc, NT, "ExternalInput")
        O_d = nc.dram_tensor("O_d", [NT, D], BF16, kind="Internal").ap()
        xa = nc.dram_tensor("xa", [NT, D], F32, kind="Internal").ap()
        y = nc.dram_tensor("y", [NT, D], F32, kind="ExternalOutput").ap()
        emit_part_b(cx, nc, w, xmid, y, NT, NB, gathered_from_dict(al, NT), QT, gates, O_d, xa)
        print("part B instructions:", cx.S.n_inst)
    return nc


TABLE_KEYS = ("ident", "tril", "cos", "sin", "cosc", "sinc", "cmpmask", "keep", "addt", "caus", "wmask", "ebig")
GATHER_KEYS = ("KsT", "KwT", "KcT", "VcT", "Vs", "Vw")


def run_unfused(inputs, NB):
    NT = NB * 128
    x = np.asarray(inputs["x"], dtype=np.float32)
    B = x.shape[0]
    tabs = [attn_tables(c % 4, NB) for c in range(NCORES)]
    wts = {k: np.ascontiguousarray(np.asarray(v, dtype=np.float32)) for k, v in inputs.items() if k != "x"}
    in_a = []
    for c in range(NCORES):
        b, r = c // 4, c % 4
        m = {k: tabs[c][k] for k in TABLE_KEYS}
        m["x"] = np.ascontiguousarray(x[b].reshape(NB, 4, 128, D)[:, r].reshape(NT, D))
        for k in L0_W + L1A_W:
            m[k] = wts[k]
        in_a.append(m)
    res_a = run_bass_kernel_spmd(build_a(NB), in_a, core_ids=list(range(NCORES))).results
    in_b = []
    for c in range(NCORES):
        b = c // 4
        m = {k: tabs[c][k] for k in TABLE_KEYS}
        m["xmid"] = res_a[c]["xmid"]
        m["QT_d"] = res_a[c]["QT_d"]
        m["gates_d"] = res_a[c]["gates_d"]
        for k in GATHER_KEYS:
            m[k + "_all"] = np.ascontiguousarray(np.stack([np.asarray(res_a[4 * b + rr][k + "_d"]) for rr in range(4)]))
        for k in L1B_W:
            m[k] = wts[k]
        in_b.append(m)
    res_b = run_bass_kernel_spmd(build_b(NB), in_b, core_ids=list(range(NCORES))).results
    out = np.zeros((B, 4 * NT, D), np.float32)
    for c in range(NCORES):
        b, r = c // 4, c % 4
        out[b].reshape(NB, 4, 128, D)[:, r] = np.asarray(res_b[c]["y"]).reshape(NB, 128, D)
    return out


def build_fused(NB):
    NT = NB * 128
    nc = bass.Bass("TRN2", target_bir_lowering=False)
    with ExitStack() as stack:
        cx = Ctx(nc, stack)
        declare_tables(cx, nc, NB)
        w = declare_w(nc, L0_W + L1A_W + L1B_W)
        x = nc.dram_tensor("x", [NT, D], F32, kind="ExternalInput").ap()
        y = nc.dram_tensor("y", [NT, D], F32, kind="ExternalOutput").ap()
        xa = nc.dram_tensor("xa", [NT, D], F32, kind="Internal").ap()
        xb = nc.dram_tensor("xb", [NT, D], F32, kind="Internal").ap()
        uT_d = nc.dram_tensor("uT_d", [GW, NT], BF16, kind="Internal").ap()
        vln_d = nc.dram_tensor("vln_d", [NT, GW], BF16, kind="Internal").ap()
        O_d = nc.dram_tensor("O_d", [NT, D], BF16, kind="Internal").ap()
        QT = nc.dram_tensor("QT_d", [64, 16, NT], BF16, kind="Internal").ap()
        gates = nc.dram_tensor("gates_d", [NT, 48], F32, kind="Internal").ap()
        loc, gat = {}, {}
        for k in GATHER_KEYS:
            for hf in range(2):
                loc[k, hf] = nc.dram_tensor("%s_loc%d" % (k, hf), [128, NT], BF16, kind="Internal").ap()
                gat[k, hf] = nc.dram_tensor("%s_gat%d" % (k, hf), [4 * 128, NT], BF16, kind="Internal").ap()
        tokv = lambda a: a.rearrange("r (x c) -> (r x) c", c=256)
        H = NT // 2
        nd = NsaLocal(lambda k, g: loc[k, g // 2][(g % 2) * 64:(g % 2) * 64 + 64, :],
                      lambda k, t0: tokv(loc[k, t0 // H])[t0 % H:t0 % H + 128, :], QT, gates)
        al = NsaGathered(lambda k, r, g: gat[k, g // 2][r * 128 + (g % 2) * 64:r * 128 + (g % 2) * 64 + 64, :],
                         lambda k, r, hf: tokv(gat[k, hf][r * 128:(r + 1) * 128, :]))
        xmid = emit_part_a(cx, nc, w, x, NT, xa, xb, uT_d, vln_d, nd)
        for k in ("KcT", "VcT"):
            for hf in range(2):
                cx.S.cc("AllGather", [[0, 1, 2, 3], [4, 5, 6, 7]], loc[k, hf], gat[k, hf], key="cc_%s%d" % (k, hf))
        cx.S.flush()
        for k in ("KsT", "KwT", "Vs", "Vw"):
            for hf in range(2):
                cx.S.cc("AllGather", [[0, 1, 2, 3], [4, 5, 6, 7]], loc[k, hf], gat[k, hf], key="cc_%s%d" % (k, hf))
        emit_part_b(cx, nc, w, xmid, y, NT, NB, al, QT, gates, O_d, xa)
        print("fused instructions:", cx.S.n_inst)
    return nc


def run_fused(inputs, NB):
    NT = NB * 128
    x = np.asarray(inputs["x"], dtype=np.float32)
    B = x.shape[0]
    wts = {k: np.ascontiguousarray(np.asarray(v, dtype=np.float32)) for k, v in inputs.items() if k != "x"}
    in_maps = []
    for c in range(NCORES):
        b, r = c // 4, c % 4
        tabs = attn_tables(r, NB)
        m = {k: tabs[k] for k in TABLE_KEYS}
        m["x"] = np.ascontiguousarray(x[b].reshape(NB, 4, 128, D)[:, r].reshape(NT, D))
        for k in L0_W + L1A_W + L1B_W:
            m[k] = wts[k]
        in_maps.append(m)
    res = run_bass_kernel_spmd(build_fused(NB), in_maps, core_ids=list(range(NCORES))).results
    out = np.zeros((B, 4 * NT, D), np.float32)
    for c in range(NCORES):
        b, r = c // 4, c % 4
        out[b].reshape(NB, 4, 128, D)[:, r] = np.asarray(res[c]["y"]).reshape(NB, 128, D)
    return out


def kernel(**inputs):
    return run_fused(inputs, 32)
```
